# Optimizing a Trainium2 kernel written in Bass

```python
import math
import jax, jax.numpy as jnp
from jax import lax
import numpy as np

D_MODEL = 1024
BATCH = 8
SEQ = 4096
DEPTH = 4

GRID_W = 64
CTX_LEN = 256
EPS = 1e-6
N_EVEN = (DEPTH + 1) // 2
N_ODD = DEPTH // 2
MIX_WIDTH = D_MODEL

FNET_WIDTH = D_MODEL // 4
FNET_GROUP_DIM = 64
FNET_GROUPS = FNET_WIDTH // FNET_GROUP_DIM
NA_WIDTH = MIX_WIDTH - FNET_WIDTH
NA_HEAD_DIM = 64
NA_HEADS = NA_WIDTH // NA_HEAD_DIM
NA_ROWS = 8
NA_COLS = 16
EVEN_SPLITS = (FNET_WIDTH, FNET_WIDTH, NA_WIDTH, NA_WIDTH, NA_WIDTH, NA_WIDTH)
EVEN_IN = sum(EVEN_SPLITS)

SGU_CHUNK = 128
SGU_WIDTH = MIX_WIDTH // 2
SGU_GROUP_DIM = 128
SGU_GROUPS = SGU_WIDTH // SGU_GROUP_DIM
S5_WIDTH = MIX_WIDTH - SGU_WIDTH
S5_GROUP_DIM = 16
S5_GROUPS = S5_WIDTH // S5_GROUP_DIM
S5_STATE = 64
ODD_SPLITS = (SGU_WIDTH, SGU_WIDTH, SGU_WIDTH, S5_WIDTH, S5_WIDTH)
ODD_IN = sum(ODD_SPLITS)

kernel_name = "hybrid_fnet_natten_gmlp_s5_dit"


def split_cols(p, sizes):
    idx = np.cumsum(sizes)[:-1].tolist()
    return jnp.split(p, idx, axis=-1)


def rms_norm(x, g):
    xf = x.astype(jnp.float32)
    y = xf * lax.rsqrt(jnp.mean(xf * xf, axis=-1, keepdims=True) + EPS)
    return (y * g.astype(jnp.float32)).astype(x.dtype)


def layer_norm(x, g):
    xf = x.astype(jnp.float32)
    xc = xf - jnp.mean(xf, axis=-1, keepdims=True)
    y = xc * lax.rsqrt(jnp.mean(xc * xc, axis=-1, keepdims=True) + EPS)
    return (y * g.astype(jnp.float32)).astype(x.dtype)


def heads(t):
    return t.reshape(t.shape[0], t.shape[1], NA_HEADS, NA_HEAD_DIM)


def fourier_mix(v):
    b, l, _ = v.shape
    vg = v.astype(jnp.float32).reshape(b, l, FNET_GROUPS, FNET_GROUP_DIM)
    y = jnp.fft.fftn(vg, axes=(1, 3), norm="ortho").real
    return y.reshape(b, l, FNET_WIDTH).astype(v.dtype)


def neighbourhood_attention(q, k, v, k_ctx, v_ctx, rpb):
    b, s, h, dh = q.shape
    rows = s // GRID_W
    kh = min(NA_ROWS, rows)
    scale = dh ** -0.5
    r = jnp.arange(rows)
    r0 = jnp.clip(r - kh // 2, 0, rows - kh)
    key_rows = r0[:, None] + jnp.arange(kh)[None, :]
    cq = jnp.arange(GRID_W)
    c0 = jnp.clip(cq - NA_COLS // 2, 0, GRID_W - NA_COLS)
    ck = jnp.arange(GRID_W)
    col_in = (ck[None, :] >= c0[:, None]) & (ck[None, :] < c0[:, None] + NA_COLS)

    qg = q.reshape(b, rows, GRID_W, h, dh)
    kg = k.reshape(b, rows, GRID_W, h, dh)[:, key_rows]
    vg = v.reshape(b, rows, GRID_W, h, dh)[:, key_rows]

    dr = key_rows - r[:, None] + (NA_ROWS - 1)
    dc = jnp.clip(ck[None, :] - cq[:, None] + (NA_COLS - 1), 0, 2 * NA_COLS - 2)
    bias = rpb[:, dr[:, :, None, None], dc[None, None, :, :]]
    bias = bias.transpose(1, 0, 3, 2, 4).astype(jnp.float32)

    s_loc = jnp.einsum('brqhd,brkwhd->brhqkw', qg, kg).astype(jnp.float32) * scale + bias
    s_loc = jnp.where(col_in[:, None, :], s_loc, -1e30)
    s_ctx = jnp.einsum('brqhd,bchd->brhqc', qg, k_ctx).astype(jnp.float32) * scale
    n_loc = kh * GRID_W
    logits = jnp.concatenate([s_loc.reshape(b, rows, h, GRID_W, n_loc), s_ctx], axis=-1)
    p = jax.nn.softmax(logits, axis=-1).astype(v.dtype)
    p_loc = p[..., :n_loc].reshape(b, rows, h, GRID_W, kh, GRID_W)
    p_ctx = p[..., n_loc:]
    o = (jnp.einsum('brhqkw,brkwhd->brqhd', p_loc, vg)
         + jnp.einsum('brhqc,bchd->brqhd', p_ctx, v_ctx))
    return o.reshape(b, s, h * dh)


def context_attention(q, k, v):
    b, l, h, dh = q.shape
    s = jnp.einsum('bqhd,bkhd->bhqk', q, k).astype(jnp.float32) * (dh ** -0.5)
    p = jax.nn.softmax(s, axis=-1).astype(v.dtype)
    return jnp.einsum('bhqk,bkhd->bqhd', p, v).reshape(b, l, h * dh)


def spatial_gating(u, v, w_s, b_s, g):
    b, l, _ = u.shape
    n = l // SGU_CHUNK
    vc = layer_norm(v, g).reshape(b, n, SGU_CHUNK, SGU_GROUPS, SGU_GROUP_DIM)
    mixed = jnp.einsum('gpq,bnqgc->bnpgc', w_s, vc) + b_s.T[None, None, :, :, None]
    return u * mixed.reshape(b, l, SGU_WIDTH).astype(u.dtype)


def s5_discretise(lam_re, lam_im, log_step, b_re, b_im):
    lam = lax.complex(jnp.minimum(lam_re.astype(jnp.float32), -1e-4), lam_im.astype(jnp.float32))
    dt = jnp.exp(log_step.astype(jnp.float32))[:, None]
    lam_bar = jnp.exp(lam * dt)
    b_bar = ((lam_bar - 1.0) / lam)[..., None] * lax.complex(b_re.astype(jnp.float32), b_im.astype(jnp.float32))
    return lam_bar, b_bar


def linear_scan(lam_bar, bu, h0, reverse):
    if h0 is not None:
        first = -1 if reverse else 0
        bu = bu.at[:, first].add(lam_bar[None] * h0)
    a = jnp.broadcast_to(lam_bar, (1, bu.shape[1]) + lam_bar.shape)

    def combine(x, y):
        a1, b1 = x
        a2, b2 = y
        return a1 * a2, a2 * b1 + b2

    _, h = lax.associative_scan(combine, (a, bu), reverse=reverse, axis=1)
    return h


def s5_direction(u_lat, u_ctx, lam_re, lam_im, log_step, b_re, b_im, c_re, c_im, reverse, with_ctx_out):
    lam_bar, b_bar = s5_discretise(lam_re, lam_im, log_step, b_re, b_im)
    b_r, b_i = jnp.real(b_bar), jnp.imag(b_bar)
    cr, ci = c_re.astype(jnp.float32), c_im.astype(jnp.float32)

    def drive(u):
        ug = u.astype(jnp.float32).reshape(u.shape[0], u.shape[1], S5_GROUPS, S5_GROUP_DIM)
        return lax.complex(jnp.einsum('blgm,gpm->blgp', ug, b_r), jnp.einsum('blgm,gpm->blgp', ug, b_i))

    def read(h):
        y = jnp.einsum('gmp,blgp->blgm', cr, jnp.real(h)) - jnp.einsum('gmp,blgp->blgm', ci, jnp.imag(h))
        return y.reshape(h.shape[0], h.shape[1], S5_WIDTH)

    h_ctx = linear_scan(lam_bar, drive(u_ctx), None, reverse)
    h_end = h_ctx[:, 0] if reverse else h_ctx[:, -1]
    h_lat = linear_scan(lam_bar, drive(u_lat), h_end, reverse)
    return read(h_lat), (read(h_ctx) if with_ctx_out else None)


def s5_readout(y, u, d_skip, w_glu, b_glu):
    z = jax.nn.gelu(y + d_skip.astype(jnp.float32) * u.astype(jnp.float32))
    z = z @ w_glu.astype(jnp.float32) + b_glu.astype(jnp.float32)
    val, gt = jnp.split(z, 2, axis=-1)
    return (val * jax.nn.sigmoid(gt)).astype(u.dtype)


def even_mixer(hl, hc, w_in, w_out, rpb, with_ctx_out):
    fa_l, ga_l, q_l, k_l, v_l, gb_l = split_cols(hl @ w_in, EVEN_SPLITS)
    fa_c, ga_c, q_c, k_c, v_c, gb_c = split_cols(hc @ w_in, EVEN_SPLITS)
    k_c, v_c = heads(k_c), heads(v_c)
    a_l = fourier_mix(fa_l) * jax.nn.silu(ga_l)
    n_l = neighbourhood_attention(heads(q_l), heads(k_l), heads(v_l), k_c, v_c, rpb) * jax.nn.silu(gb_l)
    y_l = jnp.concatenate([a_l, n_l], axis=-1) @ w_out
    if not with_ctx_out:
        return y_l, None
    a_c = fourier_mix(fa_c) * jax.nn.silu(ga_c)
    n_c = context_attention(heads(q_c), k_c, v_c) * jax.nn.silu(gb_c)
    y_c = jnp.concatenate([a_c, n_c], axis=-1) @ w_out
    return y_l, y_c


def odd_mixer(hl, hc, w_in, w_out, sgu_w, sgu_b, sgu_g, lam_re, lam_im, log_step,
              b_re, b_im, c_re, c_im, d_skip, w_glu, b_glu, with_ctx_out):
    u_l, v_l, gc_l, s_l, gd_l = split_cols(hl @ w_in, ODD_SPLITS)
    u_c, v_c, gc_c, s_c, gd_c = split_cols(hc @ w_in, ODD_SPLITS)
    sg_l = spatial_gating(u_l, v_l, sgu_w, sgu_b, sgu_g) * jax.nn.silu(gc_l)
    yf_l, yf_c = s5_direction(s_l, s_c, lam_re[0], lam_im[0], log_step[0], b_re[0], b_im[0],
                              c_re[0], c_im[0], False, with_ctx_out)
    yb_l, yb_c = s5_direction(s_l, s_c, lam_re[1], lam_im[1], log_step[1], b_re[1], b_im[1],
                              c_re[1], c_im[1], True, with_ctx_out)
    ss_l = s5_readout(yf_l + yb_l, s_l, d_skip, w_glu, b_glu) * jax.nn.silu(gd_l)
    y_l = jnp.concatenate([sg_l, ss_l], axis=-1) @ w_out
    if not with_ctx_out:
        return y_l, None
    sg_c = spatial_gating(u_c, v_c, sgu_w, sgu_b, sgu_g) * jax.nn.silu(gc_c)
    ss_c = s5_readout(yf_c + yb_c, s_c, d_skip, w_glu, b_glu) * jax.nn.silu(gd_c)
    y_c = jnp.concatenate([sg_c, ss_c], axis=-1) @ w_out
    return y_l, y_c


def setup_inputs(seed: int = 0) -> dict:
    key = jax.random.key(seed)
    ks = jax.random.split(key, 26)
    f32 = jnp.float32
    D = D_MODEL
    nrm = lambda k, shape, s: jax.random.normal(k, shape, f32) * s
    lam_im_init = jnp.pi * jnp.arange(S5_STATE, dtype=f32)
    return {
        "x": nrm(ks[0], (BATCH, SEQ, D), 1.0),
        "c": nrm(ks[1], (BATCH, D), 1.0),
        "ctx": nrm(ks[2], (BATCH, CTX_LEN, D), 1.0),
        "c_ctx": nrm(ks[3], (D,), 1.0),
        "w_ada": nrm(ks[4], (DEPTH, D, 3 * D), 0.5 * D ** -0.5),
        "b_ada": nrm(ks[5], (DEPTH, 3 * D), 0.01),
        "pre_g": 1.0 + nrm(ks[6], (DEPTH, D), 0.02),
        "post_g": 1.0 + nrm(ks[7], (DEPTH, D), 0.02),
        "w_in_even": nrm(ks[8], (N_EVEN, D, EVEN_IN), D ** -0.5),
        "w_out_even": nrm(ks[9], (N_EVEN, MIX_WIDTH, D), MIX_WIDTH ** -0.5),
        "na_rpb": nrm(ks[10], (N_EVEN, NA_HEADS, 2 * NA_ROWS - 1, 2 * NA_COLS - 1), 0.02),
        "w_in_odd": nrm(ks[11], (N_ODD, D, ODD_IN), D ** -0.5),
        "w_out_odd": nrm(ks[12], (N_ODD, MIX_WIDTH, D), MIX_WIDTH ** -0.5),
        "sgu_w": nrm(ks[13], (N_ODD, SGU_GROUPS, SGU_CHUNK, SGU_CHUNK), SGU_CHUNK ** -0.5),
        "sgu_b": 1.0 + nrm(ks[14], (N_ODD, SGU_GROUPS, SGU_CHUNK), 0.01),
        "sgu_g": 1.0 + nrm(ks[15], (N_ODD, SGU_WIDTH), 0.02),
        "s5_lam_re": -0.5 + nrm(ks[16], (N_ODD, 2, S5_GROUPS, S5_STATE), 0.01),
        "s5_lam_im": lam_im_init + nrm(ks[17], (N_ODD, 2, S5_GROUPS, S5_STATE), 0.01),
        "s5_log_step": jax.random.uniform(ks[18], (N_ODD, 2, S5_GROUPS), f32,
                                          minval=math.log(1e-3), maxval=math.log(1e-1)),
        "s5_b_re": nrm(ks[19], (N_ODD, 2, S5_GROUPS, S5_STATE, S5_GROUP_DIM), (2 * S5_GROUP_DIM) ** -0.5),
        "s5_b_im": nrm(ks[20], (N_ODD, 2, S5_GROUPS, S5_STATE, S5_GROUP_DIM), (2 * S5_GROUP_DIM) ** -0.5),
        "s5_c_re": nrm(ks[21], (N_ODD, 2, S5_GROUPS, S5_GROUP_DIM, S5_STATE), (2 * S5_STATE) ** -0.5),
        "s5_c_im": nrm(ks[22], (N_ODD, 2, S5_GROUPS, S5_GROUP_DIM, S5_STATE), (2 * S5_STATE) ** -0.5),
        "s5_d": nrm(ks[23], (N_ODD, S5_WIDTH), 1.0),
        "glu_w": nrm(ks[24], (N_ODD, S5_WIDTH, 2 * S5_WIDTH), S5_WIDTH ** -0.5),
        "glu_b": nrm(ks[25], (N_ODD, 2 * S5_WIDTH), 0.01),
    }


def reference(x, c, ctx, c_ctx, w_ada, b_ada, pre_g, post_g, w_in_even, w_out_even, na_rpb,
              w_in_odd, w_out_odd, sgu_w, sgu_b, sgu_g, s5_lam_re, s5_lam_im, s5_log_step,
              s5_b_re, s5_b_im, s5_c_re, s5_c_im, s5_d, glu_w, glu_b):
    xl = x
    xc = ctx
    for i in range(DEPTH):
        last = i == DEPTH - 1
        mod = jax.nn.silu(c) @ w_ada[i] + b_ada[i]
        shift, scale, gate = jnp.split(mod[:, None, :], 3, axis=-1)
        mod_c = jax.nn.silu(c_ctx) @ w_ada[i] + b_ada[i]
        shift_c, scale_c, gate_c = jnp.split(mod_c, 3)
        hl = rms_norm(xl, pre_g[i]) * (1.0 + scale) + shift
        hc = rms_norm(xc, pre_g[i]) * (1.0 + scale_c) + shift_c
        if i % 2 == 0:
            j = i // 2
            yl, yc = even_mixer(hl, hc, w_in_even[j], w_out_even[j], na_rpb[j], not last)
        else:
            j = i // 2
            yl, yc = odd_mixer(hl, hc, w_in_odd[j], w_out_odd[j], sgu_w[j], sgu_b[j], sgu_g[j],
                               s5_lam_re[j], s5_lam_im[j], s5_log_step[j], s5_b_re[j], s5_b_im[j],
                               s5_c_re[j], s5_c_im[j], s5_d[j], glu_w[j], glu_b[j], not last)
        xl = xl + gate * rms_norm(yl.astype(xl.dtype), post_g[i])
        if not last:
            xc = xc + gate_c * rms_norm(yc.astype(xc.dtype), post_g[i])
    return xl
```

```python
import contextlib
import numpy as np
import concourse.bass as bass
import concourse.mybir as mybir
from concourse.bass_utils import run_bass_kernel_spmd

F32 = mybir.dt.float32
BF16 = mybir.dt.bfloat16
AF = mybir.ActivationFunctionType
ALU = mybir.AluOpType

ENGS = ("pe", "act", "dve", "pool", "sp")
RING = 8
SELF_SYNC = ("act", "dve", "pool")

D = 1024
T = 4352
TL = 4096
NT = 34
EPS = 1e-6
DEPTH = 4


class _Op:
    __slots__ = ("eng", "fn", "deps", "dma", "flag", "cnt", "ring", "target")


class DF:
    def __init__(self, nc):
        self.nc = nc
        self.ops = []
        self.lw = {}
        self.rd = {}
        self.ndma = {e: 0 for e in ENGS}
        self.last_on = {}
        self.bar = None
        self.dma_since_bar = []

    def add(self, eng, fn, reads=(), writes=(), dma=False):
        idx = len(self.ops)
        deps = set()
        if self.bar is not None:
            deps.add(self.bar)
        for r in reads:
            w = self.lw.get(r)
            if w is not None:
                deps.add(w)
        for r in writes:
            w = self.lw.get(r)
            if w is not None:
                deps.add(w)
            deps.update(self.rd.get(r, ()))
        for r in reads:
            self.rd.setdefault(r, []).append(idx)
        for r in writes:
            self.lw[r] = idx
            self.rd[r] = []
        o = _Op()
        o.eng, o.fn, o.deps, o.dma, o.flag, o.cnt = eng, fn, deps, dma, False, 0
        o.ring = o.target = None
        if dma:
            n = self.ndma[eng]
            self.ndma[eng] = n + 1
            o.ring = n % RING
            o.target = 16 * (n // RING + 1)
            self.dma_since_bar.append(idx)
        else:
            self.last_on[eng] = idx
        self.ops.append(o)
        return idx

    def barrier(self, tile):
        idx = len(self.ops)
        deps = set(self.last_on.values()) | set(self.dma_since_bar)
        if self.bar is not None:
            deps.add(self.bar)
        o = _Op()
        o.eng, o.fn, o.deps, o.dma, o.flag, o.cnt = "pool", (lambda e: e.memset(tile, 0.0)), deps, False, False, 0
        o.ring = o.target = None
        self.ops.append(o)
        self.last_on["pool"] = idx
        self.bar = idx
        self.dma_since_bar = []
        self.lw = {}
        self.rd = {}

    def emit(self):
        nc = self.nc
        ops = self.ops
        for o in ops:
            for d in o.deps:
                p = ops[d]
                if p.dma:
                    continue
                if p.eng != o.eng or (o.eng in SELF_SYNC) or o.dma:
                    p.flag = True
        cnt = {e: 0 for e in ENGS}
        for o in ops:
            if o.flag and not o.dma:
                cnt[o.eng] += 1
                o.cnt = cnt[o.eng]
        with contextlib.ExitStack() as st:
            csem = {e: st.enter_context(nc.semaphore("c_" + e)) for e in ENGS}
            dsem = {e: [st.enter_context(nc.semaphore("d_%s%d" % (e, i))) for i in range(RING)]
                    for e in ("sp", "pool", "act")}
            block = st.enter_context(nc.Block())
            ndma = self.ndma

            def run(engname, eng):
                waited_c = {e: 0 for e in ENGS}
                waited_d = {}
                for o in ops:
                    if o.eng != engname:
                        continue
                    for d in sorted(o.deps):
                        p = ops[d]
                        if p.dma:
                            key = (p.eng, p.ring)
                            if waited_d.get(key, 0) < p.target:
                                eng.wait_ge(dsem[p.eng][p.ring], p.target)
                                waited_d[key] = p.target
                        else:
                            if p.eng == engname and not (engname in SELF_SYNC or o.dma):
                                continue
                            if waited_c[p.eng] < p.cnt:
                                eng.wait_ge(csem[p.eng], p.cnt)
                                waited_c[p.eng] = p.cnt
                    if o.dma and o.target > 16:
                        key = (engname, o.ring)
                        if waited_d.get(key, 0) < o.target - 16:
                            eng.wait_ge(dsem[engname][o.ring], o.target - 16)
                            waited_d[key] = o.target - 16
                    ins = o.fn(eng)
                    if o.dma:
                        ins.then_inc(dsem[engname][o.ring], 16)
                    elif o.flag:
                        ins.then_inc(csem[engname], 1)
                if engname in dsem:
                    n = ndma[engname]
                    for r in range(min(n, RING)):
                        last = ((n - 1 - r) // RING) * RING + r
                        eng.wait_ge(dsem[engname][r], 16 * (last // RING + 1))

            @block.tensor
            def _(eng):
                run("pe", eng)

            @block.scalar
            def _(eng):
                run("act", eng)

            @block.vector
            def _(eng):
                run("dve", eng)

            @block.gpsimd
            def _(eng):
                run("pool", eng)

            @block.sync
            def _(eng):
                run("sp", eng)


NA_COMBOS = ([(2, kt) for kt in range(0, 5)] + [(0, kt) for kt in range(4)] + [(1, kt) for kt in range(4)]
             + [(30, kt) for kt in range(28, 32)] + [(31, kt) for kt in range(28, 32)])


def na_pattern_base(i):
    if 2 <= i <= 29:
        return 0, list(range(i - 2, i + 3))
    if i == 0:
        return 5, [0, 1, 2, 3]
    if i == 1:
        return 9, [0, 1, 2, 3]
    if i == 30:
        return 13, [28, 29, 30, 31]
    return 17, [28, 29, 30, 31]


def build_ebias(rpb):
    out = np.empty((12, 128, 21, 128), np.float32)
    a = np.arange(2)
    c = np.arange(64)
    for pi, (i, kt) in enumerate(NA_COMBOS):
        kr = (2 * kt + a)[:, None, None, None]
        r = (2 * i + a)[None, None, :, None]
        ck = c[None, :, None, None]
        cq = c[None, None, None, :]
        r0 = np.clip(r - 4, 0, 56)
        c0 = np.clip(cq - 8, 0, 48)
        ok = (kr >= r0) & (kr < r0 + 8) & (ck >= c0) & (ck < c0 + 16)
        dr = np.clip(kr - r + 7, 0, 14)
        dc = np.clip(ck - cq + 15, 0, 30)
        ok, dr, dc = np.broadcast_arrays(ok, dr, dc)
        vals = rpb[:, dr, dc]
        vals = np.where(ok[None], vals, np.float32(-30000.0))
        out[:, :, pi, :] = vals.reshape(12, 128, 128)
    return out


def fnet_consts():
    i64 = np.arange(64)
    ang = 2 * np.pi * np.outer(i64, i64) / 64.0
    F1 = np.concatenate([np.cos(ang), -np.sin(ang)], axis=1)
    t2 = i64[:, None, None]
    k1 = i64[None, :, None]
    k2 = i64[None, None, :]
    ph = -2 * np.pi * (t2 * k1 / 4096.0 + t2 * k2 / 64.0)
    Gr, Gi = np.cos(ph) / 64.0, np.sin(ph) / 64.0
    H = np.empty((64, 64, 2, 128))
    H[:, :, 0, 0:64] = Gr
    H[:, :, 0, 64:128] = Gi
    H[:, :, 1, 0:64] = -Gi
    H[:, :, 1, 64:128] = Gr
    A = np.cos(ang) / 8.0
    B = np.sin(ang) / 8.0
    CH = np.zeros((128, 6, 128))
    CH[0:64, 0, 0:64] = A
    CH[0:64, 1, 0:64] = B
    CH[0:64, 2, 64:128] = A
    CH[0:64, 3, 64:128] = B
    CH[0:64, 4, 0:64] = A
    CH[64:128, 4, 64:128] = A
    CH[0:64, 5, 0:64] = B
    CH[64:128, 5, 64:128] = B
    i256 = np.arange(256)
    a256 = 2 * np.pi * np.outer(i256, i256) / 256.0
    F256 = np.concatenate([np.cos(a256), -np.sin(a256)], axis=1) / 16.0
    F256 = F256.reshape(2, 128, 512).transpose(1, 0, 2)
    f = lambda x: np.ascontiguousarray(x, dtype=np.float32)
    return f(F1), f(H), f(CH), f(F256)


def s5_consts():
    s8 = (np.arange(128) // 16)
    ex = np.zeros((128, 4, 8), np.float32)
    sv = np.arange(8, dtype=np.float32)
    ex[0:64, 0] = -sv
    ex[64:128, 0] = sv
    ex[0:64, 1] = 7 - sv
    ex[64:128, 1] = sv
    ex[0:64, 2] = sv
    ex[64:128, 2] = -sv
    ex[0:64, 3] = sv + 1
    ex[64:128, 3] = 8 - sv
    mk = np.zeros((128, 2, 128), np.float32)
    mk[:, 0, :] = (s8[:, None] <= s8[None, :])
    mk[:, 1, :] = (s8[:, None] >= s8[None, :])
    io = np.ascontiguousarray(np.broadcast_to(np.arange(544, dtype=np.float32), (128, 544)))
    return ex, mk, io


import os
_SKIP = set(os.environ.get("MK_SKIP", "").split(","))


def build_program(n_layers=DEPTH):
    nc = bass.Bass("TRN2", target_bir_lowering=False)
    dt_in = lambda name, shape: nc.dram_tensor(name, list(shape), F32, kind="ExternalInput").ap()
    xr = dt_in("xr", [T, D])
    cT = dt_in("cT", [128, 16])
    w_ada = dt_in("w_ada", [DEPTH, D, 3 * D])
    b_ada = dt_in("b_ada", [DEPTH, 3 * D])
    pre_g = dt_in("pre_g", [DEPTH, D])
    post_g = dt_in("post_g", [DEPTH, D])
    w_in_even = dt_in("w_in_even", [2, D, 3584])
    w_out_even = dt_in("w_out_even", [2, D, D])
    ebias = dt_in("ebias", [2, 12, 128, 21 * 128])
    cF1 = dt_in("cF1", [64, 128])
    cH = dt_in("cH", [64, 64, 2, 128])
    cCH = dt_in("cCH", [128, 6, 128])
    cF256 = dt_in("cF256", [128, 2, 512])
    w_in_odd = dt_in("w_in_odd", [2, D, 2560])
    w_out_odd = dt_in("w_out_odd", [2, D, D])
    sgu_wT = dt_in("sgu_wT", [2, 128, 4, 128])
    sgu_gT = dt_in("sgu_gT", [2, 128, 4])
    sgu_b = dt_in("sgu_b", [2, 512])
    s5_lam1 = dt_in("s5_lam1", [2, 2, 128, 32])
    s5_ls = dt_in("s5_ls", [2, 2, 32])
    s5_b1 = dt_in("s5_b1", [2, 2, 128, 32, 16])
    s5_c1 = dt_in("s5_c1", [2, 2, 128, 32, 16])
    s5_drep = dt_in("s5_drep", [2, 128, 32])
    glu_w = dt_in("glu_w", [2, 512, 1024])
    glu_bT = dt_in("glu_bT", [2, 128, 8])
    cEXPS = dt_in("cEXPS", [128, 4, 8])
    cMASK = dt_in("cMASK", [128, 2, 128])
    cIOTA = dt_in("cIOTA", [128, 544])
    zs = nc.dram_tensor("zs", [8, 32, 16, 544], BF16).ap()
    out = nc.dram_tensor("out", [TL, D], F32, kind="ExternalOutput").ap()
    xs = nc.dram_tensor("xs", [T, D], F32).ap()
    modscr = nc.dram_tensor("modscr", [DEPTH, 2, 3 * D], F32).ap()

    df = DF(nc)
    A = df.add
    _uid = [0]

    def uniq(name):
        _uid[0] += 1
        return "%s_%d" % (name, _uid[0])

    def dma(eng, o, i, r=(), w=()):
        if eng == "pool":
            A(eng, lambda e, o=o, i=i: e.dma_start(out=o, in_=i, max_dma_last_dim=2048), r, w, dma=True)
        else:
            A(eng, lambda e, o=o, i=i: e.dma_start(out=o, in_=i), r, w, dma=True)

    def xbkeys(n0, nn):
        return [("xb", t) for t in range(n0 // 128, (n0 + nn + 127) // 128)]

    NTILES = [(n * 512, 512) for n in range(8)] + [(4096, 256)]

    with contextlib.ExitStack() as gst:
        sbg = lambda name, shape, dt: gst.enter_context(nc.sbuf_tensor(uniq(name), shape, dt))
        psg = lambda name, shape, dt: gst.enter_context(nc.psum_tensor(name, shape, dt))
        xb = sbg("xb", [128, 8, T], BF16)
        mT = sbg("mT", [128, 8, T], BF16)
        idn = sbg("idn", [128, 128], BF16)
        idn32 = sbg("idn32", [128, 128], F32)
        bart = sbg("bart", [128, 2], F32)
        mm = [psg("mm%d" % i, [128, 512], F32) for i in range(2)]
        stp = [psg("stp%d" % i, [128, 1024], F32) for i in range(2)]
        pv = psg("pv", [128, 512], F32)
        trp = psg("trp", [128, 8, 128], BF16)
        mmi = [0]

        def next_mm():
            mmi[0] ^= 1
            return mm[mmi[0]], "mm%d" % mmi[0]

        A("pool", lambda e: e.memset(idn32[:], 1.0), (), ["idn32"])
        A("pool", lambda e: e.affine_select(out=idn32[:], in_=idn32[:], pattern=[[-1, 128]], compare_op=ALU.is_equal,
                                            fill=0.0, base=0, channel_multiplier=1), ["idn32"], ["idn32"])
        A("dve", lambda e: e.tensor_copy(out=idn[:], in_=idn32[:]), ["idn32"], ["idn"])

        with contextlib.ExitStack() as st:
            sb = lambda name, shape, dt: st.enter_context(nc.sbuf_tensor(uniq(name), shape, dt))
            c32 = sb("c32", [128, 16], F32)
            sc = sb("sc", [128, 16], F32)
            LC = sb("LC", [128, 8, 64], BF16)
            wa = [sb("wa%d" % i, [128, 8, 512], BF16) for i in range(2)]
            bada = sb("bada", [64, 3 * D], F32)
            modrow = sb("modrow", [64, 3 * D], F32)
            dma("sp", c32[:], cT, (), ["c32"])
            A("act", lambda e: e.activation(out=sc[:], in_=c32[:], func=AF.Silu), ["c32"], ["sc"])
            A("pool", lambda e: e.memset(LC[:], 0.0), (), ["LC"])
            A("dve", lambda e: e.tensor_copy(out=LC[:, :, 0:1], in_=sc[:, 0:8].rearrange("p (k o) -> p k o", o=1)), ["sc", "LC"], ["LC"])
            A("dve", lambda e: e.tensor_copy(out=LC[:, :, 32:33], in_=sc[:, 8:16].rearrange("p (k o) -> p k o", o=1)), ["sc", "LC"], ["LC"])
            wi = 0
            for L in range(n_layers):
                dma("sp", bada[:], b_ada[L:L + 1, :].partition_broadcast(64), (), ["bada"])
                for n in range(6):
                    wt = wa[wi % 2]
                    wk = "wa%d" % (wi % 2)
                    wi += 1
                    dma("pool", wt[:], w_ada[L].rearrange("(k p) n -> p k n", p=128)[:, :, n * 512:(n + 1) * 512], (), [wk])
                    bank, bk = next_mm()
                    for k in range(8):
                        A("pe", lambda e, bank=bank, wt=wt, k=k: e.matmul(bank[0:64, :], lhsT=LC[:, k, :], rhs=wt[:, k, :], start=(k == 0), stop=(k == 7)),
                          ["LC", wk], [bk])
                    A("dve", lambda e, bank=bank, n=n: e.tensor_tensor(out=modrow[:, n * 512:(n + 1) * 512], in0=bank[0:64, :], in1=bada[:, n * 512:(n + 1) * 512], op=ALU.add),
                      [bk, "bada"], ["modrow"])
                dma("sp", modscr[L, 0:1, :], modrow[0:1, :], ["modrow"], [("modscr", L)])
                dma("sp", modscr[L, 1:2, :], modrow[32:33, :], ["modrow"], [("modscr", L)])
        df.barrier(bart[:, 0:1])

        def rstd_from_ss(ssum, rs, keys_in, key_out, scale):
            A("dve", lambda e: e.tensor_scalar(out=rs, in0=ssum, scalar1=scale, scalar2=EPS, op0=ALU.mult, op1=ALU.add), keys_in, [key_out])
            A("act", lambda e: e.activation(out=rs, in_=rs, func=AF.Sqrt), [key_out], [key_out])
            A("dve", lambda e: e.reciprocal(out=rs, in_=rs), [key_out], [key_out])

        def prenorm_old(L):
            src = xr if L == 0 else xs
            with contextlib.ExitStack() as st:
                sb = lambda name, shape, dt: st.enter_context(nc.sbuf_tensor(uniq(name), shape, dt))
                xt = [sb("xt%d" % i, [128, D], F32) for i in range(3)]
                tmp = [sb("ptmp%d" % i, [128, D], F32) for i in range(2)]
                hl = [sb("hl%d" % i, [128, D], BF16) for i in range(2)]
                junk = sb("junk", [128, D], BF16)
                gsc = [sb("gsc%d" % i, [128, D], F32) for i in range(2)]
                shb = [sb("shb%d" % i, [128, D], F32) for i in range(2)]
                pgb = sb("pgb", [128, D], F32)
                stat = sb("pstat", [128, 4 * NT], F32)
                dma("sp", pgb[:], pre_g[L:L + 1, :].partition_broadcast(128), (), ["pgb"])
                for w in range(2):
                    dma("sp", gsc[w][:], modscr[L, w:w + 1, D:2 * D].partition_broadcast(128), [("modscr", L)], ["gsc%d" % w])
                    dma("sp", shb[w][:], modscr[L, w:w + 1, 0:D].partition_broadcast(128), [("modscr", L)], ["shb%d" % w])
                    A("dve", lambda e, w=w: e.scalar_tensor_tensor(out=gsc[w][:], in0=gsc[w][:], scalar=1.0, in1=pgb[:], op0=ALU.add, op1=ALU.mult),
                      ["gsc%d" % w, "pgb"], ["gsc%d" % w])
                for t in range(NT):
                    w = 0 if t < 32 else 1
                    x_ = xt[t % 3]
                    xk = "xt%d" % (t % 3)
                    tm = tmp[t % 2]
                    tk = "ptmp%d" % (t % 2)
                    h_ = hl[t % 2]
                    hk = "hl%d" % (t % 2)
                    ss = stat[:, 4 * t:4 * t + 1]
                    rs = stat[:, 4 * t + 1:4 * t + 2]
                    dma("sp", x_[:], src[t * 128:(t + 1) * 128, :], [("xres", t)], [xk])
                    A("act", lambda e, x_=x_, ss=ss: e.activation(out=junk[:], in_=x_[:], func=AF.Square, accum_out=ss), [xk], ["junk", ("pss", t)])
                    rstd_from_ss(ss, rs, [("pss", t)], ("prs", t), 1.0 / D)
                    A("dve", lambda e, x_=x_, rs=rs, tm=tm, w=w: e.scalar_tensor_tensor(out=tm[:], in0=x_[:], scalar=rs, in1=gsc[w][:], op0=ALU.mult, op1=ALU.mult),
                      [xk, ("prs", t), "gsc%d" % w], [tk])
                    A("pool", lambda e, tm=tm, h_=h_, w=w: e.tensor_tensor(out=h_[:], in0=tm[:], in1=shb[w][:], op=ALU.add), [tk, "shb%d" % w], [hk])
                    for k in range(8):
                        A("pe", lambda e, h_=h_, k=k: e.transpose(trp[:, k, :], h_[:, k * 128:(k + 1) * 128], idn[:]), [hk, "idn"], ["trp"])
                    if t % 2 == 0:
                        A("act", lambda e, t=t: e.copy(out=xb[:, :, t * 128:(t + 1) * 128], in_=trp[:]), ["trp"], [("xb", t)])
                    else:
                        A("dve", lambda e, t=t: e.tensor_copy(out=xb[:, :, t * 128:(t + 1) * 128], in_=trp[:]), ["trp"], [("xb", t)])
            df.barrier(bart[:, 0:1])

        def post_old(L, last):
            src = xr if L == 0 else xs
            j = L // 2
            wo_d = w_out_even[j] if L % 2 == 0 else w_out_odd[j]
            ntile = 32 if last else NT
            with contextlib.ExitStack() as st:
                sb = lambda name, shape, dt: st.enter_context(nc.sbuf_tensor(uniq(name), shape, dt))
                wo = sb("wo", [128, 8, D], BF16)
                xt = [sb("qxt%d" % i, [128, D], F32) for i in range(2)]
                t1 = [sb("qt1%d" % i, [128, D], F32) for i in range(2)]
                t2 = [sb("qt2%d" % i, [128, D], F32) for i in range(2)]
                junk = sb("qjunk", [128, 512], BF16)
                gp = [sb("gp%d" % i, [128, D], F32) for i in range(2)]
                pgb = sb("qpgb", [128, D], F32)
                stat = sb("qstat", [128, 4 * NT], F32)
                for h in range(2):
                    dma("pool", wo[:, :, h * 512:(h + 1) * 512], wo_d.rearrange("(k p) n -> p k n", p=128)[:, :, h * 512:(h + 1) * 512], (), ["wo"])
                dma("sp", pgb[:], post_g[L:L + 1, :].partition_broadcast(128), (), ["qpgb"])
                for w in range(2):
                    dma("sp", gp[w][:], modscr[L, w:w + 1, 2 * D:3 * D].partition_broadcast(128), [("modscr", L)], ["gp%d" % w])
                    A("dve", lambda e, w=w: e.tensor_tensor(out=gp[w][:], in0=gp[w][:], in1=pgb[:], op=ALU.mult), ["gp%d" % w, "qpgb"], ["gp%d" % w])
                for t in range(ntile):
                    w = 0 if t < 32 else 1
                    yps = stp[t % 2]
                    yk = "stp%d" % (t % 2)
                    x_ = xt[t % 2]
                    xk = "qxt%d" % (t % 2)
                    a_ = t1[t % 2]
                    ak = "qt1%d" % (t % 2)
                    b_ = t2[t % 2]
                    bk = "qt2%d" % (t % 2)
                    for h in range(2):
                        for k in range(8):
                            A("pe", lambda e, yps=yps, h=h, k=k, t=t: e.matmul(yps[:, h * 512:(h + 1) * 512], lhsT=mT[:, k, t * 128:(t + 1) * 128], rhs=wo[:, k, h * 512:(h + 1) * 512], start=(k == 0), stop=(k == 7)),
                              [("mT", k, t), "wo"], [yk])
                    dma("sp", x_[:], src[t * 128:(t + 1) * 128, :], [("xres", t)], [xk])
                    for h in range(2):
                        A("act", lambda e, yps=yps, h=h, t=t: e.activation(out=junk[:], in_=yps[:, h * 512:(h + 1) * 512], func=AF.Square, accum_out=stat[:, 4 * t + h:4 * t + h + 1]),
                          [yk], ["qjunk", ("qss", t, h)])
                    A("dve", lambda e, t=t: e.tensor_tensor(out=stat[:, 4 * t + 2:4 * t + 3], in0=stat[:, 4 * t:4 * t + 1], in1=stat[:, 4 * t + 1:4 * t + 2], op=ALU.add),
                      [("qss", t, 0), ("qss", t, 1)], [("qs2", t)])
                    rs = stat[:, 4 * t + 3:4 * t + 4]
                    rstd_from_ss(stat[:, 4 * t + 2:4 * t + 3], rs, [("qs2", t)], ("qrs", t), 1.0 / D)
                    A("dve", lambda e, yps=yps, a_=a_, w=w: e.tensor_tensor(out=a_[:], in0=yps[:], in1=gp[w][:], op=ALU.mult), [yk, "gp%d" % w], [ak])
                    A("act", lambda e, a_=a_, b_=b_, rs=rs: e.activation(out=b_[:], in_=a_[:], func=AF.Copy, scale=rs), [ak, ("qrs", t)], [bk])
                    A("pool", lambda e, b_=b_, x_=x_: e.tensor_tensor(out=b_[:], in0=b_[:], in1=x_[:], op=ALU.add), [bk, xk], [bk])
                    dst = out[t * 128:(t + 1) * 128, :] if (last and t < 32) else xs[t * 128:(t + 1) * 128, :]
                    dma("sp", dst, b_[:], [bk], [("xres", t)])
            df.barrier(bart[:, 0:1])

        def prenorm(L):
            src = xr if L == 0 else xs
            with contextlib.ExitStack() as st:
                sb = lambda name, shape, dt: st.enter_context(nc.sbuf_tensor(uniq(name), shape, dt))
                NX = 6
                xt = [sb("xt%d" % i, [128, D], F32) for i in range(NX)]
                tmp = [sb("ptmp%d" % i, [128, D], F32) for i in range(2)]
                hl = [sb("hl%d" % i, [128, D], BF16) for i in range(2)]
                junk = sb("junk", [128, D], BF16)
                gsc = [sb("gsc%d" % i, [128, D], F32) for i in range(2)]
                shb = [sb("shb%d" % i, [128, D], F32) for i in range(2)]
                pgb = sb("pgb", [128, D], F32)
                stat = sb("pstat", [128, 4 * NT], F32)
                dma("sp", pgb[:], pre_g[L:L + 1, :].partition_broadcast(128), (), ["pgb"])
                for w in range(2):
                    dma("sp", gsc[w][:], modscr[L, w:w + 1, D:2 * D].partition_broadcast(128), [("modscr", L)], ["gsc%d" % w])
                    dma("sp", shb[w][:], modscr[L, w:w + 1, 0:D].partition_broadcast(128), [("modscr", L)], ["shb%d" % w])
                    A("dve", lambda e, w=w: e.scalar_tensor_tensor(out=gsc[w][:], in0=gsc[w][:], scalar=1.0, in1=pgb[:], op0=ALU.add, op1=ALU.mult),
                      ["gsc%d" % w, "pgb"], ["gsc%d" % w])
                X = lambda t: (xt[t % NX], "xt%d" % (t % NX))
                SS = lambda t: stat[:, 4 * t:4 * t + 1]
                RS = lambda t: stat[:, 4 * t + 1:4 * t + 2]

                def p_load(t):
                    x_, xk = X(t)
                    dma("sp", x_[:], src[t * 128:(t + 1) * 128, :], [("xres", t)], [xk])

                def p_sq(t):
                    x_, xk = X(t)
                    A("act", lambda e, x_=x_, ss=SS(t): e.activation(out=junk[:], in_=x_[:], func=AF.Square, accum_out=ss), [xk], ["junk", ("pss", t)])

                def p_r1(t):
                    A("dve", lambda e, t=t: e.tensor_scalar(out=RS(t), in0=SS(t), scalar1=1.0 / D, scalar2=EPS, op0=ALU.mult, op1=ALU.add), [("pss", t)], [("prs", t)])

                def p_r2(t):
                    A("act", lambda e, t=t: e.activation(out=RS(t), in_=RS(t), func=AF.Sqrt), [("prs", t)], [("prs", t)])

                def p_r3(t):
                    A("dve", lambda e, t=t: e.reciprocal(out=RS(t), in_=RS(t)), [("prs", t)], [("prs", t)])

                def p_stt(t):
                    w = 0 if t < 32 else 1
                    x_, xk = X(t)
                    tm, tk = tmp[t % 2], "ptmp%d" % (t % 2)
                    A("dve", lambda e, x_=x_, t=t, tm=tm, w=w: e.scalar_tensor_tensor(out=tm[:], in0=x_[:], scalar=RS(t), in1=gsc[w][:], op0=ALU.mult, op1=ALU.mult),
                      [xk, ("prs", t), "gsc%d" % w], [tk])

                def p_add(t):
                    w = 0 if t < 32 else 1
                    tm, tk = tmp[t % 2], "ptmp%d" % (t % 2)
                    h_, hk = hl[t % 2], "hl%d" % (t % 2)
                    A("pool", lambda e, tm=tm, h_=h_, w=w: e.tensor_tensor(out=h_[:], in0=tm[:], in1=shb[w][:], op=ALU.add), [tk, "shb%d" % w], [hk])

                def p_tr(t):
                    h_, hk = hl[t % 2], "hl%d" % (t % 2)
                    for k in range(8):
                        A("pe", lambda e, h_=h_, k=k: e.transpose(trp[:, k, :], h_[:, k * 128:(k + 1) * 128], idn[:]), [hk, "idn"], ["trp"])

                def p_ev(t):
                    if t % 2 == 0:
                        A("act", lambda e, t=t: e.copy(out=xb[:, :, t * 128:(t + 1) * 128], in_=trp[:]), ["trp"], [("xb", t)])
                    else:
                        A("dve", lambda e, t=t: e.tensor_copy(out=xb[:, :, t * 128:(t + 1) * 128], in_=trp[:]), ["trp"], [("xb", t)])

                pipeline([p_load, p_sq, p_r1, p_r2, p_r3, p_stt, p_add, p_tr, p_ev], NT, "pre")
            df.barrier(bart[:, 0:1])

        def post(L, last):
            src = xr if L == 0 else xs
            j = L // 2
            wo_d = w_out_even[j] if L % 2 == 0 else w_out_odd[j]
            ntile = 32 if last else NT
            with contextlib.ExitStack() as st:
                sb = lambda name, shape, dt: st.enter_context(nc.sbuf_tensor(uniq(name), shape, dt))
                wo = sb("wo", [128, 8, D], BF16)
                NA_, NB_, NXq = 4, 3, 3
                xt = [sb("qxt%d" % i, [128, D], F32) for i in range(NXq)]
                t1 = [sb("qt1%d" % i, [128, D], F32) for i in range(NA_)]
                t2 = [sb("qt2%d" % i, [128, D], F32) for i in range(NB_)]
                junk = sb("qjunk", [128, D], BF16)
                gp = [sb("gp%d" % i, [128, D], F32) for i in range(2)]
                pgb = sb("qpgb", [128, D], F32)
                stat = sb("qstat", [128, 4 * NT], F32)
                for h in range(2):
                    dma("pool", wo[:, :, h * 512:(h + 1) * 512], wo_d.rearrange("(k p) n -> p k n", p=128)[:, :, h * 512:(h + 1) * 512], (), ["wo"])
                dma("sp", pgb[:], post_g[L:L + 1, :].partition_broadcast(128), (), ["qpgb"])
                for w in range(2):
                    dma("sp", gp[w][:], modscr[L, w:w + 1, 2 * D:3 * D].partition_broadcast(128), [("modscr", L)], ["gp%d" % w])
                    A("dve", lambda e, w=w: e.tensor_tensor(out=gp[w][:], in0=gp[w][:], in1=pgb[:], op=ALU.mult), ["gp%d" % w, "qpgb"], ["gp%d" % w])
                YP = lambda t: (stp[t % 2], "stp%d" % (t % 2))
                XQ = lambda t: (xt[t % NXq], "qxt%d" % (t % NXq))
                TA = lambda t: (t1[t % NA_], "qt1%d" % (t % NA_))
                TB = lambda t: (t2[t % NB_], "qt2%d" % (t % NB_))
                SS = lambda t: stat[:, 4 * t:4 * t + 1]
                RS = lambda t: stat[:, 4 * t + 1:4 * t + 2]

                def q_mm(t):
                    yps, yk = YP(t)
                    for h in range(2):
                        for k in range(8):
                            A("pe", lambda e, yps=yps, h=h, k=k, t=t: e.matmul(yps[:, h * 512:(h + 1) * 512], lhsT=mT[:, k, t * 128:(t + 1) * 128], rhs=wo[:, k, h * 512:(h + 1) * 512], start=(k == 0), stop=(k == 7)),
                              [("mT", k, t), "wo"], [yk])

                def q_sq(t):
                    yps, yk = YP(t)
                    a_, ak = TA(t)
                    w = 0 if t < 32 else 1
                    for h in range(2):
                        A("act", lambda e, yps=yps, t=t, h=h: e.activation(out=junk[:, h * 512:(h + 1) * 512], in_=yps[:, h * 512:(h + 1) * 512], func=AF.Square, accum_out=stat[:, 4 * t + 2 + h:4 * t + 3 + h]), [yk], ["qjunk", ("qssh", t, h)])
                    A("dve", lambda e, yps=yps, a_=a_, w=w: e.tensor_tensor(out=a_[:], in0=yps[:], in1=gp[w][:], op=ALU.mult), [yk, "gp%d" % w, ("qssh", t, 0), ("qssh", t, 1)], [ak])

                def q_r1(t):
                    A("dve", lambda e, t=t: e.tensor_tensor(out=SS(t), in0=stat[:, 4 * t + 2:4 * t + 3], in1=stat[:, 4 * t + 3:4 * t + 4], op=ALU.add), [("qssh", t, 0), ("qssh", t, 1)], [("qss", t)])
                    A("dve", lambda e, t=t: e.tensor_scalar(out=RS(t), in0=SS(t), scalar1=1.0 / D, scalar2=EPS, op0=ALU.mult, op1=ALU.add), [("qss", t)], [("qrs", t)])

                def q_r2(t):
                    A("act", lambda e, t=t: e.activation(out=RS(t), in_=RS(t), func=AF.Sqrt), [("qrs", t)], [("qrs", t)])
                    x_, xk = XQ(t)
                    dma("sp", x_[:], src[t * 128:(t + 1) * 128, :], [("xres", t)], [xk])

                def q_r3(t):
                    A("dve", lambda e, t=t: e.reciprocal(out=RS(t), in_=RS(t)), [("qrs", t)], [("qrs", t)])

                def q_sc(t):
                    a_, ak = TA(t)
                    b_, bk = TB(t)
                    A("act", lambda e, a_=a_, b_=b_, t=t: e.activation(out=b_[:], in_=a_[:], func=AF.Copy, scale=RS(t)), [ak, ("qrs", t)], [bk])

                def q_add(t):
                    b_, bk = TB(t)
                    x_, xk = XQ(t)
                    A("pool", lambda e, b_=b_, x_=x_: e.tensor_tensor(out=b_[:], in0=b_[:], in1=x_[:], op=ALU.add), [bk, xk], [bk])

                def q_st(t):
                    b_, bk = TB(t)
                    dst = out[t * 128:(t + 1) * 128, :] if (last and t < 32) else xs[t * 128:(t + 1) * 128, :]
                    dma("sp", dst, b_[:], [bk], [("xres", t)])

                pipeline([q_mm, q_sq, q_r1, q_r2, q_r3, q_sc, q_add, q_st], ntile, "post")
            df.barrier(bart[:, 0:1])

        def pipeline(stages, N, key=""):
            if "noskew" in _SKIP or ("noskew_" + key) in _SKIP:
                for n_ in range(N):
                    for st_ in stages:
                        st_(n_)
                return
            K_ = len(stages)
            for step in range(N + K_ - 1):
                for k_ in reversed(range(K_)):
                    n_ = step - k_
                    if 0 <= n_ < N:
                        stages[k_](n_)

        def inproj_fm(wt, wk, tiles, evac):
            for (n0, nn) in tiles:
                bank, bk = next_mm()
                for k in range(8):
                    A("pe", lambda e, bank=bank, k=k, n0=n0, nn=nn: e.matmul(bank[:, 0:nn], lhsT=wt[:, k, :], rhs=xb[:, k, n0:n0 + nn], start=(k == 0), stop=(k == 7)),
                      [wk] + xbkeys(n0, nn), [bk])
                evac(bank, bk, n0, nn)

        def even_mixer(L):
            j = L // 2
            Wd = w_in_even[j].rearrange("(k p) n -> p k n", p=128)
            with contextlib.ExitStack() as st:
                sb = lambda name, shape, dt: st.enter_context(nc.sbuf_tensor(uniq(name), shape, dt))
                wch = [sb("wch%d" % i, [128, 8, 128], BF16) for i in range(3)]
                wci = [0]

                def load_w(c0, ncols=128):
                    i = wci[0] % 3
                    wci[0] += 1
                    dma("pool", wch[i][:, :, 0:ncols], Wd[:, :, c0:c0 + ncols], (), ["wch%d" % i])
                    return wch[i], "wch%d" % i

                with contextlib.ExitStack() as st2:
                    sb2 = lambda name, shape, dt: st2.enter_context(nc.sbuf_tensor(uniq(name), shape, dt))
                    sga = sb2("sga", [128, T], BF16)
                    X = sb2("fX", [64, 64, 128], BF16)
                    Z = sb2("fZ", [64, 64, 128], BF16)
                    Pg = [sb2("fP%d" % i, [128, 8, 128], BF16) for i in range(2)]
                    Hs = [sb2("fH%d" % i, [64, 8, 2, 128], BF16) for i in range(2)]
                    F1 = sb2("fF1", [64, 128], BF16)
                    CH = sb2("fCH", [128, 6, 128], BF16)
                    F256 = sb2("fF256", [128, 2, 512], BF16)
                    Xc = sb2("fXc", [128, 2, 128], BF16)
                    Pc = sb2("fPc", [128, 512], BF16)
                    dma("pool", F1[:], cF1, (), ["fF1"])
                    dma("pool", CH[:], cCH, (), ["fCH"])
                    dma("pool", F256[:], cF256, (), ["fF256"])
                    hcount = 0
                    pcount = 0
                    for half in range(2):
                        wt, wk = load_w(256 + half * 128)
                        inproj_fm(wt, wk, NTILES, lambda bank, bk, n0, nn: A(
                            "act", lambda e: e.activation(out=sga[:, n0:n0 + nn], in_=bank[:, 0:nn], func=AF.Silu), [bk], [("sga", n0)]))
                        sgakeys = [("sga", n0) for (n0, nn) in NTILES]
                        wt, wk = load_w(half * 128)
                        for g4 in range(16):
                            bank, bk = next_mm()
                            for q in range(4):
                                t2 = g4 * 4 + q
                                for k in range(8):
                                    A("pe", lambda e, bank=bank, q=q, k=k, t2=t2, wt=wt: e.matmul(bank[0:64, q * 128:(q + 1) * 128], lhsT=xb[:, k, t2:TL:64], rhs=wt[:, k, :], start=(k == 0), stop=(k == 7)),
                                      [wk] + [("xb", t) for t in range(32)], [bk])
                            A("act", lambda e, bank=bank, g4=g4: e.copy(out=X[:, g4 * 4:(g4 + 1) * 4, :], in_=bank[0:64, :].rearrange("p (q c) -> p q c", q=4)), [bk], ["fX"])
                        for tl in range(2):
                            bank, bk = next_mm()
                            for k in range(8):
                                A("pe", lambda e, bank=bank, k=k, tl=tl, wt=wt: e.matmul(bank[:, 0:128], lhsT=xb[:, k, TL + tl * 128:TL + (tl + 1) * 128], rhs=wt[:, k, :], start=(k == 0), stop=(k == 7)),
                                  [wk, ("xb", 32 + tl)], [bk])
                            A("dve", lambda e, bank=bank, tl=tl: e.tensor_copy(out=Xc[:, tl, :], in_=bank[:, 0:128]), [bk], ["fXc"])
                        bank, bk = next_mm()
                        for tl in range(2):
                            A("pe", lambda e, bank=bank, tl=tl: e.matmul(bank[:, :], lhsT=Xc[:, tl, :], rhs=F256[:, tl, :], start=(tl == 0), stop=(tl == 1)), ["fXc", "fF256"], [bk])
                        A("dve", lambda e, bank=bank: e.tensor_copy(out=Pc[:], in_=bank[:, :]), [bk], ["fPc"])
                        bank, bk = next_mm()
                        A("pe", lambda e, bank=bank: e.matmul(bank[:, 0:256], lhsT=CH[:, 4, :], rhs=Pc[:, 0:256], start=True, stop=False), ["fPc", "fCH"], [bk])
                        A("pe", lambda e, bank=bank: e.matmul(bank[:, 0:256], lhsT=CH[:, 5, :], rhs=Pc[:, 256:512], start=False, stop=True), ["fPc", "fCH"], [bk])
                        A("dve", lambda e, bank=bank, half=half: e.tensor_tensor(out=mT[:, half, TL:T], in0=bank[:, 0:256], in1=sga[:, TL:T], op=ALU.mult),
                          [bk] + sgakeys, [("mT", half, 32), ("mT", half, 33)])
                        for qd in range(2):
                            pb = qd * 64
                            for c4 in range(16):
                                bank, bk = next_mm()
                                for q in range(4):
                                    c = qd * 64 + c4 * 4 + q
                                    A("pe", lambda e, bank=bank, q=q, c=c: e.matmul(bank[0:64, q * 128:(q + 1) * 128], lhsT=X[:, :, c], rhs=F1[:, :], start=True, stop=True),
                                      ["fX", "fF1"], [bk])
                                if c4 % 2 == 0:
                                    A("act", lambda e, bank=bank, c4=c4: e.copy(out=Z[:, c4 * 4:(c4 + 1) * 4, :], in_=bank[0:64, :].rearrange("p (q c) -> p q c", q=4)), [bk], ["fZ"])
                                else:
                                    A("dve", lambda e, bank=bank, c4=c4: e.tensor_copy(out=Z[:, c4 * 4:(c4 + 1) * 4, :], in_=bank[0:64, :].rearrange("p (q c) -> p q c", q=4)), [bk], ["fZ"])
                            for g8 in range(8):
                                Hb = Hs[hcount % 2]
                                hk = "fH%d" % (hcount % 2)
                                hcount += 1
                                dma("pool", Hb[:], cH[:, g8 * 8:(g8 + 1) * 8, :, :], (), [hk])
                                Pb = Pg[pcount % 2]
                                pk = "fP%d" % (pcount % 2)
                                pcount += 1
                                for b2 in range(2):
                                    bank, bk = next_mm()
                                    for q in range(4):
                                        kk = b2 * 4 + q
                                        k1 = g8 * 8 + kk
                                        for ri in range(2):
                                            A("pe", lambda e, bank=bank, q=q, kk=kk, k1=k1, ri=ri, Hb=Hb: e.matmul(bank[0:64, q * 128:(q + 1) * 128], lhsT=Z[:, :, ri * 64 + k1], rhs=Hb[:, kk, ri, :], start=(ri == 0), stop=(ri == 1)),
                                              ["fZ", hk], [bk])
                                    A("act" if b2 == 0 else "dve",
                                      (lambda e, bank=bank, b2=b2, Pb=Pb: e.copy(out=Pb[0:64, b2 * 4:(b2 + 1) * 4, :], in_=bank[0:64, :].rearrange("p (q c) -> p q c", q=4))) if b2 == 0 else
                                      (lambda e, bank=bank, b2=b2, Pb=Pb: e.tensor_copy(out=Pb[0:64, b2 * 4:(b2 + 1) * 4, :], in_=bank[0:64, :].rearrange("p (q c) -> p q c", q=4))),
                                      [bk], [pk])
                                bank, bk = next_mm()
                                ia, ib = (0, 1) if qd == 0 else (2, 3)
                                mcols = 64 if qd == 0 else 128
                                A("pe", lambda e, bank=bank, Pb=Pb, ia=ia, mcols=mcols: e.matmul(bank[0:mcols, :], lhsT=CH[0:64, ia, 0:mcols], rhs=Pb[0:64, :, 0:64], start=True, stop=False), [pk, "fCH"], [bk])
                                A("pe", lambda e, bank=bank, Pb=Pb, ib=ib, mcols=mcols: e.matmul(bank[0:mcols, :], lhsT=CH[0:64, ib, 0:mcols], rhs=Pb[0:64, :, 64:128], start=False, stop=True), [pk, "fCH"], [bk])
                                A("dve", lambda e, bank=bank, pb=pb, half=half, g8=g8: e.tensor_tensor(
                                    out=mT[pb:pb + 64, half, 0:TL].rearrange("p (k2 k1) -> p k1 k2", k1=64)[:, g8 * 8:(g8 + 1) * 8, :],
                                    in0=bank[pb:pb + 64, :].rearrange("p (a b) -> p a b", a=8),
                                    in1=sga[pb:pb + 64, 0:TL].rearrange("p (k2 k1) -> p k1 k2", k1=64)[:, g8 * 8:(g8 + 1) * 8, :], op=ALU.mult),
                                  [bk] + sgakeys, [("mT", half, t) for t in range(32)])
                df.barrier(bart[:, 0:1])

                with contextlib.ExitStack() as st2:
                    sb2 = lambda name, shape, dt: st2.enter_context(nc.sbuf_tensor(uniq(name), shape, dt))
                    qT = sb2("qT", [128, T], BF16)
                    kT = sb2("kT", [128, T], BF16)
                    sgb = sb2("sgb", [128, T], BF16)
                    vaug = sb2("vaug", [128, NT, 130], BF16)
                    eb32 = sb2("eb32", [128, 7 * 128], F32)
                    Eh = [sb2("Eh%d" % i, [128, 21 * 128], BF16) for i in range(2)]
                    PT = [sb2("PT%d" % i, [128, 7 * 128], BF16) for i in range(3)]
                    onb = sb2("onb", [128, NT, 128], BF16)
                    rden = sb2("rden", [128, 64], F32)
                    A("pool", lambda e: e.memset(vaug[:], 1.0), (), [("vaug", t4) for t4 in range(9)])
                    pti = 0
                    rdi = 0
                    for hp in range(6):
                        wt, wk = load_w(512 + hp * 128)
                        inproj_fm(wt, wk, NTILES, lambda bank, bk, n0, nn: A(
                            "act", lambda e: e.copy(out=qT[:, n0:n0 + nn], in_=bank[:, 0:nn]), [bk], [("qT", n0)]))
                        wt, wk = load_w(1280 + hp * 128)
                        inproj_fm(wt, wk, NTILES, lambda bank, bk, n0, nn: A(
                            "dve", lambda e: e.tensor_copy(out=kT[:, n0:n0 + nn], in_=bank[:, 0:nn]), [bk], [("kT", n0)]))
                        wt, wk = load_w(2816 + hp * 128)
                        inproj_fm(wt, wk, NTILES, lambda bank, bk, n0, nn: A(
                            "act", lambda e: e.activation(out=sgb[:, n0:n0 + nn], in_=bank[:, 0:nn], func=AF.Silu), [bk], [("sgb", n0)]))
                        wt, wk = load_w(2048 + hp * 128)
                        for t4 in range(9):
                            bank, bk = next_mm()
                            nq = 4 if t4 < 8 else 2
                            for q in range(nq):
                                t = t4 * 4 + q
                                for k in range(8):
                                    A("pe", lambda e, bank=bank, q=q, k=k, t=t, wt=wt: e.matmul(bank[:, q * 128:(q + 1) * 128], lhsT=xb[:, k, t * 128:(t + 1) * 128], rhs=wt[:, k, :], start=(k == 0), stop=(k == 7)),
                                      [wk, ("xb", t)], [bk])
                            for hh in range(2):
                                A("dve" if hh == 0 else "act",
                                  (lambda e, bank=bank, t4=t4, nq=nq, hh=hh: e.tensor_copy(out=vaug[:, t4 * 4:t4 * 4 + nq, hh * 65:hh * 65 + 64], in_=bank[:, 0:nq * 128].rearrange("p (q c) -> p q c", q=nq)[:, :, hh * 64:(hh + 1) * 64])) if hh == 0 else
                                  (lambda e, bank=bank, t4=t4, nq=nq, hh=hh: e.copy(out=vaug[:, t4 * 4:t4 * 4 + nq, hh * 65:hh * 65 + 64], in_=bank[:, 0:nq * 128].rearrange("p (q c) -> p q c", q=nq)[:, :, hh * 64:(hh + 1) * 64])),
                                  [bk], [("vaug", t4)])
                        for hh in range(2):
                            h = hp * 2 + hh
                            E = Eh[hh]
                            ek = "Eh%d" % hh
                            for part in range(3):
                                dma("sp", eb32[:], ebias[j, h, :, part * 896:(part + 1) * 896], (), ["eb32"])
                                A("act", lambda e, E=E, part=part: e.activation(out=E[:, part * 896:(part + 1) * 896], in_=eb32[:], func=AF.Exp), ["eb32"], [ek])
                        its = [(hh, i) for hh in range(2) for i in range(NT)]

                        def geo(n):
                            hh, i = its[n]
                            if i < 32:
                                pbase, lt = na_pattern_base(i)
                                kts = lt + [32, 33]
                            else:
                                pbase, lt = None, []
                                kts = [32, 33]
                            return hh, i, pbase, lt, kts

                        def s_qk(n):
                            hh, i, pbase, lt, kts = geo(n)
                            hb = hh * 64
                            sp_ = stp[n % 2]
                            sk = "stp%d" % (n % 2)
                            for a_, kt in enumerate(kts):
                                A("pe", lambda e, sp_=sp_, a_=a_, kt=kt, i=i, hb=hb: e.matmul(sp_[:, a_ * 128:(a_ + 1) * 128], lhsT=kT[hb:hb + 64, kt * 128:(kt + 1) * 128], rhs=qT[hb:hb + 64, i * 128:(i + 1) * 128], start=True, stop=True),
                                  [("kT", (kt // 4) * 512), ("qT", (i // 4) * 512)], [sk])

                        def s_exp(n):
                            hh, i, pbase, lt, kts = geo(n)
                            nk = len(kts)
                            sp_ = stp[n % 2]
                            sk = "stp%d" % (n % 2)
                            P_ = PT[n % 3]
                            pk = "PT%d" % (n % 3)
                            A("act", lambda e, sp_=sp_, P_=P_, nk=nk: e.activation(out=P_[:, 0:nk * 128], in_=sp_[:, 0:nk * 128], func=AF.Exp, scale=0.125), [sk], [pk])

                        def s_mul(n):
                            hh, i, pbase, lt, kts = geo(n)
                            P_ = PT[n % 3]
                            pk = "PT%d" % (n % 3)
                            if lt:
                                nl = len(lt)
                                E = Eh[hh]
                                A("dve", lambda e, P_=P_, nl=nl, E=E, pbase=pbase: e.tensor_tensor(out=P_[:, 0:nl * 128], in0=P_[:, 0:nl * 128], in1=E[:, pbase * 128:(pbase + nl) * 128], op=ALU.mult),
                                  [pk, "Eh%d" % hh], [pk])

                        def s_pv(n):
                            hh, i, pbase, lt, kts = geo(n)
                            nk = len(kts)
                            P_ = PT[n % 3]
                            pk = "PT%d" % (n % 3)
                            pvb = mm[n % 2]
                            for a_, kt in enumerate(kts):
                                A("pe", lambda e, P_=P_, a_=a_, kt=kt, hh=hh, nk=nk, pvb=pvb: e.matmul(pvb[:, 0:65], lhsT=P_[:, a_ * 128:(a_ + 1) * 128], rhs=vaug[:, kt, hh * 65:hh * 65 + 65], start=(a_ == 0), stop=(a_ == nk - 1)),
                                  [pk, ("vaug", kt // 4)], ["mm%d" % (n % 2)])

                        def s_rec(n):
                            pvb = mm[n % 2]
                            rd = rden[:, n % 64:n % 64 + 1]
                            A("dve", lambda e, rd=rd, pvb=pvb: e.reciprocal(out=rd, in_=pvb[:, 64:65]), ["mm%d" % (n % 2)], [("rden", n % 64)])

                        def s_norm(n):
                            hh, i, pbase, lt, kts = geo(n)
                            pvb = mm[n % 2]
                            rd = rden[:, n % 64:n % 64 + 1]
                            A("act", lambda e, i=i, hh=hh, rd=rd, pvb=pvb: e.activation(out=onb[:, i, hh * 64:(hh + 1) * 64], in_=pvb[:, 0:64], func=AF.Copy, scale=rd), ["mm%d" % (n % 2), ("rden", n % 64)], [("on", i, hh)])

                        def burst(i):
                            if i % 8 == 7 or i == NT - 1:
                                i0 = (i // 8) * 8
                                return i0, i - i0 + 1
                            return None

                        def s_tr(n):
                            hh, i, pbase, lt, kts = geo(n)
                            if hh == 1 and burst(i):
                                i0, nb_ = burst(i)
                                for r_ in range(nb_):
                                    ii = i0 + r_
                                    A("pe", lambda e, ii=ii, r_=r_: e.transpose(trp[:, r_, :], onb[:, ii, :], idn[:]), [("on", ii, 0), ("on", ii, 1), "idn"], ["trp"])

                        def s_gate(n):
                            hh, i, pbase, lt, kts = geo(n)
                            if hh == 1 and burst(i):
                                i0, nb_ = burst(i)
                                A("dve", lambda e, i0=i0, nb_=nb_, hp=hp: e.tensor_tensor(out=mT[:, 2 + hp, i0 * 128:(i0 + nb_) * 128], in0=trp[:, 0:nb_, :].rearrange("p a b -> p (a b)"), in1=sgb[:, i0 * 128:(i0 + nb_) * 128], op=ALU.mult),
                                  ["trp"] + [("sgb", ((i0 + r_) // 4) * 512) for r_ in range(nb_)], [("mT", 2 + hp, i0 + r_) for r_ in range(nb_)])

                        pipeline([s_qk, s_exp, s_mul, s_pv, s_rec, s_norm, s_tr, s_gate], len(its), "att")
            df.barrier(bart[:, 0:1])

        MAG = 12582912.0
        TWO_PI = 2.0 * np.pi

        def odd_mixer(L):
            j = L // 2
            Wd = w_in_odd[j].rearrange("(k p) n -> p k n", p=128)
            Uv = mT[:, 0:4, :].rearrange("p c t -> p (c t)").rearrange("p (g b) -> p g b", b=544)
            PIECES = [(0, 256), (256, 256), (512, 32)]

            with contextlib.ExitStack() as st:
              if "s5a" not in _SKIP:
                sb = lambda name, shape, dt: st.enter_context(nc.sbuf_tensor(uniq(name), shape, dt))
                Ws = sb("Ws", [128, 8, 512], BF16)
                Stm = [sb("Stm%d" % i, [128, 32, 8, 16], BF16) for i in range(2)]
                dma("pool", Ws[:], Wd[:, :, 1536:2048], (), ["Ws"])
                for bt in range(5):
                    nb = 128 if bt < 4 else 32
                    tok0 = 1024 * bt
                    S_ = Stm[bt % 2]
                    sk = "Stm%d" % (bt % 2)
                    for t8 in range(8):
                        bank, bk = next_mm()
                        for k in range(8):
                            A("pe", lambda e, bank=bank, k=k, nb=nb, tok0=tok0, t8=t8: e.matmul(bank[0:nb, :], lhsT=xb[:, k, tok0 + t8:tok0 + 8 * nb:8], rhs=Ws[:, k, :], start=(k == 0), stop=(k == 7)),
                              ["Ws"] + xbkeys(tok0, 8 * nb), [bk])
                        if t8 % 2 == 0:
                            A("act", lambda e, bank=bank, nb=nb, S_=S_, t8=t8: e.copy(out=S_[0:nb, :, t8, :], in_=bank[0:nb, :].rearrange("p (g m) -> p g m", m=16)), [bk], [sk])
                        else:
                            A("dve", lambda e, bank=bank, nb=nb, S_=S_, t8=t8: e.tensor_copy(out=S_[0:nb, :, t8, :], in_=bank[0:nb, :].rearrange("p (g m) -> p g m", m=16)), [bk], [sk])
                    for g8 in range(4):
                        for q in range(8):
                            g = g8 * 8 + q
                            A("pe", lambda e, S_=S_, nb=nb, g=g, q=q: e.transpose(trp[:, q, 0:nb], S_[0:nb, g, :, :].rearrange("p a b -> p (a b)"), idn[0:nb, 0:nb]), [sk, "idn"], ["trp"])
                        if g8 % 2 == 0:
                            A("act", lambda e, g8=g8, bt=bt, nb=nb: e.copy(out=Uv[:, g8 * 8:(g8 + 1) * 8, bt * 128:bt * 128 + nb], in_=trp[:, :, 0:nb]), ["trp"], ["U"])
                        else:
                            A("dve", lambda e, g8=g8, bt=bt, nb=nb: e.tensor_copy(out=Uv[:, g8 * 8:(g8 + 1) * 8, bt * 128:bt * 128 + nb], in_=trp[:, :, 0:nb]), ["trp"], ["U"])
            df.barrier(bart[:, 0:1])

            with contextlib.ExitStack() as st:
              if "s5b" not in _SKIP:
                sb = lambda name, shape, dt: st.enter_context(nc.sbuf_tensor(uniq(name), shape, dt))
                V = lambda e: e
                lam_r = sb("lam_r", [128, 32], F32)
                lam_i = sb("lam_i", [128, 32], F32)
                ls1 = sb("ls1", [128, 32], F32)
                st_tmp = contextlib.ExitStack()
                sbt = lambda name, shape, dt: st_tmp.enter_context(nc.sbuf_tensor(uniq(name), shape, dt))
                cr1 = sb("cr1", [128, 32, 16], F32)
                ci1 = sb("ci1", [128, 32, 16], F32)
                exps = sb("exps", [128, 4, 8], F32)
                mask = sb("mask", [128, 2, 128], F32)
                iota = sb("iota", [128, 544], F32)
                drep = sb("drep", [128, 32], F32)
                sm = [sb("sm%d" % i, [128, 32], F32) for i in range(14)]
                Bbr = sb("Bbr", [128, 32, 16], F32)
                Bbi = sb("Bbi", [128, 32, 16], F32)
                Wr = sb("Wr", [128, 4, 8, 32], F32)
                Wi = sb("Wi", [128, 4, 8, 32], F32)
                rho8 = sb("rho8", [128, 32], F32)
                tt8 = sb("tt8", [128, 32], F32)
                br1 = sbt("br1", [128, 32, 16], F32)
                bi1 = sbt("bi1", [128, 32, 16], F32)
                tb1 = sbt("tb1", [128, 32, 16], F32)
                tb2 = sbt("tb2", [128, 32, 16], F32)
                EA = sbt("EA", [128, 4, 8, 32], F32)
                ET = sbt("ET", [128, 4, 8, 32], F32)
                tw = sbt("tw", [128, 4, 8, 32], F32)
                dma("sp", lam_r[:], s5_lam1[j, 0], (), ["lam_r"])
                dma("sp", lam_i[:], s5_lam1[j, 1], (), ["lam_i"])
                for d_ in range(2):
                    dma("sp", ls1[d_ * 64:(d_ + 1) * 64, :], s5_ls[j, d_:d_ + 1, :].partition_broadcast(64), (), ["ls1"])
                dma("sp", br1[:], s5_b1[j, 0], (), ["br1"])
                dma("sp", bi1[:], s5_b1[j, 1], (), ["bi1"])
                dma("sp", cr1[:], s5_c1[j, 0], (), ["cr1"])
                dma("sp", ci1[:], s5_c1[j, 1], (), ["ci1"])
                dma("sp", exps[:], cEXPS, (), ["exps"])
                dma("sp", mask[:], cMASK, (), ["mask"])
                dma("sp", iota[:], cIOTA, (), ["iota"])
                dma("sp", drep[:], s5_drep[j], (), ["drep"])
                PK = ["pre"]

                def dv(fn):
                    A("dve", fn, PK + ["lam_r", "lam_i", "ls1", "br1", "bi1", "cr1", "ci1", "exps", "mask", "iota", "drep"], PK)

                def ac(fn):
                    A("act", fn, PK, PK)

                def sincos(tt, sn, cs, tmp, tmp2):
                    dv(lambda e: e.tensor_scalar(out=tmp, in0=tt, scalar1=MAG, scalar2=MAG, op0=ALU.add, op1=ALU.subtract))
                    dv(lambda e: e.tensor_tensor(out=tmp, in0=tt, in1=tmp, op=ALU.subtract))
                    ac(lambda e: e.activation(out=sn, in_=tmp, func=AF.Sin, scale=TWO_PI))
                    dv(lambda e: e.tensor_scalar(out=tmp2, in0=tt, scalar1=0.25, scalar2=None, op0=ALU.add))
                    dv(lambda e: e.tensor_scalar(out=tmp, in0=tmp2, scalar1=MAG, scalar2=MAG, op0=ALU.add, op1=ALU.subtract))
                    dv(lambda e: e.tensor_tensor(out=tmp, in0=tmp2, in1=tmp, op=ALU.subtract))
                    ac(lambda e: e.activation(out=cs, in_=tmp, func=AF.Sin, scale=TWO_PI))

                lr, dtt, a_, tht, mag1, s1, c1, w1r, w1i, den, cfr, cfi, x1, x2 = [t[:] for t in sm]
                dv(lambda e: e.tensor_scalar(out=lr, in0=lam_r[:], scalar1=-1e-4, scalar2=None, op0=ALU.min))
                ac(lambda e: e.activation(out=dtt, in_=ls1[:], func=AF.Exp))
                dv(lambda e: e.tensor_tensor(out=a_, in0=lr, in1=dtt, op=ALU.mult))
                dv(lambda e: e.tensor_tensor(out=tht, in0=lam_i[:], in1=dtt, op=ALU.mult))
                dv(lambda e: e.tensor_scalar(out=tht, in0=tht, scalar1=1.0 / TWO_PI, scalar2=None, op0=ALU.mult))
                ac(lambda e: e.activation(out=mag1, in_=a_, func=AF.Exp))
                sincos(tht, s1, c1, x1, x2)
                dv(lambda e: e.tensor_tensor(out=w1r, in0=mag1, in1=c1, op=ALU.mult))
                dv(lambda e: e.tensor_tensor(out=w1i, in0=mag1, in1=s1, op=ALU.mult))
                dv(lambda e: e.tensor_scalar(out=w1r, in0=w1r, scalar1=-1.0, scalar2=None, op0=ALU.add))
                dv(lambda e: e.tensor_tensor(out=den, in0=lr, in1=lr, op=ALU.mult))
                dv(lambda e: e.tensor_tensor(out=x1, in0=lam_i[:], in1=lam_i[:], op=ALU.mult))
                dv(lambda e: e.tensor_tensor(out=den, in0=den, in1=x1, op=ALU.add))
                dv(lambda e: e.reciprocal(out=den, in_=den))
                dv(lambda e: e.tensor_tensor(out=x1, in0=w1r, in1=lr, op=ALU.mult))
                dv(lambda e: e.tensor_tensor(out=x2, in0=w1i, in1=lam_i[:], op=ALU.mult))
                dv(lambda e: e.tensor_tensor(out=x1, in0=x1, in1=x2, op=ALU.add))
                dv(lambda e: e.tensor_tensor(out=cfr, in0=x1, in1=den, op=ALU.mult))
                dv(lambda e: e.tensor_tensor(out=x1, in0=w1i, in1=lr, op=ALU.mult))
                dv(lambda e: e.tensor_tensor(out=x2, in0=w1r, in1=lam_i[:], op=ALU.mult))
                dv(lambda e: e.tensor_tensor(out=x1, in0=x1, in1=x2, op=ALU.subtract))
                dv(lambda e: e.tensor_tensor(out=cfi, in0=x1, in1=den, op=ALU.mult))
                bc = lambda t: t.unsqueeze(2).to_broadcast([128, 32, 16])
                dv(lambda e: e.tensor_tensor(out=tb1[:], in0=br1[:], in1=bc(cfr), op=ALU.mult))
                dv(lambda e: e.tensor_tensor(out=tb2[:], in0=bi1[:], in1=bc(cfi), op=ALU.mult))
                dv(lambda e: e.tensor_tensor(out=Bbr[:], in0=tb1[:], in1=tb2[:], op=ALU.subtract))
                dv(lambda e: e.tensor_tensor(out=tb1[:], in0=bi1[:], in1=bc(cfr), op=ALU.mult))
                dv(lambda e: e.tensor_tensor(out=tb2[:], in0=br1[:], in1=bc(cfi), op=ALU.mult))
                dv(lambda e: e.tensor_tensor(out=Bbi[:], in0=tb1[:], in1=tb2[:], op=ALU.add))
                exb = exps[:].unsqueeze(3).to_broadcast([128, 4, 8, 32])
                ab = lambda t: t.unsqueeze(1).unsqueeze(1).to_broadcast([128, 4, 8, 32])
                dv(lambda e: e.tensor_tensor(out=EA[:], in0=exb, in1=ab(a_), op=ALU.mult))
                dv(lambda e: e.tensor_tensor(out=ET[:], in0=exb, in1=ab(tht), op=ALU.mult))
                ac(lambda e: e.activation(out=EA[:], in_=EA[:], func=AF.Exp))
                sincos(ET[:], Wi[:], Wr[:], tw[:], ET[:])
                dv(lambda e: e.tensor_tensor(out=Wr[:], in0=Wr[:], in1=EA[:], op=ALU.mult))
                dv(lambda e: e.tensor_tensor(out=Wi[:], in0=Wi[:], in1=EA[:], op=ALU.mult))
                dv(lambda e: e.tensor_scalar(out=x1, in0=a_, scalar1=8.0, scalar2=None, op0=ALU.mult))
                ac(lambda e: e.activation(out=rho8[:], in_=x1, func=AF.Exp))
                dv(lambda e: e.tensor_scalar(out=tt8[:], in0=tht, scalar1=8.0, scalar2=None, op0=ALU.mult))

                st_tmp.close()
                df.barrier(bart[:, 0:1])
                PK = ["pre"]
                KT = sb("KT", [128, 4, 128], BF16)
                ELTr = sb("ELTr", [128, 4, 128], BF16)
                ELTi = sb("ELTi", [128, 4, 128], BF16)
                CLr = sb("CLr", [128, 4, 128], BF16)
                nCLi = sb("nCLi", [128, 4, 128], BF16)
                Rr = sb("Rr", [128, 4, 128], BF16)
                Ri = sb("Ri", [128, 4, 128], BF16)
                Qr = sb("Qr", [128, 4, 128], BF16)
                nQi = sb("nQi", [128, 4, 128], BF16)
                ELr = sb("ELr", [128, 4, 128], BF16)
                ELi = sb("ELi", [128, 4, 128], BF16)
                p1 = sb("p1", [128, 4, 128], F32)
                p2 = sb("p2", [128, 4, 128], F32)
                scr = mT[:, 4:8, :].rearrange("p c t -> p (c t)").bitcast(F32).rearrange("p (n b) -> p n b", b=544)
                SETS = []
                for si_ in range(2):
                    d_ = {}
                    for ti_, nm in enumerate(("Er", "Ei", "cos", "sin", "gr", "gi", "sr", "si")):
                        d_[nm] = scr[:, si_ * 8 + ti_, :]
                    d_["tA"] = sb("tA%d" % si_, [128, 544], F32)[:]
                    d_["Zr"] = sb("Zr%d" % si_, [128, 544], BF16)
                    d_["Zi"] = sb("Zi%d" % si_, [128, 544], BF16)
                    d_["zg"] = sb("zg%d" % si_, [128, 544], BF16)
                    d_["id"] = si_
                    SETS.append(d_)
                    A("pool", lambda e, d_=d_: e.memset(d_["Zr"][:], 0.0), (), ["Zr%d" % si_])
                    A("pool", lambda e, d_=d_: e.memset(d_["Zi"][:], 0.0), (), ["Zi%d" % si_])

                def cprod(outr, outi, l, Xr_, Xi_, g0, neg_i):
                    wv = lambda W_: W_[:, l, :, g0:g0 + 4].rearrange("p s g -> p g s").unsqueeze(3).to_broadcast([128, 4, 8, 16])
                    xv = lambda X_: X_[:, g0:g0 + 4, :].unsqueeze(2).to_broadcast([128, 4, 8, 16])
                    o4 = lambda t: t[:].rearrange("p g (s m) -> p g s m", m=16)
                    dv(lambda e: e.tensor_tensor(out=o4(p1), in0=wv(Wr), in1=xv(Xr_), op=ALU.mult))
                    dv(lambda e: e.tensor_tensor(out=o4(p2), in0=wv(Wi), in1=xv(Xi_), op=ALU.mult))
                    dv(lambda e: e.tensor_tensor(out=outr[:], in0=p1[:], in1=p2[:], op=ALU.subtract))
                    dv(lambda e: e.tensor_tensor(out=o4(p1), in0=wv(Wr), in1=xv(Xi_), op=ALU.mult))
                    dv(lambda e: e.tensor_tensor(out=o4(p2), in0=wv(Wi), in1=xv(Xr_), op=ALU.mult))
                    if neg_i:
                        dv(lambda e: e.scalar_tensor_tensor(out=outi[:], in0=p1[:], scalar=-1.0, in1=p2[:], op0=ALU.mult, op1=ALU.subtract))
                    else:
                        dv(lambda e: e.tensor_tensor(out=outi[:], in0=p1[:], in1=p2[:], op=ALU.add))

                for qq in range(8):
                    g0 = qq * 4
                    cprod(Rr, Ri, 0, Bbr, Bbi, g0, False)
                    cprod(ELr, ELi, 1, Bbr, Bbi, g0, False)
                    cprod(Qr, nQi, 2, cr1, ci1, g0, True)
                    cprod(CLr, nCLi, 3, cr1, ci1, g0, True)
                    for q in range(4):
                        for h_ in range(2):
                            hb = h_ * 64
                            bank = mm[h_]
                            bk = "mm%d" % h_
                            A("pe", lambda e, bank=bank, hb=hb, q=q: e.matmul(bank[:, 0:128], lhsT=Rr[hb:hb + 64, q, :], rhs=Qr[hb:hb + 64, q, :], start=True, stop=False), PK, [bk])
                            A("pe", lambda e, bank=bank, hb=hb, q=q: e.matmul(bank[:, 0:128], lhsT=Ri[hb:hb + 64, q, :], rhs=nQi[hb:hb + 64, q, :], start=False, stop=True), PK, [bk])
                        A("dve", lambda e: e.tensor_tensor(out=p1[:, 0, :], in0=mm[0][:, 0:128], in1=mask[:, 0, :], op=ALU.mult), ["mm0"] + PK, PK)
                        A("dve", lambda e: e.tensor_tensor(out=p2[:, 0, :], in0=mm[1][:, 0:128], in1=mask[:, 1, :], op=ALU.mult), ["mm1"] + PK, PK)
                        A("dve", lambda e, q=q: e.tensor_tensor(out=KT[:, q, :], in0=p1[:, 0, :], in1=p2[:, 0, :], op=ALU.add), PK, PK)
                        A("pe", lambda e, q=q: e.transpose(trp[:, 0, :], ELr[:, q, :], idn[:]), PK + ["idn"], ["trp"])
                        A("pe", lambda e, q=q: e.transpose(trp[:, 1, :], ELi[:, q, :], idn[:]), PK + ["idn"], ["trp"])
                        A("act", lambda e, q=q: e.copy(out=ELTr[:, q, :], in_=trp[:, 0, :]), ["trp"] + PK, PK)
                        A("act", lambda e, q=q: e.copy(out=ELTi[:, q, :], in_=trp[:, 1, :]), ["trp"] + PK, PK)
                    def grp(q, g, S):
                        sid = S["id"]
                        K_ = lambda nm: "%s%d" % (nm, sid)
                        Eb = stp[sid]
                        ebk = "stp%d" % sid
                        pvo = sid * 128
                        Er, Ei, cosT, sinT, gr, gi, sr, si, tA = S["Er"], S["Ei"], S["cos"], S["sin"], S["gr"], S["gi"], S["sr"], S["si"], S["tA"]
                        Zr_, Zi_, z_ = S["Zr"], S["Zi"], S["zg"]

                        def st0():
                            for ri, ELT_ in enumerate((ELTr, ELTi)):
                                for (b0, nb) in PIECES:
                                    if b0 < 512:
                                        yo = Eb[:, ri * 512 + b0:ri * 512 + b0 + nb]
                                        wk_ = [ebk]
                                    else:
                                        yo = pv[:, pvo + ri * 64:pvo + ri * 64 + nb]
                                        wk_ = ["pv"]
                                    A("pe", lambda e, yo=yo, ELT_=ELT_, b0=b0, nb=nb: e.matmul(yo, lhsT=ELT_[:, q, :], rhs=Uv[:, g, b0:b0 + nb], start=True, stop=True), PK + ["U"], wk_)

                        def st1():
                            for ri, E_ in enumerate((Er, Ei)):
                                ek = K_("Er" if ri == 0 else "Ei")
                                A("act", lambda e, E_=E_, ri=ri: e.copy(out=E_[0:64, 32:544], in_=Eb[0:64, ri * 512:(ri + 1) * 512]), [ebk], [ek])
                                A("act", lambda e, E_=E_, ri=ri: e.copy(out=E_[0:64, 0:32], in_=pv[0:64, pvo + ri * 64:pvo + ri * 64 + 32]), ["pv"], [ek])
                                A("act", lambda e, E_=E_, ri=ri: e.copy(out=E_[64:128, 543:31:-1], in_=Eb[64:128, ri * 512:(ri + 1) * 512]), [ebk], [ek])
                                A("act", lambda e, E_=E_, ri=ri: e.copy(out=E_[64:128, 31::-1], in_=pv[64:128, pvo + ri * 64:pvo + ri * 64 + 32]), ["pv"], [ek])
                            A("dve", lambda e: e.tensor_scalar(out=gr, in0=iota[:], scalar1=tt8[:, g:g + 1], scalar2=None, op0=ALU.mult), PK + ["iota"], [K_("gr")])
                            A("dve", lambda e: e.tensor_scalar(out=gi, in0=gr, scalar1=MAG, scalar2=MAG, op0=ALU.add, op1=ALU.subtract), [K_("gr")], [K_("gi")])
                            A("dve", lambda e: e.tensor_tensor(out=gi, in0=gr, in1=gi, op=ALU.subtract), [K_("gr"), K_("gi")], [K_("gi")])

                        def st2():
                            A("act", lambda e: e.activation(out=sinT, in_=gi, func=AF.Sin, scale=TWO_PI), [K_("gi")], [K_("sin")])
                            A("dve", lambda e: e.tensor_scalar(out=gr, in0=gr, scalar1=0.25, scalar2=None, op0=ALU.add), [K_("gr")], [K_("gr")])
                            A("dve", lambda e: e.tensor_scalar(out=gi, in0=gr, scalar1=MAG, scalar2=MAG, op0=ALU.add, op1=ALU.subtract), [K_("gr"), K_("sin")], [K_("gi")])
                            A("dve", lambda e: e.tensor_tensor(out=gi, in0=gr, in1=gi, op=ALU.subtract), [K_("gr"), K_("gi")], [K_("gi")])

                        def st3():
                            A("act", lambda e: e.activation(out=cosT, in_=gi, func=AF.Sin, scale=TWO_PI), [K_("gi")], [K_("cos")])

                        def st4():
                            A("pool", lambda e: e.tensor_tensor(out=gr, in0=Er, in1=cosT, op=ALU.mult), [K_("Er"), K_("cos"), K_("gr")], [K_("gr")])
                            A("pool", lambda e: e.tensor_tensor(out=tA, in0=Ei, in1=sinT, op=ALU.mult), [K_("Ei"), K_("sin")], [K_("tA")])
                            A("pool", lambda e: e.tensor_tensor(out=gr, in0=gr, in1=tA, op=ALU.add), [K_("gr"), K_("tA")], [K_("gr")])
                            A("pool", lambda e: e.tensor_tensor(out=gi, in0=Ei, in1=cosT, op=ALU.mult), [K_("Ei"), K_("cos"), K_("gi")], [K_("gi")])
                            A("pool", lambda e: e.tensor_tensor(out=tA, in0=Er, in1=sinT, op=ALU.mult), [K_("Er"), K_("sin"), K_("tA")], [K_("tA")])
                            A("pool", lambda e: e.tensor_tensor(out=gi, in0=gi, in1=tA, op=ALU.subtract), [K_("gi"), K_("tA")], [K_("gi")])

                        def st5():
                            rb = rho8[:, g:g + 1].to_broadcast([128, 544])
                            A("dve", lambda e: e.tensor_tensor_scan(out=sr, data0=rb, data1=gr, initial=0.0, op0=ALU.mult, op1=ALU.add), [K_("gr")] + PK, [K_("sr")])
                            A("dve", lambda e: e.tensor_tensor_scan(out=si, data0=rb, data1=gi, initial=0.0, op0=ALU.mult, op1=ALU.add), [K_("gi")] + PK, [K_("si")])

                        def rot_out(Z_, zk, c1, k1, c2, k2, op):
                            A("pool", lambda e: e.tensor_tensor(out=Er, in0=sr, in1=c1, op=ALU.mult), [K_("sr"), k1, K_("Er")], [K_("Er")])
                            A("pool", lambda e: e.tensor_tensor(out=Ei, in0=si, in1=c2, op=ALU.mult), [K_("si"), k2, K_("Ei")], [K_("Ei")])
                            A("dve", lambda e: e.tensor_tensor(out=Z_[0:64, 0:512], in0=Er[0:64, 31:543], in1=Ei[0:64, 31:543], op=op), [K_("Er"), K_("Ei")], [zk])
                            A("dve", lambda e: e.tensor_tensor(out=Z_[0:64, 513:544], in0=Er[0:64, 0:31], in1=Ei[0:64, 0:31], op=op), [K_("Er"), K_("Ei")], [zk])
                            A("dve", lambda e: e.tensor_tensor(out=Z_[64:128, 542::-1], in0=Er[64:128, 0:543], in1=Ei[64:128, 0:543], op=op), [K_("Er"), K_("Ei")], [zk])

                        def st6():
                            rot_out(Zr_, K_("Zr"), cosT, K_("cos"), sinT, K_("sin"), ALU.subtract)

                        def st7():
                            rot_out(Zi_, K_("Zi"), sinT, K_("sin"), cosT, K_("cos"), ALU.add)

                        def st8():
                            for (b0, nb) in PIECES:
                                if b0 < 512:
                                    yo = mm[0][:, b0:b0 + nb]
                                    wk_ = ["mm0"]
                                else:
                                    yo = mm[1][:, 0:nb]
                                    wk_ = ["mm1"]
                                A("pe", lambda e, yo=yo, b0=b0, nb=nb: e.matmul(yo, lhsT=KT[:, q, :], rhs=Uv[:, g, b0:b0 + nb], start=True, stop=False), PK + ["U"], wk_)
                                A("pe", lambda e, yo=yo, b0=b0, nb=nb: e.matmul(yo, lhsT=CLr[:, q, :], rhs=Zr_[:, b0:b0 + nb], start=False, stop=False), PK + [K_("Zr")], wk_)
                                A("pe", lambda e, yo=yo, b0=b0, nb=nb: e.matmul(yo, lhsT=nCLi[:, q, :], rhs=Zi_[:, b0:b0 + nb], start=False, stop=True), PK + [K_("Zi")], wk_)
                            A("dve", lambda e: e.scalar_tensor_tensor(out=gr[:, 0:512], in0=Uv[:, g, 0:512], scalar=drep[:, g:g + 1], in1=mm[0][:, 0:512], op0=ALU.mult, op1=ALU.add),
                              ["mm0", "U", "drep", K_("gr")], [K_("gr")])
                            A("dve", lambda e: e.scalar_tensor_tensor(out=gr[:, 512:544], in0=Uv[:, g, 512:544], scalar=drep[:, g:g + 1], in1=mm[1][:, 0:32], op0=ALU.mult, op1=ALU.add),
                              ["mm1", "U", "drep", K_("gr")], [K_("gr")])
                            A("act", lambda e: e.activation(out=z_[:], in_=gr, func=AF.Gelu), [K_("gr")], [K_("zg")])
                            for t8 in range(8):
                                dma("sp", zs[t8, g], z_[t8 * 16:(t8 + 1) * 16, :], [K_("zg")], ["zs"])

                        return [st0, st1, st2, st3, st4, st5, st6, st7], st8

                    for pr in range(2):
                        gA, gB = g0 + 2 * pr, g0 + 2 * pr + 1
                        stA, yA = grp(2 * pr, gA, SETS[0])
                        stB, yB = grp(2 * pr + 1, gB, SETS[1])
                        for k_ in range(len(stA)):
                            stA[k_]()
                            stB[k_]()
                        yA()
                        yB()
            df.barrier(bart[:, 0:1])

            with contextlib.ExitStack() as st:
              if "s5c" not in _SKIP:
                sb = lambda name, shape, dt: st.enter_context(nc.sbuf_tensor(uniq(name), shape, dt))
                Wg = sb("Wg", [128, 4, 1024], BF16)
                bg = sb("bg", [128, 8], F32)
                sgd = sb("sgd", [128, T], BF16)
                zsb = [sb("zsb%d" % i, [128, 4, 544], BF16) for i in range(2)]
                sig = [sb("sig%d" % i, [128, 544], F32) for i in range(2)]
                v1 = [sb("v1%d" % i, [128, 544], F32) for i in range(2)]
                wgd = sb("wgd", [128, 8, 128], BF16)
                for h_ in range(2):
                    dma("pool", Wg[:, :, h_ * 512:(h_ + 1) * 512], glu_w[j].rearrange("(c p) n -> p c n", p=128)[:, :, h_ * 512:(h_ + 1) * 512], (), ["Wg"])
                dma("sp", bg[:], glu_bT[j], (), ["bg"])
                for k in range(4):
                    dma("pool", wgd[:], Wd[:, :, 2048 + k * 128:2048 + (k + 1) * 128], (), ["wgd"])
                    inproj_fm(wgd, "wgd", NTILES, lambda bank, bk, n0, nn: A(
                        "act", lambda e: e.activation(out=sgd[:, n0:n0 + nn], in_=bank[:, 0:nn], func=AF.Silu), [bk], ["sgd"]))

                    def banks(n):
                        if n % 2 == 0:
                            return (mm[0], "mm0"), (mm[1], "mm1"), (pv[:, 0:32], "pv"), (pv[:, 256:288], "pv")
                        return (stp[0][:, 0:512], "stp0a"), (stp[0][:, 512:1024], "stp0b"), (stp[1][:, 0:32], "stp1a"), (stp[1][:, 512:544], "stp1b")

                    def c_load(n):
                        zb, zbk = zsb[n % 2], "zsb%d" % (n % 2)
                        dma("sp", zb[:], zs[n].rearrange("g m b -> (g m) b").rearrange("(c p) b -> p c b", p=128), ["zs"], [zbk])

                    def c_mm(n):
                        zb, zbk = zsb[n % 2], "zsb%d" % (n % 2)
                        (vb, vk), (gb_, gk), (vp, vpk), (gp_, gpk) = banks(n)
                        for (b0, nb) in PIECES:
                            for vg in range(2):
                                if b0 < 512:
                                    yo = (vb if vg == 0 else gb_)[:, b0:b0 + nb]
                                    wk_ = [vk if vg == 0 else gk]
                                else:
                                    yo = vp if vg == 0 else gp_
                                    wk_ = [vpk if vg == 0 else gpk]
                                col = (vg * 4 + k) * 128
                                for c in range(4):
                                    A("pe", lambda e, yo=yo, c=c, col=col, zb=zb, b0=b0, nb=nb: e.matmul(yo, lhsT=Wg[:, c, col:col + 128], rhs=zb[:, c, b0:b0 + nb], start=(c == 0), stop=(c == 3)),
                                      ["Wg", zbk], wk_)

                    def c_sig(n):
                        (vb, vk), (gb_, gk), (vp, vpk), (gp_, gpk) = banks(n)
                        sg_, sgk = sig[n % 2], "sig%d" % (n % 2)
                        A("act", lambda e, gb_=gb_, sg_=sg_, k=k: e.activation(out=sg_[:, 0:512], in_=gb_[:, 0:512], func=AF.Sigmoid, bias=bg[:, 4 + k:5 + k]), [gk, "bg"], [sgk])
                        A("act", lambda e, gp_=gp_, sg_=sg_, k=k: e.activation(out=sg_[:, 512:544], in_=gp_, func=AF.Sigmoid, bias=bg[:, 4 + k:5 + k]), [gpk, "bg"], [sgk])

                    def c_stt(n):
                        (vb, vk), (gb_, gk), (vp, vpk), (gp_, gpk) = banks(n)
                        sg_, sgk = sig[n % 2], "sig%d" % (n % 2)
                        v_, v1k = v1[n % 2], "v1%d" % (n % 2)
                        A("dve", lambda e, vb=vb, sg_=sg_, v_=v_, k=k: e.scalar_tensor_tensor(out=v_[:, 0:512], in0=vb[:, 0:512], scalar=bg[:, k:k + 1], in1=sg_[:, 0:512], op0=ALU.add, op1=ALU.mult), [vk, sgk, "bg"], [v1k])
                        A("dve", lambda e, vp=vp, sg_=sg_, v_=v_, k=k: e.scalar_tensor_tensor(out=v_[:, 512:544], in0=vp, scalar=bg[:, k:k + 1], in1=sg_[:, 512:544], op0=ALU.add, op1=ALU.mult), [vpk, sgk, "bg"], [v1k])

                    def c_out(n):
                        v_, v1k = v1[n % 2], "v1%d" % (n % 2)
                        A("pool", lambda e, v_=v_, n=n, k=k: e.tensor_tensor(out=mT[:, 4 + k, n::8], in0=v_[:], in1=sgd[:, n::8], op=ALU.mult), [v1k, "sgd"], [("mT", 4 + k, t) for t in range(NT)])

                    pipeline([c_load, c_mm, c_sig, c_stt, c_out], 8, "s5c")
            df.barrier(bart[:, 0:1])

            with contextlib.ExitStack() as st:
              if "gmlp" not in _SKIP:
                sb = lambda name, shape, dt: st.enter_context(nc.sbuf_tensor(uniq(name), shape, dt))
                Wv = sb("Wv", [128, 8, 512], BF16)
                Wu = sb("Wu", [128, 8, 512], BF16)
                Wc = sb("Wc", [128, 8, 512], BF16)
                wsT = sb("wsT", [128, 4, 128], BF16)
                sgT = sb("sgT", [128, 4], F32)
                bsb = sb("bsb", [128, 512], F32)
                st6 = sb("st6", [128, 12], F32)
                mv = sb("mv", [128, 4 * NT], F32)
                vn = [sb("vn%d" % i, [128, 512], BF16) for i in range(2)]
                sgc = [sb("sgc%d" % i, [128, 512], BF16) for i in range(2)]
                mx = [sb("mx%d" % i, [128, 512], F32) for i in range(2)]
                dma("pool", Wu[:], Wd[:, :, 0:512], (), ["Wu"])
                dma("pool", Wv[:], Wd[:, :, 512:1024], (), ["Wv"])
                dma("pool", Wc[:], Wd[:, :, 1024:1536], (), ["Wc"])
                dma("pool", wsT[:], sgu_wT[j], (), ["wsT"])
                dma("sp", sgT[:], sgu_gT[j], (), ["sgT"])
                dma("sp", bsb[:], sgu_b[j:j + 1, :].partition_broadcast(128), (), ["bsb"])
                vsb = [sb("vsb%d" % i, [128, 512], F32) for i in range(4)]
                tmx = [sb("tmx%d" % i, [128, 512], F32) for i in range(2)]
                MEAN = lambda n: mv[:, 4 * n:4 * n + 1]
                VAR = lambda n: mv[:, 4 * n + 1:4 * n + 2]
                RSg = lambda n: mv[:, 4 * n + 2:4 * n + 3]
                VS = lambda n: (vsb[n % 4], "vsb%d" % (n % 4))

                def g_v(n):
                    bank, bk = mm[n % 2], "mm%d" % (n % 2)
                    for k in range(8):
                        A("pe", lambda e, bank=bank, k=k, n=n: e.matmul(bank[:, :], lhsT=xb[:, k, n * 128:(n + 1) * 128], rhs=Wv[:, k, :], start=(k == 0), stop=(k == 7)), ["Wv", ("xb", n)], [bk])

                def g_cp(n):
                    bank, bk = mm[n % 2], "mm%d" % (n % 2)
                    vs_, vsk = VS(n)
                    A("act", lambda e, bank=bank, vs_=vs_: e.copy(out=vs_[:], in_=bank[:, :]), [bk], [vsk])

                def g_bn(n):
                    vs_, vsk = VS(n)
                    s6 = st6[:, (n % 2) * 6:(n % 2) * 6 + 6]
                    A("dve", lambda e, vs_=vs_, s6=s6: e.bn_stats(out=s6, in_=vs_[:]), [vsk], [("st6", n % 2)])
                    A("dve", lambda e, n=n, s6=s6: e.bn_aggr(out=mv[:, 4 * n:4 * n + 2], in_=s6), [("st6", n % 2)], [("mv", n)])

                def g_r1(n):
                    A("dve", lambda e, n=n: e.tensor_scalar(out=RSg(n), in0=VAR(n), scalar1=1.0, scalar2=EPS, op0=ALU.mult, op1=ALU.add), [("mv", n)], [("grs", n)])

                def g_r2(n):
                    A("act", lambda e, n=n: e.activation(out=RSg(n), in_=RSg(n), func=AF.Sqrt), [("grs", n)], [("grs", n)])

                def g_vn(n):
                    vs_, vsk = VS(n)
                    v_, vk = vn[n % 2], "vn%d" % (n % 2)
                    A("dve", lambda e, n=n: e.reciprocal(out=RSg(n), in_=RSg(n)), [("grs", n)], [("grs", n)])
                    A("dve", lambda e, vs_=vs_, v_=v_, n=n: e.tensor_scalar(out=v_[:], in0=vs_[:], scalar1=MEAN(n), scalar2=RSg(n), op0=ALU.subtract, op1=ALU.mult), [vsk, ("mv", n), ("grs", n)], [vk])

                def g_mm(n):
                    v_, vk = vn[n % 2], "vn%d" % (n % 2)
                    for g in range(4):
                        A("pe", lambda e, v_=v_, g=g: e.matmul(pv[:, g * 128:(g + 1) * 128], lhsT=v_[:, g * 128:(g + 1) * 128], rhs=wsT[:, g, :], start=True, stop=True), [vk, "wsT"], ["pv"])
                    sp_ = stp[n % 2]
                    for half, W_, wkk in ((0, Wu, "Wu"), (1, Wc, "Wc")):
                        sk = "stp%d%s" % (n % 2, "ab"[half])
                        for g in range(4):
                            for k in range(8):
                                A("pe", lambda e, sp_=sp_, half=half, W_=W_, g=g, k=k, n=n: e.matmul(sp_[:, half * 512 + g * 128:half * 512 + (g + 1) * 128], lhsT=W_[:, k, g * 128:(g + 1) * 128], rhs=xb[:, k, n * 128:(n + 1) * 128], start=(k == 0), stop=(k == 7)),
                                  [wkk, ("xb", n)], [sk])

                def g_ep(n):
                    sp_ = stp[n % 2]
                    c_, ck = sgc[n % 2], "sgc%d" % (n % 2)
                    m_, mk = mx[n % 2], "mx%d" % (n % 2)
                    t_, tk_ = tmx[n % 2], "tmx%d" % (n % 2)
                    A("act", lambda e, sp_=sp_, c_=c_: e.activation(out=c_[:], in_=sp_[:, 512:1024], func=AF.Silu), ["stp%db" % (n % 2)], [ck])
                    for g in range(4):
                        A("dve", lambda e, m_=m_, g=g: e.scalar_tensor_tensor(out=m_[:, g * 128:(g + 1) * 128], in0=pv[:, g * 128:(g + 1) * 128], scalar=sgT[:, g:g + 1], in1=bsb[:, g * 128:(g + 1) * 128], op0=ALU.mult, op1=ALU.add),
                          ["pv", "sgT", "bsb"], [mk])
                    A("dve", lambda e, m_=m_, sp_=sp_, t_=t_: e.tensor_tensor(out=t_[:], in0=m_[:], in1=sp_[:, 0:512], op=ALU.mult), [mk, "stp%da" % (n % 2)], [tk_])

                def g_out(n):
                    c_, ck = sgc[n % 2], "sgc%d" % (n % 2)
                    t_, tk_ = tmx[n % 2], "tmx%d" % (n % 2)
                    A("pool", lambda e, t_=t_, c_=c_, n=n: e.tensor_tensor(out=mT[:, 0:4, n * 128:(n + 1) * 128], in0=t_[:].rearrange("p (g c) -> p g c", g=4), in1=c_[:].rearrange("p (g c) -> p g c", g=4), op=ALU.mult),
                      [tk_, ck], [("mT", g, n) for g in range(4)])

                pipeline([g_v, g_cp, g_bn, g_r1, g_r2, g_vn, g_mm, g_ep, g_out], NT, "gmlp")
            df.barrier(bart[:, 0:1])

        for L in range(n_layers):
            last = (L == n_layers - 1)
            (prenorm_old if "oldpre" in _SKIP else prenorm)(L)
            if L % 2 == 0:
                if "even" not in _SKIP:
                    even_mixer(L)
            else:
                odd_mixer(L)
            (post_old if "oldpost" in _SKIP else post)(L, last)
        df.emit()
    return nc


_CACHE = {}


def kernel(x, c, ctx, c_ctx, w_ada, b_ada, pre_g, post_g, w_in_even, w_out_even, na_rpb,
           w_in_odd, w_out_odd, sgu_w, sgu_b, sgu_g, s5_lam_re, s5_lam_im, s5_log_step,
           s5_b_re, s5_b_im, s5_c_re, s5_c_im, s5_d, glu_w, glu_b, _n_layers=DEPTH):
    f = lambda a: np.ascontiguousarray(np.asarray(a), dtype=np.float32)
    B = x.shape[0]
    if _n_layers not in _CACHE:
        _CACHE[_n_layers] = build_program(_n_layers)
    nc = _CACHE[_n_layers]
    F1, H, CH, F256 = fnet_consts()
    eb = np.stack([build_ebias(f(na_rpb[j])) for j in range(2)]).reshape(2, 12, 128, 21 * 128)
    shared = dict(w_ada=f(w_ada), b_ada=f(b_ada), pre_g=f(pre_g), post_g=f(post_g), w_in_even=f(w_in_even),
                  w_out_even=f(w_out_even), ebias=eb, cF1=F1, cH=H, cCH=CH, cF256=F256)
    ex, mk, io = s5_consts()
    lam1 = np.stack([f(s5_lam_re), f(s5_lam_im)], axis=1)
    lam1 = np.ascontiguousarray(lam1.transpose(0, 1, 2, 4, 3)).reshape(2, 2, 128, 32)
    b1 = np.stack([f(s5_b_re), f(s5_b_im)], axis=1)
    b1 = np.ascontiguousarray(b1.transpose(0, 1, 2, 4, 3, 5)).reshape(2, 2, 128, 32, 16)
    c1 = np.stack([f(s5_c_re), f(s5_c_im)], axis=1)
    c1 = np.ascontiguousarray(c1.transpose(0, 1, 2, 5, 3, 4)).reshape(2, 2, 128, 32, 16)
    drep = np.ascontiguousarray(np.tile(f(s5_d).reshape(2, 32, 16).transpose(0, 2, 1), (1, 8, 1)))
    shared.update(dict(
        w_in_odd=f(w_in_odd), w_out_odd=f(w_out_odd),
        sgu_wT=np.ascontiguousarray(f(sgu_w).transpose(0, 3, 1, 2)),
        sgu_gT=np.ascontiguousarray(f(sgu_g).reshape(2, 4, 128).transpose(0, 2, 1)),
        sgu_b=f(sgu_b).reshape(2, 512), s5_lam1=lam1, s5_ls=f(s5_log_step), s5_b1=b1, s5_c1=c1, s5_drep=drep,
        glu_w=f(glu_w), glu_bT=np.ascontiguousarray(f(glu_b).reshape(2, 8, 128).transpose(0, 2, 1)),
        cEXPS=ex, cMASK=mk, cIOTA=io))
    cc = f(c_ctx).reshape(8, 128).T
    in_maps = []
    for b in range(B):
        m = dict(shared)
        m["xr"] = np.concatenate([f(x[b]), f(ctx[b])], axis=0)
        m["cT"] = np.ascontiguousarray(np.concatenate([f(c[b]).reshape(8, 128).T, cc], axis=1))
        in_maps.append(m)
    res = run_bass_kernel_spmd(nc, in_maps, core_ids=list(range(B)))
    return np.stack([np.asarray(r["out"], dtype=np.float32) for r in res.results], axis=0)
```

```python
import contextlib
import numpy as np
import concourse.bass as bass
import concourse.mybir as mybir
from concourse.bass_utils import run_bass_kernel_spmd

F32 = mybir.dt.float32
BF16 = mybir.dt.bfloat16
AF = mybir.ActivationFunctionType
ALU = mybir.AluOpType

ENGS = ("pe", "act", "dve", "pool", "sp")
RING = 8
SELF_SYNC = ("act", "dve", "pool")

D = 1024
T = 4352
TL = 4096
NT = 34
EPS = 1e-6
DEPTH = 4


class _Op:
    __slots__ = ("eng", "fn", "deps", "dma", "flag", "cnt", "ring", "target")


class DF:
    def __init__(self, nc):
        self.nc = nc
        self.ops = []
        self.lw = {}
        self.rd = {}
        self.ndma = {e: 0 for e in ENGS}
        self.last_on = {}
        self.bar = None
        self.dma_since_bar = []

    def add(self, eng, fn, reads=(), writes=(), dma=False):
        idx = len(self.ops)
        deps = set()
        if self.bar is not None:
            deps.add(self.bar)
        for r in reads:
            w = self.lw.get(r)
            if w is not None:
                deps.add(w)
        for r in writes:
            w = self.lw.get(r)
            if w is not None:
                deps.add(w)
            deps.update(self.rd.get(r, ()))
        for r in reads:
            self.rd.setdefault(r, []).append(idx)
        for r in writes:
            self.lw[r] = idx
            self.rd[r] = []
        o = _Op()
        o.eng, o.fn, o.deps, o.dma, o.flag, o.cnt = eng, fn, deps, dma, False, 0
        o.ring = o.target = None
        if dma:
            n = self.ndma[eng]
            self.ndma[eng] = n + 1
            o.ring = n % RING
            o.target = 16 * (n // RING + 1)
            self.dma_since_bar.append(idx)
        else:
            self.last_on[eng] = idx
        self.ops.append(o)
        return idx

    def barrier(self, tile):
        idx = len(self.ops)
        deps = set(self.last_on.values()) | set(self.dma_since_bar)
        if self.bar is not None:
            deps.add(self.bar)
        o = _Op()
        o.eng, o.fn, o.deps, o.dma, o.flag, o.cnt = "pool", (lambda e: e.memset(tile, 0.0)), deps, False, False, 0
        o.ring = o.target = None
        self.ops.append(o)
        self.last_on["pool"] = idx
        self.bar = idx
        self.dma_since_bar = []
        self.lw = {}
        self.rd = {}

    def emit(self):
        nc = self.nc
        ops = self.ops
        for o in ops:
            for d in o.deps:
                p = ops[d]
                if p.dma:
                    continue
                if p.eng != o.eng or (o.eng in SELF_SYNC) or o.dma:
                    p.flag = True
        cnt = {e: 0 for e in ENGS}
        for o in ops:
            if o.flag and not o.dma:
                cnt[o.eng] += 1
                o.cnt = cnt[o.eng]
        with contextlib.ExitStack() as st:
            csem = {e: st.enter_context(nc.semaphore("c_" + e)) for e in ENGS}
            dsem = {e: [st.enter_context(nc.semaphore("d_%s%d" % (e, i))) for i in range(RING)]
                    for e in ("sp", "pool", "act")}
            block = st.enter_context(nc.Block())
            ndma = self.ndma

            def run(engname, eng):
                waited_c = {e: 0 for e in ENGS}
                waited_d = {}
                for o in ops:
                    if o.eng != engname:
                        continue
                    for d in sorted(o.deps):
                        p = ops[d]
                        if p.dma:
                            key = (p.eng, p.ring)
                            if waited_d.get(key, 0) < p.target:
                                eng.wait_ge(dsem[p.eng][p.ring], p.target)
                                waited_d[key] = p.target
                        else:
                            if p.eng == engname and not (engname in SELF_SYNC or o.dma):
                                continue
                            if waited_c[p.eng] < p.cnt:
                                eng.wait_ge(csem[p.eng], p.cnt)
                                waited_c[p.eng] = p.cnt
                    if o.dma and o.target > 16:
                        key = (engname, o.ring)
                        if waited_d.get(key, 0) < o.target - 16:
                            eng.wait_ge(dsem[engname][o.ring], o.target - 16)
                            waited_d[key] = o.target - 16
                    ins = o.fn(eng)
                    if o.dma:
                        ins.then_inc(dsem[engname][o.ring], 16)
                    elif o.flag:
                        ins.then_inc(csem[engname], 1)
                if engname in dsem:
                    n = ndma[engname]
                    for r in range(min(n, RING)):
                        last = ((n - 1 - r) // RING) * RING + r
                        eng.wait_ge(dsem[engname][r], 16 * (last // RING + 1))

            @block.tensor
            def _(eng):
                run("pe", eng)

            @block.scalar
            def _(eng):
                run("act", eng)

            @block.vector
            def _(eng):
                run("dve", eng)

            @block.gpsimd
            def _(eng):
                run("pool", eng)

            @block.sync
            def _(eng):
                run("sp", eng)


NA_COMBOS = ([(2, kt) for kt in range(0, 5)] + [(0, kt) for kt in range(4)] + [(1, kt) for kt in range(4)]
             + [(30, kt) for kt in range(28, 32)] + [(31, kt) for kt in range(28, 32)])


def na_pattern_base(i):
    if 2 <= i <= 29:
        return 0, list(range(i - 2, i + 3))
    if i == 0:
        return 5, [0, 1, 2, 3]
    if i == 1:
        return 9, [0, 1, 2, 3]
    if i == 30:
        return 13, [28, 29, 30, 31]
    return 17, [28, 29, 30, 31]


def build_ebias(rpb):
    out = np.empty((12, 128, 21, 128), np.float32)
    a = np.arange(2)
    c = np.arange(64)
    for pi, (i, kt) in enumerate(NA_COMBOS):
        kr = (2 * kt + a)[:, None, None, None]
        r = (2 * i + a)[None, None, :, None]
        ck = c[None, :, None, None]
        cq = c[None, None, None, :]
        r0 = np.clip(r - 4, 0, 56)
        c0 = np.clip(cq - 8, 0, 48)
        ok = (kr >= r0) & (kr < r0 + 8) & (ck >= c0) & (ck < c0 + 16)
        dr = np.clip(kr - r + 7, 0, 14)
        dc = np.clip(ck - cq + 15, 0, 30)
        ok, dr, dc = np.broadcast_arrays(ok, dr, dc)
        vals = rpb[:, dr, dc]
        vals = np.where(ok[None], vals, np.float32(-30000.0))
        out[:, :, pi, :] = vals.reshape(12, 128, 128)
    return out


def fnet_consts():
    i64 = np.arange(64)
    ang = 2 * np.pi * np.outer(i64, i64) / 64.0
    F1 = np.concatenate([np.cos(ang), -np.sin(ang)], axis=1)
    t2 = i64[:, None, None]
    k1 = i64[None, :, None]
    k2 = i64[None, None, :]
    ph = -2 * np.pi * (t2 * k1 / 4096.0 + t2 * k2 / 64.0)
    Gr, Gi = np.cos(ph) / 64.0, np.sin(ph) / 64.0
    H = np.empty((64, 64, 2, 128))
    H[:, :, 0, 0:64] = Gr
    H[:, :, 0, 64:128] = Gi
    H[:, :, 1, 0:64] = -Gi
    H[:, :, 1, 64:128] = Gr
    A = np.cos(ang) / 8.0
    B = np.sin(ang) / 8.0
    CH = np.zeros((128, 6, 128))
    CH[0:64, 0, 0:64] = A
    CH[0:64, 1, 0:64] = B
    CH[0:64, 2, 64:128] = A
    CH[0:64, 3, 64:128] = B
    CH[0:64, 4, 0:64] = A
    CH[64:128, 4, 64:128] = A
    CH[0:64, 5, 0:64] = B
    CH[64:128, 5, 64:128] = B
    i256 = np.arange(256)
    a256 = 2 * np.pi * np.outer(i256, i256) / 256.0
    F256 = np.concatenate([np.cos(a256), -np.sin(a256)], axis=1) / 16.0
    F256 = F256.reshape(2, 128, 512).transpose(1, 0, 2)
    f = lambda x: np.ascontiguousarray(x, dtype=np.float32)
    return f(F1), f(H), f(CH), f(F256)


def s5_consts():
    s8 = (np.arange(128) // 16)
    ex = np.zeros((128, 4, 8), np.float32)
    sv = np.arange(8, dtype=np.float32)
    ex[0:64, 0] = -sv
    ex[64:128, 0] = sv
    ex[0:64, 1] = 7 - sv
    ex[64:128, 1] = sv
    ex[0:64, 2] = sv
    ex[64:128, 2] = -sv
    ex[0:64, 3] = sv + 1
    ex[64:128, 3] = 8 - sv
    mk = np.zeros((128, 2, 128), np.float32)
    mk[:, 0, :] = (s8[:, None] <= s8[None, :])
    mk[:, 1, :] = (s8[:, None] >= s8[None, :])
    io = np.ascontiguousarray(np.broadcast_to(np.arange(544, dtype=np.float32), (128, 544)))
    return ex, mk, io


import os
_SKIP = set(os.environ.get("MK_SKIP", "").split(","))


def build_program(n_layers=DEPTH):
    nc = bass.Bass("TRN2", target_bir_lowering=False)
    dt_in = lambda name, shape: nc.dram_tensor(name, list(shape), F32, kind="ExternalInput").ap()
    xr = dt_in("xr", [T, D])
    cT = dt_in("cT", [128, 16])
    w_ada = dt_in("w_ada", [DEPTH, D, 3 * D])
    b_ada = dt_in("b_ada", [DEPTH, 3 * D])
    pre_g = dt_in("pre_g", [DEPTH, D])
    post_g = dt_in("post_g", [DEPTH, D])
    w_in_even = dt_in("w_in_even", [2, D, 3584])
    w_out_even = dt_in("w_out_even", [2, D, D])
    ebias = dt_in("ebias", [2, 12, 128, 21 * 128])
    cF1 = dt_in("cF1", [64, 128])
    cH = dt_in("cH", [64, 64, 2, 128])
    cCH = dt_in("cCH", [128, 6, 128])
    cF256 = dt_in("cF256", [128, 2, 512])
    w_in_odd = dt_in("w_in_odd", [2, D, 2560])
    w_out_odd = dt_in("w_out_odd", [2, D, D])
    sgu_wT = dt_in("sgu_wT", [2, 128, 4, 128])
    sgu_gT = dt_in("sgu_gT", [2, 128, 4])
    sgu_b = dt_in("sgu_b", [2, 512])
    s5_lam1 = dt_in("s5_lam1", [2, 2, 128, 32])
    s5_ls = dt_in("s5_ls", [2, 2, 32])
    s5_b1 = dt_in("s5_b1", [2, 2, 128, 32, 16])
    s5_c1 = dt_in("s5_c1", [2, 2, 128, 32, 16])
    s5_drep = dt_in("s5_drep", [2, 128, 32])
    glu_w = dt_in("glu_w", [2, 512, 1024])
    glu_bT = dt_in("glu_bT", [2, 128, 8])
    cEXPS = dt_in("cEXPS", [128, 4, 8])
    cMASK = dt_in("cMASK", [128, 2, 128])
    cIOTA = dt_in("cIOTA", [128, 544])
    zs = nc.dram_tensor("zs", [8, 32, 16, 544], BF16).ap()
    out = nc.dram_tensor("out", [TL, D], F32, kind="ExternalOutput").ap()
    xs = nc.dram_tensor("xs", [T, D], F32).ap()
    modscr = nc.dram_tensor("modscr", [DEPTH, 2, 3 * D], F32).ap()

    df = DF(nc)
    A = df.add
    _uid = [0]

    def uniq(name):
        _uid[0] += 1
        return "%s_%d" % (name, _uid[0])

    def dma(eng, o, i, r=(), w=()):
        if eng == "pool":
            A(eng, lambda e, o=o, i=i: e.dma_start(out=o, in_=i, max_dma_last_dim=2048), r, w, dma=True)
        else:
            A(eng, lambda e, o=o, i=i: e.dma_start(out=o, in_=i), r, w, dma=True)

    def xbkeys(n0, nn):
        return [("xb", t) for t in range(n0 // 128, (n0 + nn + 127) // 128)]

    NTILES = [(n * 512, 512) for n in range(8)] + [(4096, 256)]

    with contextlib.ExitStack() as gst:
        sbg = lambda name, shape, dt: gst.enter_context(nc.sbuf_tensor(uniq(name), shape, dt))
        psg = lambda name, shape, dt: gst.enter_context(nc.psum_tensor(name, shape, dt))
        xb = sbg("xb", [128, 8, T], BF16)
        mT = sbg("mT", [128, 8, T], BF16)
        idn = sbg("idn", [128, 128], BF16)
        idn32 = sbg("idn32", [128, 128], F32)
        bart = sbg("bart", [128, 2], F32)
        mm = [psg("mm%d" % i, [128, 512], F32) for i in range(2)]
        stp = [psg("stp%d" % i, [128, 1024], F32) for i in range(2)]
        pv = psg("pv", [128, 512], F32)
        trp = psg("trp", [128, 8, 128], BF16)
        mmi = [0]

        def next_mm():
            mmi[0] ^= 1
            return mm[mmi[0]], "mm%d" % mmi[0]

        A("pool", lambda e: e.memset(idn32[:], 1.0), (), ["idn32"])
        A("pool", lambda e: e.affine_select(out=idn32[:], in_=idn32[:], pattern=[[-1, 128]], compare_op=ALU.is_equal,
                                            fill=0.0, base=0, channel_multiplier=1), ["idn32"], ["idn32"])
        A("dve", lambda e: e.tensor_copy(out=idn[:], in_=idn32[:]), ["idn32"], ["idn"])

        with contextlib.ExitStack() as st:
            sb = lambda name, shape, dt: st.enter_context(nc.sbuf_tensor(uniq(name), shape, dt))
            c32 = sb("c32", [128, 16], F32)
            sc = sb("sc", [128, 16], F32)
            LC = sb("LC", [128, 8, 64], BF16)
            wa = [sb("wa%d" % i, [128, 8, 512], BF16) for i in range(2)]
            bada = sb("bada", [64, 3 * D], F32)
            modrow = sb("modrow", [64, 3 * D], F32)
            dma("sp", c32[:], cT, (), ["c32"])
            A("act", lambda e: e.activation(out=sc[:], in_=c32[:], func=AF.Silu), ["c32"], ["sc"])
            A("pool", lambda e: e.memset(LC[:], 0.0), (), ["LC"])
            A("dve", lambda e: e.tensor_copy(out=LC[:, :, 0:1], in_=sc[:, 0:8].rearrange("p (k o) -> p k o", o=1)), ["sc", "LC"], ["LC"])
            A("dve", lambda e: e.tensor_copy(out=LC[:, :, 32:33], in_=sc[:, 8:16].rearrange("p (k o) -> p k o", o=1)), ["sc", "LC"], ["LC"])
            wi = 0
            for L in range(n_layers):
                dma("sp", bada[:], b_ada[L:L + 1, :].partition_broadcast(64), (), ["bada"])
                for n in range(6):
                    wt = wa[wi % 2]
                    wk = "wa%d" % (wi % 2)
                    wi += 1
                    dma("pool", wt[:], w_ada[L].rearrange("(k p) n -> p k n", p=128)[:, :, n * 512:(n + 1) * 512], (), [wk])
                    bank, bk = next_mm()
                    for k in range(8):
                        A("pe", lambda e, bank=bank, wt=wt, k=k: e.matmul(bank[0:64, :], lhsT=LC[:, k, :], rhs=wt[:, k, :], start=(k == 0), stop=(k == 7)),
                          ["LC", wk], [bk])
                    A("dve", lambda e, bank=bank, n=n: e.tensor_tensor(out=modrow[:, n * 512:(n + 1) * 512], in0=bank[0:64, :], in1=bada[:, n * 512:(n + 1) * 512], op=ALU.add),
                      [bk, "bada"], ["modrow"])
                dma("sp", modscr[L, 0:1, :], modrow[0:1, :], ["modrow"], [("modscr", L)])
                dma("sp", modscr[L, 1:2, :], modrow[32:33, :], ["modrow"], [("modscr", L)])
        df.barrier(bart[:, 0:1])

        def rstd_from_ss(ssum, rs, keys_in, key_out, scale):
            A("dve", lambda e: e.tensor_scalar(out=rs, in0=ssum, scalar1=scale, scalar2=EPS, op0=ALU.mult, op1=ALU.add), keys_in, [key_out])
            A("act", lambda e: e.activation(out=rs, in_=rs, func=AF.Sqrt), [key_out], [key_out])
            A("dve", lambda e: e.reciprocal(out=rs, in_=rs), [key_out], [key_out])

        def prenorm_old(L):
            src = xr if L == 0 else xs
            with contextlib.ExitStack() as st:
                sb = lambda name, shape, dt: st.enter_context(nc.sbuf_tensor(uniq(name), shape, dt))
                xt = [sb("xt%d" % i, [128, D], F32) for i in range(3)]
                tmp = [sb("ptmp%d" % i, [128, D], F32) for i in range(2)]
                hl = [sb("hl%d" % i, [128, D], BF16) for i in range(2)]
                junk = sb("junk", [128, D], BF16)
                gsc = [sb("gsc%d" % i, [128, D], F32) for i in range(2)]
                shb = [sb("shb%d" % i, [128, D], F32) for i in range(2)]
                pgb = sb("pgb", [128, D], F32)
                stat = sb("pstat", [128, 4 * NT], F32)
                dma("sp", pgb[:], pre_g[L:L + 1, :].partition_broadcast(128), (), ["pgb"])
                for w in range(2):
                    dma("sp", gsc[w][:], modscr[L, w:w + 1, D:2 * D].partition_broadcast(128), [("modscr", L)], ["gsc%d" % w])
                    dma("sp", shb[w][:], modscr[L, w:w + 1, 0:D].partition_broadcast(128), [("modscr", L)], ["shb%d" % w])
                    A("dve", lambda e, w=w: e.scalar_tensor_tensor(out=gsc[w][:], in0=gsc[w][:], scalar=1.0, in1=pgb[:], op0=ALU.add, op1=ALU.mult),
                      ["gsc%d" % w, "pgb"], ["gsc%d" % w])
                for t in range(NT):
                    w = 0 if t < 32 else 1
                    x_ = xt[t % 3]
                    xk = "xt%d" % (t % 3)
                    tm = tmp[t % 2]
                    tk = "ptmp%d" % (t % 2)
                    h_ = hl[t % 2]
                    hk = "hl%d" % (t % 2)
                    ss = stat[:, 4 * t:4 * t + 1]
                    rs = stat[:, 4 * t + 1:4 * t + 2]
                    dma("sp", x_[:], src[t * 128:(t + 1) * 128, :], [("xres", t)], [xk])
                    A("act", lambda e, x_=x_, ss=ss: e.activation(out=junk[:], in_=x_[:], func=AF.Square, accum_out=ss), [xk], ["junk", ("pss", t)])
                    rstd_from_ss(ss, rs, [("pss", t)], ("prs", t), 1.0 / D)
                    A("dve", lambda e, x_=x_, rs=rs, tm=tm, w=w: e.scalar_tensor_tensor(out=tm[:], in0=x_[:], scalar=rs, in1=gsc[w][:], op0=ALU.mult, op1=ALU.mult),
                      [xk, ("prs", t), "gsc%d" % w], [tk])
                    A("pool", lambda e, tm=tm, h_=h_, w=w: e.tensor_tensor(out=h_[:], in0=tm[:], in1=shb[w][:], op=ALU.add), [tk, "shb%d" % w], [hk])
                    for k in range(8):
                        A("pe", lambda e, h_=h_, k=k: e.transpose(trp[:, k, :], h_[:, k * 128:(k + 1) * 128], idn[:]), [hk, "idn"], ["trp"])
                    if t % 2 == 0:
                        A("act", lambda e, t=t: e.copy(out=xb[:, :, t * 128:(t + 1) * 128], in_=trp[:]), ["trp"], [("xb", t)])
                    else:
                        A("dve", lambda e, t=t: e.tensor_copy(out=xb[:, :, t * 128:(t + 1) * 128], in_=trp[:]), ["trp"], [("xb", t)])
            df.barrier(bart[:, 0:1])

        def post_old(L, last):
            src = xr if L == 0 else xs
            j = L // 2
            wo_d = w_out_even[j] if L % 2 == 0 else w_out_odd[j]
            ntile = 32 if last else NT
            with contextlib.ExitStack() as st:
                sb = lambda name, shape, dt: st.enter_context(nc.sbuf_tensor(uniq(name), shape, dt))
                wo = sb("wo", [128, 8, D], BF16)
                xt = [sb("qxt%d" % i, [128, D], F32) for i in range(2)]
                t1 = [sb("qt1%d" % i, [128, D], F32) for i in range(2)]
                t2 = [sb("qt2%d" % i, [128, D], F32) for i in range(2)]
                junk = sb("qjunk", [128, 512], BF16)
                gp = [sb("gp%d" % i, [128, D], F32) for i in range(2)]
                pgb = sb("qpgb", [128, D], F32)
                stat = sb("qstat", [128, 4 * NT], F32)
                for h in range(2):
                    dma("pool", wo[:, :, h * 512:(h + 1) * 512], wo_d.rearrange("(k p) n -> p k n", p=128)[:, :, h * 512:(h + 1) * 512], (), ["wo"])
                dma("sp", pgb[:], post_g[L:L + 1, :].partition_broadcast(128), (), ["qpgb"])
                for w in range(2):
                    dma("sp", gp[w][:], modscr[L, w:w + 1, 2 * D:3 * D].partition_broadcast(128), [("modscr", L)], ["gp%d" % w])
                    A("dve", lambda e, w=w: e.tensor_tensor(out=gp[w][:], in0=gp[w][:], in1=pgb[:], op=ALU.mult), ["gp%d" % w, "qpgb"], ["gp%d" % w])
                for t in range(ntile):
                    w = 0 if t < 32 else 1
                    yps = stp[t % 2]
                    yk = "stp%d" % (t % 2)
                    x_ = xt[t % 2]
                    xk = "qxt%d" % (t % 2)
                    a_ = t1[t % 2]
                    ak = "qt1%d" % (t % 2)
                    b_ = t2[t % 2]
                    bk = "qt2%d" % (t % 2)
                    for h in range(2):
                        for k in range(8):
                            A("pe", lambda e, yps=yps, h=h, k=k, t=t: e.matmul(yps[:, h * 512:(h + 1) * 512], lhsT=mT[:, k, t * 128:(t + 1) * 128], rhs=wo[:, k, h * 512:(h + 1) * 512], start=(k == 0), stop=(k == 7)),
                              [("mT", k, t), "wo"], [yk])
                    dma("sp", x_[:], src[t * 128:(t + 1) * 128, :], [("xres", t)], [xk])
                    for h in range(2):
                        A("act", lambda e, yps=yps, h=h, t=t: e.activation(out=junk[:], in_=yps[:, h * 512:(h + 1) * 512], func=AF.Square, accum_out=stat[:, 4 * t + h:4 * t + h + 1]),
                          [yk], ["qjunk", ("qss", t, h)])
                    A("dve", lambda e, t=t: e.tensor_tensor(out=stat[:, 4 * t + 2:4 * t + 3], in0=stat[:, 4 * t:4 * t + 1], in1=stat[:, 4 * t + 1:4 * t + 2], op=ALU.add),
                      [("qss", t, 0), ("qss", t, 1)], [("qs2", t)])
                    rs = stat[:, 4 * t + 3:4 * t + 4]
                    rstd_from_ss(stat[:, 4 * t + 2:4 * t + 3], rs, [("qs2", t)], ("qrs", t), 1.0 / D)
                    A("dve", lambda e, yps=yps, a_=a_, w=w: e.tensor_tensor(out=a_[:], in0=yps[:], in1=gp[w][:], op=ALU.mult), [yk, "gp%d" % w], [ak])
                    A("act", lambda e, a_=a_, b_=b_, rs=rs: e.activation(out=b_[:], in_=a_[:], func=AF.Copy, scale=rs), [ak, ("qrs", t)], [bk])
                    A("pool", lambda e, b_=b_, x_=x_: e.tensor_tensor(out=b_[:], in0=b_[:], in1=x_[:], op=ALU.add), [bk, xk], [bk])
                    dst = out[t * 128:(t + 1) * 128, :] if (last and t < 32) else xs[t * 128:(t + 1) * 128, :]
                    dma("sp", dst, b_[:], [bk], [("xres", t)])
            df.barrier(bart[:, 0:1])

        def prenorm(L):
            src = xr if L == 0 else xs
            with contextlib.ExitStack() as st:
                sb = lambda name, shape, dt: st.enter_context(nc.sbuf_tensor(uniq(name), shape, dt))
                NX = 6
                xt = [sb("xt%d" % i, [128, D], F32) for i in range(NX)]
                tmp = [sb("ptmp%d" % i, [128, D], F32) for i in range(2)]
                hl = [sb("hl%d" % i, [128, D], BF16) for i in range(2)]
                junk = sb("junk", [128, D], BF16)
                gsc = [sb("gsc%d" % i, [128, D], F32) for i in range(2)]
                shb = [sb("shb%d" % i, [128, D], F32) for i in range(2)]
                pgb = sb("pgb", [128, D], F32)
                stat = sb("pstat", [128, 4 * NT], F32)
                dma("sp", pgb[:], pre_g[L:L + 1, :].partition_broadcast(128), (), ["pgb"])
                for w in range(2):
                    dma("sp", gsc[w][:], modscr[L, w:w + 1, D:2 * D].partition_broadcast(128), [("modscr", L)], ["gsc%d" % w])
                    dma("sp", shb[w][:], modscr[L, w:w + 1, 0:D].partition_broadcast(128), [("modscr", L)], ["shb%d" % w])
                    A("dve", lambda e, w=w: e.scalar_tensor_tensor(out=gsc[w][:], in0=gsc[w][:], scalar=1.0, in1=pgb[:], op0=ALU.add, op1=ALU.mult),
                      ["gsc%d" % w, "pgb"], ["gsc%d" % w])
                X = lambda t: (xt[t % NX], "xt%d" % (t % NX))
                SS = lambda t: stat[:, 4 * t:4 * t + 1]
                RS = lambda t: stat[:, 4 * t + 1:4 * t + 2]

                def p_load(t):
                    x_, xk = X(t)
                    dma("sp", x_[:], src[t * 128:(t + 1) * 128, :], [("xres", t)], [xk])

                def p_sq(t):
                    x_, xk = X(t)
                    A("act", lambda e, x_=x_, ss=SS(t): e.activation(out=junk[:], in_=x_[:], func=AF.Square, accum_out=ss), [xk], ["junk", ("pss", t)])

                def p_r1(t):
                    A("dve", lambda e, t=t: e.tensor_scalar(out=RS(t), in0=SS(t), scalar1=1.0 / D, scalar2=EPS, op0=ALU.mult, op1=ALU.add), [("pss", t)], [("prs", t)])

                def p_r2(t):
                    A("act", lambda e, t=t: e.activation(out=RS(t), in_=RS(t), func=AF.Sqrt), [("prs", t)], [("prs", t)])

                def p_r3(t):
                    A("dve", lambda e, t=t: e.reciprocal(out=RS(t), in_=RS(t)), [("prs", t)], [("prs", t)])

                def p_stt(t):
                    w = 0 if t < 32 else 1
                    x_, xk = X(t)
                    tm, tk = tmp[t % 2], "ptmp%d" % (t % 2)
                    A("dve", lambda e, x_=x_, t=t, tm=tm, w=w: e.scalar_tensor_tensor(out=tm[:], in0=x_[:], scalar=RS(t), in1=gsc[w][:], op0=ALU.mult, op1=ALU.mult),
                      [xk, ("prs", t), "gsc%d" % w], [tk])

                def p_add(t):
                    w = 0 if t < 32 else 1
                    tm, tk = tmp[t % 2], "ptmp%d" % (t % 2)
                    h_, hk = hl[t % 2], "hl%d" % (t % 2)
                    A("pool", lambda e, tm=tm, h_=h_, w=w: e.tensor_tensor(out=h_[:], in0=tm[:], in1=shb[w][:], op=ALU.add), [tk, "shb%d" % w], [hk])

                def p_tr(t):
                    h_, hk = hl[t % 2], "hl%d" % (t % 2)
                    for k in range(8):
                        A("pe", lambda e, h_=h_, k=k: e.transpose(trp[:, k, :], h_[:, k * 128:(k + 1) * 128], idn[:]), [hk, "idn"], ["trp"])

                def p_ev(t):
                    if t % 2 == 0:
                        A("act", lambda e, t=t: e.copy(out=xb[:, :, t * 128:(t + 1) * 128], in_=trp[:]), ["trp"], [("xb", t)])
                    else:
                        A("dve", lambda e, t=t: e.tensor_copy(out=xb[:, :, t * 128:(t + 1) * 128], in_=trp[:]), ["trp"], [("xb", t)])

                pipeline([p_load, p_sq, p_r1, p_r2, p_r3, p_stt, p_add, p_tr, p_ev], NT, "pre")
            df.barrier(bart[:, 0:1])

        def post(L, last):
            src = xr if L == 0 else xs
            j = L // 2
            wo_d = w_out_even[j] if L % 2 == 0 else w_out_odd[j]
            ntile = 32 if last else NT
            with contextlib.ExitStack() as st:
                sb = lambda name, shape, dt: st.enter_context(nc.sbuf_tensor(uniq(name), shape, dt))
                wo = sb("wo", [128, 8, D], BF16)
                NA_, NB_, NXq = 4, 3, 3
                xt = [sb("qxt%d" % i, [128, D], F32) for i in range(NXq)]
                t1 = [sb("qt1%d" % i, [128, D], F32) for i in range(NA_)]
                t2 = [sb("qt2%d" % i, [128, D], F32) for i in range(NB_)]
                junk = sb("qjunk", [128, D], BF16)
                gp = [sb("gp%d" % i, [128, D], F32) for i in range(2)]
                pgb = sb("qpgb", [128, D], F32)
                stat = sb("qstat", [128, 4 * NT], F32)
                for h in range(2):
                    dma("pool", wo[:, :, h * 512:(h + 1) * 512], wo_d.rearrange("(k p) n -> p k n", p=128)[:, :, h * 512:(h + 1) * 512], (), ["wo"])
                dma("sp", pgb[:], post_g[L:L + 1, :].partition_broadcast(128), (), ["qpgb"])
                for w in range(2):
                    dma("sp", gp[w][:], modscr[L, w:w + 1, 2 * D:3 * D].partition_broadcast(128), [("modscr", L)], ["gp%d" % w])
                    A("dve", lambda e, w=w: e.tensor_tensor(out=gp[w][:], in0=gp[w][:], in1=pgb[:], op=ALU.mult), ["gp%d" % w, "qpgb"], ["gp%d" % w])
                YP = lambda t: (stp[t % 2], "stp%d" % (t % 2))
                XQ = lambda t: (xt[t % NXq], "qxt%d" % (t % NXq))
                TA = lambda t: (t1[t % NA_], "qt1%d" % (t % NA_))
                TB = lambda t: (t2[t % NB_], "qt2%d" % (t % NB_))
                SS = lambda t: stat[:, 4 * t:4 * t + 1]
                RS = lambda t: stat[:, 4 * t + 1:4 * t + 2]

                def q_mm(t):
                    yps, yk = YP(t)
                    for h in range(2):
                        for k in range(8):
                            A("pe", lambda e, yps=yps, h=h, k=k, t=t: e.matmul(yps[:, h * 512:(h + 1) * 512], lhsT=mT[:, k, t * 128:(t + 1) * 128], rhs=wo[:, k, h * 512:(h + 1) * 512], start=(k == 0), stop=(k == 7)),
                              [("mT", k, t), "wo"], [yk])

                def q_sq(t):
                    yps, yk = YP(t)
                    a_, ak = TA(t)
                    w = 0 if t < 32 else 1
                    for h in range(2):
                        A("act", lambda e, yps=yps, t=t, h=h: e.activation(out=junk[:, h * 512:(h + 1) * 512], in_=yps[:, h * 512:(h + 1) * 512], func=AF.Square, accum_out=stat[:, 4 * t + 2 + h:4 * t + 3 + h]), [yk], ["qjunk", ("qssh", t, h)])
                    A("dve", lambda e, yps=yps, a_=a_, w=w: e.tensor_tensor(out=a_[:], in0=yps[:], in1=gp[w][:], op=ALU.mult), [yk, "gp%d" % w, ("qssh", t, 0), ("qssh", t, 1)], [ak])

                def q_r1(t):
                    A("dve", lambda e, t=t: e.tensor_tensor(out=SS(t), in0=stat[:, 4 * t + 2:4 * t + 3], in1=stat[:, 4 * t + 3:4 * t + 4], op=ALU.add), [("qssh", t, 0), ("qssh", t, 1)], [("qss", t)])
                    A("dve", lambda e, t=t: e.tensor_scalar(out=RS(t), in0=SS(t), scalar1=1.0 / D, scalar2=EPS, op0=ALU.mult, op1=ALU.add), [("qss", t)], [("qrs", t)])

                def q_r2(t):
                    A("act", lambda e, t=t: e.activation(out=RS(t), in_=RS(t), func=AF.Sqrt), [("qrs", t)], [("qrs", t)])
                    x_, xk = XQ(t)
                    dma("sp", x_[:], src[t * 128:(t + 1) * 128, :], [("xres", t)], [xk])

                def q_r3(t):
                    A("dve", lambda e, t=t: e.reciprocal(out=RS(t), in_=RS(t)), [("qrs", t)], [("qrs", t)])

                def q_sc(t):
                    a_, ak = TA(t)
                    b_, bk = TB(t)
                    A("act", lambda e, a_=a_, b_=b_, t=t: e.activation(out=b_[:], in_=a_[:], func=AF.Copy, scale=RS(t)), [ak, ("qrs", t)], [bk])

                def q_add(t):
                    b_, bk = TB(t)
                    x_, xk = XQ(t)
                    A("pool", lambda e, b_=b_, x_=x_: e.tensor_tensor(out=b_[:], in0=b_[:], in1=x_[:], op=ALU.add), [bk, xk], [bk])

                def q_st(t):
                    b_, bk = TB(t)
                    dst = out[t * 128:(t + 1) * 128, :] if (last and t < 32) else xs[t * 128:(t + 1) * 128, :]
                    dma("sp", dst, b_[:], [bk], [("xres", t)])

                pipeline([q_mm, q_sq, q_r1, q_r2, q_r3, q_sc, q_add, q_st], ntile, "post")
            df.barrier(bart[:, 0:1])

        def pipeline(stages, N, key=""):
            if "noskew" in _SKIP or ("noskew_" + key) in _SKIP:
                for n_ in range(N):
                    for st_ in stages:
                        st_(n_)
                return
            K_ = len(stages)
            for step in range(N + K_ - 1):
                for k_ in reversed(range(K_)):
                    n_ = step - k_
                    if 0 <= n_ < N:
                        stages[k_](n_)

        def inproj_fm(wt, wk, tiles, evac):
            for (n0, nn) in tiles:
                bank, bk = next_mm()
                for k in range(8):
                    A("pe", lambda e, bank=bank, k=k, n0=n0, nn=nn: e.matmul(bank[:, 0:nn], lhsT=wt[:, k, :], rhs=xb[:, k, n0:n0 + nn], start=(k == 0), stop=(k == 7)),
                      [wk] + xbkeys(n0, nn), [bk])
                evac(bank, bk, n0, nn)

        def even_mixer(L):
            j = L // 2
            Wd = w_in_even[j].rearrange("(k p) n -> p k n", p=128)
            with contextlib.ExitStack() as st:
                sb = lambda name, shape, dt: st.enter_context(nc.sbuf_tensor(uniq(name), shape, dt))
                wch = [sb("wch%d" % i, [128, 8, 128], BF16) for i in range(3)]
                wci = [0]

                def load_w(c0, ncols=128):
                    i = wci[0] % 3
                    wci[0] += 1
                    dma("pool", wch[i][:, :, 0:ncols], Wd[:, :, c0:c0 + ncols], (), ["wch%d" % i])
                    return wch[i], "wch%d" % i

                with contextlib.ExitStack() as st2:
                    sb2 = lambda name, shape, dt: st2.enter_context(nc.sbuf_tensor(uniq(name), shape, dt))
                    sga = sb2("sga", [128, T], BF16)
                    X = sb2("fX", [64, 64, 128], BF16)
                    Z = sb2("fZ", [64, 64, 128], BF16)
                    Pg = [sb2("fP%d" % i, [128, 8, 128], BF16) for i in range(2)]
                    Hs = [sb2("fH%d" % i, [64, 8, 2, 128], BF16) for i in range(2)]
                    F1 = sb2("fF1", [64, 128], BF16)
                    CH = sb2("fCH", [128, 6, 128], BF16)
                    F256 = sb2("fF256", [128, 2, 512], BF16)
                    Xc = sb2("fXc", [128, 2, 128], BF16)
                    Pc = sb2("fPc", [128, 512], BF16)
                    dma("pool", F1[:], cF1, (), ["fF1"])
                    dma("pool", CH[:], cCH, (), ["fCH"])
                    dma("pool", F256[:], cF256, (), ["fF256"])
                    hcount = 0
                    pcount = 0
                    for half in range(2):
                        wt, wk = load_w(256 + half * 128)
                        inproj_fm(wt, wk, NTILES, lambda bank, bk, n0, nn: A(
                            "act", lambda e: e.activation(out=sga[:, n0:n0 + nn], in_=bank[:, 0:nn], func=AF.Silu), [bk], [("sga", n0)]))
                        sgakeys = [("sga", n0) for (n0, nn) in NTILES]
                        wt, wk = load_w(half * 128)
                        for g4 in range(16):
                            bank, bk = next_mm()
                            for q in range(4):
                                t2 = g4 * 4 + q
                                for k in range(8):
                                    A("pe", lambda e, bank=bank, q=q, k=k, t2=t2, wt=wt: e.matmul(bank[0:64, q * 128:(q + 1) * 128], lhsT=xb[:, k, t2:TL:64], rhs=wt[:, k, :], start=(k == 0), stop=(k == 7)),
                                      [wk] + [("xb", t) for t in range(32)], [bk])
                            A("act", lambda e, bank=bank, g4=g4: e.copy(out=X[:, g4 * 4:(g4 + 1) * 4, :], in_=bank[0:64, :].rearrange("p (q c) -> p q c", q=4)), [bk], ["fX"])
                        for tl in range(2):
                            bank, bk = next_mm()
                            for k in range(8):
                                A("pe", lambda e, bank=bank, k=k, tl=tl, wt=wt: e.matmul(bank[:, 0:128], lhsT=xb[:, k, TL + tl * 128:TL + (tl + 1) * 128], rhs=wt[:, k, :], start=(k == 0), stop=(k == 7)),
                                  [wk, ("xb", 32 + tl)], [bk])
                            A("dve", lambda e, bank=bank, tl=tl: e.tensor_copy(out=Xc[:, tl, :], in_=bank[:, 0:128]), [bk], ["fXc"])
                        bank, bk = next_mm()
                        for tl in range(2):
                            A("pe", lambda e, bank=bank, tl=tl: e.matmul(bank[:, :], lhsT=Xc[:, tl, :], rhs=F256[:, tl, :], start=(tl == 0), stop=(tl == 1)), ["fXc", "fF256"], [bk])
                        A("dve", lambda e, bank=bank: e.tensor_copy(out=Pc[:], in_=bank[:, :]), [bk], ["fPc"])
                        bank, bk = next_mm()
                        A("pe", lambda e, bank=bank: e.matmul(bank[:, 0:256], lhsT=CH[:, 4, :], rhs=Pc[:, 0:256], start=True, stop=False), ["fPc", "fCH"], [bk])
                        A("pe", lambda e, bank=bank: e.matmul(bank[:, 0:256], lhsT=CH[:, 5, :], rhs=Pc[:, 256:512], start=False, stop=True), ["fPc", "fCH"], [bk])
                        A("dve", lambda e, bank=bank, half=half: e.tensor_tensor(out=mT[:, half, TL:T], in0=bank[:, 0:256], in1=sga[:, TL:T], op=ALU.mult),
                          [bk] + sgakeys, [("mT", half, 32), ("mT", half, 33)])
                        for qd in range(2):
                            pb = qd * 64
                            for c4 in range(16):
                                bank, bk = next_mm()
                                for q in range(4):
                                    c = qd * 64 + c4 * 4 + q
                                    A("pe", lambda e, bank=bank, q=q, c=c: e.matmul(bank[0:64, q * 128:(q + 1) * 128], lhsT=X[:, :, c], rhs=F1[:, :], start=True, stop=True),
                                      ["fX", "fF1"], [bk])
                                if c4 % 2 == 0:
                                    A("act", lambda e, bank=bank, c4=c4: e.copy(out=Z[:, c4 * 4:(c4 + 1) * 4, :], in_=bank[0:64, :].rearrange("p (q c) -> p q c", q=4)), [bk], ["fZ"])
                                else:
                                    A("dve", lambda e, bank=bank, c4=c4: e.tensor_copy(out=Z[:, c4 * 4:(c4 + 1) * 4, :], in_=bank[0:64, :].rearrange("p (q c) -> p q c", q=4)), [bk], ["fZ"])
                            for g8 in range(8):
                                Hb = Hs[hcount % 2]
                                hk = "fH%d" % (hcount % 2)
                                hcount += 1
                                dma("pool", Hb[:], cH[:, g8 * 8:(g8 + 1) * 8, :, :], (), [hk])
                                Pb = Pg[pcount % 2]
                                pk = "fP%d" % (pcount % 2)
                                pcount += 1
                                for b2 in range(2):
                                    bank, bk = next_mm()
                                    for q in range(4):
                                        kk = b2 * 4 + q
                                        k1 = g8 * 8 + kk
                                        for ri in range(2):
                                            A("pe", lambda e, bank=bank, q=q, kk=kk, k1=k1, ri=ri, Hb=Hb: e.matmul(bank[0:64, q * 128:(q + 1) * 128], lhsT=Z[:, :, ri * 64 + k1], rhs=Hb[:, kk, ri, :], start=(ri == 0), stop=(ri == 1)),
                                              ["fZ", hk], [bk])
                                    A("act" if b2 == 0 else "dve",
                                      (lambda e, bank=bank, b2=b2, Pb=Pb: e.copy(out=Pb[0:64, b2 * 4:(b2 + 1) * 4, :], in_=bank[0:64, :].rearrange("p (q c) -> p q c", q=4))) if b2 == 0 else
                                      (lambda e, bank=bank, b2=b2, Pb=Pb: e.tensor_copy(out=Pb[0:64, b2 * 4:(b2 + 1) * 4, :], in_=bank[0:64, :].rearrange("p (q c) -> p q c", q=4))),
                                      [bk], [pk])
                                bank, bk = next_mm()
                                ia, ib = (0, 1) if qd == 0 else (2, 3)
                                mcols = 64 if qd == 0 else 128
                                A("pe", lambda e, bank=bank, Pb=Pb, ia=ia, mcols=mcols: e.matmul(bank[0:mcols, :], lhsT=CH[0:64, ia, 0:mcols], rhs=Pb[0:64, :, 0:64], start=True, stop=False), [pk, "fCH"], [bk])
                                A("pe", lambda e, bank=bank, Pb=Pb, ib=ib, mcols=mcols: e.matmul(bank[0:mcols, :], lhsT=CH[0:64, ib, 0:mcols], rhs=Pb[0:64, :, 64:128], start=False, stop=True), [pk, "fCH"], [bk])
                                A("dve", lambda e, bank=bank, pb=pb, half=half, g8=g8: e.tensor_tensor(
                                    out=mT[pb:pb + 64, half, 0:TL].rearrange("p (k2 k1) -> p k1 k2", k1=64)[:, g8 * 8:(g8 + 1) * 8, :],
                                    in0=bank[pb:pb + 64, :].rearrange("p (a b) -> p a b", a=8),
                                    in1=sga[pb:pb + 64, 0:TL].rearrange("p (k2 k1) -> p k1 k2", k1=64)[:, g8 * 8:(g8 + 1) * 8, :], op=ALU.mult),
                                  [bk] + sgakeys, [("mT", half, t) for t in range(32)])
                df.barrier(bart[:, 0:1])

                with contextlib.ExitStack() as st2:
                    sb2 = lambda name, shape, dt: st2.enter_context(nc.sbuf_tensor(uniq(name), shape, dt))
                    qT = sb2("qT", [128, T], BF16)
                    kT = sb2("kT", [128, T], BF16)
                    sgb = sb2("sgb", [128, T], BF16)
                    vaug = sb2("vaug", [128, NT, 130], BF16)
                    eb32 = sb2("eb32", [128, 7 * 128], F32)
                    Eh = [sb2("Eh%d" % i, [128, 21 * 128], BF16) for i in range(2)]
                    PT = [sb2("PT%d" % i, [128, 7 * 128], BF16) for i in range(3)]
                    onb = sb2("onb", [128, NT, 128], BF16)
                    rden = sb2("rden", [128, 64], F32)
                    A("pool", lambda e: e.memset(vaug[:], 1.0), (), [("vaug", t4) for t4 in range(9)])
                    pti = 0
                    rdi = 0
                    for hp in range(6):
                        wt, wk = load_w(512 + hp * 128)
                        inproj_fm(wt, wk, NTILES, lambda bank, bk, n0, nn: A(
                            "act", lambda e: e.copy(out=qT[:, n0:n0 + nn], in_=bank[:, 0:nn]), [bk], [("qT", n0)]))
                        wt, wk = load_w(1280 + hp * 128)
                        inproj_fm(wt, wk, NTILES, lambda bank, bk, n0, nn: A(
                            "dve", lambda e: e.tensor_copy(out=kT[:, n0:n0 + nn], in_=bank[:, 0:nn]), [bk], [("kT", n0)]))
                        wt, wk = load_w(2816 + hp * 128)
                        inproj_fm(wt, wk, NTILES, lambda bank, bk, n0, nn: A(
                            "act", lambda e: e.activation(out=sgb[:, n0:n0 + nn], in_=bank[:, 0:nn], func=AF.Silu), [bk], [("sgb", n0)]))
                        wt, wk = load_w(2048 + hp * 128)
                        for t4 in range(9):
                            bank, bk = next_mm()
                            nq = 4 if t4 < 8 else 2
                            for q in range(nq):
                                t = t4 * 4 + q
                                for k in range(8):
                                    A("pe", lambda e, bank=bank, q=q, k=k, t=t, wt=wt: e.matmul(bank[:, q * 128:(q + 1) * 128], lhsT=xb[:, k, t * 128:(t + 1) * 128], rhs=wt[:, k, :], start=(k == 0), stop=(k == 7)),
                                      [wk, ("xb", t)], [bk])
                            for hh in range(2):
                                A("dve" if hh == 0 else "act",
                                  (lambda e, bank=bank, t4=t4, nq=nq, hh=hh: e.tensor_copy(out=vaug[:, t4 * 4:t4 * 4 + nq, hh * 65:hh * 65 + 64], in_=bank[:, 0:nq * 128].rearrange("p (q c) -> p q c", q=nq)[:, :, hh * 64:(hh + 1) * 64])) if hh == 0 else
                                  (lambda e, bank=bank, t4=t4, nq=nq, hh=hh: e.copy(out=vaug[:, t4 * 4:t4 * 4 + nq, hh * 65:hh * 65 + 64], in_=bank[:, 0:nq * 128].rearrange("p (q c) -> p q c", q=nq)[:, :, hh * 64:(hh + 1) * 64])),
                                  [bk], [("vaug", t4)])
                        for hh in range(2):
                            h = hp * 2 + hh
                            E = Eh[hh]
                            ek = "Eh%d" % hh
                            for part in range(3):
                                dma("sp", eb32[:], ebias[j, h, :, part * 896:(part + 1) * 896], (), ["eb32"])
                                A("act", lambda e, E=E, part=part: e.activation(out=E[:, part * 896:(part + 1) * 896], in_=eb32[:], func=AF.Exp), ["eb32"], [ek])
                        its = [(hh, i) for hh in range(2) for i in range(NT)]

                        def geo(n):
                            hh, i = its[n]
                            if i < 32:
                                pbase, lt = na_pattern_base(i)
                                kts = lt + [32, 33]
                            else:
                                pbase, lt = None, []
                                kts = [32, 33]
                            return hh, i, pbase, lt, kts

                        def s_qk(n):
                            hh, i, pbase, lt, kts = geo(n)
                            hb = hh * 64
                            sp_ = stp[n % 2]
                            sk = "stp%d" % (n % 2)
                            for a_, kt in enumerate(kts):
                                A("pe", lambda e, sp_=sp_, a_=a_, kt=kt, i=i, hb=hb: e.matmul(sp_[:, a_ * 128:(a_ + 1) * 128], lhsT=kT[hb:hb + 64, kt * 128:(kt + 1) * 128], rhs=qT[hb:hb + 64, i * 128:(i + 1) * 128], start=True, stop=True),
                                  [("kT", (kt // 4) * 512), ("qT", (i // 4) * 512)], [sk])

                        def s_exp(n):
                            hh, i, pbase, lt, kts = geo(n)
                            nk = len(kts)
                            sp_ = stp[n % 2]
                            sk = "stp%d" % (n % 2)
                            P_ = PT[n % 3]
                            pk = "PT%d" % (n % 3)
                            A("act", lambda e, sp_=sp_, P_=P_, nk=nk: e.activation(out=P_[:, 0:nk * 128], in_=sp_[:, 0:nk * 128], func=AF.Exp, scale=0.125), [sk], [pk])

                        def s_mul(n):
                            hh, i, pbase, lt, kts = geo(n)
                            P_ = PT[n % 3]
                            pk = "PT%d" % (n % 3)
                            if lt:
                                nl = len(lt)
                                E = Eh[hh]
                                A("dve", lambda e, P_=P_, nl=nl, E=E, pbase=pbase: e.tensor_tensor(out=P_[:, 0:nl * 128], in0=P_[:, 0:nl * 128], in1=E[:, pbase * 128:(pbase + nl) * 128], op=ALU.mult),
                                  [pk, "Eh%d" % hh], [pk])

                        def s_pv(n):
                            hh, i, pbase, lt, kts = geo(n)
                            nk = len(kts)
                            P_ = PT[n % 3]
                            pk = "PT%d" % (n % 3)
                            pvb = mm[n % 2]
                            for a_, kt in enumerate(kts):
                                A("pe", lambda e, P_=P_, a_=a_, kt=kt, hh=hh, nk=nk, pvb=pvb: e.matmul(pvb[:, 0:65], lhsT=P_[:, a_ * 128:(a_ + 1) * 128], rhs=vaug[:, kt, hh * 65:hh * 65 + 65], start=(a_ == 0), stop=(a_ == nk - 1)),
                                  [pk, ("vaug", kt // 4)], ["mm%d" % (n % 2)])

                        def s_rec(n):
                            pvb = mm[n % 2]
                            rd = rden[:, n % 64:n % 64 + 1]
                            A("dve", lambda e, rd=rd, pvb=pvb: e.reciprocal(out=rd, in_=pvb[:, 64:65]), ["mm%d" % (n % 2)], [("rden", n % 64)])

                        def s_norm(n):
                            hh, i, pbase, lt, kts = geo(n)
                            pvb = mm[n % 2]
                            rd = rden[:, n % 64:n % 64 + 1]
                            A("dve", lambda e, i=i, hh=hh, rd=rd, pvb=pvb: e.tensor_scalar(out=onb[:, i, hh * 64:(hh + 1) * 64], in0=pvb[:, 0:64], scalar1=rd, scalar2=None, op0=ALU.mult), ["mm%d" % (n % 2), ("rden", n % 64)], [("on", i, hh)])

                        def burst(i):
                            if i % 8 == 7 or i == NT - 1:
                                i0 = (i // 8) * 8
                                return i0, i - i0 + 1
                            return None

                        def s_tr(n):
                            hh, i, pbase, lt, kts = geo(n)
                            if hh == 1 and burst(i):
                                i0, nb_ = burst(i)
                                for r_ in range(nb_):
                                    ii = i0 + r_
                                    A("pe", lambda e, ii=ii, r_=r_: e.transpose(trp[:, r_, :], onb[:, ii, :], idn[:]), [("on", ii, 0), ("on", ii, 1), "idn"], ["trp"])

                        def s_gate(n):
                            hh, i, pbase, lt, kts = geo(n)
                            if hh == 1 and burst(i):
                                i0, nb_ = burst(i)
                                A("dve", lambda e, i0=i0, nb_=nb_, hp=hp: e.tensor_tensor(out=mT[:, 2 + hp, i0 * 128:(i0 + nb_) * 128], in0=trp[:, 0:nb_, :].rearrange("p a b -> p (a b)"), in1=sgb[:, i0 * 128:(i0 + nb_) * 128], op=ALU.mult),
                                  ["trp"] + [("sgb", ((i0 + r_) // 4) * 512) for r_ in range(nb_)], [("mT", 2 + hp, i0 + r_) for r_ in range(nb_)])

                        pipeline([s_qk, s_exp, s_mul, s_pv, s_rec, s_norm, s_tr, s_gate], len(its), "att")
            df.barrier(bart[:, 0:1])

        MAG = 12582912.0
        TWO_PI = 2.0 * np.pi

        def odd_mixer(L):
            j = L // 2
            Wd = w_in_odd[j].rearrange("(k p) n -> p k n", p=128)
            Uv = mT[:, 0:4, :].rearrange("p c t -> p (c t)").rearrange("p (g b) -> p g b", b=544)
            PIECES = [(0, 256), (256, 256), (512, 32)]

            with contextlib.ExitStack() as st:
              if "s5a" not in _SKIP:
                sb = lambda name, shape, dt: st.enter_context(nc.sbuf_tensor(uniq(name), shape, dt))
                Ws = sb("Ws", [128, 8, 512], BF16)
                Stm = [sb("Stm%d" % i, [128, 32, 8, 16], BF16) for i in range(2)]
                dma("pool", Ws[:], Wd[:, :, 1536:2048], (), ["Ws"])
                for bt in range(5):
                    nb = 128 if bt < 4 else 32
                    tok0 = 1024 * bt
                    S_ = Stm[bt % 2]
                    sk = "Stm%d" % (bt % 2)
                    for t8 in range(8):
                        bank, bk = next_mm()
                        for k in range(8):
                            A("pe", lambda e, bank=bank, k=k, nb=nb, tok0=tok0, t8=t8: e.matmul(bank[0:nb, :], lhsT=xb[:, k, tok0 + t8:tok0 + 8 * nb:8], rhs=Ws[:, k, :], start=(k == 0), stop=(k == 7)),
                              ["Ws"] + xbkeys(tok0, 8 * nb), [bk])
                        if t8 % 2 == 0:
                            A("act", lambda e, bank=bank, nb=nb, S_=S_, t8=t8: e.copy(out=S_[0:nb, :, t8, :], in_=bank[0:nb, :].rearrange("p (g m) -> p g m", m=16)), [bk], [sk])
                        else:
                            A("dve", lambda e, bank=bank, nb=nb, S_=S_, t8=t8: e.tensor_copy(out=S_[0:nb, :, t8, :], in_=bank[0:nb, :].rearrange("p (g m) -> p g m", m=16)), [bk], [sk])
                    for g8 in range(4):
                        for q in range(8):
                            g = g8 * 8 + q
                            A("pe", lambda e, S_=S_, nb=nb, g=g, q=q: e.transpose(trp[:, q, 0:nb], S_[0:nb, g, :, :].rearrange("p a b -> p (a b)"), idn[0:nb, 0:nb]), [sk, "idn"], ["trp"])
                        if g8 % 2 == 0:
                            A("act", lambda e, g8=g8, bt=bt, nb=nb: e.copy(out=Uv[:, g8 * 8:(g8 + 1) * 8, bt * 128:bt * 128 + nb], in_=trp[:, :, 0:nb]), ["trp"], ["U"])
                        else:
                            A("dve", lambda e, g8=g8, bt=bt, nb=nb: e.tensor_copy(out=Uv[:, g8 * 8:(g8 + 1) * 8, bt * 128:bt * 128 + nb], in_=trp[:, :, 0:nb]), ["trp"], ["U"])
            df.barrier(bart[:, 0:1])

            with contextlib.ExitStack() as st:
              if "s5b" not in _SKIP:
                sb = lambda name, shape, dt: st.enter_context(nc.sbuf_tensor(uniq(name), shape, dt))
                V = lambda e: e
                lam_r = sb("lam_r", [128, 32], F32)
                lam_i = sb("lam_i", [128, 32], F32)
                ls1 = sb("ls1", [128, 32], F32)
                st_tmp = contextlib.ExitStack()
                sbt = lambda name, shape, dt: st_tmp.enter_context(nc.sbuf_tensor(uniq(name), shape, dt))
                cr1 = sb("cr1", [128, 32, 16], F32)
                ci1 = sb("ci1", [128, 32, 16], F32)
                exps = sb("exps", [128, 4, 8], F32)
                mask = sb("mask", [128, 2, 128], F32)
                iota = sb("iota", [128, 544], F32)
                drep = sb("drep", [128, 32], F32)
                sm = [sb("sm%d" % i, [128, 32], F32) for i in range(14)]
                Bbr = sb("Bbr", [128, 32, 16], F32)
                Bbi = sb("Bbi", [128, 32, 16], F32)
                Wr = sb("Wr", [128, 4, 8, 32], F32)
                Wi = sb("Wi", [128, 4, 8, 32], F32)
                rho8 = sb("rho8", [128, 32], F32)
                tt8 = sb("tt8", [128, 32], F32)
                br1 = sbt("br1", [128, 32, 16], F32)
                bi1 = sbt("bi1", [128, 32, 16], F32)
                tb1 = sbt("tb1", [128, 32, 16], F32)
                tb2 = sbt("tb2", [128, 32, 16], F32)
                EA = sbt("EA", [128, 4, 8, 32], F32)
                ET = sbt("ET", [128, 4, 8, 32], F32)
                tw = sbt("tw", [128, 4, 8, 32], F32)
                dma("sp", lam_r[:], s5_lam1[j, 0], (), ["lam_r"])
                dma("sp", lam_i[:], s5_lam1[j, 1], (), ["lam_i"])
                for d_ in range(2):
                    dma("sp", ls1[d_ * 64:(d_ + 1) * 64, :], s5_ls[j, d_:d_ + 1, :].partition_broadcast(64), (), ["ls1"])
                dma("sp", br1[:], s5_b1[j, 0], (), ["br1"])
                dma("sp", bi1[:], s5_b1[j, 1], (), ["bi1"])
                dma("sp", cr1[:], s5_c1[j, 0], (), ["cr1"])
                dma("sp", ci1[:], s5_c1[j, 1], (), ["ci1"])
                dma("sp", exps[:], cEXPS, (), ["exps"])
                dma("sp", mask[:], cMASK, (), ["mask"])
                dma("sp", iota[:], cIOTA, (), ["iota"])
                dma("sp", drep[:], s5_drep[j], (), ["drep"])
                PK = ["pre"]

                def dv(fn):
                    A("dve", fn, PK + ["lam_r", "lam_i", "ls1", "br1", "bi1", "cr1", "ci1", "exps", "mask", "iota", "drep"], PK)

                def ac(fn):
                    A("act", fn, PK, PK)

                def sincos(tt, sn, cs, tmp, tmp2):
                    dv(lambda e: e.tensor_scalar(out=tmp, in0=tt, scalar1=MAG, scalar2=MAG, op0=ALU.add, op1=ALU.subtract))
                    dv(lambda e: e.tensor_tensor(out=tmp, in0=tt, in1=tmp, op=ALU.subtract))
                    ac(lambda e: e.activation(out=sn, in_=tmp, func=AF.Sin, scale=TWO_PI))
                    dv(lambda e: e.tensor_scalar(out=tmp2, in0=tt, scalar1=0.25, scalar2=None, op0=ALU.add))
                    dv(lambda e: e.tensor_scalar(out=tmp, in0=tmp2, scalar1=MAG, scalar2=MAG, op0=ALU.add, op1=ALU.subtract))
                    dv(lambda e: e.tensor_tensor(out=tmp, in0=tmp2, in1=tmp, op=ALU.subtract))
                    ac(lambda e: e.activation(out=cs, in_=tmp, func=AF.Sin, scale=TWO_PI))

                lr, dtt, a_, tht, mag1, s1, c1, w1r, w1i, den, cfr, cfi, x1, x2 = [t[:] for t in sm]
                dv(lambda e: e.tensor_scalar(out=lr, in0=lam_r[:], scalar1=-1e-4, scalar2=None, op0=ALU.min))
                ac(lambda e: e.activation(out=dtt, in_=ls1[:], func=AF.Exp))
                dv(lambda e: e.tensor_tensor(out=a_, in0=lr, in1=dtt, op=ALU.mult))
                dv(lambda e: e.tensor_tensor(out=tht, in0=lam_i[:], in1=dtt, op=ALU.mult))
                dv(lambda e: e.tensor_scalar(out=tht, in0=tht, scalar1=1.0 / TWO_PI, scalar2=None, op0=ALU.mult))
                ac(lambda e: e.activation(out=mag1, in_=a_, func=AF.Exp))
                sincos(tht, s1, c1, x1, x2)
                dv(lambda e: e.tensor_tensor(out=w1r, in0=mag1, in1=c1, op=ALU.mult))
                dv(lambda e: e.tensor_tensor(out=w1i, in0=mag1, in1=s1, op=ALU.mult))
                dv(lambda e: e.tensor_scalar(out=w1r, in0=w1r, scalar1=-1.0, scalar2=None, op0=ALU.add))
                dv(lambda e: e.tensor_tensor(out=den, in0=lr, in1=lr, op=ALU.mult))
                dv(lambda e: e.tensor_tensor(out=x1, in0=lam_i[:], in1=lam_i[:], op=ALU.mult))
                dv(lambda e: e.tensor_tensor(out=den, in0=den, in1=x1, op=ALU.add))
                dv(lambda e: e.reciprocal(out=den, in_=den))
                dv(lambda e: e.tensor_tensor(out=x1, in0=w1r, in1=lr, op=ALU.mult))
                dv(lambda e: e.tensor_tensor(out=x2, in0=w1i, in1=lam_i[:], op=ALU.mult))
                dv(lambda e: e.tensor_tensor(out=x1, in0=x1, in1=x2, op=ALU.add))
                dv(lambda e: e.tensor_tensor(out=cfr, in0=x1, in1=den, op=ALU.mult))
                dv(lambda e: e.tensor_tensor(out=x1, in0=w1i, in1=lr, op=ALU.mult))
                dv(lambda e: e.tensor_tensor(out=x2, in0=w1r, in1=lam_i[:], op=ALU.mult))
                dv(lambda e: e.tensor_tensor(out=x1, in0=x1, in1=x2, op=ALU.subtract))
                dv(lambda e: e.tensor_tensor(out=cfi, in0=x1, in1=den, op=ALU.mult))
                bc = lambda t: t.unsqueeze(2).to_broadcast([128, 32, 16])
                dv(lambda e: e.tensor_tensor(out=tb1[:], in0=br1[:], in1=bc(cfr), op=ALU.mult))
                dv(lambda e: e.tensor_tensor(out=tb2[:], in0=bi1[:], in1=bc(cfi), op=ALU.mult))
                dv(lambda e: e.tensor_tensor(out=Bbr[:], in0=tb1[:], in1=tb2[:], op=ALU.subtract))
                dv(lambda e: e.tensor_tensor(out=tb1[:], in0=bi1[:], in1=bc(cfr), op=ALU.mult))
                dv(lambda e: e.tensor_tensor(out=tb2[:], in0=br1[:], in1=bc(cfi), op=ALU.mult))
                dv(lambda e: e.tensor_tensor(out=Bbi[:], in0=tb1[:], in1=tb2[:], op=ALU.add))
                exb = exps[:].unsqueeze(3).to_broadcast([128, 4, 8, 32])
                ab = lambda t: t.unsqueeze(1).unsqueeze(1).to_broadcast([128, 4, 8, 32])
                dv(lambda e: e.tensor_tensor(out=EA[:], in0=exb, in1=ab(a_), op=ALU.mult))
                dv(lambda e: e.tensor_tensor(out=ET[:], in0=exb, in1=ab(tht), op=ALU.mult))
                ac(lambda e: e.activation(out=EA[:], in_=EA[:], func=AF.Exp))
                sincos(ET[:], Wi[:], Wr[:], tw[:], ET[:])
                dv(lambda e: e.tensor_tensor(out=Wr[:], in0=Wr[:], in1=EA[:], op=ALU.mult))
                dv(lambda e: e.tensor_tensor(out=Wi[:], in0=Wi[:], in1=EA[:], op=ALU.mult))
                dv(lambda e: e.tensor_scalar(out=x1, in0=a_, scalar1=8.0, scalar2=None, op0=ALU.mult))
                ac(lambda e: e.activation(out=rho8[:], in_=x1, func=AF.Exp))
                dv(lambda e: e.tensor_scalar(out=tt8[:], in0=tht, scalar1=8.0, scalar2=None, op0=ALU.mult))

                st_tmp.close()
                df.barrier(bart[:, 0:1])
                PK = ["pre"]
                KT = sb("KT", [128, 4, 128], BF16)
                ELTr = sb("ELTr", [128, 4, 128], BF16)
                ELTi = sb("ELTi", [128, 4, 128], BF16)
                CLr = sb("CLr", [128, 4, 128], BF16)
                nCLi = sb("nCLi", [128, 4, 128], BF16)
                Rr = sb("Rr", [128, 4, 128], BF16)
                Ri = sb("Ri", [128, 4, 128], BF16)
                Qr = sb("Qr", [128, 4, 128], BF16)
                nQi = sb("nQi", [128, 4, 128], BF16)
                ELr = sb("ELr", [128, 4, 128], BF16)
                ELi = sb("ELi", [128, 4, 128], BF16)
                p1 = sb("p1", [128, 4, 128], F32)
                p2 = sb("p2", [128, 4, 128], F32)
                scr = mT[:, 4:8, :].rearrange("p c t -> p (c t)").bitcast(F32).rearrange("p (n b) -> p n b", b=544)
                SETS = []
                for si_ in range(2):
                    d_ = {}
                    for ti_, nm in enumerate(("Er", "Ei", "cos", "sin", "gr", "gi", "sr", "si")):
                        d_[nm] = scr[:, si_ * 8 + ti_, :]
                    d_["tA"] = sb("tA%d" % si_, [128, 544], F32)[:]
                    d_["Zr"] = sb("Zr%d" % si_, [128, 544], BF16)
                    d_["Zi"] = sb("Zi%d" % si_, [128, 544], BF16)
                    d_["zg"] = sb("zg%d" % si_, [128, 544], BF16)
                    d_["id"] = si_
                    SETS.append(d_)
                    A("pool", lambda e, d_=d_: e.memset(d_["Zr"][:], 0.0), (), ["Zr%d" % si_])
                    A("pool", lambda e, d_=d_: e.memset(d_["Zi"][:], 0.0), (), ["Zi%d" % si_])

                def cprod(outr, outi, l, Xr_, Xi_, g0, neg_i):
                    wv = lambda W_: W_[:, l, :, g0:g0 + 4].rearrange("p s g -> p g s").unsqueeze(3).to_broadcast([128, 4, 8, 16])
                    xv = lambda X_: X_[:, g0:g0 + 4, :].unsqueeze(2).to_broadcast([128, 4, 8, 16])
                    o4 = lambda t: t[:].rearrange("p g (s m) -> p g s m", m=16)
                    dv(lambda e: e.tensor_tensor(out=o4(p1), in0=wv(Wr), in1=xv(Xr_), op=ALU.mult))
                    dv(lambda e: e.tensor_tensor(out=o4(p2), in0=wv(Wi), in1=xv(Xi_), op=ALU.mult))
                    dv(lambda e: e.tensor_tensor(out=outr[:], in0=p1[:], in1=p2[:], op=ALU.subtract))
                    dv(lambda e: e.tensor_tensor(out=o4(p1), in0=wv(Wr), in1=xv(Xi_), op=ALU.mult))
                    dv(lambda e: e.tensor_tensor(out=o4(p2), in0=wv(Wi), in1=xv(Xr_), op=ALU.mult))
                    if neg_i:
                        dv(lambda e: e.scalar_tensor_tensor(out=outi[:], in0=p1[:], scalar=-1.0, in1=p2[:], op0=ALU.mult, op1=ALU.subtract))
                    else:
                        dv(lambda e: e.tensor_tensor(out=outi[:], in0=p1[:], in1=p2[:], op=ALU.add))

                for qq in range(8):
                    g0 = qq * 4
                    cprod(Rr, Ri, 0, Bbr, Bbi, g0, False)
                    cprod(ELr, ELi, 1, Bbr, Bbi, g0, False)
                    cprod(Qr, nQi, 2, cr1, ci1, g0, True)
                    cprod(CLr, nCLi, 3, cr1, ci1, g0, True)
                    for q in range(4):
                        for h_ in range(2):
                            hb = h_ * 64
                            bank = mm[h_]
                            bk = "mm%d" % h_
                            A("pe", lambda e, bank=bank, hb=hb, q=q: e.matmul(bank[:, 0:128], lhsT=Rr[hb:hb + 64, q, :], rhs=Qr[hb:hb + 64, q, :], start=True, stop=False), PK, [bk])
                            A("pe", lambda e, bank=bank, hb=hb, q=q: e.matmul(bank[:, 0:128], lhsT=Ri[hb:hb + 64, q, :], rhs=nQi[hb:hb + 64, q, :], start=False, stop=True), PK, [bk])
                        A("dve", lambda e: e.tensor_tensor(out=p1[:, 0, :], in0=mm[0][:, 0:128], in1=mask[:, 0, :], op=ALU.mult), ["mm0"] + PK, PK)
                        A("dve", lambda e: e.tensor_tensor(out=p2[:, 0, :], in0=mm[1][:, 0:128], in1=mask[:, 1, :], op=ALU.mult), ["mm1"] + PK, PK)
                        A("dve", lambda e, q=q: e.tensor_tensor(out=KT[:, q, :], in0=p1[:, 0, :], in1=p2[:, 0, :], op=ALU.add), PK, PK)
                        A("pe", lambda e, q=q: e.transpose(trp[:, 0, :], ELr[:, q, :], idn[:]), PK + ["idn"], ["trp"])
                        A("pe", lambda e, q=q: e.transpose(trp[:, 1, :], ELi[:, q, :], idn[:]), PK + ["idn"], ["trp"])
                        A("act", lambda e, q=q: e.copy(out=ELTr[:, q, :], in_=trp[:, 0, :]), ["trp"] + PK, PK)
                        A("act", lambda e, q=q: e.copy(out=ELTi[:, q, :], in_=trp[:, 1, :]), ["trp"] + PK, PK)
                    def grp(q, g, S):
                        sid = S["id"]
                        K_ = lambda nm: "%s%d" % (nm, sid)
                        Eb = stp[sid]
                        ebk = "stp%d" % sid
                        pvo = sid * 128
                        Er, Ei, cosT, sinT, gr, gi, sr, si, tA = S["Er"], S["Ei"], S["cos"], S["sin"], S["gr"], S["gi"], S["sr"], S["si"], S["tA"]
                        Zr_, Zi_, z_ = S["Zr"], S["Zi"], S["zg"]

                        def st0():
                            for ri, ELT_ in enumerate((ELTr, ELTi)):
                                for (b0, nb) in PIECES:
                                    if b0 < 512:
                                        yo = Eb[:, ri * 512 + b0:ri * 512 + b0 + nb]
                                        wk_ = [ebk]
                                    else:
                                        yo = pv[:, pvo + ri * 64:pvo + ri * 64 + nb]
                                        wk_ = ["pv"]
                                    A("pe", lambda e, yo=yo, ELT_=ELT_, b0=b0, nb=nb: e.matmul(yo, lhsT=ELT_[:, q, :], rhs=Uv[:, g, b0:b0 + nb], start=True, stop=True), PK + ["U"], wk_)

                        def st1():
                            for ri, E_ in enumerate((Er, Ei)):
                                ek = K_("Er" if ri == 0 else "Ei")
                                A("act", lambda e, E_=E_, ri=ri: e.copy(out=E_[0:64, 32:544], in_=Eb[0:64, ri * 512:(ri + 1) * 512]), [ebk], [ek])
                                A("act", lambda e, E_=E_, ri=ri: e.copy(out=E_[0:64, 0:32], in_=pv[0:64, pvo + ri * 64:pvo + ri * 64 + 32]), ["pv"], [ek])
                                A("act", lambda e, E_=E_, ri=ri: e.copy(out=E_[64:128, 543:31:-1], in_=Eb[64:128, ri * 512:(ri + 1) * 512]), [ebk], [ek])
                                A("act", lambda e, E_=E_, ri=ri: e.copy(out=E_[64:128, 31::-1], in_=pv[64:128, pvo + ri * 64:pvo + ri * 64 + 32]), ["pv"], [ek])
                            A("dve", lambda e: e.tensor_scalar(out=gr, in0=iota[:], scalar1=tt8[:, g:g + 1], scalar2=None, op0=ALU.mult), PK + ["iota"], [K_("gr")])
                            A("dve", lambda e: e.tensor_scalar(out=gi, in0=gr, scalar1=MAG, scalar2=MAG, op0=ALU.add, op1=ALU.subtract), [K_("gr")], [K_("gi")])
                            A("dve", lambda e: e.tensor_tensor(out=gi, in0=gr, in1=gi, op=ALU.subtract), [K_("gr"), K_("gi")], [K_("gi")])

                        def st2():
                            A("act", lambda e: e.activation(out=sinT, in_=gi, func=AF.Sin, scale=TWO_PI), [K_("gi")], [K_("sin")])
                            A("dve", lambda e: e.tensor_scalar(out=gr, in0=gr, scalar1=0.25, scalar2=None, op0=ALU.add), [K_("gr")], [K_("gr")])
                            A("dve", lambda e: e.tensor_scalar(out=gi, in0=gr, scalar1=MAG, scalar2=MAG, op0=ALU.add, op1=ALU.subtract), [K_("gr"), K_("sin")], [K_("gi")])
                            A("dve", lambda e: e.tensor_tensor(out=gi, in0=gr, in1=gi, op=ALU.subtract), [K_("gr"), K_("gi")], [K_("gi")])

                        def st3():
                            A("act", lambda e: e.activation(out=cosT, in_=gi, func=AF.Sin, scale=TWO_PI), [K_("gi")], [K_("cos")])

                        def st4():
                            A("pool", lambda e: e.tensor_tensor(out=gr, in0=Er, in1=cosT, op=ALU.mult), [K_("Er"), K_("cos"), K_("gr")], [K_("gr")])
                            A("pool", lambda e: e.tensor_tensor(out=tA, in0=Ei, in1=sinT, op=ALU.mult), [K_("Ei"), K_("sin")], [K_("tA")])
                            A("pool", lambda e: e.tensor_tensor(out=gr, in0=gr, in1=tA, op=ALU.add), [K_("gr"), K_("tA")], [K_("gr")])
                            A("pool", lambda e: e.tensor_tensor(out=gi, in0=Ei, in1=cosT, op=ALU.mult), [K_("Ei"), K_("cos"), K_("gi")], [K_("gi")])
                            A("pool", lambda e: e.tensor_tensor(out=tA, in0=Er, in1=sinT, op=ALU.mult), [K_("Er"), K_("sin"), K_("tA")], [K_("tA")])
                            A("pool", lambda e: e.tensor_tensor(out=gi, in0=gi, in1=tA, op=ALU.subtract), [K_("gi"), K_("tA")], [K_("gi")])

                        def st5():
                            rb = rho8[:, g:g + 1].to_broadcast([128, 544])
                            A("dve", lambda e: e.tensor_tensor_scan(out=sr, data0=rb, data1=gr, initial=0.0, op0=ALU.mult, op1=ALU.add), [K_("gr")] + PK, [K_("sr")])
                            A("dve", lambda e: e.tensor_tensor_scan(out=si, data0=rb, data1=gi, initial=0.0, op0=ALU.mult, op1=ALU.add), [K_("gi")] + PK, [K_("si")])

                        def rot_out(Z_, zk, c1, k1, c2, k2, op):
                            A("pool", lambda e: e.tensor_tensor(out=Er, in0=sr, in1=c1, op=ALU.mult), [K_("sr"), k1, K_("Er")], [K_("Er")])
                            A("pool", lambda e: e.tensor_tensor(out=Ei, in0=si, in1=c2, op=ALU.mult), [K_("si"), k2, K_("Ei")], [K_("Ei")])
                            A("dve", lambda e: e.tensor_tensor(out=Z_[0:64, 0:512], in0=Er[0:64, 31:543], in1=Ei[0:64, 31:543], op=op), [K_("Er"), K_("Ei")], [zk])
                            A("dve", lambda e: e.tensor_tensor(out=Z_[0:64, 513:544], in0=Er[0:64, 0:31], in1=Ei[0:64, 0:31], op=op), [K_("Er"), K_("Ei")], [zk])
                            A("dve", lambda e: e.tensor_tensor(out=Z_[64:128, 542::-1], in0=Er[64:128, 0:543], in1=Ei[64:128, 0:543], op=op), [K_("Er"), K_("Ei")], [zk])

                        def st6():
                            rot_out(Zr_, K_("Zr"), cosT, K_("cos"), sinT, K_("sin"), ALU.subtract)

                        def st7():
                            rot_out(Zi_, K_("Zi"), sinT, K_("sin"), cosT, K_("cos"), ALU.add)

                        def st8():
                            for (b0, nb) in PIECES:
                                if b0 < 512:
                                    yo = mm[0][:, b0:b0 + nb]
                                    wk_ = ["mm0"]
                                else:
                                    yo = mm[1][:, 0:nb]
                                    wk_ = ["mm1"]
                                A("pe", lambda e, yo=yo, b0=b0, nb=nb: e.matmul(yo, lhsT=KT[:, q, :], rhs=Uv[:, g, b0:b0 + nb], start=True, stop=False), PK + ["U"], wk_)
                                A("pe", lambda e, yo=yo, b0=b0, nb=nb: e.matmul(yo, lhsT=CLr[:, q, :], rhs=Zr_[:, b0:b0 + nb], start=False, stop=False), PK + [K_("Zr")], wk_)
                                A("pe", lambda e, yo=yo, b0=b0, nb=nb: e.matmul(yo, lhsT=nCLi[:, q, :], rhs=Zi_[:, b0:b0 + nb], start=False, stop=True), PK + [K_("Zi")], wk_)
                            A("dve", lambda e: e.scalar_tensor_tensor(out=gr[:, 0:512], in0=Uv[:, g, 0:512], scalar=drep[:, g:g + 1], in1=mm[0][:, 0:512], op0=ALU.mult, op1=ALU.add),
                              ["mm0", "U", "drep", K_("gr")], [K_("gr")])
                            A("dve", lambda e: e.scalar_tensor_tensor(out=gr[:, 512:544], in0=Uv[:, g, 512:544], scalar=drep[:, g:g + 1], in1=mm[1][:, 0:32], op0=ALU.mult, op1=ALU.add),
                              ["mm1", "U", "drep", K_("gr")], [K_("gr")])
                            A("act", lambda e: e.activation(out=z_[:], in_=gr, func=AF.Gelu), [K_("gr")], [K_("zg")])
                            for t8 in range(8):
                                dma("sp", zs[t8, g], z_[t8 * 16:(t8 + 1) * 16, :], [K_("zg")], ["zs"])

                        return [st0, st1, st2, st3, st4, st5, st6, st7], st8

                    for pr in range(2):
                        gA, gB = g0 + 2 * pr, g0 + 2 * pr + 1
                        stA, yA = grp(2 * pr, gA, SETS[0])
                        stB, yB = grp(2 * pr + 1, gB, SETS[1])
                        for k_ in range(len(stA)):
                            stA[k_]()
                            stB[k_]()
                        yA()
                        yB()
            df.barrier(bart[:, 0:1])

            with contextlib.ExitStack() as st:
              if "s5c" not in _SKIP:
                sb = lambda name, shape, dt: st.enter_context(nc.sbuf_tensor(uniq(name), shape, dt))
                Wg = sb("Wg", [128, 4, 1024], BF16)
                bg = sb("bg", [128, 8], F32)
                sgd = sb("sgd", [128, T], BF16)
                zsb = [sb("zsb%d" % i, [128, 4, 544], BF16) for i in range(2)]
                sig = [sb("sig%d" % i, [128, 544], F32) for i in range(2)]
                v1 = [sb("v1%d" % i, [128, 544], F32) for i in range(2)]
                wgd = sb("wgd", [128, 8, 128], BF16)
                for h_ in range(2):
                    dma("pool", Wg[:, :, h_ * 512:(h_ + 1) * 512], glu_w[j].rearrange("(c p) n -> p c n", p=128)[:, :, h_ * 512:(h_ + 1) * 512], (), ["Wg"])
                dma("sp", bg[:], glu_bT[j], (), ["bg"])
                for k in range(4):
                    dma("pool", wgd[:], Wd[:, :, 2048 + k * 128:2048 + (k + 1) * 128], (), ["wgd"])
                    inproj_fm(wgd, "wgd", NTILES, lambda bank, bk, n0, nn: A(
                        "act", lambda e: e.activation(out=sgd[:, n0:n0 + nn], in_=bank[:, 0:nn], func=AF.Silu), [bk], ["sgd"]))

                    def banks(n):
                        if n % 2 == 0:
                            return (mm[0], "mm0"), (mm[1], "mm1"), (pv[:, 0:32], "pv"), (pv[:, 256:288], "pv")
                        return (stp[0][:, 0:512], "stp0a"), (stp[0][:, 512:1024], "stp0b"), (stp[1][:, 0:32], "stp1a"), (stp[1][:, 512:544], "stp1b")

                    def c_load(n):
                        zb, zbk = zsb[n % 2], "zsb%d" % (n % 2)
                        dma("sp", zb[:], zs[n].rearrange("g m b -> (g m) b").rearrange("(c p) b -> p c b", p=128), ["zs"], [zbk])

                    def c_mm(n):
                        zb, zbk = zsb[n % 2], "zsb%d" % (n % 2)
                        (vb, vk), (gb_, gk), (vp, vpk), (gp_, gpk) = banks(n)
                        for (b0, nb) in PIECES:
                            for vg in range(2):
                                if b0 < 512:
                                    yo = (vb if vg == 0 else gb_)[:, b0:b0 + nb]
                                    wk_ = [vk if vg == 0 else gk]
                                else:
                                    yo = vp if vg == 0 else gp_
                                    wk_ = [vpk if vg == 0 else gpk]
                                col = (vg * 4 + k) * 128
                                for c in range(4):
                                    A("pe", lambda e, yo=yo, c=c, col=col, zb=zb, b0=b0, nb=nb: e.matmul(yo, lhsT=Wg[:, c, col:col + 128], rhs=zb[:, c, b0:b0 + nb], start=(c == 0), stop=(c == 3)),
                                      ["Wg", zbk], wk_)

                    def c_sig(n):
                        (vb, vk), (gb_, gk), (vp, vpk), (gp_, gpk) = banks(n)
                        sg_, sgk = sig[n % 2], "sig%d" % (n % 2)
                        A("act", lambda e, gb_=gb_, sg_=sg_, k=k: e.activation(out=sg_[:, 0:512], in_=gb_[:, 0:512], func=AF.Sigmoid, bias=bg[:, 4 + k:5 + k]), [gk, "bg"], [sgk])
                        A("act", lambda e, gp_=gp_, sg_=sg_, k=k: e.activation(out=sg_[:, 512:544], in_=gp_, func=AF.Sigmoid, bias=bg[:, 4 + k:5 + k]), [gpk, "bg"], [sgk])

                    def c_stt(n):
                        (vb, vk), (gb_, gk), (vp, vpk), (gp_, gpk) = banks(n)
                        sg_, sgk = sig[n % 2], "sig%d" % (n % 2)
                        v_, v1k = v1[n % 2], "v1%d" % (n % 2)
                        A("dve", lambda e, vb=vb, sg_=sg_, v_=v_, k=k: e.scalar_tensor_tensor(out=v_[:, 0:512], in0=vb[:, 0:512], scalar=bg[:, k:k + 1], in1=sg_[:, 0:512], op0=ALU.add, op1=ALU.mult), [vk, sgk, "bg"], [v1k])
                        A("dve", lambda e, vp=vp, sg_=sg_, v_=v_, k=k: e.scalar_tensor_tensor(out=v_[:, 512:544], in0=vp, scalar=bg[:, k:k + 1], in1=sg_[:, 512:544], op0=ALU.add, op1=ALU.mult), [vpk, sgk, "bg"], [v1k])

                    def c_out(n):
                        v_, v1k = v1[n % 2], "v1%d" % (n % 2)
                        A("pool", lambda e, v_=v_, n=n, k=k: e.tensor_tensor(out=mT[:, 4 + k, n::8], in0=v_[:], in1=sgd[:, n::8], op=ALU.mult), [v1k, "sgd"], [("mT", 4 + k, t) for t in range(NT)])

                    pipeline([c_load, c_mm, c_sig, c_stt, c_out], 8, "s5c")
            df.barrier(bart[:, 0:1])

            with contextlib.ExitStack() as st:
              if "gmlp" not in _SKIP:
                sb = lambda name, shape, dt: st.enter_context(nc.sbuf_tensor(uniq(name), shape, dt))
                Wv = sb("Wv", [128, 8, 512], BF16)
                Wu = sb("Wu", [128, 8, 512], BF16)
                Wc = sb("Wc", [128, 8, 512], BF16)
                wsT = sb("wsT", [128, 4, 128], BF16)
                sgT = sb("sgT", [128, 4], F32)
                bsb = sb("bsb", [128, 512], F32)
                st6 = sb("st6", [128, 12], F32)
                mv = sb("mv", [128, 4 * NT], F32)
                vn = [sb("vn%d" % i, [128, 512], BF16) for i in range(2)]
                sgc = [sb("sgc%d" % i, [128, 512], BF16) for i in range(2)]
                mx = [sb("mx%d" % i, [128, 512], F32) for i in range(2)]
                dma("pool", Wu[:], Wd[:, :, 0:512], (), ["Wu"])
                dma("pool", Wv[:], Wd[:, :, 512:1024], (), ["Wv"])
                dma("pool", Wc[:], Wd[:, :, 1024:1536], (), ["Wc"])
                dma("pool", wsT[:], sgu_wT[j], (), ["wsT"])
                dma("sp", sgT[:], sgu_gT[j], (), ["sgT"])
                dma("sp", bsb[:], sgu_b[j:j + 1, :].partition_broadcast(128), (), ["bsb"])
                vsb = [sb("vsb%d" % i, [128, 512], F32) for i in range(4)]
                tmx = [sb("tmx%d" % i, [128, 512], F32) for i in range(2)]
                MEAN = lambda n: mv[:, 4 * n:4 * n + 1]
                VAR = lambda n: mv[:, 4 * n + 1:4 * n + 2]
                RSg = lambda n: mv[:, 4 * n + 2:4 * n + 3]
                VS = lambda n: (vsb[n % 4], "vsb%d" % (n % 4))

                def g_v(n):
                    bank, bk = mm[n % 2], "mm%d" % (n % 2)
                    for k in range(8):
                        A("pe", lambda e, bank=bank, k=k, n=n: e.matmul(bank[:, :], lhsT=xb[:, k, n * 128:(n + 1) * 128], rhs=Wv[:, k, :], start=(k == 0), stop=(k == 7)), ["Wv", ("xb", n)], [bk])

                def g_cp(n):
                    bank, bk = mm[n % 2], "mm%d" % (n % 2)
                    vs_, vsk = VS(n)
                    A("act", lambda e, bank=bank, vs_=vs_: e.copy(out=vs_[:], in_=bank[:, :]), [bk], [vsk])

                def g_bn(n):
                    vs_, vsk = VS(n)
                    s6 = st6[:, (n % 2) * 6:(n % 2) * 6 + 6]
                    A("dve", lambda e, vs_=vs_, s6=s6: e.bn_stats(out=s6, in_=vs_[:]), [vsk], [("st6", n % 2)])
                    A("dve", lambda e, n=n, s6=s6: e.bn_aggr(out=mv[:, 4 * n:4 * n + 2], in_=s6), [("st6", n % 2)], [("mv", n)])

                def g_r1(n):
                    A("dve", lambda e, n=n: e.tensor_scalar(out=RSg(n), in0=VAR(n), scalar1=1.0, scalar2=EPS, op0=ALU.mult, op1=ALU.add), [("mv", n)], [("grs", n)])

                def g_r2(n):
                    A("act", lambda e, n=n: e.activation(out=RSg(n), in_=RSg(n), func=AF.Sqrt), [("grs", n)], [("grs", n)])

                def g_vn(n):
                    vs_, vsk = VS(n)
                    v_, vk = vn[n % 2], "vn%d" % (n % 2)
                    A("dve", lambda e, n=n: e.reciprocal(out=RSg(n), in_=RSg(n)), [("grs", n)], [("grs", n)])
                    A("dve", lambda e, vs_=vs_, v_=v_, n=n: e.tensor_scalar(out=v_[:], in0=vs_[:], scalar1=MEAN(n), scalar2=RSg(n), op0=ALU.subtract, op1=ALU.mult), [vsk, ("mv", n), ("grs", n)], [vk])

                def g_mm(n):
                    v_, vk = vn[n % 2], "vn%d" % (n % 2)
                    for g in range(4):
                        A("pe", lambda e, v_=v_, g=g: e.matmul(pv[:, g * 128:(g + 1) * 128], lhsT=v_[:, g * 128:(g + 1) * 128], rhs=wsT[:, g, :], start=True, stop=True), [vk, "wsT"], ["pv"])
                    sp_ = stp[n % 2]
                    for half, W_, wkk in ((0, Wu, "Wu"), (1, Wc, "Wc")):
                        sk = "stp%d%s" % (n % 2, "ab"[half])
                        for g in range(4):
                            for k in range(8):
                                A("pe", lambda e, sp_=sp_, half=half, W_=W_, g=g, k=k, n=n: e.matmul(sp_[:, half * 512 + g * 128:half * 512 + (g + 1) * 128], lhsT=W_[:, k, g * 128:(g + 1) * 128], rhs=xb[:, k, n * 128:(n + 1) * 128], start=(k == 0), stop=(k == 7)),
                                  [wkk, ("xb", n)], [sk])

                def g_ep(n):
                    sp_ = stp[n % 2]
                    c_, ck = sgc[n % 2], "sgc%d" % (n % 2)
                    m_, mk = mx[n % 2], "mx%d" % (n % 2)
                    t_, tk_ = tmx[n % 2], "tmx%d" % (n % 2)
                    A("act", lambda e, sp_=sp_, c_=c_: e.activation(out=c_[:], in_=sp_[:, 512:1024], func=AF.Silu), ["stp%db" % (n % 2)], [ck])
                    for g in range(4):
                        A("dve", lambda e, m_=m_, g=g: e.scalar_tensor_tensor(out=m_[:, g * 128:(g + 1) * 128], in0=pv[:, g * 128:(g + 1) * 128], scalar=sgT[:, g:g + 1], in1=bsb[:, g * 128:(g + 1) * 128], op0=ALU.mult, op1=ALU.add),
                          ["pv", "sgT", "bsb"], [mk])
                    A("dve", lambda e, m_=m_, sp_=sp_, t_=t_: e.tensor_tensor(out=t_[:], in0=m_[:], in1=sp_[:, 0:512], op=ALU.mult), [mk, "stp%da" % (n % 2)], [tk_])

                def g_out(n):
                    c_, ck = sgc[n % 2], "sgc%d" % (n % 2)
                    t_, tk_ = tmx[n % 2], "tmx%d" % (n % 2)
                    A("pool", lambda e, t_=t_, c_=c_, n=n: e.tensor_tensor(out=mT[:, 0:4, n * 128:(n + 1) * 128], in0=t_[:].rearrange("p (g c) -> p g c", g=4), in1=c_[:].rearrange("p (g c) -> p g c", g=4), op=ALU.mult),
                      [tk_, ck], [("mT", g, n) for g in range(4)])

                pipeline([g_v, g_cp, g_bn, g_r1, g_r2, g_vn, g_mm, g_ep, g_out], NT, "gmlp")
            df.barrier(bart[:, 0:1])

        for L in range(n_layers):
            last = (L == n_layers - 1)
            (prenorm_old if "oldpre" in _SKIP else prenorm)(L)
            if L % 2 == 0:
                if "even" not in _SKIP:
                    even_mixer(L)
            else:
                odd_mixer(L)
            (post_old if "oldpost" in _SKIP else post)(L, last)
        df.emit()
    return nc


_CACHE = {}


def kernel(x, c, ctx, c_ctx, w_ada, b_ada, pre_g, post_g, w_in_even, w_out_even, na_rpb,
           w_in_odd, w_out_odd, sgu_w, sgu_b, sgu_g, s5_lam_re, s5_lam_im, s5_log_step,
           s5_b_re, s5_b_im, s5_c_re, s5_c_im, s5_d, glu_w, glu_b, _n_layers=DEPTH):
    f = lambda a: np.ascontiguousarray(np.asarray(a), dtype=np.float32)
    B = x.shape[0]
    if _n_layers not in _CACHE:
        _CACHE[_n_layers] = build_program(_n_layers)
    nc = _CACHE[_n_layers]
    F1, H, CH, F256 = fnet_consts()
    eb = np.stack([build_ebias(f(na_rpb[j])) for j in range(2)]).reshape(2, 12, 128, 21 * 128)
    shared = dict(w_ada=f(w_ada), b_ada=f(b_ada), pre_g=f(pre_g), post_g=f(post_g), w_in_even=f(w_in_even),
                  w_out_even=f(w_out_even), ebias=eb, cF1=F1, cH=H, cCH=CH, cF256=F256)
    ex, mk, io = s5_consts()
    lam1 = np.stack([f(s5_lam_re), f(s5_lam_im)], axis=1)
    lam1 = np.ascontiguousarray(lam1.transpose(0, 1, 2, 4, 3)).reshape(2, 2, 128, 32)
    b1 = np.stack([f(s5_b_re), f(s5_b_im)], axis=1)
    b1 = np.ascontiguousarray(b1.transpose(0, 1, 2, 4, 3, 5)).reshape(2, 2, 128, 32, 16)
    c1 = np.stack([f(s5_c_re), f(s5_c_im)], axis=1)
    c1 = np.ascontiguousarray(c1.transpose(0, 1, 2, 5, 3, 4)).reshape(2, 2, 128, 32, 16)
    drep = np.ascontiguousarray(np.tile(f(s5_d).reshape(2, 32, 16).transpose(0, 2, 1), (1, 8, 1)))
    shared.update(dict(
        w_in_odd=f(w_in_odd), w_out_odd=f(w_out_odd),
        sgu_wT=np.ascontiguousarray(f(sgu_w).transpose(0, 3, 1, 2)),
        sgu_gT=np.ascontiguousarray(f(sgu_g).reshape(2, 4, 128).transpose(0, 2, 1)),
        sgu_b=f(sgu_b).reshape(2, 512), s5_lam1=lam1, s5_ls=f(s5_log_step), s5_b1=b1, s5_c1=c1, s5_drep=drep,
        glu_w=f(glu_w), glu_bT=np.ascontiguousarray(f(glu_b).reshape(2, 8, 128).transpose(0, 2, 1)),
        cEXPS=ex, cMASK=mk, cIOTA=io))
    cc = f(c_ctx).reshape(8, 128).T
    in_maps = []
    for b in range(B):
        m = dict(shared)
        m["xr"] = np.concatenate([f(x[b]), f(ctx[b])], axis=0)
        m["cT"] = np.ascontiguousarray(np.concatenate([f(c[b]).reshape(8, 128).T, cc], axis=1))
        in_maps.append(m)
    res = run_bass_kernel_spmd(nc, in_maps, core_ids=list(range(B)))
    return np.stack([np.asarray(r["out"], dtype=np.float32) for r in res.results], axis=0)
```

```python
import contextlib
import numpy as np
import concourse.bass as bass
import concourse.mybir as mybir
from concourse.bass_utils import run_bass_kernel_spmd

F32 = mybir.dt.float32
BF16 = mybir.dt.bfloat16
AF = mybir.ActivationFunctionType
ALU = mybir.AluOpType

ENGS = ("pe", "act", "dve", "pool", "sp")
RING = 8
SELF_SYNC = ("act", "dve", "pool")

D = 1024
T = 4352
TL = 4096
NT = 34
EPS = 1e-6
DEPTH = 4


class _Op:
    __slots__ = ("eng", "fn", "deps", "dma", "flag", "cnt", "ring", "target")


class DF:
    def __init__(self, nc):
        self.nc = nc
        self.ops = []
        self.lw = {}
        self.rd = {}
        self.ndma = {e: 0 for e in ENGS}
        self.last_on = {}
        self.bar = None
        self.dma_since_bar = []

    def add(self, eng, fn, reads=(), writes=(), dma=False):
        idx = len(self.ops)
        deps = set()
        if self.bar is not None:
            deps.add(self.bar)
        for r in reads:
            w = self.lw.get(r)
            if w is not None:
                deps.add(w)
        for r in writes:
            w = self.lw.get(r)
            if w is not None:
                deps.add(w)
            deps.update(self.rd.get(r, ()))
        for r in reads:
            self.rd.setdefault(r, []).append(idx)
        for r in writes:
            self.lw[r] = idx
            self.rd[r] = []
        o = _Op()
        o.eng, o.fn, o.deps, o.dma, o.flag, o.cnt = eng, fn, deps, dma, False, 0
        o.ring = o.target = None
        if dma:
            n = self.ndma[eng]
            self.ndma[eng] = n + 1
            o.ring = n % RING
            o.target = 16 * (n // RING + 1)
            self.dma_since_bar.append(idx)
        else:
            self.last_on[eng] = idx
        self.ops.append(o)
        return idx

    def barrier(self, tile):
        idx = len(self.ops)
        deps = set(self.last_on.values()) | set(self.dma_since_bar)
        if self.bar is not None:
            deps.add(self.bar)
        o = _Op()
        o.eng, o.fn, o.deps, o.dma, o.flag, o.cnt = "pool", (lambda e: e.memset(tile, 0.0)), deps, False, False, 0
        o.ring = o.target = None
        self.ops.append(o)
        self.last_on["pool"] = idx
        self.bar = idx
        self.dma_since_bar = []
        self.lw = {}
        self.rd = {}

    def emit(self):
        nc = self.nc
        ops = self.ops
        for o in ops:
            for d in o.deps:
                p = ops[d]
                if p.dma:
                    continue
                if p.eng != o.eng or (o.eng in SELF_SYNC) or o.dma:
                    p.flag = True
        cnt = {e: 0 for e in ENGS}
        for o in ops:
            if o.flag and not o.dma:
                cnt[o.eng] += 1
                o.cnt = cnt[o.eng]
        with contextlib.ExitStack() as st:
            csem = {e: st.enter_context(nc.semaphore("c_" + e)) for e in ENGS}
            dsem = {e: [st.enter_context(nc.semaphore("d_%s%d" % (e, i))) for i in range(RING)]
                    for e in ("sp", "pool", "act")}
            block = st.enter_context(nc.Block())
            ndma = self.ndma

            def run(engname, eng):
                waited_c = {e: 0 for e in ENGS}
                waited_d = {}
                for o in ops:
                    if o.eng != engname:
                        continue
                    for d in sorted(o.deps):
                        p = ops[d]
                        if p.dma:
                            key = (p.eng, p.ring)
                            if waited_d.get(key, 0) < p.target:
                                eng.wait_ge(dsem[p.eng][p.ring], p.target)
                                waited_d[key] = p.target
                        else:
                            if p.eng == engname and not (engname in SELF_SYNC or o.dma):
                                continue
                            if waited_c[p.eng] < p.cnt:
                                eng.wait_ge(csem[p.eng], p.cnt)
                                waited_c[p.eng] = p.cnt
                    if o.dma and o.target > 16:
                        key = (engname, o.ring)
                        if waited_d.get(key, 0) < o.target - 16:
                            eng.wait_ge(dsem[engname][o.ring], o.target - 16)
                            waited_d[key] = o.target - 16
                    ins = o.fn(eng)
                    if o.dma:
                        ins.then_inc(dsem[engname][o.ring], 16)
                    elif o.flag:
                        ins.then_inc(csem[engname], 1)
                if engname in dsem:
                    n = ndma[engname]
                    for r in range(min(n, RING)):
                        last = ((n - 1 - r) // RING) * RING + r
                        eng.wait_ge(dsem[engname][r], 16 * (last // RING + 1))

            @block.tensor
            def _(eng):
                run("pe", eng)

            @block.scalar
            def _(eng):
                run("act", eng)

            @block.vector
            def _(eng):
                run("dve", eng)

            @block.gpsimd
            def _(eng):
                run("pool", eng)

            @block.sync
            def _(eng):
                run("sp", eng)


NA_COMBOS = ([(2, kt) for kt in range(0, 5)] + [(0, kt) for kt in range(4)] + [(1, kt) for kt in range(4)]
             + [(30, kt) for kt in range(28, 32)] + [(31, kt) for kt in range(28, 32)])


def na_pattern_base(i):
    if 2 <= i <= 29:
        return 0, list(range(i - 2, i + 3))
    if i == 0:
        return 5, [0, 1, 2, 3]
    if i == 1:
        return 9, [0, 1, 2, 3]
    if i == 30:
        return 13, [28, 29, 30, 31]
    return 17, [28, 29, 30, 31]


def build_ebias(rpb):
    out = np.empty((12, 128, 21, 128), np.float32)
    a = np.arange(2)
    c = np.arange(64)
    for pi, (i, kt) in enumerate(NA_COMBOS):
        kr = (2 * kt + a)[:, None, None, None]
        r = (2 * i + a)[None, None, :, None]
        ck = c[None, :, None, None]
        cq = c[None, None, None, :]
        r0 = np.clip(r - 4, 0, 56)
        c0 = np.clip(cq - 8, 0, 48)
        ok = (kr >= r0) & (kr < r0 + 8) & (ck >= c0) & (ck < c0 + 16)
        dr = np.clip(kr - r + 7, 0, 14)
        dc = np.clip(ck - cq + 15, 0, 30)
        ok, dr, dc = np.broadcast_arrays(ok, dr, dc)
        vals = rpb[:, dr, dc]
        vals = np.where(ok[None], vals, np.float32(-30000.0))
        out[:, :, pi, :] = vals.reshape(12, 128, 128)
    return out


def fnet_consts():
    i64 = np.arange(64)
    ang = 2 * np.pi * np.outer(i64, i64) / 64.0
    F1 = np.concatenate([np.cos(ang), -np.sin(ang)], axis=1)
    t2 = i64[:, None, None]
    k1 = i64[None, :, None]
    k2 = i64[None, None, :]
    ph = -2 * np.pi * (t2 * k1 / 4096.0 + t2 * k2 / 64.0)
    Gr, Gi = np.cos(ph) / 64.0, np.sin(ph) / 64.0
    H = np.empty((64, 64, 2, 128))
    H[:, :, 0, 0:64] = Gr
    H[:, :, 0, 64:128] = Gi
    H[:, :, 1, 0:64] = -Gi
    H[:, :, 1, 64:128] = Gr
    A = np.cos(ang) / 8.0
    B = np.sin(ang) / 8.0
    CH = np.zeros((128, 6, 128))
    CH[0:64, 0, 0:64] = A
    CH[0:64, 1, 0:64] = B
    CH[0:64, 2, 64:128] = A
    CH[0:64, 3, 64:128] = B
    CH[0:64, 4, 0:64] = A
    CH[64:128, 4, 64:128] = A
    CH[0:64, 5, 0:64] = B
    CH[64:128, 5, 64:128] = B
    i256 = np.arange(256)
    a256 = 2 * np.pi * np.outer(i256, i256) / 256.0
    F256 = np.concatenate([np.cos(a256), -np.sin(a256)], axis=1) / 16.0
    F256 = F256.reshape(2, 128, 512).transpose(1, 0, 2)
    f = lambda x: np.ascontiguousarray(x, dtype=np.float32)
    return f(F1), f(H), f(CH), f(F256)


def s5_consts():
    s8 = (np.arange(128) // 16)
    ex = np.zeros((128, 4, 8), np.float32)
    sv = np.arange(8, dtype=np.float32)
    ex[0:64, 0] = -sv
    ex[64:128, 0] = sv
    ex[0:64, 1] = 7 - sv
    ex[64:128, 1] = sv
    ex[0:64, 2] = sv
    ex[64:128, 2] = -sv
    ex[0:64, 3] = sv + 1
    ex[64:128, 3] = 8 - sv
    mk = np.zeros((128, 2, 128), np.float32)
    mk[:, 0, :] = (s8[:, None] <= s8[None, :])
    mk[:, 1, :] = (s8[:, None] >= s8[None, :])
    io = np.ascontiguousarray(np.broadcast_to(np.arange(544, dtype=np.float32), (128, 544)))
    return ex, mk, io


import os
_SKIP = set(os.environ.get("MK_SKIP", "").split(","))


def build_program(n_layers=DEPTH):
    nc = bass.Bass("TRN2", target_bir_lowering=False)
    dt_in = lambda name, shape: nc.dram_tensor(name, list(shape), F32, kind="ExternalInput").ap()
    xr = dt_in("xr", [T, D])
    cT = dt_in("cT", [128, 16])
    w_ada = dt_in("w_ada", [DEPTH, D, 3 * D])
    b_ada = dt_in("b_ada", [DEPTH, 3 * D])
    pre_g = dt_in("pre_g", [DEPTH, D])
    post_g = dt_in("post_g", [DEPTH, D])
    w_in_even = dt_in("w_in_even", [2, D, 3584])
    w_out_even = dt_in("w_out_even", [2, D, D])
    ebias = dt_in("ebias", [2, 12, 128, 21 * 128])
    cF1 = dt_in("cF1", [64, 128])
    cH = dt_in("cH", [64, 64, 2, 128])
    cCH = dt_in("cCH", [128, 6, 128])
    cF256 = dt_in("cF256", [128, 2, 512])
    w_in_odd = dt_in("w_in_odd", [2, D, 2560])
    w_out_odd = dt_in("w_out_odd", [2, D, D])
    sgu_wT = dt_in("sgu_wT", [2, 128, 4, 128])
    sgu_gT = dt_in("sgu_gT", [2, 128, 4])
    sgu_b = dt_in("sgu_b", [2, 512])
    s5_lam1 = dt_in("s5_lam1", [2, 2, 128, 32])
    s5_ls = dt_in("s5_ls", [2, 2, 32])
    s5_b1 = dt_in("s5_b1", [2, 2, 128, 32, 16])
    s5_c1 = dt_in("s5_c1", [2, 2, 128, 32, 16])
    s5_drep = dt_in("s5_drep", [2, 128, 32])
    glu_w = dt_in("glu_w", [2, 512, 1024])
    glu_bT = dt_in("glu_bT", [2, 128, 8])
    cEXPS = dt_in("cEXPS", [128, 4, 8])
    cMASK = dt_in("cMASK", [128, 2, 128])
    cIOTA = dt_in("cIOTA", [128, 544])
    zs = nc.dram_tensor("zs", [8, 32, 16, 544], BF16).ap()
    cHb = nc.dram_tensor("cHb", [64, 64, 2, 128], BF16).ap()
    out = nc.dram_tensor("out", [TL, D], F32, kind="ExternalOutput").ap()
    xs = nc.dram_tensor("xs", [T, D], F32).ap()
    modscr = nc.dram_tensor("modscr", [DEPTH, 2, 3 * D], F32).ap()

    df = DF(nc)
    A = df.add
    _uid = [0]

    def uniq(name):
        _uid[0] += 1
        return "%s_%d" % (name, _uid[0])

    def dma(eng, o, i, r=(), w=()):
        if eng == "pool":
            A(eng, lambda e, o=o, i=i: e.dma_start(out=o, in_=i, max_dma_last_dim=2048), r, w, dma=True)
        else:
            A(eng, lambda e, o=o, i=i: e.dma_start(out=o, in_=i), r, w, dma=True)

    def xbkeys(n0, nn):
        return [("xb", t) for t in range(n0 // 128, (n0 + nn + 127) // 128)]

    NTILES = [(n * 512, 512) for n in range(8)] + [(4096, 256)]

    with contextlib.ExitStack() as gst:
        sbg = lambda name, shape, dt: gst.enter_context(nc.sbuf_tensor(uniq(name), shape, dt))
        psg = lambda name, shape, dt: gst.enter_context(nc.psum_tensor(name, shape, dt))
        xb = sbg("xb", [128, 8, T], BF16)
        mT = sbg("mT", [128, 8, T], BF16)
        idn = sbg("idn", [128, 128], BF16)
        idn32 = sbg("idn32", [128, 128], F32)
        bart = sbg("bart", [128, 2], F32)
        mm = [psg("mm%d" % i, [128, 512], F32) for i in range(2)]
        stp = [psg("stp%d" % i, [128, 1024], F32) for i in range(2)]
        pv = psg("pv", [128, 512], F32)
        trp = psg("trp", [128, 8, 128], BF16)
        mmi = [0]

        def next_mm():
            mmi[0] ^= 1
            return mm[mmi[0]], "mm%d" % mmi[0]

        A("pool", lambda e: e.memset(idn32[:], 1.0), (), ["idn32"])
        A("pool", lambda e: e.affine_select(out=idn32[:], in_=idn32[:], pattern=[[-1, 128]], compare_op=ALU.is_equal,
                                            fill=0.0, base=0, channel_multiplier=1), ["idn32"], ["idn32"])
        A("dve", lambda e: e.tensor_copy(out=idn[:], in_=idn32[:]), ["idn32"], ["idn"])

        with contextlib.ExitStack() as st:
            sb = lambda name, shape, dt: st.enter_context(nc.sbuf_tensor(uniq(name), shape, dt))
            c32 = sb("c32", [128, 16], F32)
            sc = sb("sc", [128, 16], F32)
            LC = sb("LC", [128, 8, 64], BF16)
            wa = [sb("wa%d" % i, [128, 8, 512], BF16) for i in range(2)]
            bada = sb("bada", [64, 3 * D], F32)
            modrow = sb("modrow", [64, 3 * D], F32)
            for g8_ in range(8):
                dma("pool", cHb[:, g8_ * 8:(g8_ + 1) * 8, :, :], cH[:, g8_ * 8:(g8_ + 1) * 8, :, :], (), [("cHb", g8_)])
            dma("sp", c32[:], cT, (), ["c32"])
            A("act", lambda e: e.activation(out=sc[:], in_=c32[:], func=AF.Silu), ["c32"], ["sc"])
            A("pool", lambda e: e.memset(LC[:], 0.0), (), ["LC"])
            A("dve", lambda e: e.tensor_copy(out=LC[:, :, 0:1], in_=sc[:, 0:8].rearrange("p (k o) -> p k o", o=1)), ["sc", "LC"], ["LC"])
            A("dve", lambda e: e.tensor_copy(out=LC[:, :, 32:33], in_=sc[:, 8:16].rearrange("p (k o) -> p k o", o=1)), ["sc", "LC"], ["LC"])
            wi = 0
            for L in range(n_layers):
                dma("sp", bada[:], b_ada[L:L + 1, :].partition_broadcast(64), (), ["bada"])
                for n in range(6):
                    wt = wa[wi % 2]
                    wk = "wa%d" % (wi % 2)
                    wi += 1
                    dma("pool", wt[:], w_ada[L].rearrange("(k p) n -> p k n", p=128)[:, :, n * 512:(n + 1) * 512], (), [wk])
                    bank, bk = next_mm()
                    for k in range(8):
                        A("pe", lambda e, bank=bank, wt=wt, k=k: e.matmul(bank[0:64, :], lhsT=LC[:, k, :], rhs=wt[:, k, :], start=(k == 0), stop=(k == 7)),
                          ["LC", wk], [bk])
                    A("dve", lambda e, bank=bank, n=n: e.tensor_tensor(out=modrow[:, n * 512:(n + 1) * 512], in0=bank[0:64, :], in1=bada[:, n * 512:(n + 1) * 512], op=ALU.add),
                      [bk, "bada"], ["modrow"])
                dma("sp", modscr[L, 0:1, :], modrow[0:1, :], ["modrow"], [("modscr", L)])
                dma("sp", modscr[L, 1:2, :], modrow[32:33, :], ["modrow"], [("modscr", L)])
        df.barrier(bart[:, 0:1])

        def rstd_from_ss(ssum, rs, keys_in, key_out, scale):
            A("dve", lambda e: e.tensor_scalar(out=rs, in0=ssum, scalar1=scale, scalar2=EPS, op0=ALU.mult, op1=ALU.add), keys_in, [key_out])
            A("act", lambda e: e.activation(out=rs, in_=rs, func=AF.Sqrt), [key_out], [key_out])
            A("dve", lambda e: e.reciprocal(out=rs, in_=rs), [key_out], [key_out])

        def prenorm_old(L):
            src = xr if L == 0 else xs
            with contextlib.ExitStack() as st:
                sb = lambda name, shape, dt: st.enter_context(nc.sbuf_tensor(uniq(name), shape, dt))
                xt = [sb("xt%d" % i, [128, D], F32) for i in range(3)]
                tmp = [sb("ptmp%d" % i, [128, D], F32) for i in range(2)]
                hl = [sb("hl%d" % i, [128, D], BF16) for i in range(2)]
                junk = sb("junk", [128, D], BF16)
                gsc = [sb("gsc%d" % i, [128, D], F32) for i in range(2)]
                shb = [sb("shb%d" % i, [128, D], F32) for i in range(2)]
                pgb = sb("pgb", [128, D], F32)
                stat = sb("pstat", [128, 4 * NT], F32)
                dma("sp", pgb[:], pre_g[L:L + 1, :].partition_broadcast(128), (), ["pgb"])
                for w in range(2):
                    dma("sp", gsc[w][:], modscr[L, w:w + 1, D:2 * D].partition_broadcast(128), [("modscr", L)], ["gsc%d" % w])
                    dma("sp", shb[w][:], modscr[L, w:w + 1, 0:D].partition_broadcast(128), [("modscr", L)], ["shb%d" % w])
                    A("dve", lambda e, w=w: e.scalar_tensor_tensor(out=gsc[w][:], in0=gsc[w][:], scalar=1.0, in1=pgb[:], op0=ALU.add, op1=ALU.mult),
                      ["gsc%d" % w, "pgb"], ["gsc%d" % w])
                for t in range(NT):
                    w = 0 if t < 32 else 1
                    x_ = xt[t % 3]
                    xk = "xt%d" % (t % 3)
                    tm = tmp[t % 2]
                    tk = "ptmp%d" % (t % 2)
                    h_ = hl[t % 2]
                    hk = "hl%d" % (t % 2)
                    ss = stat[:, 4 * t:4 * t + 1]
                    rs = stat[:, 4 * t + 1:4 * t + 2]
                    dma("sp", x_[:], src[t * 128:(t + 1) * 128, :], [("xres", t)], [xk])
                    A("act", lambda e, x_=x_, ss=ss: e.activation(out=junk[:], in_=x_[:], func=AF.Square, accum_out=ss), [xk], ["junk", ("pss", t)])
                    rstd_from_ss(ss, rs, [("pss", t)], ("prs", t), 1.0 / D)
                    A("dve", lambda e, x_=x_, rs=rs, tm=tm, w=w: e.scalar_tensor_tensor(out=tm[:], in0=x_[:], scalar=rs, in1=gsc[w][:], op0=ALU.mult, op1=ALU.mult),
                      [xk, ("prs", t), "gsc%d" % w], [tk])
                    A("pool", lambda e, tm=tm, h_=h_, w=w: e.tensor_tensor(out=h_[:], in0=tm[:], in1=shb[w][:], op=ALU.add), [tk, "shb%d" % w], [hk])
                    for k in range(8):
                        A("pe", lambda e, h_=h_, k=k: e.transpose(trp[:, k, :], h_[:, k * 128:(k + 1) * 128], idn[:]), [hk, "idn"], ["trp"])
                    if t % 2 == 0:
                        A("act", lambda e, t=t: e.copy(out=xb[:, :, t * 128:(t + 1) * 128], in_=trp[:]), ["trp"], [("xb", t)])
                    else:
                        A("dve", lambda e, t=t: e.tensor_copy(out=xb[:, :, t * 128:(t + 1) * 128], in_=trp[:]), ["trp"], [("xb", t)])
            df.barrier(bart[:, 0:1])

        def post_old(L, last):
            src = xr if L == 0 else xs
            j = L // 2
            wo_d = w_out_even[j] if L % 2 == 0 else w_out_odd[j]
            ntile = 32 if last else NT
            with contextlib.ExitStack() as st:
                sb = lambda name, shape, dt: st.enter_context(nc.sbuf_tensor(uniq(name), shape, dt))
                wo = sb("wo", [128, 8, D], BF16)
                xt = [sb("qxt%d" % i, [128, D], F32) for i in range(2)]
                t1 = [sb("qt1%d" % i, [128, D], F32) for i in range(2)]
                t2 = [sb("qt2%d" % i, [128, D], F32) for i in range(2)]
                junk = sb("qjunk", [128, 512], BF16)
                gp = [sb("gp%d" % i, [128, D], F32) for i in range(2)]
                pgb = sb("qpgb", [128, D], F32)
                stat = sb("qstat", [128, 4 * NT], F32)
                for h in range(2):
                    dma("pool", wo[:, :, h * 512:(h + 1) * 512], wo_d.rearrange("(k p) n -> p k n", p=128)[:, :, h * 512:(h + 1) * 512], (), ["wo"])
                dma("sp", pgb[:], post_g[L:L + 1, :].partition_broadcast(128), (), ["qpgb"])
                for w in range(2):
                    dma("sp", gp[w][:], modscr[L, w:w + 1, 2 * D:3 * D].partition_broadcast(128), [("modscr", L)], ["gp%d" % w])
                    A("dve", lambda e, w=w: e.tensor_tensor(out=gp[w][:], in0=gp[w][:], in1=pgb[:], op=ALU.mult), ["gp%d" % w, "qpgb"], ["gp%d" % w])
                for t in range(ntile):
                    w = 0 if t < 32 else 1
                    yps = stp[t % 2]
                    yk = "stp%d" % (t % 2)
                    x_ = xt[t % 2]
                    xk = "qxt%d" % (t % 2)
                    a_ = t1[t % 2]
                    ak = "qt1%d" % (t % 2)
                    b_ = t2[t % 2]
                    bk = "qt2%d" % (t % 2)
                    for h in range(2):
                        for k in range(8):
                            A("pe", lambda e, yps=yps, h=h, k=k, t=t: e.matmul(yps[:, h * 512:(h + 1) * 512], lhsT=mT[:, k, t * 128:(t + 1) * 128], rhs=wo[:, k, h * 512:(h + 1) * 512], start=(k == 0), stop=(k == 7)),
                              [("mT", k, t), "wo"], [yk])
                    dma("sp", x_[:], src[t * 128:(t + 1) * 128, :], [("xres", t)], [xk])
                    for h in range(2):
                        A("act", lambda e, yps=yps, h=h, t=t: e.activation(out=junk[:], in_=yps[:, h * 512:(h + 1) * 512], func=AF.Square, accum_out=stat[:, 4 * t + h:4 * t + h + 1]),
                          [yk], ["qjunk", ("qss", t, h)])
                    A("dve", lambda e, t=t: e.tensor_tensor(out=stat[:, 4 * t + 2:4 * t + 3], in0=stat[:, 4 * t:4 * t + 1], in1=stat[:, 4 * t + 1:4 * t + 2], op=ALU.add),
                      [("qss", t, 0), ("qss", t, 1)], [("qs2", t)])
                    rs = stat[:, 4 * t + 3:4 * t + 4]
                    rstd_from_ss(stat[:, 4 * t + 2:4 * t + 3], rs, [("qs2", t)], ("qrs", t), 1.0 / D)
                    A("dve", lambda e, yps=yps, a_=a_, w=w: e.tensor_tensor(out=a_[:], in0=yps[:], in1=gp[w][:], op=ALU.mult), [yk, "gp%d" % w], [ak])
                    A("act", lambda e, a_=a_, b_=b_, rs=rs: e.activation(out=b_[:], in_=a_[:], func=AF.Copy, scale=rs), [ak, ("qrs", t)], [bk])
                    A("pool", lambda e, b_=b_, x_=x_: e.tensor_tensor(out=b_[:], in0=b_[:], in1=x_[:], op=ALU.add), [bk, xk], [bk])
                    dst = out[t * 128:(t + 1) * 128, :] if (last and t < 32) else xs[t * 128:(t + 1) * 128, :]
                    dma("sp", dst, b_[:], [bk], [("xres", t)])
            df.barrier(bart[:, 0:1])

        def prenorm(L):
            src = xr if L == 0 else xs
            with contextlib.ExitStack() as st:
                sb = lambda name, shape, dt: st.enter_context(nc.sbuf_tensor(uniq(name), shape, dt))
                NX = 6
                xt = [sb("xt%d" % i, [128, D], F32) for i in range(NX)]
                tmp = [sb("ptmp%d" % i, [128, D], F32) for i in range(2)]
                hl = [sb("hl%d" % i, [128, D], BF16) for i in range(2)]
                junk = sb("junk", [128, D], BF16)
                gsc = [sb("gsc%d" % i, [128, D], F32) for i in range(2)]
                shb = [sb("shb%d" % i, [128, D], F32) for i in range(2)]
                pgb = sb("pgb", [128, D], F32)
                stat = sb("pstat", [128, 4 * NT], F32)
                dma("sp", pgb[:], pre_g[L:L + 1, :].partition_broadcast(128), (), ["pgb"])
                for w in range(2):
                    dma("sp", gsc[w][:], modscr[L, w:w + 1, D:2 * D].partition_broadcast(128), [("modscr", L)], ["gsc%d" % w])
                    dma("sp", shb[w][:], modscr[L, w:w + 1, 0:D].partition_broadcast(128), [("modscr", L)], ["shb%d" % w])
                    A("dve", lambda e, w=w: e.scalar_tensor_tensor(out=gsc[w][:], in0=gsc[w][:], scalar=1.0, in1=pgb[:], op0=ALU.add, op1=ALU.mult),
                      ["gsc%d" % w, "pgb"], ["gsc%d" % w])
                X = lambda t: (xt[t % NX], "xt%d" % (t % NX))
                SS = lambda t: stat[:, 4 * t:4 * t + 1]
                RS = lambda t: stat[:, 4 * t + 1:4 * t + 2]

                def p_load(t):
                    x_, xk = X(t)
                    dma("sp", x_[:], src[t * 128:(t + 1) * 128, :], [("xres", t)], [xk])

                def p_sq(t):
                    x_, xk = X(t)
                    A("act", lambda e, x_=x_, ss=SS(t): e.activation(out=junk[:], in_=x_[:], func=AF.Square, accum_out=ss), [xk], ["junk", ("pss", t)])

                def p_r1(t):
                    A("dve", lambda e, t=t: e.tensor_scalar(out=RS(t), in0=SS(t), scalar1=1.0 / D, scalar2=EPS, op0=ALU.mult, op1=ALU.add), [("pss", t)], [("prs", t)])

                def p_r2(t):
                    A("act", lambda e, t=t: e.activation(out=RS(t), in_=RS(t), func=AF.Sqrt), [("prs", t)], [("prs", t)])

                def p_r3(t):
                    A("dve", lambda e, t=t: e.reciprocal(out=RS(t), in_=RS(t)), [("prs", t)], [("prs", t)])

                def p_stt(t):
                    w = 0 if t < 32 else 1
                    x_, xk = X(t)
                    tm, tk = tmp[t % 2], "ptmp%d" % (t % 2)
                    A("dve", lambda e, x_=x_, t=t, tm=tm, w=w: e.scalar_tensor_tensor(out=tm[:], in0=x_[:], scalar=RS(t), in1=gsc[w][:], op0=ALU.mult, op1=ALU.mult),
                      [xk, ("prs", t), "gsc%d" % w], [tk])

                def p_add(t):
                    w = 0 if t < 32 else 1
                    tm, tk = tmp[t % 2], "ptmp%d" % (t % 2)
                    h_, hk = hl[t % 2], "hl%d" % (t % 2)
                    A("pool", lambda e, tm=tm, h_=h_, w=w: e.tensor_tensor(out=h_[:], in0=tm[:], in1=shb[w][:], op=ALU.add), [tk, "shb%d" % w], [hk])

                def p_tr(t):
                    h_, hk = hl[t % 2], "hl%d" % (t % 2)
                    for k in range(8):
                        A("pe", lambda e, h_=h_, k=k: e.transpose(trp[:, k, :], h_[:, k * 128:(k + 1) * 128], idn[:]), [hk, "idn"], ["trp"])

                def p_ev(t):
                    if t % 2 == 0:
                        A("act", lambda e, t=t: e.copy(out=xb[:, :, t * 128:(t + 1) * 128], in_=trp[:]), ["trp"], [("xb", t)])
                    else:
                        A("dve", lambda e, t=t: e.tensor_copy(out=xb[:, :, t * 128:(t + 1) * 128], in_=trp[:]), ["trp"], [("xb", t)])

                pipeline([p_load, p_sq, p_r1, p_r2, p_r3, p_stt, p_add, p_tr, p_ev], NT, "pre")
            df.barrier(bart[:, 0:1])

        def post(L, last):
            src = xr if L == 0 else xs
            j = L // 2
            wo_d = w_out_even[j] if L % 2 == 0 else w_out_odd[j]
            ntile = 32 if last else NT
            with contextlib.ExitStack() as st:
                sb = lambda name, shape, dt: st.enter_context(nc.sbuf_tensor(uniq(name), shape, dt))
                wo = sb("wo", [128, 8, D], BF16)
                NA_, NB_, NXq = 4, 3, 3
                xt = [sb("qxt%d" % i, [128, D], F32) for i in range(NXq)]
                t1 = [sb("qt1%d" % i, [128, D], F32) for i in range(NA_)]
                t2 = [sb("qt2%d" % i, [128, D], F32) for i in range(NB_)]
                junk = sb("qjunk", [128, D], BF16)
                gp = [sb("gp%d" % i, [128, D], F32) for i in range(2)]
                pgb = sb("qpgb", [128, D], F32)
                stat = sb("qstat", [128, 4 * NT], F32)
                for h in range(2):
                    dma("pool", wo[:, :, h * 512:(h + 1) * 512], wo_d.rearrange("(k p) n -> p k n", p=128)[:, :, h * 512:(h + 1) * 512], (), ["wo"])
                dma("sp", pgb[:], post_g[L:L + 1, :].partition_broadcast(128), (), ["qpgb"])
                for w in range(2):
                    dma("sp", gp[w][:], modscr[L, w:w + 1, 2 * D:3 * D].partition_broadcast(128), [("modscr", L)], ["gp%d" % w])
                    A("dve", lambda e, w=w: e.tensor_tensor(out=gp[w][:], in0=gp[w][:], in1=pgb[:], op=ALU.mult), ["gp%d" % w, "qpgb"], ["gp%d" % w])
                YP = lambda t: (stp[t % 2], "stp%d" % (t % 2))
                XQ = lambda t: (xt[t % NXq], "qxt%d" % (t % NXq))
                TA = lambda t: (t1[t % NA_], "qt1%d" % (t % NA_))
                TB = lambda t: (t2[t % NB_], "qt2%d" % (t % NB_))
                SS = lambda t: stat[:, 4 * t:4 * t + 1]
                RS = lambda t: stat[:, 4 * t + 1:4 * t + 2]

                def q_mm(t):
                    yps, yk = YP(t)
                    for h in range(2):
                        for k in range(8):
                            A("pe", lambda e, yps=yps, h=h, k=k, t=t: e.matmul(yps[:, h * 512:(h + 1) * 512], lhsT=mT[:, k, t * 128:(t + 1) * 128], rhs=wo[:, k, h * 512:(h + 1) * 512], start=(k == 0), stop=(k == 7)),
                              [("mT", k, t), "wo"], [yk])

                def q_sq(t):
                    yps, yk = YP(t)
                    a_, ak = TA(t)
                    w = 0 if t < 32 else 1
                    for h in range(2):
                        A("act", lambda e, yps=yps, t=t, h=h: e.activation(out=junk[:, h * 512:(h + 1) * 512], in_=yps[:, h * 512:(h + 1) * 512], func=AF.Square, accum_out=stat[:, 4 * t + 2 + h:4 * t + 3 + h]), [yk], ["qjunk", ("qssh", t, h)])
                    A("dve", lambda e, yps=yps, a_=a_, w=w: e.tensor_tensor(out=a_[:], in0=yps[:], in1=gp[w][:], op=ALU.mult), [yk, "gp%d" % w, ("qssh", t, 0), ("qssh", t, 1)], [ak])

                def q_r1(t):
                    A("dve", lambda e, t=t: e.tensor_tensor(out=SS(t), in0=stat[:, 4 * t + 2:4 * t + 3], in1=stat[:, 4 * t + 3:4 * t + 4], op=ALU.add), [("qssh", t, 0), ("qssh", t, 1)], [("qss", t)])
                    A("dve", lambda e, t=t: e.tensor_scalar(out=RS(t), in0=SS(t), scalar1=1.0 / D, scalar2=EPS, op0=ALU.mult, op1=ALU.add), [("qss", t)], [("qrs", t)])

                def q_r2(t):
                    A("act", lambda e, t=t: e.activation(out=RS(t), in_=RS(t), func=AF.Sqrt), [("qrs", t)], [("qrs", t)])
                    x_, xk = XQ(t)
                    dma("sp", x_[:], src[t * 128:(t + 1) * 128, :], [("xres", t)], [xk])

                def q_r3(t):
                    A("dve", lambda e, t=t: e.reciprocal(out=RS(t), in_=RS(t)), [("qrs", t)], [("qrs", t)])

                def q_sc(t):
                    a_, ak = TA(t)
                    b_, bk = TB(t)
                    A("act", lambda e, a_=a_, b_=b_, t=t: e.activation(out=b_[:], in_=a_[:], func=AF.Copy, scale=RS(t)), [ak, ("qrs", t)], [bk])

                def q_add(t):
                    b_, bk = TB(t)
                    x_, xk = XQ(t)
                    A("pool", lambda e, b_=b_, x_=x_: e.tensor_tensor(out=b_[:], in0=b_[:], in1=x_[:], op=ALU.add), [bk, xk], [bk])

                def q_st(t):
                    b_, bk = TB(t)
                    dst = out[t * 128:(t + 1) * 128, :] if (last and t < 32) else xs[t * 128:(t + 1) * 128, :]
                    dma("sp", dst, b_[:], [bk], [("xres", t)])

                pipeline([q_mm, q_sq, q_r1, q_r2, q_r3, q_sc, q_add, q_st], ntile, "post")
            df.barrier(bart[:, 0:1])

        def pipeline(stages, N, key=""):
            if "noskew" in _SKIP or ("noskew_" + key) in _SKIP:
                for n_ in range(N):
                    for st_ in stages:
                        st_(n_)
                return
            K_ = len(stages)
            for step in range(N + K_ - 1):
                for k_ in reversed(range(K_)):
                    n_ = step - k_
                    if 0 <= n_ < N:
                        stages[k_](n_)

        def inproj_fm(wt, wk, tiles, evac):
            for (n0, nn) in tiles:
                bank, bk = next_mm()
                for k in range(8):
                    A("pe", lambda e, bank=bank, k=k, n0=n0, nn=nn: e.matmul(bank[:, 0:nn], lhsT=wt[:, k, :], rhs=xb[:, k, n0:n0 + nn], start=(k == 0), stop=(k == 7)),
                      [wk] + xbkeys(n0, nn), [bk])
                evac(bank, bk, n0, nn)

        def even_mixer(L):
            j = L // 2
            Wd = w_in_even[j].rearrange("(k p) n -> p k n", p=128)
            with contextlib.ExitStack() as st:
                sb = lambda name, shape, dt: st.enter_context(nc.sbuf_tensor(uniq(name), shape, dt))
                wch = [sb("wch%d" % i, [128, 8, 128], BF16) for i in range(3)]
                wci = [0]

                def load_w(c0, ncols=128):
                    i = wci[0] % 3
                    wci[0] += 1
                    dma("pool", wch[i][:, :, 0:ncols], Wd[:, :, c0:c0 + ncols], (), ["wch%d" % i])
                    return wch[i], "wch%d" % i

                with contextlib.ExitStack() as st2:
                    sb2 = lambda name, shape, dt: st2.enter_context(nc.sbuf_tensor(uniq(name), shape, dt))
                    sga = sb2("sga", [128, T], BF16)
                    X = sb2("fX", [64, 64, 128], BF16)
                    Z = sb2("fZ", [64, 64, 128], BF16)
                    Pg = [sb2("fP%d" % i, [128, 8, 128], BF16) for i in range(2)]
                    Hs = [sb2("fH%d" % i, [64, 8, 2, 128], BF16) for i in range(2)]
                    F1 = sb2("fF1", [64, 128], BF16)
                    CH = sb2("fCH", [128, 6, 128], BF16)
                    F256 = sb2("fF256", [128, 2, 512], BF16)
                    Xc = sb2("fXc", [128, 2, 128], BF16)
                    Pc = sb2("fPc", [128, 512], BF16)
                    dma("pool", F1[:], cF1, (), ["fF1"])
                    dma("pool", CH[:], cCH, (), ["fCH"])
                    dma("pool", F256[:], cF256, (), ["fF256"])
                    hcount = 0
                    pcount = 0
                    for half in range(2):
                        wt, wk = load_w(256 + half * 128)
                        inproj_fm(wt, wk, NTILES, lambda bank, bk, n0, nn: A(
                            "act", lambda e: e.activation(out=sga[:, n0:n0 + nn], in_=bank[:, 0:nn], func=AF.Silu), [bk], [("sga", n0)]))
                        sgakeys = [("sga", n0) for (n0, nn) in NTILES]
                        wt, wk = load_w(half * 128)
                        for g4 in range(16):
                            bank, bk = next_mm()
                            for q in range(4):
                                t2 = g4 * 4 + q
                                for k in range(8):
                                    A("pe", lambda e, bank=bank, q=q, k=k, t2=t2, wt=wt: e.matmul(bank[0:64, q * 128:(q + 1) * 128], lhsT=xb[:, k, t2:TL:64], rhs=wt[:, k, :], start=(k == 0), stop=(k == 7)),
                                      [wk] + [("xb", t) for t in range(32)], [bk])
                            A("act", lambda e, bank=bank, g4=g4: e.copy(out=X[:, g4 * 4:(g4 + 1) * 4, :], in_=bank[0:64, :].rearrange("p (q c) -> p q c", q=4)), [bk], ["fX"])
                        for tl in range(2):
                            bank, bk = next_mm()
                            for k in range(8):
                                A("pe", lambda e, bank=bank, k=k, tl=tl, wt=wt: e.matmul(bank[:, 0:128], lhsT=xb[:, k, TL + tl * 128:TL + (tl + 1) * 128], rhs=wt[:, k, :], start=(k == 0), stop=(k == 7)),
                                  [wk, ("xb", 32 + tl)], [bk])
                            A("dve", lambda e, bank=bank, tl=tl: e.tensor_copy(out=Xc[:, tl, :], in_=bank[:, 0:128]), [bk], ["fXc"])
                        bank, bk = next_mm()
                        for tl in range(2):
                            A("pe", lambda e, bank=bank, tl=tl: e.matmul(bank[:, :], lhsT=Xc[:, tl, :], rhs=F256[:, tl, :], start=(tl == 0), stop=(tl == 1)), ["fXc", "fF256"], [bk])
                        A("dve", lambda e, bank=bank: e.tensor_copy(out=Pc[:], in_=bank[:, :]), [bk], ["fPc"])
                        bank, bk = next_mm()
                        A("pe", lambda e, bank=bank: e.matmul(bank[:, 0:256], lhsT=CH[:, 4, :], rhs=Pc[:, 0:256], start=True, stop=False), ["fPc", "fCH"], [bk])
                        A("pe", lambda e, bank=bank: e.matmul(bank[:, 0:256], lhsT=CH[:, 5, :], rhs=Pc[:, 256:512], start=False, stop=True), ["fPc", "fCH"], [bk])
                        A("dve", lambda e, bank=bank, half=half: e.tensor_tensor(out=mT[:, half, TL:T], in0=bank[:, 0:256], in1=sga[:, TL:T], op=ALU.mult),
                          [bk] + sgakeys, [("mT", half, 32), ("mT", half, 33)])
                        for qd in range(2):
                            pb = qd * 64
                            for c4 in range(16):
                                bank, bk = next_mm()
                                for q in range(4):
                                    c = qd * 64 + c4 * 4 + q
                                    A("pe", lambda e, bank=bank, q=q, c=c: e.matmul(bank[0:64, q * 128:(q + 1) * 128], lhsT=X[:, :, c], rhs=F1[:, :], start=True, stop=True),
                                      ["fX", "fF1"], [bk])
                                if c4 % 2 == 0:
                                    A("act", lambda e, bank=bank, c4=c4: e.copy(out=Z[:, c4 * 4:(c4 + 1) * 4, :], in_=bank[0:64, :].rearrange("p (q c) -> p q c", q=4)), [bk], ["fZ"])
                                else:
                                    A("dve", lambda e, bank=bank, c4=c4: e.tensor_copy(out=Z[:, c4 * 4:(c4 + 1) * 4, :], in_=bank[0:64, :].rearrange("p (q c) -> p q c", q=4)), [bk], ["fZ"])
                            for g8 in range(8):
                                Hb = Hs[hcount % 2]
                                hk = "fH%d" % (hcount % 2)
                                hcount += 1
                                dma("sp", Hb[:], cHb[:, g8 * 8:(g8 + 1) * 8, :, :], (), [hk])
                                Pb = Pg[pcount % 2]
                                pk = "fP%d" % (pcount % 2)
                                pcount += 1
                                for b2 in range(2):
                                    bank, bk = next_mm()
                                    for q in range(4):
                                        kk = b2 * 4 + q
                                        k1 = g8 * 8 + kk
                                        for ri in range(2):
                                            A("pe", lambda e, bank=bank, q=q, kk=kk, k1=k1, ri=ri, Hb=Hb: e.matmul(bank[0:64, q * 128:(q + 1) * 128], lhsT=Z[:, :, ri * 64 + k1], rhs=Hb[:, kk, ri, :], start=(ri == 0), stop=(ri == 1)),
                                              ["fZ", hk], [bk])
                                    A("act" if b2 == 0 else "dve",
                                      (lambda e, bank=bank, b2=b2, Pb=Pb: e.copy(out=Pb[0:64, b2 * 4:(b2 + 1) * 4, :], in_=bank[0:64, :].rearrange("p (q c) -> p q c", q=4))) if b2 == 0 else
                                      (lambda e, bank=bank, b2=b2, Pb=Pb: e.tensor_copy(out=Pb[0:64, b2 * 4:(b2 + 1) * 4, :], in_=bank[0:64, :].rearrange("p (q c) -> p q c", q=4))),
                                      [bk], [pk])
                                bank, bk = next_mm()
                                ia, ib = (0, 1) if qd == 0 else (2, 3)
                                mcols = 64 if qd == 0 else 128
                                A("pe", lambda e, bank=bank, Pb=Pb, ia=ia, mcols=mcols: e.matmul(bank[0:mcols, :], lhsT=CH[0:64, ia, 0:mcols], rhs=Pb[0:64, :, 0:64], start=True, stop=False), [pk, "fCH"], [bk])
                                A("pe", lambda e, bank=bank, Pb=Pb, ib=ib, mcols=mcols: e.matmul(bank[0:mcols, :], lhsT=CH[0:64, ib, 0:mcols], rhs=Pb[0:64, :, 64:128], start=False, stop=True), [pk, "fCH"], [bk])
                                A("dve", lambda e, bank=bank, pb=pb, half=half, g8=g8: e.tensor_tensor(
                                    out=mT[pb:pb + 64, half, 0:TL].rearrange("p (k2 k1) -> p k1 k2", k1=64)[:, g8 * 8:(g8 + 1) * 8, :],
                                    in0=bank[pb:pb + 64, :].rearrange("p (a b) -> p a b", a=8),
                                    in1=sga[pb:pb + 64, 0:TL].rearrange("p (k2 k1) -> p k1 k2", k1=64)[:, g8 * 8:(g8 + 1) * 8, :], op=ALU.mult),
                                  [bk] + sgakeys, [("mT", half, t) for t in range(32)])
                df.barrier(bart[:, 0:1])

                with contextlib.ExitStack() as st2:
                    sb2 = lambda name, shape, dt: st2.enter_context(nc.sbuf_tensor(uniq(name), shape, dt))
                    qT = sb2("qT", [128, T], BF16)
                    kT = sb2("kT", [128, T], BF16)
                    sgb = sb2("sgb", [128, T], BF16)
                    vaug = sb2("vaug", [128, NT, 130], BF16)
                    eb32 = sb2("eb32", [128, 7 * 128], F32)
                    Eh = [sb2("Eh%d" % i, [128, 21 * 128], BF16) for i in range(2)]
                    PT = [sb2("PT%d" % i, [128, 7 * 128], BF16) for i in range(3)]
                    onb = sb2("onb", [128, NT, 128], BF16)
                    rden = sb2("rden", [128, 64], F32)
                    A("pool", lambda e: e.memset(vaug[:], 1.0), (), [("vaug", t4) for t4 in range(9)])
                    pti = 0
                    rdi = 0
                    for hp in range(6):
                        wt, wk = load_w(512 + hp * 128)
                        inproj_fm(wt, wk, NTILES, lambda bank, bk, n0, nn: A(
                            "act", lambda e: e.copy(out=qT[:, n0:n0 + nn], in_=bank[:, 0:nn]), [bk], [("qT", n0)]))
                        wt, wk = load_w(1280 + hp * 128)
                        inproj_fm(wt, wk, NTILES, lambda bank, bk, n0, nn: A(
                            "dve", lambda e: e.tensor_copy(out=kT[:, n0:n0 + nn], in_=bank[:, 0:nn]), [bk], [("kT", n0)]))
                        wt, wk = load_w(2816 + hp * 128)
                        inproj_fm(wt, wk, NTILES, lambda bank, bk, n0, nn: A(
                            "act", lambda e: e.activation(out=sgb[:, n0:n0 + nn], in_=bank[:, 0:nn], func=AF.Silu), [bk], [("sgb", n0)]))
                        wt, wk = load_w(2048 + hp * 128)
                        for t4 in range(9):
                            bank, bk = next_mm()
                            nq = 4 if t4 < 8 else 2
                            for q in range(nq):
                                t = t4 * 4 + q
                                for k in range(8):
                                    A("pe", lambda e, bank=bank, q=q, k=k, t=t, wt=wt: e.matmul(bank[:, q * 128:(q + 1) * 128], lhsT=xb[:, k, t * 128:(t + 1) * 128], rhs=wt[:, k, :], start=(k == 0), stop=(k == 7)),
                                      [wk, ("xb", t)], [bk])
                            for hh in range(2):
                                A("dve" if hh == 0 else "act",
                                  (lambda e, bank=bank, t4=t4, nq=nq, hh=hh: e.tensor_copy(out=vaug[:, t4 * 4:t4 * 4 + nq, hh * 65:hh * 65 + 64], in_=bank[:, 0:nq * 128].rearrange("p (q c) -> p q c", q=nq)[:, :, hh * 64:(hh + 1) * 64])) if hh == 0 else
                                  (lambda e, bank=bank, t4=t4, nq=nq, hh=hh: e.copy(out=vaug[:, t4 * 4:t4 * 4 + nq, hh * 65:hh * 65 + 64], in_=bank[:, 0:nq * 128].rearrange("p (q c) -> p q c", q=nq)[:, :, hh * 64:(hh + 1) * 64])),
                                  [bk], [("vaug", t4)])
                        for hh in range(2):
                            h = hp * 2 + hh
                            E = Eh[hh]
                            ek = "Eh%d" % hh
                            for part in range(3):
                                dma("sp", eb32[:], ebias[j, h, :, part * 896:(part + 1) * 896], (), ["eb32"])
                                A("act", lambda e, E=E, part=part: e.activation(out=E[:, part * 896:(part + 1) * 896], in_=eb32[:], func=AF.Exp), ["eb32"], [ek])
                        its = [(hh, i) for hh in range(2) for i in range(NT)]

                        def geo(n):
                            hh, i = its[n]
                            if i < 32:
                                pbase, lt = na_pattern_base(i)
                                kts = lt + [32, 33]
                            else:
                                pbase, lt = None, []
                                kts = [32, 33]
                            return hh, i, pbase, lt, kts

                        def s_qk(n):
                            hh, i, pbase, lt, kts = geo(n)
                            hb = hh * 64
                            sp_ = stp[n % 2]
                            sk = "stp%d" % (n % 2)
                            for a_, kt in enumerate(kts):
                                A("pe", lambda e, sp_=sp_, a_=a_, kt=kt, i=i, hb=hb: e.matmul(sp_[:, a_ * 128:(a_ + 1) * 128], lhsT=kT[hb:hb + 64, kt * 128:(kt + 1) * 128], rhs=qT[hb:hb + 64, i * 128:(i + 1) * 128], start=True, stop=True),
                                  [("kT", (kt // 4) * 512), ("qT", (i // 4) * 512)], [sk])

                        def s_exp(n):
                            hh, i, pbase, lt, kts = geo(n)
                            nk = len(kts)
                            sp_ = stp[n % 2]
                            sk = "stp%d" % (n % 2)
                            P_ = PT[n % 3]
                            pk = "PT%d" % (n % 3)
                            A("act", lambda e, sp_=sp_, P_=P_, nk=nk: e.activation(out=P_[:, 0:nk * 128], in_=sp_[:, 0:nk * 128], func=AF.Exp, scale=0.125), [sk], [pk])

                        def s_mul(n):
                            hh, i, pbase, lt, kts = geo(n)
                            P_ = PT[n % 3]
                            pk = "PT%d" % (n % 3)
                            if lt:
                                nl = len(lt)
                                E = Eh[hh]
                                A("dve", lambda e, P_=P_, nl=nl, E=E, pbase=pbase: e.tensor_tensor(out=P_[:, 0:nl * 128], in0=P_[:, 0:nl * 128], in1=E[:, pbase * 128:(pbase + nl) * 128], op=ALU.mult),
                                  [pk, "Eh%d" % hh], [pk])

                        def s_pv(n):
                            hh, i, pbase, lt, kts = geo(n)
                            nk = len(kts)
                            P_ = PT[n % 3]
                            pk = "PT%d" % (n % 3)
                            pvb = mm[n % 2]
                            for a_, kt in enumerate(kts):
                                A("pe", lambda e, P_=P_, a_=a_, kt=kt, hh=hh, nk=nk, pvb=pvb: e.matmul(pvb[:, 0:65], lhsT=P_[:, a_ * 128:(a_ + 1) * 128], rhs=vaug[:, kt, hh * 65:hh * 65 + 65], start=(a_ == 0), stop=(a_ == nk - 1)),
                                  [pk, ("vaug", kt // 4)], ["mm%d" % (n % 2)])

                        def s_rec(n):
                            pvb = mm[n % 2]
                            rd = rden[:, n % 64:n % 64 + 1]
                            A("dve", lambda e, rd=rd, pvb=pvb: e.reciprocal(out=rd, in_=pvb[:, 64:65]), ["mm%d" % (n % 2)], [("rden", n % 64)])

                        def s_norm(n):
                            hh, i, pbase, lt, kts = geo(n)
                            pvb = mm[n % 2]
                            rd = rden[:, n % 64:n % 64 + 1]
                            A("dve", lambda e, i=i, hh=hh, rd=rd, pvb=pvb: e.tensor_scalar(out=onb[:, i, hh * 64:(hh + 1) * 64], in0=pvb[:, 0:64], scalar1=rd, scalar2=None, op0=ALU.mult), ["mm%d" % (n % 2), ("rden", n % 64)], [("on", i, hh)])

                        def burst(i):
                            if i % 8 == 7 or i == NT - 1:
                                i0 = (i // 8) * 8
                                return i0, i - i0 + 1
                            return None

                        def s_tr(n):
                            hh, i, pbase, lt, kts = geo(n)
                            if hh == 1 and burst(i):
                                i0, nb_ = burst(i)
                                for r_ in range(nb_):
                                    ii = i0 + r_
                                    A("pe", lambda e, ii=ii, r_=r_: e.transpose(trp[:, r_, :], onb[:, ii, :], idn[:]), [("on", ii, 0), ("on", ii, 1), "idn"], ["trp"])

                        def s_gate(n):
                            hh, i, pbase, lt, kts = geo(n)
                            if hh == 1 and burst(i):
                                i0, nb_ = burst(i)
                                A("dve", lambda e, i0=i0, nb_=nb_, hp=hp: e.tensor_tensor(out=mT[:, 2 + hp, i0 * 128:(i0 + nb_) * 128], in0=trp[:, 0:nb_, :].rearrange("p a b -> p (a b)"), in1=sgb[:, i0 * 128:(i0 + nb_) * 128], op=ALU.mult),
                                  ["trp"] + [("sgb", ((i0 + r_) // 4) * 512) for r_ in range(nb_)], [("mT", 2 + hp, i0 + r_) for r_ in range(nb_)])

                        pipeline([s_qk, s_exp, s_mul, s_pv, s_rec, s_norm, s_tr, s_gate], len(its), "att")
            df.barrier(bart[:, 0:1])

        MAG = 12582912.0
        TWO_PI = 2.0 * np.pi

        def odd_mixer(L):
            j = L // 2
            Wd = w_in_odd[j].rearrange("(k p) n -> p k n", p=128)
            Uv = mT[:, 0:4, :].rearrange("p c t -> p (c t)").rearrange("p (g b) -> p g b", b=544)
            PIECES = [(0, 256), (256, 256), (512, 32)]

            with contextlib.ExitStack() as st:
              if "s5a" not in _SKIP:
                sb = lambda name, shape, dt: st.enter_context(nc.sbuf_tensor(uniq(name), shape, dt))
                Ws = sb("Ws", [128, 8, 512], BF16)
                Stm = [sb("Stm%d" % i, [128, 32, 8, 16], BF16) for i in range(5)]
                dma("pool", Ws[:], Wd[:, :, 1536:2048], (), ["Ws"])
                its_a = [(bt, t8) for bt in range(5) for t8 in range(8)]

                def a_mm(n):
                    bt, t8 = its_a[n]
                    nb = 128 if bt < 4 else 32
                    tok0 = 1024 * bt
                    bank, bk = mm[n % 2], "mm%d" % (n % 2)
                    for k in range(8):
                        A("pe", lambda e, bank=bank, k=k, nb=nb, tok0=tok0, t8=t8: e.matmul(bank[0:nb, :], lhsT=xb[:, k, tok0 + t8:tok0 + 8 * nb:8], rhs=Ws[:, k, :], start=(k == 0), stop=(k == 7)),
                          ["Ws"] + xbkeys(tok0, 8 * nb), [bk])

                def a_ev(n):
                    bt, t8 = its_a[n]
                    nb = 128 if bt < 4 else 32
                    bank, bk = mm[n % 2], "mm%d" % (n % 2)
                    S_ = Stm[bt]
                    sk = ("Stm", bt, t8)
                    if n % 2 == 0:
                        A("act", lambda e, bank=bank, nb=nb, S_=S_, t8=t8: e.copy(out=S_[0:nb, :, t8, :], in_=bank[0:nb, :].rearrange("p (g m) -> p g m", m=16)), [bk], [sk])
                    else:
                        A("dve", lambda e, bank=bank, nb=nb, S_=S_, t8=t8: e.tensor_copy(out=S_[0:nb, :, t8, :], in_=bank[0:nb, :].rearrange("p (g m) -> p g m", m=16)), [bk], [sk])

                pipeline([a_mm, a_ev], len(its_a), "s5a")
                its_b = [(bt, g8) for bt in range(5) for g8 in range(4)]

                def a_tr(n):
                    bt, g8 = its_b[n]
                    nb = 128 if bt < 4 else 32
                    S_ = Stm[bt]
                    for q in range(8):
                        g = g8 * 8 + q
                        A("pe", lambda e, S_=S_, nb=nb, g=g, q=q: e.transpose(trp[:, q, 0:nb], S_[0:nb, g, :, :].rearrange("p a b -> p (a b)"), idn[0:nb, 0:nb]), [("Stm", bt, t8) for t8 in range(8)] + ["idn"], ["trp"])

                def a_ut(n):
                    bt, g8 = its_b[n]
                    nb = 128 if bt < 4 else 32
                    if n % 2 == 0:
                        A("act", lambda e, g8=g8, bt=bt, nb=nb: e.copy(out=Uv[:, g8 * 8:(g8 + 1) * 8, bt * 128:bt * 128 + nb], in_=trp[:, :, 0:nb]), ["trp"], ["U"])
                    else:
                        A("dve", lambda e, g8=g8, bt=bt, nb=nb: e.tensor_copy(out=Uv[:, g8 * 8:(g8 + 1) * 8, bt * 128:bt * 128 + nb], in_=trp[:, :, 0:nb]), ["trp"], ["U"])

                pipeline([a_tr, a_ut], len(its_b), "s5a2")
            df.barrier(bart[:, 0:1])

            with contextlib.ExitStack() as st:
              if "s5b" not in _SKIP:
                sb = lambda name, shape, dt: st.enter_context(nc.sbuf_tensor(uniq(name), shape, dt))
                V = lambda e: e
                lam_r = sb("lam_r", [128, 32], F32)
                lam_i = sb("lam_i", [128, 32], F32)
                ls1 = sb("ls1", [128, 32], F32)
                st_tmp = contextlib.ExitStack()
                sbt = lambda name, shape, dt: st_tmp.enter_context(nc.sbuf_tensor(uniq(name), shape, dt))
                cr1 = sb("cr1", [128, 32, 16], F32)
                ci1 = sb("ci1", [128, 32, 16], F32)
                exps = sb("exps", [128, 4, 8], F32)
                mask = sb("mask", [128, 2, 128], F32)
                iota = sb("iota", [128, 544], F32)
                drep = sb("drep", [128, 32], F32)
                sm = [sb("sm%d" % i, [128, 32], F32) for i in range(14)]
                Bbr = sb("Bbr", [128, 32, 16], F32)
                Bbi = sb("Bbi", [128, 32, 16], F32)
                Wr = sb("Wr", [128, 4, 8, 32], F32)
                Wi = sb("Wi", [128, 4, 8, 32], F32)
                rho8 = sb("rho8", [128, 32], F32)
                tt8 = sb("tt8", [128, 32], F32)
                br1 = sbt("br1", [128, 32, 16], F32)
                bi1 = sbt("bi1", [128, 32, 16], F32)
                tb1 = sbt("tb1", [128, 32, 16], F32)
                tb2 = sbt("tb2", [128, 32, 16], F32)
                EA = sbt("EA", [128, 4, 8, 32], F32)
                ET = sbt("ET", [128, 4, 8, 32], F32)
                tw = sbt("tw", [128, 4, 8, 32], F32)
                dma("sp", lam_r[:], s5_lam1[j, 0], (), ["lam_r"])
                dma("sp", lam_i[:], s5_lam1[j, 1], (), ["lam_i"])
                for d_ in range(2):
                    dma("sp", ls1[d_ * 64:(d_ + 1) * 64, :], s5_ls[j, d_:d_ + 1, :].partition_broadcast(64), (), ["ls1"])
                dma("sp", br1[:], s5_b1[j, 0], (), ["br1"])
                dma("sp", bi1[:], s5_b1[j, 1], (), ["bi1"])
                dma("sp", cr1[:], s5_c1[j, 0], (), ["cr1"])
                dma("sp", ci1[:], s5_c1[j, 1], (), ["ci1"])
                dma("sp", exps[:], cEXPS, (), ["exps"])
                dma("sp", mask[:], cMASK, (), ["mask"])
                dma("sp", iota[:], cIOTA, (), ["iota"])
                dma("sp", drep[:], s5_drep[j], (), ["drep"])
                PK = ["pre"]

                def dv(fn):
                    A("dve", fn, PK + ["lam_r", "lam_i", "ls1", "br1", "bi1", "cr1", "ci1", "exps", "mask", "iota", "drep"], PK)

                def ac(fn):
                    A("act", fn, PK, PK)

                def sincos(tt, sn, cs, tmp, tmp2):
                    dv(lambda e: e.tensor_scalar(out=tmp, in0=tt, scalar1=MAG, scalar2=MAG, op0=ALU.add, op1=ALU.subtract))
                    dv(lambda e: e.tensor_tensor(out=tmp, in0=tt, in1=tmp, op=ALU.subtract))
                    ac(lambda e: e.activation(out=sn, in_=tmp, func=AF.Sin, scale=TWO_PI))
                    dv(lambda e: e.tensor_scalar(out=tmp2, in0=tt, scalar1=0.25, scalar2=None, op0=ALU.add))
                    dv(lambda e: e.tensor_scalar(out=tmp, in0=tmp2, scalar1=MAG, scalar2=MAG, op0=ALU.add, op1=ALU.subtract))
                    dv(lambda e: e.tensor_tensor(out=tmp, in0=tmp2, in1=tmp, op=ALU.subtract))
                    ac(lambda e: e.activation(out=cs, in_=tmp, func=AF.Sin, scale=TWO_PI))

                lr, dtt, a_, tht, mag1, s1, c1, w1r, w1i, den, cfr, cfi, x1, x2 = [t[:] for t in sm]
                dv(lambda e: e.tensor_scalar(out=lr, in0=lam_r[:], scalar1=-1e-4, scalar2=None, op0=ALU.min))
                ac(lambda e: e.activation(out=dtt, in_=ls1[:], func=AF.Exp))
                dv(lambda e: e.tensor_tensor(out=a_, in0=lr, in1=dtt, op=ALU.mult))
                dv(lambda e: e.tensor_tensor(out=tht, in0=lam_i[:], in1=dtt, op=ALU.mult))
                dv(lambda e: e.tensor_scalar(out=tht, in0=tht, scalar1=1.0 / TWO_PI, scalar2=None, op0=ALU.mult))
                ac(lambda e: e.activation(out=mag1, in_=a_, func=AF.Exp))
                sincos(tht, s1, c1, x1, x2)
                dv(lambda e: e.tensor_tensor(out=w1r, in0=mag1, in1=c1, op=ALU.mult))
                dv(lambda e: e.tensor_tensor(out=w1i, in0=mag1, in1=s1, op=ALU.mult))
                dv(lambda e: e.tensor_scalar(out=w1r, in0=w1r, scalar1=-1.0, scalar2=None, op0=ALU.add))
                dv(lambda e: e.tensor_tensor(out=den, in0=lr, in1=lr, op=ALU.mult))
                dv(lambda e: e.tensor_tensor(out=x1, in0=lam_i[:], in1=lam_i[:], op=ALU.mult))
                dv(lambda e: e.tensor_tensor(out=den, in0=den, in1=x1, op=ALU.add))
                dv(lambda e: e.reciprocal(out=den, in_=den))
                dv(lambda e: e.tensor_tensor(out=x1, in0=w1r, in1=lr, op=ALU.mult))
                dv(lambda e: e.tensor_tensor(out=x2, in0=w1i, in1=lam_i[:], op=ALU.mult))
                dv(lambda e: e.tensor_tensor(out=x1, in0=x1, in1=x2, op=ALU.add))
                dv(lambda e: e.tensor_tensor(out=cfr, in0=x1, in1=den, op=ALU.mult))
                dv(lambda e: e.tensor_tensor(out=x1, in0=w1i, in1=lr, op=ALU.mult))
                dv(lambda e: e.tensor_tensor(out=x2, in0=w1r, in1=lam_i[:], op=ALU.mult))
                dv(lambda e: e.tensor_tensor(out=x1, in0=x1, in1=x2, op=ALU.subtract))
                dv(lambda e: e.tensor_tensor(out=cfi, in0=x1, in1=den, op=ALU.mult))
                bc = lambda t: t.unsqueeze(2).to_broadcast([128, 32, 16])
                dv(lambda e: e.tensor_tensor(out=tb1[:], in0=br1[:], in1=bc(cfr), op=ALU.mult))
                dv(lambda e: e.tensor_tensor(out=tb2[:], in0=bi1[:], in1=bc(cfi), op=ALU.mult))
                dv(lambda e: e.tensor_tensor(out=Bbr[:], in0=tb1[:], in1=tb2[:], op=ALU.subtract))
                dv(lambda e: e.tensor_tensor(out=tb1[:], in0=bi1[:], in1=bc(cfr), op=ALU.mult))
                dv(lambda e: e.tensor_tensor(out=tb2[:], in0=br1[:], in1=bc(cfi), op=ALU.mult))
                dv(lambda e: e.tensor_tensor(out=Bbi[:], in0=tb1[:], in1=tb2[:], op=ALU.add))
                exb = exps[:].unsqueeze(3).to_broadcast([128, 4, 8, 32])
                ab = lambda t: t.unsqueeze(1).unsqueeze(1).to_broadcast([128, 4, 8, 32])
                dv(lambda e: e.tensor_tensor(out=EA[:], in0=exb, in1=ab(a_), op=ALU.mult))
                dv(lambda e: e.tensor_tensor(out=ET[:], in0=exb, in1=ab(tht), op=ALU.mult))
                ac(lambda e: e.activation(out=EA[:], in_=EA[:], func=AF.Exp))
                sincos(ET[:], Wi[:], Wr[:], tw[:], ET[:])
                dv(lambda e: e.tensor_tensor(out=Wr[:], in0=Wr[:], in1=EA[:], op=ALU.mult))
                dv(lambda e: e.tensor_tensor(out=Wi[:], in0=Wi[:], in1=EA[:], op=ALU.mult))
                dv(lambda e: e.tensor_scalar(out=x1, in0=a_, scalar1=8.0, scalar2=None, op0=ALU.mult))
                ac(lambda e: e.activation(out=rho8[:], in_=x1, func=AF.Exp))
                dv(lambda e: e.tensor_scalar(out=tt8[:], in0=tht, scalar1=8.0, scalar2=None, op0=ALU.mult))

                st_tmp.close()
                df.barrier(bart[:, 0:1])
                PK = ["pre"]
                KT = sb("KT", [128, 4, 128], BF16)
                ELTr = sb("ELTr", [128, 4, 128], BF16)
                ELTi = sb("ELTi", [128, 4, 128], BF16)
                CLr = sb("CLr", [128, 4, 128], BF16)
                nCLi = sb("nCLi", [128, 4, 128], BF16)
                Rr = sb("Rr", [128, 4, 128], BF16)
                Ri = sb("Ri", [128, 4, 128], BF16)
                Qr = sb("Qr", [128, 4, 128], BF16)
                nQi = sb("nQi", [128, 4, 128], BF16)
                ELr = sb("ELr", [128, 4, 128], BF16)
                ELi = sb("ELi", [128, 4, 128], BF16)
                p1 = sb("p1", [128, 4, 128], F32)
                p2 = sb("p2", [128, 4, 128], F32)
                scr = mT[:, 4:8, :].rearrange("p c t -> p (c t)").bitcast(F32).rearrange("p (n b) -> p n b", b=544)
                SETS = []
                for si_ in range(2):
                    d_ = {}
                    for ti_, nm in enumerate(("Er", "Ei", "cos", "sin", "gr", "gi", "sr", "si")):
                        d_[nm] = scr[:, si_ * 8 + ti_, :]
                    d_["tA"] = sb("tA%d" % si_, [128, 544], F32)[:]
                    d_["Zr"] = sb("Zr%d" % si_, [128, 544], BF16)
                    d_["Zi"] = sb("Zi%d" % si_, [128, 544], BF16)
                    d_["zg"] = sb("zg%d" % si_, [128, 544], BF16)
                    d_["id"] = si_
                    SETS.append(d_)
                    A("pool", lambda e, d_=d_: e.memset(d_["Zr"][:], 0.0), (), ["Zr%d" % si_])
                    A("pool", lambda e, d_=d_: e.memset(d_["Zi"][:], 0.0), (), ["Zi%d" % si_])

                def cprod(outr, outi, l, Xr_, Xi_, g0, neg_i):
                    wv = lambda W_: W_[:, l, :, g0:g0 + 4].rearrange("p s g -> p g s").unsqueeze(3).to_broadcast([128, 4, 8, 16])
                    xv = lambda X_: X_[:, g0:g0 + 4, :].unsqueeze(2).to_broadcast([128, 4, 8, 16])
                    o4 = lambda t: t[:].rearrange("p g (s m) -> p g s m", m=16)
                    dv(lambda e: e.tensor_tensor(out=o4(p1), in0=wv(Wr), in1=xv(Xr_), op=ALU.mult))
                    dv(lambda e: e.tensor_tensor(out=o4(p2), in0=wv(Wi), in1=xv(Xi_), op=ALU.mult))
                    dv(lambda e: e.tensor_tensor(out=outr[:], in0=p1[:], in1=p2[:], op=ALU.subtract))
                    dv(lambda e: e.tensor_tensor(out=o4(p1), in0=wv(Wr), in1=xv(Xi_), op=ALU.mult))
                    dv(lambda e: e.tensor_tensor(out=o4(p2), in0=wv(Wi), in1=xv(Xr_), op=ALU.mult))
                    if neg_i:
                        dv(lambda e: e.scalar_tensor_tensor(out=outi[:], in0=p1[:], scalar=-1.0, in1=p2[:], op0=ALU.mult, op1=ALU.subtract))
                    else:
                        dv(lambda e: e.tensor_tensor(out=outi[:], in0=p1[:], in1=p2[:], op=ALU.add))

                for qq in range(8):
                    g0 = qq * 4
                    cprod(Rr, Ri, 0, Bbr, Bbi, g0, False)
                    cprod(ELr, ELi, 1, Bbr, Bbi, g0, False)
                    cprod(Qr, nQi, 2, cr1, ci1, g0, True)
                    cprod(CLr, nCLi, 3, cr1, ci1, g0, True)
                    for q in range(4):
                        for h_ in range(2):
                            hb = h_ * 64
                            bank = mm[h_]
                            bk = "mm%d" % h_
                            A("pe", lambda e, bank=bank, hb=hb, q=q: e.matmul(bank[:, 0:128], lhsT=Rr[hb:hb + 64, q, :], rhs=Qr[hb:hb + 64, q, :], start=True, stop=False), PK, [bk])
                            A("pe", lambda e, bank=bank, hb=hb, q=q: e.matmul(bank[:, 0:128], lhsT=Ri[hb:hb + 64, q, :], rhs=nQi[hb:hb + 64, q, :], start=False, stop=True), PK, [bk])
                        A("dve", lambda e: e.tensor_tensor(out=p1[:, 0, :], in0=mm[0][:, 0:128], in1=mask[:, 0, :], op=ALU.mult), ["mm0"] + PK, PK)
                        A("dve", lambda e: e.tensor_tensor(out=p2[:, 0, :], in0=mm[1][:, 0:128], in1=mask[:, 1, :], op=ALU.mult), ["mm1"] + PK, PK)
                        A("dve", lambda e, q=q: e.tensor_tensor(out=KT[:, q, :], in0=p1[:, 0, :], in1=p2[:, 0, :], op=ALU.add), PK, PK)
                        A("pe", lambda e, q=q: e.transpose(trp[:, 0, :], ELr[:, q, :], idn[:]), PK + ["idn"], ["trp"])
                        A("pe", lambda e, q=q: e.transpose(trp[:, 1, :], ELi[:, q, :], idn[:]), PK + ["idn"], ["trp"])
                        A("act", lambda e, q=q: e.copy(out=ELTr[:, q, :], in_=trp[:, 0, :]), ["trp"] + PK, PK)
                        A("act", lambda e, q=q: e.copy(out=ELTi[:, q, :], in_=trp[:, 1, :]), ["trp"] + PK, PK)
                    def grp(q, g, S):
                        sid = S["id"]
                        K_ = lambda nm: "%s%d" % (nm, sid)
                        Eb = stp[sid]
                        ebk = "stp%d" % sid
                        pvo = sid * 128
                        Er, Ei, cosT, sinT, gr, gi, sr, si, tA = S["Er"], S["Ei"], S["cos"], S["sin"], S["gr"], S["gi"], S["sr"], S["si"], S["tA"]
                        Zr_, Zi_, z_ = S["Zr"], S["Zi"], S["zg"]

                        def st0():
                            for ri, ELT_ in enumerate((ELTr, ELTi)):
                                for (b0, nb) in PIECES:
                                    if b0 < 512:
                                        yo = Eb[:, ri * 512 + b0:ri * 512 + b0 + nb]
                                        wk_ = [ebk]
                                    else:
                                        yo = pv[:, pvo + ri * 64:pvo + ri * 64 + nb]
                                        wk_ = ["pv"]
                                    A("pe", lambda e, yo=yo, ELT_=ELT_, b0=b0, nb=nb: e.matmul(yo, lhsT=ELT_[:, q, :], rhs=Uv[:, g, b0:b0 + nb], start=True, stop=True), PK + ["U"], wk_)

                        def st1():
                            for ri, E_ in enumerate((Er, Ei)):
                                ek = K_("Er" if ri == 0 else "Ei")
                                A("act", lambda e, E_=E_, ri=ri: e.copy(out=E_[0:64, 32:544], in_=Eb[0:64, ri * 512:(ri + 1) * 512]), [ebk], [ek])
                                A("act", lambda e, E_=E_, ri=ri: e.copy(out=E_[0:64, 0:32], in_=pv[0:64, pvo + ri * 64:pvo + ri * 64 + 32]), ["pv"], [ek])
                                A("act", lambda e, E_=E_, ri=ri: e.copy(out=E_[64:128, 543:31:-1], in_=Eb[64:128, ri * 512:(ri + 1) * 512]), [ebk], [ek])
                                A("act", lambda e, E_=E_, ri=ri: e.copy(out=E_[64:128, 31::-1], in_=pv[64:128, pvo + ri * 64:pvo + ri * 64 + 32]), ["pv"], [ek])
                            A("dve", lambda e: e.tensor_scalar(out=gr, in0=iota[:], scalar1=tt8[:, g:g + 1], scalar2=None, op0=ALU.mult), PK + ["iota"], [K_("gr")])
                            A("dve", lambda e: e.tensor_scalar(out=gi, in0=gr, scalar1=MAG, scalar2=MAG, op0=ALU.add, op1=ALU.subtract), [K_("gr")], [K_("gi")])
                            A("dve", lambda e: e.tensor_tensor(out=gi, in0=gr, in1=gi, op=ALU.subtract), [K_("gr"), K_("gi")], [K_("gi")])

                        def st2():
                            A("act", lambda e: e.activation(out=sinT, in_=gi, func=AF.Sin, scale=TWO_PI), [K_("gi")], [K_("sin")])
                            A("dve", lambda e: e.tensor_scalar(out=gr, in0=gr, scalar1=0.25, scalar2=None, op0=ALU.add), [K_("gr")], [K_("gr")])
                            A("dve", lambda e: e.tensor_scalar(out=gi, in0=gr, scalar1=MAG, scalar2=MAG, op0=ALU.add, op1=ALU.subtract), [K_("gr"), K_("sin")], [K_("gi")])
                            A("dve", lambda e: e.tensor_tensor(out=gi, in0=gr, in1=gi, op=ALU.subtract), [K_("gr"), K_("gi")], [K_("gi")])

                        def st3():
                            A("act", lambda e: e.activation(out=cosT, in_=gi, func=AF.Sin, scale=TWO_PI), [K_("gi")], [K_("cos")])

                        def st4():
                            A("pool", lambda e: e.tensor_tensor(out=gr, in0=Er, in1=cosT, op=ALU.mult), [K_("Er"), K_("cos"), K_("gr")], [K_("gr")])
                            A("pool", lambda e: e.tensor_tensor(out=tA, in0=Ei, in1=sinT, op=ALU.mult), [K_("Ei"), K_("sin")], [K_("tA")])
                            A("pool", lambda e: e.tensor_tensor(out=gr, in0=gr, in1=tA, op=ALU.add), [K_("gr"), K_("tA")], [K_("gr")])
                            A("pool", lambda e: e.tensor_tensor(out=gi, in0=Ei, in1=cosT, op=ALU.mult), [K_("Ei"), K_("cos"), K_("gi")], [K_("gi")])
                            A("pool", lambda e: e.tensor_tensor(out=tA, in0=Er, in1=sinT, op=ALU.mult), [K_("Er"), K_("sin"), K_("tA")], [K_("tA")])
                            A("pool", lambda e: e.tensor_tensor(out=gi, in0=gi, in1=tA, op=ALU.subtract), [K_("gi"), K_("tA")], [K_("gi")])

                        def st5():
                            rb = rho8[:, g:g + 1].to_broadcast([128, 544])
                            A("dve", lambda e: e.tensor_tensor_scan(out=sr, data0=rb, data1=gr, initial=0.0, op0=ALU.mult, op1=ALU.add), [K_("gr")] + PK, [K_("sr")])
                            A("dve", lambda e: e.tensor_tensor_scan(out=si, data0=rb, data1=gi, initial=0.0, op0=ALU.mult, op1=ALU.add), [K_("gi")] + PK, [K_("si")])

                        def rot_out(Z_, zk, c1, k1, c2, k2, op):
                            A("pool", lambda e: e.tensor_tensor(out=Er, in0=sr, in1=c1, op=ALU.mult), [K_("sr"), k1, K_("Er")], [K_("Er")])
                            A("pool", lambda e: e.tensor_tensor(out=Ei, in0=si, in1=c2, op=ALU.mult), [K_("si"), k2, K_("Ei")], [K_("Ei")])
                            A("dve", lambda e: e.tensor_tensor(out=Z_[0:64, 0:512], in0=Er[0:64, 31:543], in1=Ei[0:64, 31:543], op=op), [K_("Er"), K_("Ei")], [zk])
                            A("dve", lambda e: e.tensor_tensor(out=Z_[0:64, 513:544], in0=Er[0:64, 0:31], in1=Ei[0:64, 0:31], op=op), [K_("Er"), K_("Ei")], [zk])
                            A("dve", lambda e: e.tensor_tensor(out=Z_[64:128, 542::-1], in0=Er[64:128, 0:543], in1=Ei[64:128, 0:543], op=op), [K_("Er"), K_("Ei")], [zk])

                        def st6():
                            rot_out(Zr_, K_("Zr"), cosT, K_("cos"), sinT, K_("sin"), ALU.subtract)

                        def st7():
                            rot_out(Zi_, K_("Zi"), sinT, K_("sin"), cosT, K_("cos"), ALU.add)

                        def st8():
                            for (b0, nb) in PIECES:
                                if b0 < 512:
                                    yo = mm[0][:, b0:b0 + nb]
                                    wk_ = ["mm0"]
                                else:
                                    yo = mm[1][:, 0:nb]
                                    wk_ = ["mm1"]
                                A("pe", lambda e, yo=yo, b0=b0, nb=nb: e.matmul(yo, lhsT=KT[:, q, :], rhs=Uv[:, g, b0:b0 + nb], start=True, stop=False), PK + ["U"], wk_)
                                A("pe", lambda e, yo=yo, b0=b0, nb=nb: e.matmul(yo, lhsT=CLr[:, q, :], rhs=Zr_[:, b0:b0 + nb], start=False, stop=False), PK + [K_("Zr")], wk_)
                                A("pe", lambda e, yo=yo, b0=b0, nb=nb: e.matmul(yo, lhsT=nCLi[:, q, :], rhs=Zi_[:, b0:b0 + nb], start=False, stop=True), PK + [K_("Zi")], wk_)
                            A("dve", lambda e: e.scalar_tensor_tensor(out=gr[:, 0:512], in0=Uv[:, g, 0:512], scalar=drep[:, g:g + 1], in1=mm[0][:, 0:512], op0=ALU.mult, op1=ALU.add),
                              ["mm0", "U", "drep", K_("gr")], [K_("gr")])
                            A("dve", lambda e: e.scalar_tensor_tensor(out=gr[:, 512:544], in0=Uv[:, g, 512:544], scalar=drep[:, g:g + 1], in1=mm[1][:, 0:32], op0=ALU.mult, op1=ALU.add),
                              ["mm1", "U", "drep", K_("gr")], [K_("gr")])
                            A("act", lambda e: e.activation(out=z_[:], in_=gr, func=AF.Gelu), [K_("gr")], [K_("zg")])
                            for t8 in range(8):
                                dma("sp", zs[t8, g], z_[t8 * 16:(t8 + 1) * 16, :], [K_("zg")], ["zs"])

                        return [st0, st1, st2, st3, st4, st5, st6, st7], st8

                    for pr in range(2):
                        gA, gB = g0 + 2 * pr, g0 + 2 * pr + 1
                        stA, yA = grp(2 * pr, gA, SETS[0])
                        stB, yB = grp(2 * pr + 1, gB, SETS[1])
                        for k_ in range(len(stA)):
                            stA[k_]()
                            stB[k_]()
                        yA()
                        yB()
            df.barrier(bart[:, 0:1])

            with contextlib.ExitStack() as st:
              if "s5c" not in _SKIP:
                sb = lambda name, shape, dt: st.enter_context(nc.sbuf_tensor(uniq(name), shape, dt))
                Wg = sb("Wg", [128, 4, 1024], BF16)
                bg = sb("bg", [128, 8], F32)
                sgd = sb("sgd", [128, T], BF16)
                zsb = [sb("zsb%d" % i, [128, 4, 544], BF16) for i in range(2)]
                sig = [sb("sig%d" % i, [128, 544], F32) for i in range(2)]
                v1 = [sb("v1%d" % i, [128, 544], F32) for i in range(2)]
                wgd = sb("wgd", [128, 8, 128], BF16)
                for h_ in range(2):
                    dma("pool", Wg[:, :, h_ * 512:(h_ + 1) * 512], glu_w[j].rearrange("(c p) n -> p c n", p=128)[:, :, h_ * 512:(h_ + 1) * 512], (), ["Wg"])
                dma("sp", bg[:], glu_bT[j], (), ["bg"])
                for k in range(4):
                    dma("pool", wgd[:], Wd[:, :, 2048 + k * 128:2048 + (k + 1) * 128], (), ["wgd"])
                    inproj_fm(wgd, "wgd", NTILES, lambda bank, bk, n0, nn: A(
                        "act", lambda e: e.activation(out=sgd[:, n0:n0 + nn], in_=bank[:, 0:nn], func=AF.Silu), [bk], ["sgd"]))

                    def banks(n):
                        if n % 2 == 0:
                            return (mm[0], "mm0"), (mm[1], "mm1"), (pv[:, 0:32], "pv"), (pv[:, 256:288], "pv")
                        return (stp[0][:, 0:512], "stp0a"), (stp[0][:, 512:1024], "stp0b"), (stp[1][:, 0:32], "stp1a"), (stp[1][:, 512:544], "stp1b")

                    def c_load(n):
                        zb, zbk = zsb[n % 2], "zsb%d" % (n % 2)
                        dma("sp", zb[:], zs[n].rearrange("g m b -> (g m) b").rearrange("(c p) b -> p c b", p=128), ["zs"], [zbk])

                    def c_mm(n):
                        zb, zbk = zsb[n % 2], "zsb%d" % (n % 2)
                        (vb, vk), (gb_, gk), (vp, vpk), (gp_, gpk) = banks(n)
                        for (b0, nb) in PIECES:
                            for vg in range(2):
                                if b0 < 512:
                                    yo = (vb if vg == 0 else gb_)[:, b0:b0 + nb]
                                    wk_ = [vk if vg == 0 else gk]
                                else:
                                    yo = vp if vg == 0 else gp_
                                    wk_ = [vpk if vg == 0 else gpk]
                                col = (vg * 4 + k) * 128
                                for c in range(4):
                                    A("pe", lambda e, yo=yo, c=c, col=col, zb=zb, b0=b0, nb=nb: e.matmul(yo, lhsT=Wg[:, c, col:col + 128], rhs=zb[:, c, b0:b0 + nb], start=(c == 0), stop=(c == 3)),
                                      ["Wg", zbk], wk_)

                    def c_sig(n):
                        (vb, vk), (gb_, gk), (vp, vpk), (gp_, gpk) = banks(n)
                        sg_, sgk = sig[n % 2], "sig%d" % (n % 2)
                        A("act", lambda e, gb_=gb_, sg_=sg_, k=k: e.activation(out=sg_[:, 0:512], in_=gb_[:, 0:512], func=AF.Sigmoid, bias=bg[:, 4 + k:5 + k]), [gk, "bg"], [sgk])
                        A("act", lambda e, gp_=gp_, sg_=sg_, k=k: e.activation(out=sg_[:, 512:544], in_=gp_, func=AF.Sigmoid, bias=bg[:, 4 + k:5 + k]), [gpk, "bg"], [sgk])

                    def c_stt(n):
                        (vb, vk), (gb_, gk), (vp, vpk), (gp_, gpk) = banks(n)
                        sg_, sgk = sig[n % 2], "sig%d" % (n % 2)
                        v_, v1k = v1[n % 2], "v1%d" % (n % 2)
                        A("dve", lambda e, vb=vb, sg_=sg_, v_=v_, k=k: e.scalar_tensor_tensor(out=v_[:, 0:512], in0=vb[:, 0:512], scalar=bg[:, k:k + 1], in1=sg_[:, 0:512], op0=ALU.add, op1=ALU.mult), [vk, sgk, "bg"], [v1k])
                        A("dve", lambda e, vp=vp, sg_=sg_, v_=v_, k=k: e.scalar_tensor_tensor(out=v_[:, 512:544], in0=vp, scalar=bg[:, k:k + 1], in1=sg_[:, 512:544], op0=ALU.add, op1=ALU.mult), [vpk, sgk, "bg"], [v1k])

                    def c_out(n):
                        v_, v1k = v1[n % 2], "v1%d" % (n % 2)
                        A("pool", lambda e, v_=v_, n=n, k=k: e.tensor_tensor(out=mT[:, 4 + k, n::8], in0=v_[:], in1=sgd[:, n::8], op=ALU.mult), [v1k, "sgd"], [("mT", 4 + k, t) for t in range(NT)])

                    pipeline([c_load, c_mm, c_sig, c_stt, c_out], 8, "s5c")
            df.barrier(bart[:, 0:1])

            with contextlib.ExitStack() as st:
              if "gmlp" not in _SKIP:
                sb = lambda name, shape, dt: st.enter_context(nc.sbuf_tensor(uniq(name), shape, dt))
                Wv = sb("Wv", [128, 8, 512], BF16)
                Wu = sb("Wu", [128, 8, 512], BF16)
                Wc = sb("Wc", [128, 8, 512], BF16)
                wsT = sb("wsT", [128, 4, 128], BF16)
                sgT = sb("sgT", [128, 4], F32)
                bsb = sb("bsb", [128, 512], F32)
                st6 = sb("st6", [128, 12], F32)
                mv = sb("mv", [128, 4 * NT], F32)
                vn = [sb("vn%d" % i, [128, 512], BF16) for i in range(2)]
                sgc = [sb("sgc%d" % i, [128, 512], BF16) for i in range(2)]
                mx = [sb("mx%d" % i, [128, 512], F32) for i in range(2)]
                dma("pool", Wu[:], Wd[:, :, 0:512], (), ["Wu"])
                dma("pool", Wv[:], Wd[:, :, 512:1024], (), ["Wv"])
                dma("pool", Wc[:], Wd[:, :, 1024:1536], (), ["Wc"])
                dma("pool", wsT[:], sgu_wT[j], (), ["wsT"])
                dma("sp", sgT[:], sgu_gT[j], (), ["sgT"])
                dma("sp", bsb[:], sgu_b[j:j + 1, :].partition_broadcast(128), (), ["bsb"])
                vsb = [sb("vsb%d" % i, [128, 512], F32) for i in range(4)]
                tmx = [sb("tmx%d" % i, [128, 512], F32) for i in range(2)]
                MEAN = lambda n: mv[:, 4 * n:4 * n + 1]
                VAR = lambda n: mv[:, 4 * n + 1:4 * n + 2]
                RSg = lambda n: mv[:, 4 * n + 2:4 * n + 3]
                VS = lambda n: (vsb[n % 4], "vsb%d" % (n % 4))

                def g_v(n):
                    bank, bk = mm[n % 2], "mm%d" % (n % 2)
                    for k in range(8):
                        A("pe", lambda e, bank=bank, k=k, n=n: e.matmul(bank[:, :], lhsT=xb[:, k, n * 128:(n + 1) * 128], rhs=Wv[:, k, :], start=(k == 0), stop=(k == 7)), ["Wv", ("xb", n)], [bk])

                def g_cp(n):
                    bank, bk = mm[n % 2], "mm%d" % (n % 2)
                    vs_, vsk = VS(n)
                    A("act", lambda e, bank=bank, vs_=vs_: e.copy(out=vs_[:], in_=bank[:, :]), [bk], [vsk])

                def g_bn(n):
                    vs_, vsk = VS(n)
                    s6 = st6[:, (n % 2) * 6:(n % 2) * 6 + 6]
                    A("dve", lambda e, vs_=vs_, s6=s6: e.bn_stats(out=s6, in_=vs_[:]), [vsk], [("st6", n % 2)])
                    A("dve", lambda e, n=n, s6=s6: e.bn_aggr(out=mv[:, 4 * n:4 * n + 2], in_=s6), [("st6", n % 2)], [("mv", n)])

                def g_r1(n):
                    A("dve", lambda e, n=n: e.tensor_scalar(out=RSg(n), in0=VAR(n), scalar1=1.0, scalar2=EPS, op0=ALU.mult, op1=ALU.add), [("mv", n)], [("grs", n)])

                def g_r2(n):
                    A("act", lambda e, n=n: e.activation(out=RSg(n), in_=RSg(n), func=AF.Sqrt), [("grs", n)], [("grs", n)])

                def g_vn(n):
                    vs_, vsk = VS(n)
                    v_, vk = vn[n % 2], "vn%d" % (n % 2)
                    A("dve", lambda e, n=n: e.reciprocal(out=RSg(n), in_=RSg(n)), [("grs", n)], [("grs", n)])
                    A("dve", lambda e, vs_=vs_, v_=v_, n=n: e.tensor_scalar(out=v_[:], in0=vs_[:], scalar1=MEAN(n), scalar2=RSg(n), op0=ALU.subtract, op1=ALU.mult), [vsk, ("mv", n), ("grs", n)], [vk])

                def g_mm(n):
                    v_, vk = vn[n % 2], "vn%d" % (n % 2)
                    for g in range(4):
                        A("pe", lambda e, v_=v_, g=g: e.matmul(pv[:, g * 128:(g + 1) * 128], lhsT=v_[:, g * 128:(g + 1) * 128], rhs=wsT[:, g, :], start=True, stop=True), [vk, "wsT"], ["pv"])
                    sp_ = stp[n % 2]
                    for half, W_, wkk in ((0, Wu, "Wu"), (1, Wc, "Wc")):
                        sk = "stp%d%s" % (n % 2, "ab"[half])
                        for g in range(4):
                            for k in range(8):
                                A("pe", lambda e, sp_=sp_, half=half, W_=W_, g=g, k=k, n=n: e.matmul(sp_[:, half * 512 + g * 128:half * 512 + (g + 1) * 128], lhsT=W_[:, k, g * 128:(g + 1) * 128], rhs=xb[:, k, n * 128:(n + 1) * 128], start=(k == 0), stop=(k == 7)),
                                  [wkk, ("xb", n)], [sk])

                def g_ep(n):
                    sp_ = stp[n % 2]
                    c_, ck = sgc[n % 2], "sgc%d" % (n % 2)
                    m_, mk = mx[n % 2], "mx%d" % (n % 2)
                    t_, tk_ = tmx[n % 2], "tmx%d" % (n % 2)
                    A("act", lambda e, sp_=sp_, c_=c_: e.activation(out=c_[:], in_=sp_[:, 512:1024], func=AF.Silu), ["stp%db" % (n % 2)], [ck])
                    for g in range(4):
                        A("dve", lambda e, m_=m_, g=g: e.scalar_tensor_tensor(out=m_[:, g * 128:(g + 1) * 128], in0=pv[:, g * 128:(g + 1) * 128], scalar=sgT[:, g:g + 1], in1=bsb[:, g * 128:(g + 1) * 128], op0=ALU.mult, op1=ALU.add),
                          ["pv", "sgT", "bsb"], [mk])
                    A("dve", lambda e, m_=m_, sp_=sp_, t_=t_: e.tensor_tensor(out=t_[:], in0=m_[:], in1=sp_[:, 0:512], op=ALU.mult), [mk, "stp%da" % (n % 2)], [tk_])

                def g_out(n):
                    c_, ck = sgc[n % 2], "sgc%d" % (n % 2)
                    t_, tk_ = tmx[n % 2], "tmx%d" % (n % 2)
                    A("pool", lambda e, t_=t_, c_=c_, n=n: e.tensor_tensor(out=mT[:, 0:4, n * 128:(n + 1) * 128], in0=t_[:].rearrange("p (g c) -> p g c", g=4), in1=c_[:].rearrange("p (g c) -> p g c", g=4), op=ALU.mult),
                      [tk_, ck], [("mT", g, n) for g in range(4)])

                pipeline([g_v, g_cp, g_bn, g_r1, g_r2, g_vn, g_mm, g_ep, g_out], NT, "gmlp")
            df.barrier(bart[:, 0:1])

        for L in range(n_layers):
            last = (L == n_layers - 1)
            (prenorm_old if "oldpre" in _SKIP else prenorm)(L)
            if L % 2 == 0:
                if "even" not in _SKIP:
                    even_mixer(L)
            else:
                odd_mixer(L)
            (post_old if "oldpost" in _SKIP else post)(L, last)
        df.emit()
    return nc


_CACHE = {}


def kernel(x, c, ctx, c_ctx, w_ada, b_ada, pre_g, post_g, w_in_even, w_out_even, na_rpb,
           w_in_odd, w_out_odd, sgu_w, sgu_b, sgu_g, s5_lam_re, s5_lam_im, s5_log_step,
           s5_b_re, s5_b_im, s5_c_re, s5_c_im, s5_d, glu_w, glu_b, _n_layers=DEPTH):
    f = lambda a: np.ascontiguousarray(np.asarray(a), dtype=np.float32)
    B = x.shape[0]
    if _n_layers not in _CACHE:
        _CACHE[_n_layers] = build_program(_n_layers)
    nc = _CACHE[_n_layers]
    F1, H, CH, F256 = fnet_consts()
    eb = np.stack([build_ebias(f(na_rpb[j])) for j in range(2)]).reshape(2, 12, 128, 21 * 128)
    shared = dict(w_ada=f(w_ada), b_ada=f(b_ada), pre_g=f(pre_g), post_g=f(post_g), w_in_even=f(w_in_even),
                  w_out_even=f(w_out_even), ebias=eb, cF1=F1, cH=H, cCH=CH, cF256=F256)
    ex, mk, io = s5_consts()
    lam1 = np.stack([f(s5_lam_re), f(s5_lam_im)], axis=1)
    lam1 = np.ascontiguousarray(lam1.transpose(0, 1, 2, 4, 3)).reshape(2, 2, 128, 32)
    b1 = np.stack([f(s5_b_re), f(s5_b_im)], axis=1)
    b1 = np.ascontiguousarray(b1.transpose(0, 1, 2, 4, 3, 5)).reshape(2, 2, 128, 32, 16)
    c1 = np.stack([f(s5_c_re), f(s5_c_im)], axis=1)
    c1 = np.ascontiguousarray(c1.transpose(0, 1, 2, 5, 3, 4)).reshape(2, 2, 128, 32, 16)
    drep = np.ascontiguousarray(np.tile(f(s5_d).reshape(2, 32, 16).transpose(0, 2, 1), (1, 8, 1)))
    shared.update(dict(
        w_in_odd=f(w_in_odd), w_out_odd=f(w_out_odd),
        sgu_wT=np.ascontiguousarray(f(sgu_w).transpose(0, 3, 1, 2)),
        sgu_gT=np.ascontiguousarray(f(sgu_g).reshape(2, 4, 128).transpose(0, 2, 1)),
        sgu_b=f(sgu_b).reshape(2, 512), s5_lam1=lam1, s5_ls=f(s5_log_step), s5_b1=b1, s5_c1=c1, s5_drep=drep,
        glu_w=f(glu_w), glu_bT=np.ascontiguousarray(f(glu_b).reshape(2, 8, 128).transpose(0, 2, 1)),
        cEXPS=ex, cMASK=mk, cIOTA=io))
    cc = f(c_ctx).reshape(8, 128).T
    in_maps = []
    for b in range(B):
        m = dict(shared)
        m["xr"] = np.concatenate([f(x[b]), f(ctx[b])], axis=0)
        m["cT"] = np.ascontiguousarray(np.concatenate([f(c[b]).reshape(8, 128).T, cc], axis=1))
        in_maps.append(m)
    res = run_bass_kernel_spmd(nc, in_maps, core_ids=list(range(B)))
    return np.stack([np.asarray(r["out"], dtype=np.float32) for r in res.results], axis=0)
```

```python
import contextlib
import numpy as np
import concourse.bass as bass
import concourse.mybir as mybir
from concourse.bass_utils import run_bass_kernel_spmd

F32 = mybir.dt.float32
BF16 = mybir.dt.bfloat16
AF = mybir.ActivationFunctionType
ALU = mybir.AluOpType

ENGS = ("pe", "act", "dve", "pool", "sp")
RING = 8
SELF_SYNC = ("act", "dve", "pool")

D = 1024
T = 4352
TL = 4096
NT = 34
EPS = 1e-6
DEPTH = 4


class _Op:
    __slots__ = ("eng", "fn", "deps", "dma", "flag", "cnt", "ring", "target")


class DF:
    def __init__(self, nc):
        self.nc = nc
        self.ops = []
        self.lw = {}
        self.rd = {}
        self.ndma = {e: 0 for e in ENGS}
        self.last_on = {}
        self.bar = None
        self.dma_since_bar = []

    def add(self, eng, fn, reads=(), writes=(), dma=False):
        idx = len(self.ops)
        deps = set()
        if self.bar is not None:
            deps.add(self.bar)
        for r in reads:
            w = self.lw.get(r)
            if w is not None:
                deps.add(w)
        for r in writes:
            w = self.lw.get(r)
            if w is not None:
                deps.add(w)
            deps.update(self.rd.get(r, ()))
        for r in reads:
            self.rd.setdefault(r, []).append(idx)
        for r in writes:
            self.lw[r] = idx
            self.rd[r] = []
        o = _Op()
        o.eng, o.fn, o.deps, o.dma, o.flag, o.cnt = eng, fn, deps, dma, False, 0
        o.ring = o.target = None
        if dma:
            n = self.ndma[eng]
            self.ndma[eng] = n + 1
            o.ring = n % RING
            o.target = 16 * (n // RING + 1)
            self.dma_since_bar.append(idx)
        else:
            self.last_on[eng] = idx
        self.ops.append(o)
        return idx

    def barrier(self, tile):
        idx = len(self.ops)
        deps = set(self.last_on.values()) | set(self.dma_since_bar)
        if self.bar is not None:
            deps.add(self.bar)
        o = _Op()
        o.eng, o.fn, o.deps, o.dma, o.flag, o.cnt = "pool", (lambda e: e.memset(tile, 0.0)), deps, False, False, 0
        o.ring = o.target = None
        self.ops.append(o)
        self.last_on["pool"] = idx
        self.bar = idx
        self.dma_since_bar = []
        self.lw = {}
        self.rd = {}

    def emit(self):
        nc = self.nc
        ops = self.ops
        for o in ops:
            for d in o.deps:
                p = ops[d]
                if p.dma:
                    continue
                if p.eng != o.eng or (o.eng in SELF_SYNC) or o.dma:
                    p.flag = True
        cnt = {e: 0 for e in ENGS}
        for o in ops:
            if o.flag and not o.dma:
                cnt[o.eng] += 1
                o.cnt = cnt[o.eng]
        with contextlib.ExitStack() as st:
            csem = {e: st.enter_context(nc.semaphore("c_" + e)) for e in ENGS}
            dsem = {e: [st.enter_context(nc.semaphore("d_%s%d" % (e, i))) for i in range(RING)]
                    for e in ("sp", "pool", "act")}
            block = st.enter_context(nc.Block())
            ndma = self.ndma

            def run(engname, eng):
                waited_c = {e: 0 for e in ENGS}
                waited_d = {}
                for o in ops:
                    if o.eng != engname:
                        continue
                    for d in sorted(o.deps):
                        p = ops[d]
                        if p.dma:
                            key = (p.eng, p.ring)
                            if waited_d.get(key, 0) < p.target:
                                eng.wait_ge(dsem[p.eng][p.ring], p.target)
                                waited_d[key] = p.target
                        else:
                            if p.eng == engname and not (engname in SELF_SYNC or o.dma):
                                continue
                            if waited_c[p.eng] < p.cnt:
                                eng.wait_ge(csem[p.eng], p.cnt)
                                waited_c[p.eng] = p.cnt
                    if o.dma and o.target > 16:
                        key = (engname, o.ring)
                        if waited_d.get(key, 0) < o.target - 16:
                            eng.wait_ge(dsem[engname][o.ring], o.target - 16)
                            waited_d[key] = o.target - 16
                    ins = o.fn(eng)
                    if o.dma:
                        ins.then_inc(dsem[engname][o.ring], 16)
                    elif o.flag:
                        ins.then_inc(csem[engname], 1)
                if engname in dsem:
                    n = ndma[engname]
                    for r in range(min(n, RING)):
                        last = ((n - 1 - r) // RING) * RING + r
                        eng.wait_ge(dsem[engname][r], 16 * (last // RING + 1))

            @block.tensor
            def _(eng):
                run("pe", eng)

            @block.scalar
            def _(eng):
                run("act", eng)

            @block.vector
            def _(eng):
                run("dve", eng)

            @block.gpsimd
            def _(eng):
                run("pool", eng)

            @block.sync
            def _(eng):
                run("sp", eng)


NA_COMBOS = ([(2, kt) for kt in range(0, 5)] + [(0, kt) for kt in range(4)] + [(1, kt) for kt in range(4)]
             + [(30, kt) for kt in range(28, 32)] + [(31, kt) for kt in range(28, 32)])


def na_pattern_base(i):
    if 2 <= i <= 29:
        return 0, list(range(i - 2, i + 3))
    if i == 0:
        return 5, [0, 1, 2, 3]
    if i == 1:
        return 9, [0, 1, 2, 3]
    if i == 30:
        return 13, [28, 29, 30, 31]
    return 17, [28, 29, 30, 31]


def build_ebias(rpb):
    out = np.empty((12, 128, 21, 128), np.float32)
    a = np.arange(2)
    c = np.arange(64)
    for pi, (i, kt) in enumerate(NA_COMBOS):
        kr = (2 * kt + a)[:, None, None, None]
        r = (2 * i + a)[None, None, :, None]
        ck = c[None, :, None, None]
        cq = c[None, None, None, :]
        r0 = np.clip(r - 4, 0, 56)
        c0 = np.clip(cq - 8, 0, 48)
        ok = (kr >= r0) & (kr < r0 + 8) & (ck >= c0) & (ck < c0 + 16)
        dr = np.clip(kr - r + 7, 0, 14)
        dc = np.clip(ck - cq + 15, 0, 30)
        ok, dr, dc = np.broadcast_arrays(ok, dr, dc)
        vals = rpb[:, dr, dc]
        vals = np.where(ok[None], vals, np.float32(-30000.0))
        out[:, :, pi, :] = vals.reshape(12, 128, 128)
    return out


def fnet_consts():
    i64 = np.arange(64)
    ang = 2 * np.pi * np.outer(i64, i64) / 64.0
    F1 = np.concatenate([np.cos(ang), -np.sin(ang)], axis=1)
    t2 = i64[:, None, None]
    k1 = i64[None, :, None]
    k2 = i64[None, None, :]
    ph = -2 * np.pi * (t2 * k1 / 4096.0 + t2 * k2 / 64.0)
    Gr, Gi = np.cos(ph) / 64.0, np.sin(ph) / 64.0
    H = np.empty((64, 64, 2, 128))
    H[:, :, 0, 0:64] = Gr
    H[:, :, 0, 64:128] = Gi
    H[:, :, 1, 0:64] = -Gi
    H[:, :, 1, 64:128] = Gr
    A = np.cos(ang) / 8.0
    B = np.sin(ang) / 8.0
    CH = np.zeros((128, 6, 128))
    CH[0:64, 0, 0:64] = A
    CH[0:64, 1, 0:64] = B
    CH[0:64, 2, 64:128] = A
    CH[0:64, 3, 64:128] = B
    CH[0:64, 4, 0:64] = A
    CH[64:128, 4, 64:128] = A
    CH[0:64, 5, 0:64] = B
    CH[64:128, 5, 64:128] = B
    i256 = np.arange(256)
    a256 = 2 * np.pi * np.outer(i256, i256) / 256.0
    F256 = np.concatenate([np.cos(a256), -np.sin(a256)], axis=1) / 16.0
    F256 = F256.reshape(2, 128, 512).transpose(1, 0, 2)
    f = lambda x: np.ascontiguousarray(x, dtype=np.float32)
    return f(F1), f(H), f(CH), f(F256)


def s5_consts():
    s8 = (np.arange(128) // 16)
    ex = np.zeros((128, 4, 8), np.float32)
    sv = np.arange(8, dtype=np.float32)
    ex[0:64, 0] = -sv
    ex[64:128, 0] = sv
    ex[0:64, 1] = 7 - sv
    ex[64:128, 1] = sv
    ex[0:64, 2] = sv
    ex[64:128, 2] = -sv
    ex[0:64, 3] = sv + 1
    ex[64:128, 3] = 8 - sv
    mk = np.zeros((128, 2, 128), np.float32)
    mk[:, 0, :] = (s8[:, None] <= s8[None, :])
    mk[:, 1, :] = (s8[:, None] >= s8[None, :])
    io = np.ascontiguousarray(np.broadcast_to(np.arange(544, dtype=np.float32), (128, 544)))
    return ex, mk, io


import os
_SKIP = set(os.environ.get("MK_SKIP", "").split(","))


def build_program(n_layers=DEPTH):
    nc = bass.Bass("TRN2", target_bir_lowering=False)
    dt_in = lambda name, shape: nc.dram_tensor(name, list(shape), F32, kind="ExternalInput").ap()
    xr = dt_in("xr", [T, D])
    cT = dt_in("cT", [128, 16])
    w_ada = dt_in("w_ada", [DEPTH, D, 3 * D])
    b_ada = dt_in("b_ada", [DEPTH, 3 * D])
    pre_g = dt_in("pre_g", [DEPTH, D])
    post_g = dt_in("post_g", [DEPTH, D])
    w_in_even = dt_in("w_in_even", [2, D, 3584])
    w_out_even = dt_in("w_out_even", [2, D, D])
    ebias = dt_in("ebias", [2, 12, 128, 21 * 128])
    cF1 = dt_in("cF1", [64, 128])
    cH = dt_in("cH", [64, 64, 2, 128])
    cCH = dt_in("cCH", [128, 6, 128])
    cF256 = dt_in("cF256", [128, 2, 512])
    w_in_odd = dt_in("w_in_odd", [2, D, 2560])
    w_out_odd = dt_in("w_out_odd", [2, D, D])
    sgu_wT = dt_in("sgu_wT", [2, 128, 4, 128])
    sgu_gT = dt_in("sgu_gT", [2, 128, 4])
    sgu_b = dt_in("sgu_b", [2, 512])
    s5_lam1 = dt_in("s5_lam1", [2, 2, 128, 32])
    s5_ls = dt_in("s5_ls", [2, 2, 32])
    s5_b1 = dt_in("s5_b1", [2, 2, 128, 32, 16])
    s5_c1 = dt_in("s5_c1", [2, 2, 128, 32, 16])
    s5_drep = dt_in("s5_drep", [2, 128, 32])
    glu_w = dt_in("glu_w", [2, 512, 1024])
    glu_bT = dt_in("glu_bT", [2, 128, 8])
    cEXPS = dt_in("cEXPS", [128, 4, 8])
    cMASK = dt_in("cMASK", [128, 2, 128])
    cIOTA = dt_in("cIOTA", [128, 544])
    zs = nc.dram_tensor("zs", [8, 32, 16, 544], BF16).ap()
    cHb = nc.dram_tensor("cHb", [64, 64, 2, 128], BF16).ap()
    out = nc.dram_tensor("out", [TL, D], F32, kind="ExternalOutput").ap()
    xs = nc.dram_tensor("xs", [T, D], F32).ap()
    modscr = nc.dram_tensor("modscr", [DEPTH, 2, 3 * D], F32).ap()

    df = DF(nc)
    A = df.add
    _uid = [0]

    def uniq(name):
        _uid[0] += 1
        return "%s_%d" % (name, _uid[0])

    def dma(eng, o, i, r=(), w=()):
        if eng == "pool":
            A(eng, lambda e, o=o, i=i: e.dma_start(out=o, in_=i, max_dma_last_dim=2048), r, w, dma=True)
        else:
            A(eng, lambda e, o=o, i=i: e.dma_start(out=o, in_=i), r, w, dma=True)

    def xbkeys(n0, nn):
        return [("xb", t) for t in range(n0 // 128, (n0 + nn + 127) // 128)]

    NTILES = [(n * 512, 512) for n in range(8)] + [(4096, 256)]

    with contextlib.ExitStack() as gst:
        sbg = lambda name, shape, dt: gst.enter_context(nc.sbuf_tensor(uniq(name), shape, dt))
        psg = lambda name, shape, dt: gst.enter_context(nc.psum_tensor(name, shape, dt))
        xb = sbg("xb", [128, 8, T], BF16)
        mT = sbg("mT", [128, 8, T], BF16)
        idn = sbg("idn", [128, 128], BF16)
        idn32 = sbg("idn32", [128, 128], F32)
        bart = sbg("bart", [128, 2], F32)
        mm = [psg("mm%d" % i, [128, 512], F32) for i in range(2)]
        stp = [psg("stp%d" % i, [128, 1024], F32) for i in range(2)]
        pv = psg("pv", [128, 512], F32)
        trp = psg("trp", [128, 8, 128], BF16)
        mmi = [0]

        def next_mm():
            mmi[0] ^= 1
            return mm[mmi[0]], "mm%d" % mmi[0]

        A("pool", lambda e: e.memset(idn32[:], 1.0), (), ["idn32"])
        A("pool", lambda e: e.affine_select(out=idn32[:], in_=idn32[:], pattern=[[-1, 128]], compare_op=ALU.is_equal,
                                            fill=0.0, base=0, channel_multiplier=1), ["idn32"], ["idn32"])
        A("dve", lambda e: e.tensor_copy(out=idn[:], in_=idn32[:]), ["idn32"], ["idn"])

        with contextlib.ExitStack() as st:
            sb = lambda name, shape, dt: st.enter_context(nc.sbuf_tensor(uniq(name), shape, dt))
            c32 = sb("c32", [128, 16], F32)
            sc = sb("sc", [128, 16], F32)
            LC = sb("LC", [128, 8, 64], BF16)
            wa = [sb("wa%d" % i, [128, 8, 512], BF16) for i in range(2)]
            bada = sb("bada", [64, 3 * D], F32)
            modrow = sb("modrow", [64, 3 * D], F32)
            for g8_ in range(8):
                dma("pool", cHb[:, g8_ * 8:(g8_ + 1) * 8, :, :], cH[:, g8_ * 8:(g8_ + 1) * 8, :, :], (), [("cHb", g8_)])
            dma("sp", c32[:], cT, (), ["c32"])
            A("act", lambda e: e.activation(out=sc[:], in_=c32[:], func=AF.Silu), ["c32"], ["sc"])
            A("pool", lambda e: e.memset(LC[:], 0.0), (), ["LC"])
            A("dve", lambda e: e.tensor_copy(out=LC[:, :, 0:1], in_=sc[:, 0:8].rearrange("p (k o) -> p k o", o=1)), ["sc", "LC"], ["LC"])
            A("dve", lambda e: e.tensor_copy(out=LC[:, :, 32:33], in_=sc[:, 8:16].rearrange("p (k o) -> p k o", o=1)), ["sc", "LC"], ["LC"])
            wi = 0
            for L in range(n_layers):
                dma("sp", bada[:], b_ada[L:L + 1, :].partition_broadcast(64), (), ["bada"])
                for n in range(6):
                    wt = wa[wi % 2]
                    wk = "wa%d" % (wi % 2)
                    wi += 1
                    dma("pool", wt[:], w_ada[L].rearrange("(k p) n -> p k n", p=128)[:, :, n * 512:(n + 1) * 512], (), [wk])
                    bank, bk = next_mm()
                    for k in range(8):
                        A("pe", lambda e, bank=bank, wt=wt, k=k: e.matmul(bank[0:64, :], lhsT=LC[:, k, :], rhs=wt[:, k, :], start=(k == 0), stop=(k == 7)),
                          ["LC", wk], [bk])
                    A("dve", lambda e, bank=bank, n=n: e.tensor_tensor(out=modrow[:, n * 512:(n + 1) * 512], in0=bank[0:64, :], in1=bada[:, n * 512:(n + 1) * 512], op=ALU.add),
                      [bk, "bada"], ["modrow"])
                dma("sp", modscr[L, 0:1, :], modrow[0:1, :], ["modrow"], [("modscr", L)])
                dma("sp", modscr[L, 1:2, :], modrow[32:33, :], ["modrow"], [("modscr", L)])
        df.barrier(bart[:, 0:1])

        def rstd_from_ss(ssum, rs, keys_in, key_out, scale):
            A("dve", lambda e: e.tensor_scalar(out=rs, in0=ssum, scalar1=scale, scalar2=EPS, op0=ALU.mult, op1=ALU.add), keys_in, [key_out])
            A("act", lambda e: e.activation(out=rs, in_=rs, func=AF.Sqrt), [key_out], [key_out])
            A("dve", lambda e: e.reciprocal(out=rs, in_=rs), [key_out], [key_out])

        def prenorm_old(L):
            src = xr if L == 0 else xs
            with contextlib.ExitStack() as st:
                sb = lambda name, shape, dt: st.enter_context(nc.sbuf_tensor(uniq(name), shape, dt))
                xt = [sb("xt%d" % i, [128, D], F32) for i in range(3)]
                tmp = [sb("ptmp%d" % i, [128, D], F32) for i in range(2)]
                hl = [sb("hl%d" % i, [128, D], BF16) for i in range(2)]
                junk = sb("junk", [128, D], BF16)
                gsc = [sb("gsc%d" % i, [128, D], F32) for i in range(2)]
                shb = [sb("shb%d" % i, [128, D], F32) for i in range(2)]
                pgb = sb("pgb", [128, D], F32)
                stat = sb("pstat", [128, 4 * NT], F32)
                dma("sp", pgb[:], pre_g[L:L + 1, :].partition_broadcast(128), (), ["pgb"])
                for w in range(2):
                    dma("sp", gsc[w][:], modscr[L, w:w + 1, D:2 * D].partition_broadcast(128), [("modscr", L)], ["gsc%d" % w])
                    dma("sp", shb[w][:], modscr[L, w:w + 1, 0:D].partition_broadcast(128), [("modscr", L)], ["shb%d" % w])
                    A("dve", lambda e, w=w: e.scalar_tensor_tensor(out=gsc[w][:], in0=gsc[w][:], scalar=1.0, in1=pgb[:], op0=ALU.add, op1=ALU.mult),
                      ["gsc%d" % w, "pgb"], ["gsc%d" % w])
                for t in range(NT):
                    w = 0 if t < 32 else 1
                    x_ = xt[t % 3]
                    xk = "xt%d" % (t % 3)
                    tm = tmp[t % 2]
                    tk = "ptmp%d" % (t % 2)
                    h_ = hl[t % 2]
                    hk = "hl%d" % (t % 2)
                    ss = stat[:, 4 * t:4 * t + 1]
                    rs = stat[:, 4 * t + 1:4 * t + 2]
                    dma("sp", x_[:], src[t * 128:(t + 1) * 128, :], [("xres", t)], [xk])
                    A("act", lambda e, x_=x_, ss=ss: e.activation(out=junk[:], in_=x_[:], func=AF.Square, accum_out=ss), [xk], ["junk", ("pss", t)])
                    rstd_from_ss(ss, rs, [("pss", t)], ("prs", t), 1.0 / D)
                    A("dve", lambda e, x_=x_, rs=rs, tm=tm, w=w: e.scalar_tensor_tensor(out=tm[:], in0=x_[:], scalar=rs, in1=gsc[w][:], op0=ALU.mult, op1=ALU.mult),
                      [xk, ("prs", t), "gsc%d" % w], [tk])
                    A("pool", lambda e, tm=tm, h_=h_, w=w: e.tensor_tensor(out=h_[:], in0=tm[:], in1=shb[w][:], op=ALU.add), [tk, "shb%d" % w], [hk])
                    for k in range(8):
                        A("pe", lambda e, h_=h_, k=k: e.transpose(trp[:, k, :], h_[:, k * 128:(k + 1) * 128], idn[:]), [hk, "idn"], ["trp"])
                    if t % 2 == 0:
                        A("act", lambda e, t=t: e.copy(out=xb[:, :, t * 128:(t + 1) * 128], in_=trp[:]), ["trp"], [("xb", t)])
                    else:
                        A("dve", lambda e, t=t: e.tensor_copy(out=xb[:, :, t * 128:(t + 1) * 128], in_=trp[:]), ["trp"], [("xb", t)])
            df.barrier(bart[:, 0:1])

        def post_old(L, last):
            src = xr if L == 0 else xs
            j = L // 2
            wo_d = w_out_even[j] if L % 2 == 0 else w_out_odd[j]
            ntile = 32 if last else NT
            with contextlib.ExitStack() as st:
                sb = lambda name, shape, dt: st.enter_context(nc.sbuf_tensor(uniq(name), shape, dt))
                wo = sb("wo", [128, 8, D], BF16)
                xt = [sb("qxt%d" % i, [128, D], F32) for i in range(2)]
                t1 = [sb("qt1%d" % i, [128, D], F32) for i in range(2)]
                t2 = [sb("qt2%d" % i, [128, D], F32) for i in range(2)]
                junk = sb("qjunk", [128, 512], BF16)
                gp = [sb("gp%d" % i, [128, D], F32) for i in range(2)]
                pgb = sb("qpgb", [128, D], F32)
                stat = sb("qstat", [128, 4 * NT], F32)
                for h in range(2):
                    dma("pool", wo[:, :, h * 512:(h + 1) * 512], wo_d.rearrange("(k p) n -> p k n", p=128)[:, :, h * 512:(h + 1) * 512], (), ["wo"])
                dma("sp", pgb[:], post_g[L:L + 1, :].partition_broadcast(128), (), ["qpgb"])
                for w in range(2):
                    dma("sp", gp[w][:], modscr[L, w:w + 1, 2 * D:3 * D].partition_broadcast(128), [("modscr", L)], ["gp%d" % w])
                    A("dve", lambda e, w=w: e.tensor_tensor(out=gp[w][:], in0=gp[w][:], in1=pgb[:], op=ALU.mult), ["gp%d" % w, "qpgb"], ["gp%d" % w])
                for t in range(ntile):
                    w = 0 if t < 32 else 1
                    yps = stp[t % 2]
                    yk = "stp%d" % (t % 2)
                    x_ = xt[t % 2]
                    xk = "qxt%d" % (t % 2)
                    a_ = t1[t % 2]
                    ak = "qt1%d" % (t % 2)
                    b_ = t2[t % 2]
                    bk = "qt2%d" % (t % 2)
                    for h in range(2):
                        for k in range(8):
                            A("pe", lambda e, yps=yps, h=h, k=k, t=t: e.matmul(yps[:, h * 512:(h + 1) * 512], lhsT=mT[:, k, t * 128:(t + 1) * 128], rhs=wo[:, k, h * 512:(h + 1) * 512], start=(k == 0), stop=(k == 7)),
                              [("mT", k, t), "wo"], [yk])
                    dma("sp", x_[:], src[t * 128:(t + 1) * 128, :], [("xres", t)], [xk])
                    for h in range(2):
                        A("act", lambda e, yps=yps, h=h, t=t: e.activation(out=junk[:], in_=yps[:, h * 512:(h + 1) * 512], func=AF.Square, accum_out=stat[:, 4 * t + h:4 * t + h + 1]),
                          [yk], ["qjunk", ("qss", t, h)])
                    A("dve", lambda e, t=t: e.tensor_tensor(out=stat[:, 4 * t + 2:4 * t + 3], in0=stat[:, 4 * t:4 * t + 1], in1=stat[:, 4 * t + 1:4 * t + 2], op=ALU.add),
                      [("qss", t, 0), ("qss", t, 1)], [("qs2", t)])
                    rs = stat[:, 4 * t + 3:4 * t + 4]
                    rstd_from_ss(stat[:, 4 * t + 2:4 * t + 3], rs, [("qs2", t)], ("qrs", t), 1.0 / D)
                    A("dve", lambda e, yps=yps, a_=a_, w=w: e.tensor_tensor(out=a_[:], in0=yps[:], in1=gp[w][:], op=ALU.mult), [yk, "gp%d" % w], [ak])
                    A("act", lambda e, a_=a_, b_=b_, rs=rs: e.activation(out=b_[:], in_=a_[:], func=AF.Copy, scale=rs), [ak, ("qrs", t)], [bk])
                    A("pool", lambda e, b_=b_, x_=x_: e.tensor_tensor(out=b_[:], in0=b_[:], in1=x_[:], op=ALU.add), [bk, xk], [bk])
                    dst = out[t * 128:(t + 1) * 128, :] if (last and t < 32) else xs[t * 128:(t + 1) * 128, :]
                    dma("sp", dst, b_[:], [bk], [("xres", t)])
            df.barrier(bart[:, 0:1])

        def prenorm(L):
            src = xr if L == 0 else xs
            with contextlib.ExitStack() as st:
                sb = lambda name, shape, dt: st.enter_context(nc.sbuf_tensor(uniq(name), shape, dt))
                NX = 6
                xt = [sb("xt%d" % i, [128, D], F32) for i in range(NX)]
                tmp = [sb("ptmp%d" % i, [128, D], F32) for i in range(2)]
                hl = [sb("hl%d" % i, [128, D], BF16) for i in range(2)]
                junk = sb("junk", [128, D], BF16)
                gsc = [sb("gsc%d" % i, [128, D], F32) for i in range(2)]
                shb = [sb("shb%d" % i, [128, D], F32) for i in range(2)]
                pgb = sb("pgb", [128, D], F32)
                stat = sb("pstat", [128, 4 * NT], F32)
                dma("sp", pgb[:], pre_g[L:L + 1, :].partition_broadcast(128), (), ["pgb"])
                for w in range(2):
                    dma("sp", gsc[w][:], modscr[L, w:w + 1, D:2 * D].partition_broadcast(128), [("modscr", L)], ["gsc%d" % w])
                    dma("sp", shb[w][:], modscr[L, w:w + 1, 0:D].partition_broadcast(128), [("modscr", L)], ["shb%d" % w])
                    A("dve", lambda e, w=w: e.scalar_tensor_tensor(out=gsc[w][:], in0=gsc[w][:], scalar=1.0, in1=pgb[:], op0=ALU.add, op1=ALU.mult),
                      ["gsc%d" % w, "pgb"], ["gsc%d" % w])
                X = lambda t: (xt[t % NX], "xt%d" % (t % NX))
                SS = lambda t: stat[:, 4 * t:4 * t + 1]
                RS = lambda t: stat[:, 4 * t + 1:4 * t + 2]

                def p_load(t):
                    x_, xk = X(t)
                    dma("sp", x_[:], src[t * 128:(t + 1) * 128, :], [("xres", t)], [xk])

                def p_sq(t):
                    x_, xk = X(t)
                    A("act", lambda e, x_=x_, ss=SS(t): e.activation(out=junk[:], in_=x_[:], func=AF.Square, accum_out=ss), [xk], ["junk", ("pss", t)])

                def p_r1(t):
                    A("dve", lambda e, t=t: e.tensor_scalar(out=RS(t), in0=SS(t), scalar1=1.0 / D, scalar2=EPS, op0=ALU.mult, op1=ALU.add), [("pss", t)], [("prs", t)])

                def p_r2(t):
                    A("act", lambda e, t=t: e.activation(out=RS(t), in_=RS(t), func=AF.Sqrt), [("prs", t)], [("prs", t)])

                def p_r3(t):
                    A("dve", lambda e, t=t: e.reciprocal(out=RS(t), in_=RS(t)), [("prs", t)], [("prs", t)])

                def p_stt(t):
                    w = 0 if t < 32 else 1
                    x_, xk = X(t)
                    tm, tk = tmp[t % 2], "ptmp%d" % (t % 2)
                    A("dve", lambda e, x_=x_, t=t, tm=tm, w=w: e.scalar_tensor_tensor(out=tm[:], in0=x_[:], scalar=RS(t), in1=gsc[w][:], op0=ALU.mult, op1=ALU.mult),
                      [xk, ("prs", t), "gsc%d" % w], [tk])

                def p_add(t):
                    w = 0 if t < 32 else 1
                    tm, tk = tmp[t % 2], "ptmp%d" % (t % 2)
                    h_, hk = hl[t % 2], "hl%d" % (t % 2)
                    A("pool", lambda e, tm=tm, h_=h_, w=w: e.tensor_tensor(out=h_[:], in0=tm[:], in1=shb[w][:], op=ALU.add), [tk, "shb%d" % w], [hk])

                def p_tr(t):
                    h_, hk = hl[t % 2], "hl%d" % (t % 2)
                    for k in range(8):
                        A("pe", lambda e, h_=h_, k=k: e.transpose(trp[:, k, :], h_[:, k * 128:(k + 1) * 128], idn[:]), [hk, "idn"], ["trp"])

                def p_ev(t):
                    if t % 2 == 0:
                        A("act", lambda e, t=t: e.copy(out=xb[:, :, t * 128:(t + 1) * 128], in_=trp[:]), ["trp"], [("xb", t)])
                    else:
                        A("dve", lambda e, t=t: e.tensor_copy(out=xb[:, :, t * 128:(t + 1) * 128], in_=trp[:]), ["trp"], [("xb", t)])

                pipeline([p_load, p_sq, p_r1, p_r2, p_r3, p_stt, p_add, p_tr, p_ev], NT, "pre")
            df.barrier(bart[:, 0:1])

        def post(L, last):
            src = xr if L == 0 else xs
            j = L // 2
            wo_d = w_out_even[j] if L % 2 == 0 else w_out_odd[j]
            ntile = 32 if last else NT
            with contextlib.ExitStack() as st:
                sb = lambda name, shape, dt: st.enter_context(nc.sbuf_tensor(uniq(name), shape, dt))
                wo = sb("wo", [128, 8, D], BF16)
                NA_, NB_, NXq = 4, 3, 3
                xt = [sb("qxt%d" % i, [128, D], F32) for i in range(NXq)]
                t1 = [sb("qt1%d" % i, [128, D], F32) for i in range(NA_)]
                t2 = [sb("qt2%d" % i, [128, D], F32) for i in range(NB_)]
                junk = sb("qjunk", [128, D], BF16)
                gp = [sb("gp%d" % i, [128, D], F32) for i in range(2)]
                pgb = sb("qpgb", [128, D], F32)
                stat = sb("qstat", [128, 4 * NT], F32)
                for h in range(2):
                    dma("pool", wo[:, :, h * 512:(h + 1) * 512], wo_d.rearrange("(k p) n -> p k n", p=128)[:, :, h * 512:(h + 1) * 512], (), ["wo"])
                dma("sp", pgb[:], post_g[L:L + 1, :].partition_broadcast(128), (), ["qpgb"])
                for w in range(2):
                    dma("sp", gp[w][:], modscr[L, w:w + 1, 2 * D:3 * D].partition_broadcast(128), [("modscr", L)], ["gp%d" % w])
                    A("dve", lambda e, w=w: e.tensor_tensor(out=gp[w][:], in0=gp[w][:], in1=pgb[:], op=ALU.mult), ["gp%d" % w, "qpgb"], ["gp%d" % w])
                YP = lambda t: (stp[t % 2], "stp%d" % (t % 2))
                XQ = lambda t: (xt[t % NXq], "qxt%d" % (t % NXq))
                TA = lambda t: (t1[t % NA_], "qt1%d" % (t % NA_))
                TB = lambda t: (t2[t % NB_], "qt2%d" % (t % NB_))
                SS = lambda t: stat[:, 4 * t:4 * t + 1]
                RS = lambda t: stat[:, 4 * t + 1:4 * t + 2]

                def q_mm(t):
                    yps, yk = YP(t)
                    for h in range(2):
                        for k in range(8):
                            A("pe", lambda e, yps=yps, h=h, k=k, t=t: e.matmul(yps[:, h * 512:(h + 1) * 512], lhsT=mT[:, k, t * 128:(t + 1) * 128], rhs=wo[:, k, h * 512:(h + 1) * 512], start=(k == 0), stop=(k == 7)),
                              [("mT", k, t), "wo"], [yk])

                def q_sq(t):
                    yps, yk = YP(t)
                    a_, ak = TA(t)
                    w = 0 if t < 32 else 1
                    for h in range(2):
                        A("act", lambda e, yps=yps, t=t, h=h: e.activation(out=junk[:, h * 512:(h + 1) * 512], in_=yps[:, h * 512:(h + 1) * 512], func=AF.Square, accum_out=stat[:, 4 * t + 2 + h:4 * t + 3 + h]), [yk], ["qjunk", ("qssh", t, h)])
                    A("dve", lambda e, yps=yps, a_=a_, w=w: e.tensor_tensor(out=a_[:], in0=yps[:], in1=gp[w][:], op=ALU.mult), [yk, "gp%d" % w, ("qssh", t, 0), ("qssh", t, 1)], [ak])

                def q_r1(t):
                    A("dve", lambda e, t=t: e.tensor_tensor(out=SS(t), in0=stat[:, 4 * t + 2:4 * t + 3], in1=stat[:, 4 * t + 3:4 * t + 4], op=ALU.add), [("qssh", t, 0), ("qssh", t, 1)], [("qss", t)])
                    A("dve", lambda e, t=t: e.tensor_scalar(out=RS(t), in0=SS(t), scalar1=1.0 / D, scalar2=EPS, op0=ALU.mult, op1=ALU.add), [("qss", t)], [("qrs", t)])

                def q_r2(t):
                    A("act", lambda e, t=t: e.activation(out=RS(t), in_=RS(t), func=AF.Sqrt), [("qrs", t)], [("qrs", t)])
                    x_, xk = XQ(t)
                    dma("sp", x_[:], src[t * 128:(t + 1) * 128, :], [("xres", t)], [xk])

                def q_r3(t):
                    A("dve", lambda e, t=t: e.reciprocal(out=RS(t), in_=RS(t)), [("qrs", t)], [("qrs", t)])

                def q_sc(t):
                    a_, ak = TA(t)
                    b_, bk = TB(t)
                    A("act", lambda e, a_=a_, b_=b_, t=t: e.activation(out=b_[:], in_=a_[:], func=AF.Copy, scale=RS(t)), [ak, ("qrs", t)], [bk])

                def q_add(t):
                    b_, bk = TB(t)
                    x_, xk = XQ(t)
                    A("pool", lambda e, b_=b_, x_=x_: e.tensor_tensor(out=b_[:], in0=b_[:], in1=x_[:], op=ALU.add), [bk, xk], [bk])

                def q_st(t):
                    b_, bk = TB(t)
                    dst = out[t * 128:(t + 1) * 128, :] if (last and t < 32) else xs[t * 128:(t + 1) * 128, :]
                    dma("sp", dst, b_[:], [bk], [("xres", t)])

                pipeline([q_mm, q_sq, q_r1, q_r2, q_r3, q_sc, q_add, q_st], ntile, "post")
            df.barrier(bart[:, 0:1])

        def pipeline(stages, N, key=""):
            if "noskew" in _SKIP or ("noskew_" + key) in _SKIP:
                for n_ in range(N):
                    for st_ in stages:
                        st_(n_)
                return
            K_ = len(stages)
            for step in range(N + K_ - 1):
                for k_ in reversed(range(K_)):
                    n_ = step - k_
                    if 0 <= n_ < N:
                        stages[k_](n_)

        def inproj_fm(wt, wk, tiles, evac):
            for (n0, nn) in tiles:
                bank, bk = next_mm()
                for k in range(8):
                    A("pe", lambda e, bank=bank, k=k, n0=n0, nn=nn: e.matmul(bank[:, 0:nn], lhsT=wt[:, k, :], rhs=xb[:, k, n0:n0 + nn], start=(k == 0), stop=(k == 7)),
                      [wk] + xbkeys(n0, nn), [bk])
                evac(bank, bk, n0, nn)

        def even_mixer(L):
            j = L // 2
            Wd = w_in_even[j].rearrange("(k p) n -> p k n", p=128)
            with contextlib.ExitStack() as st:
                sb = lambda name, shape, dt: st.enter_context(nc.sbuf_tensor(uniq(name), shape, dt))
                wch = [sb("wch%d" % i, [128, 8, 128], BF16) for i in range(3)]
                wci = [0]

                def load_w(c0, ncols=128):
                    i = wci[0] % 3
                    wci[0] += 1
                    dma("pool", wch[i][:, :, 0:ncols], Wd[:, :, c0:c0 + ncols], (), ["wch%d" % i])
                    return wch[i], "wch%d" % i

                with contextlib.ExitStack() as st2:
                    sb2 = lambda name, shape, dt: st2.enter_context(nc.sbuf_tensor(uniq(name), shape, dt))
                    sga = sb2("sga", [128, T], BF16)
                    X = sb2("fX", [64, 64, 128], BF16)
                    Z = sb2("fZ", [64, 64, 128], BF16)
                    Pg = [sb2("fP%d" % i, [128, 8, 128], BF16) for i in range(2)]
                    Hs = [sb2("fH%d" % i, [64, 8, 2, 128], BF16) for i in range(2)]
                    F1 = sb2("fF1", [64, 128], BF16)
                    CH = sb2("fCH", [128, 6, 128], BF16)
                    F256 = sb2("fF256", [128, 2, 512], BF16)
                    Xc = sb2("fXc", [128, 2, 128], BF16)
                    Pc = sb2("fPc", [128, 512], BF16)
                    dma("pool", F1[:], cF1, (), ["fF1"])
                    dma("pool", CH[:], cCH, (), ["fCH"])
                    dma("pool", F256[:], cF256, (), ["fF256"])
                    hcount = 0
                    pcount = 0
                    for half in range(2):
                        wt, wk = load_w(256 + half * 128)
                        inproj_fm(wt, wk, NTILES, lambda bank, bk, n0, nn: A(
                            "act", lambda e: e.activation(out=sga[:, n0:n0 + nn], in_=bank[:, 0:nn], func=AF.Silu), [bk], [("sga", n0)]))
                        sgakeys = [("sga", n0) for (n0, nn) in NTILES]
                        wt, wk = load_w(half * 128)
                        for g4 in range(16):
                            bank, bk = next_mm()
                            for q in range(4):
                                t2 = g4 * 4 + q
                                for k in range(8):
                                    A("pe", lambda e, bank=bank, q=q, k=k, t2=t2, wt=wt: e.matmul(bank[0:64, q * 128:(q + 1) * 128], lhsT=xb[:, k, t2:TL:64], rhs=wt[:, k, :], start=(k == 0), stop=(k == 7)),
                                      [wk] + [("xb", t) for t in range(32)], [bk])
                            A("act", lambda e, bank=bank, g4=g4: e.copy(out=X[:, g4 * 4:(g4 + 1) * 4, :], in_=bank[0:64, :].rearrange("p (q c) -> p q c", q=4)), [bk], ["fX"])
                        for tl in range(2):
                            bank, bk = next_mm()
                            for k in range(8):
                                A("pe", lambda e, bank=bank, k=k, tl=tl, wt=wt: e.matmul(bank[:, 0:128], lhsT=xb[:, k, TL + tl * 128:TL + (tl + 1) * 128], rhs=wt[:, k, :], start=(k == 0), stop=(k == 7)),
                                  [wk, ("xb", 32 + tl)], [bk])
                            A("dve", lambda e, bank=bank, tl=tl: e.tensor_copy(out=Xc[:, tl, :], in_=bank[:, 0:128]), [bk], ["fXc"])
                        bank, bk = next_mm()
                        for tl in range(2):
                            A("pe", lambda e, bank=bank, tl=tl: e.matmul(bank[:, :], lhsT=Xc[:, tl, :], rhs=F256[:, tl, :], start=(tl == 0), stop=(tl == 1)), ["fXc", "fF256"], [bk])
                        A("dve", lambda e, bank=bank: e.tensor_copy(out=Pc[:], in_=bank[:, :]), [bk], ["fPc"])
                        bank, bk = next_mm()
                        A("pe", lambda e, bank=bank: e.matmul(bank[:, 0:256], lhsT=CH[:, 4, :], rhs=Pc[:, 0:256], start=True, stop=False), ["fPc", "fCH"], [bk])
                        A("pe", lambda e, bank=bank: e.matmul(bank[:, 0:256], lhsT=CH[:, 5, :], rhs=Pc[:, 256:512], start=False, stop=True), ["fPc", "fCH"], [bk])
                        A("dve", lambda e, bank=bank, half=half: e.tensor_tensor(out=mT[:, half, TL:T], in0=bank[:, 0:256], in1=sga[:, TL:T], op=ALU.mult),
                          [bk] + sgakeys, [("mT", half, 32), ("mT", half, 33)])
                        for qd in range(2):
                            pb = qd * 64
                            for c4 in range(16):
                                bank, bk = next_mm()
                                for q in range(4):
                                    c = qd * 64 + c4 * 4 + q
                                    A("pe", lambda e, bank=bank, q=q, c=c: e.matmul(bank[0:64, q * 128:(q + 1) * 128], lhsT=X[:, :, c], rhs=F1[:, :], start=True, stop=True),
                                      ["fX", "fF1"], [bk])
                                if c4 % 2 == 0:
                                    A("act", lambda e, bank=bank, c4=c4: e.copy(out=Z[:, c4 * 4:(c4 + 1) * 4, :], in_=bank[0:64, :].rearrange("p (q c) -> p q c", q=4)), [bk], ["fZ"])
                                else:
                                    A("dve", lambda e, bank=bank, c4=c4: e.tensor_copy(out=Z[:, c4 * 4:(c4 + 1) * 4, :], in_=bank[0:64, :].rearrange("p (q c) -> p q c", q=4)), [bk], ["fZ"])
                            for g8 in range(8):
                                Hb = Hs[hcount % 2]
                                hk = "fH%d" % (hcount % 2)
                                hcount += 1
                                dma("sp", Hb[:], cHb[:, g8 * 8:(g8 + 1) * 8, :, :], (), [hk])
                                Pb = Pg[pcount % 2]
                                pk = "fP%d" % (pcount % 2)
                                pcount += 1
                                for b2 in range(2):
                                    bank, bk = next_mm()
                                    for q in range(4):
                                        kk = b2 * 4 + q
                                        k1 = g8 * 8 + kk
                                        for ri in range(2):
                                            A("pe", lambda e, bank=bank, q=q, kk=kk, k1=k1, ri=ri, Hb=Hb: e.matmul(bank[0:64, q * 128:(q + 1) * 128], lhsT=Z[:, :, ri * 64 + k1], rhs=Hb[:, kk, ri, :], start=(ri == 0), stop=(ri == 1)),
                                              ["fZ", hk], [bk])
                                    A("act" if b2 == 0 else "dve",
                                      (lambda e, bank=bank, b2=b2, Pb=Pb: e.copy(out=Pb[0:64, b2 * 4:(b2 + 1) * 4, :], in_=bank[0:64, :].rearrange("p (q c) -> p q c", q=4))) if b2 == 0 else
                                      (lambda e, bank=bank, b2=b2, Pb=Pb: e.tensor_copy(out=Pb[0:64, b2 * 4:(b2 + 1) * 4, :], in_=bank[0:64, :].rearrange("p (q c) -> p q c", q=4))),
                                      [bk], [pk])
                                bank, bk = next_mm()
                                ia, ib = (0, 1) if qd == 0 else (2, 3)
                                mcols = 64 if qd == 0 else 128
                                A("pe", lambda e, bank=bank, Pb=Pb, ia=ia, mcols=mcols: e.matmul(bank[0:mcols, :], lhsT=CH[0:64, ia, 0:mcols], rhs=Pb[0:64, :, 0:64], start=True, stop=False), [pk, "fCH"], [bk])
                                A("pe", lambda e, bank=bank, Pb=Pb, ib=ib, mcols=mcols: e.matmul(bank[0:mcols, :], lhsT=CH[0:64, ib, 0:mcols], rhs=Pb[0:64, :, 64:128], start=False, stop=True), [pk, "fCH"], [bk])
                                A("dve", lambda e, bank=bank, pb=pb, half=half, g8=g8: e.tensor_tensor(
                                    out=mT[pb:pb + 64, half, 0:TL].rearrange("p (k2 k1) -> p k1 k2", k1=64)[:, g8 * 8:(g8 + 1) * 8, :],
                                    in0=bank[pb:pb + 64, :].rearrange("p (a b) -> p a b", a=8),
                                    in1=sga[pb:pb + 64, 0:TL].rearrange("p (k2 k1) -> p k1 k2", k1=64)[:, g8 * 8:(g8 + 1) * 8, :], op=ALU.mult),
                                  [bk] + sgakeys, [("mT", half, t) for t in range(32)])
                df.barrier(bart[:, 0:1])

                with contextlib.ExitStack() as st2:
                    sb2 = lambda name, shape, dt: st2.enter_context(nc.sbuf_tensor(uniq(name), shape, dt))
                    qT = sb2("qT", [128, T], BF16)
                    kT = sb2("kT", [128, T], BF16)
                    sgb = sb2("sgb", [128, T], BF16)
                    vaug = sb2("vaug", [128, NT, 130], BF16)
                    eb32 = sb2("eb32", [128, 7 * 128], F32)
                    Eh = [sb2("Eh%d" % i, [128, 21 * 128], BF16) for i in range(2)]
                    PT = [sb2("PT%d" % i, [128, 7 * 128], BF16) for i in range(3)]
                    onb = sb2("onb", [128, NT, 128], BF16)
                    rden = sb2("rden", [128, 64], F32)
                    A("pool", lambda e: e.memset(vaug[:], 1.0), (), [("vaug", t4) for t4 in range(9)])
                    pti = 0
                    rdi = 0
                    for hp in range(6):
                        wt, wk = load_w(512 + hp * 128)
                        inproj_fm(wt, wk, NTILES, lambda bank, bk, n0, nn: A(
                            "act", lambda e: e.copy(out=qT[:, n0:n0 + nn], in_=bank[:, 0:nn]), [bk], [("qT", n0)]))
                        wt, wk = load_w(1280 + hp * 128)
                        inproj_fm(wt, wk, NTILES, lambda bank, bk, n0, nn: A(
                            "dve", lambda e: e.tensor_copy(out=kT[:, n0:n0 + nn], in_=bank[:, 0:nn]), [bk], [("kT", n0)]))
                        wt, wk = load_w(2816 + hp * 128)
                        inproj_fm(wt, wk, NTILES, lambda bank, bk, n0, nn: A(
                            "act", lambda e: e.activation(out=sgb[:, n0:n0 + nn], in_=bank[:, 0:nn], func=AF.Silu), [bk], [("sgb", n0)]))
                        wt, wk = load_w(2048 + hp * 128)
                        for t4 in range(9):
                            bank, bk = next_mm()
                            nq = 4 if t4 < 8 else 2
                            for q in range(nq):
                                t = t4 * 4 + q
                                for k in range(8):
                                    A("pe", lambda e, bank=bank, q=q, k=k, t=t, wt=wt: e.matmul(bank[:, q * 128:(q + 1) * 128], lhsT=xb[:, k, t * 128:(t + 1) * 128], rhs=wt[:, k, :], start=(k == 0), stop=(k == 7)),
                                      [wk, ("xb", t)], [bk])
                            for hh in range(2):
                                A("dve" if hh == 0 else "act",
                                  (lambda e, bank=bank, t4=t4, nq=nq, hh=hh: e.tensor_copy(out=vaug[:, t4 * 4:t4 * 4 + nq, hh * 65:hh * 65 + 64], in_=bank[:, 0:nq * 128].rearrange("p (q c) -> p q c", q=nq)[:, :, hh * 64:(hh + 1) * 64])) if hh == 0 else
                                  (lambda e, bank=bank, t4=t4, nq=nq, hh=hh: e.copy(out=vaug[:, t4 * 4:t4 * 4 + nq, hh * 65:hh * 65 + 64], in_=bank[:, 0:nq * 128].rearrange("p (q c) -> p q c", q=nq)[:, :, hh * 64:(hh + 1) * 64])),
                                  [bk], [("vaug", t4)])
                        for hh in range(2):
                            h = hp * 2 + hh
                            E = Eh[hh]
                            ek = "Eh%d" % hh
                            for part in range(3):
                                dma("sp", eb32[:], ebias[j, h, :, part * 896:(part + 1) * 896], (), ["eb32"])
                                A("act", lambda e, E=E, part=part: e.activation(out=E[:, part * 896:(part + 1) * 896], in_=eb32[:], func=AF.Exp), ["eb32"], [ek])
                        its = [(hh, i) for hh in range(2) for i in range(NT)]

                        def geo(n):
                            hh, i = its[n]
                            if i < 32:
                                pbase, lt = na_pattern_base(i)
                                kts = lt + [32, 33]
                            else:
                                pbase, lt = None, []
                                kts = [32, 33]
                            return hh, i, pbase, lt, kts

                        def s_qk(n):
                            hh, i, pbase, lt, kts = geo(n)
                            hb = hh * 64
                            sp_ = stp[n % 2]
                            sk = "stp%d" % (n % 2)
                            for a_, kt in enumerate(kts):
                                A("pe", lambda e, sp_=sp_, a_=a_, kt=kt, i=i, hb=hb: e.matmul(sp_[:, a_ * 128:(a_ + 1) * 128], lhsT=kT[hb:hb + 64, kt * 128:(kt + 1) * 128], rhs=qT[hb:hb + 64, i * 128:(i + 1) * 128], start=True, stop=True),
                                  [("kT", (kt // 4) * 512), ("qT", (i // 4) * 512)], [sk])

                        def s_exp(n):
                            hh, i, pbase, lt, kts = geo(n)
                            nk = len(kts)
                            sp_ = stp[n % 2]
                            sk = "stp%d" % (n % 2)
                            P_ = PT[n % 3]
                            pk = "PT%d" % (n % 3)
                            A("act", lambda e, sp_=sp_, P_=P_, nk=nk: e.activation(out=P_[:, 0:nk * 128], in_=sp_[:, 0:nk * 128], func=AF.Exp, scale=0.125), [sk], [pk])

                        def s_mul(n):
                            hh, i, pbase, lt, kts = geo(n)
                            P_ = PT[n % 3]
                            pk = "PT%d" % (n % 3)
                            if lt:
                                nl = len(lt)
                                E = Eh[hh]
                                A("dve", lambda e, P_=P_, nl=nl, E=E, pbase=pbase: e.tensor_tensor(out=P_[:, 0:nl * 128], in0=P_[:, 0:nl * 128], in1=E[:, pbase * 128:(pbase + nl) * 128], op=ALU.mult),
                                  [pk, "Eh%d" % hh], [pk])

                        def s_pv(n):
                            hh, i, pbase, lt, kts = geo(n)
                            nk = len(kts)
                            P_ = PT[n % 3]
                            pk = "PT%d" % (n % 3)
                            pvb = mm[n % 2]
                            for a_, kt in enumerate(kts):
                                A("pe", lambda e, P_=P_, a_=a_, kt=kt, hh=hh, nk=nk, pvb=pvb: e.matmul(pvb[:, 0:65], lhsT=P_[:, a_ * 128:(a_ + 1) * 128], rhs=vaug[:, kt, hh * 65:hh * 65 + 65], start=(a_ == 0), stop=(a_ == nk - 1)),
                                  [pk, ("vaug", kt // 4)], ["mm%d" % (n % 2)])

                        def s_rec(n):
                            pvb = mm[n % 2]
                            rd = rden[:, n % 64:n % 64 + 1]
                            A("dve", lambda e, rd=rd, pvb=pvb: e.reciprocal(out=rd, in_=pvb[:, 64:65]), ["mm%d" % (n % 2)], [("rden", n % 64)])

                        def s_norm(n):
                            hh, i, pbase, lt, kts = geo(n)
                            pvb = mm[n % 2]
                            rd = rden[:, n % 64:n % 64 + 1]
                            A("dve", lambda e, i=i, hh=hh, rd=rd, pvb=pvb: e.tensor_scalar(out=onb[:, i, hh * 64:(hh + 1) * 64], in0=pvb[:, 0:64], scalar1=rd, scalar2=None, op0=ALU.mult), ["mm%d" % (n % 2), ("rden", n % 64)], [("on", i, hh)])

                        def burst(i):
                            if i % 8 == 7 or i == NT - 1:
                                i0 = (i // 8) * 8
                                return i0, i - i0 + 1
                            return None

                        def s_tr(n):
                            hh, i, pbase, lt, kts = geo(n)
                            if hh == 1 and burst(i):
                                i0, nb_ = burst(i)
                                for r_ in range(nb_):
                                    ii = i0 + r_
                                    A("pe", lambda e, ii=ii, r_=r_: e.transpose(trp[:, r_, :], onb[:, ii, :], idn[:]), [("on", ii, 0), ("on", ii, 1), "idn"], ["trp"])

                        def s_gate(n):
                            hh, i, pbase, lt, kts = geo(n)
                            if hh == 1 and burst(i):
                                i0, nb_ = burst(i)
                                A("dve", lambda e, i0=i0, nb_=nb_, hp=hp: e.tensor_tensor(out=mT[:, 2 + hp, i0 * 128:(i0 + nb_) * 128], in0=trp[:, 0:nb_, :].rearrange("p a b -> p (a b)"), in1=sgb[:, i0 * 128:(i0 + nb_) * 128], op=ALU.mult),
                                  ["trp"] + [("sgb", ((i0 + r_) // 4) * 512) for r_ in range(nb_)], [("mT", 2 + hp, i0 + r_) for r_ in range(nb_)])

                        pipeline([s_qk, s_exp, s_mul, s_pv, s_rec, s_norm, s_tr, s_gate], len(its), "att")
            df.barrier(bart[:, 0:1])

        MAG = 12582912.0
        TWO_PI = 2.0 * np.pi

        def odd_mixer(L):
            j = L // 2
            Wd = w_in_odd[j].rearrange("(k p) n -> p k n", p=128)
            Uv = mT[:, 0:4, :].rearrange("p c t -> p (c t)").rearrange("p (g b) -> p g b", b=544)
            PIECES = [(0, 256), (256, 256), (512, 32)]

            with contextlib.ExitStack() as st:
              if "s5a" not in _SKIP:
                sb = lambda name, shape, dt: st.enter_context(nc.sbuf_tensor(uniq(name), shape, dt))
                Ws = sb("Ws", [128, 8, 512], BF16)
                Stm = [sb("Stm%d" % i, [128, 32, 8, 16], BF16) for i in range(5)]
                dma("pool", Ws[:], Wd[:, :, 1536:2048], (), ["Ws"])
                its_a = [(bt, t8) for bt in range(5) for t8 in range(8)]

                def a_mm(n):
                    bt, t8 = its_a[n]
                    nb = 128 if bt < 4 else 32
                    tok0 = 1024 * bt
                    bank, bk = mm[n % 2], "mm%d" % (n % 2)
                    for k in range(8):
                        A("pe", lambda e, bank=bank, k=k, nb=nb, tok0=tok0, t8=t8: e.matmul(bank[0:nb, :], lhsT=xb[:, k, tok0 + t8:tok0 + 8 * nb:8], rhs=Ws[:, k, :], start=(k == 0), stop=(k == 7)),
                          ["Ws"] + xbkeys(tok0, 8 * nb), [bk])

                def a_ev(n):
                    bt, t8 = its_a[n]
                    nb = 128 if bt < 4 else 32
                    bank, bk = mm[n % 2], "mm%d" % (n % 2)
                    S_ = Stm[bt]
                    sk = ("Stm", bt, t8)
                    if n % 2 == 0:
                        A("act", lambda e, bank=bank, nb=nb, S_=S_, t8=t8: e.copy(out=S_[0:nb, :, t8, :], in_=bank[0:nb, :].rearrange("p (g m) -> p g m", m=16)), [bk], [sk])
                    else:
                        A("dve", lambda e, bank=bank, nb=nb, S_=S_, t8=t8: e.tensor_copy(out=S_[0:nb, :, t8, :], in_=bank[0:nb, :].rearrange("p (g m) -> p g m", m=16)), [bk], [sk])

                pipeline([a_mm, a_ev], len(its_a), "s5a")
                its_b = [(bt, g8) for bt in range(5) for g8 in range(4)]

                def a_tr(n):
                    bt, g8 = its_b[n]
                    nb = 128 if bt < 4 else 32
                    S_ = Stm[bt]
                    for q in range(8):
                        g = g8 * 8 + q
                        A("pe", lambda e, S_=S_, nb=nb, g=g, q=q: e.transpose(trp[:, q, 0:nb], S_[0:nb, g, :, :].rearrange("p a b -> p (a b)"), idn[0:nb, 0:nb]), [("Stm", bt, t8) for t8 in range(8)] + ["idn"], ["trp"])

                def a_ut(n):
                    bt, g8 = its_b[n]
                    nb = 128 if bt < 4 else 32
                    if n % 2 == 0:
                        A("act", lambda e, g8=g8, bt=bt, nb=nb: e.copy(out=Uv[:, g8 * 8:(g8 + 1) * 8, bt * 128:bt * 128 + nb], in_=trp[:, :, 0:nb]), ["trp"], ["U"])
                    else:
                        A("dve", lambda e, g8=g8, bt=bt, nb=nb: e.tensor_copy(out=Uv[:, g8 * 8:(g8 + 1) * 8, bt * 128:bt * 128 + nb], in_=trp[:, :, 0:nb]), ["trp"], ["U"])

                pipeline([a_tr, a_ut], len(its_b), "s5a2")
            df.barrier(bart[:, 0:1])

            with contextlib.ExitStack() as st:
              if "s5b" not in _SKIP:
                sb = lambda name, shape, dt: st.enter_context(nc.sbuf_tensor(uniq(name), shape, dt))
                V = lambda e: e
                lam_r = sb("lam_r", [128, 32], F32)
                lam_i = sb("lam_i", [128, 32], F32)
                ls1 = sb("ls1", [128, 32], F32)
                st_tmp = contextlib.ExitStack()
                sbt = lambda name, shape, dt: st_tmp.enter_context(nc.sbuf_tensor(uniq(name), shape, dt))
                cr1 = sb("cr1", [128, 32, 16], F32)
                ci1 = sb("ci1", [128, 32, 16], F32)
                exps = sb("exps", [128, 4, 8], F32)
                mask = sb("mask", [128, 2, 128], F32)
                iota = sb("iota", [128, 544], F32)
                drep = sb("drep", [128, 32], F32)
                sm = [sb("sm%d" % i, [128, 32], F32) for i in range(14)]
                Bbr = sb("Bbr", [128, 32, 16], F32)
                Bbi = sb("Bbi", [128, 32, 16], F32)
                Wr = sb("Wr", [128, 4, 8, 32], F32)
                Wi = sb("Wi", [128, 4, 8, 32], F32)
                rho8 = sb("rho8", [128, 32], F32)
                tt8 = sb("tt8", [128, 32], F32)
                br1 = sbt("br1", [128, 32, 16], F32)
                bi1 = sbt("bi1", [128, 32, 16], F32)
                tb1 = sbt("tb1", [128, 32, 16], F32)
                tb2 = sbt("tb2", [128, 32, 16], F32)
                EA = sbt("EA", [128, 4, 8, 32], F32)
                ET = sbt("ET", [128, 4, 8, 32], F32)
                tw = sbt("tw", [128, 4, 8, 32], F32)
                dma("sp", lam_r[:], s5_lam1[j, 0], (), ["lam_r"])
                dma("sp", lam_i[:], s5_lam1[j, 1], (), ["lam_i"])
                for d_ in range(2):
                    dma("sp", ls1[d_ * 64:(d_ + 1) * 64, :], s5_ls[j, d_:d_ + 1, :].partition_broadcast(64), (), ["ls1"])
                dma("sp", br1[:], s5_b1[j, 0], (), ["br1"])
                dma("sp", bi1[:], s5_b1[j, 1], (), ["bi1"])
                dma("sp", cr1[:], s5_c1[j, 0], (), ["cr1"])
                dma("sp", ci1[:], s5_c1[j, 1], (), ["ci1"])
                dma("sp", exps[:], cEXPS, (), ["exps"])
                dma("sp", mask[:], cMASK, (), ["mask"])
                dma("sp", iota[:], cIOTA, (), ["iota"])
                dma("sp", drep[:], s5_drep[j], (), ["drep"])
                PK = ["pre"]

                def dv(fn):
                    A("dve", fn, PK + ["lam_r", "lam_i", "ls1", "br1", "bi1", "cr1", "ci1", "exps", "mask", "iota", "drep"], PK)

                def ac(fn):
                    A("act", fn, PK, PK)

                def sincos(tt, sn, cs, tmp, tmp2):
                    dv(lambda e: e.tensor_scalar(out=tmp, in0=tt, scalar1=MAG, scalar2=MAG, op0=ALU.add, op1=ALU.subtract))
                    dv(lambda e: e.tensor_tensor(out=tmp, in0=tt, in1=tmp, op=ALU.subtract))
                    ac(lambda e: e.activation(out=sn, in_=tmp, func=AF.Sin, scale=TWO_PI))
                    dv(lambda e: e.tensor_scalar(out=tmp2, in0=tt, scalar1=0.25, scalar2=None, op0=ALU.add))
                    dv(lambda e: e.tensor_scalar(out=tmp, in0=tmp2, scalar1=MAG, scalar2=MAG, op0=ALU.add, op1=ALU.subtract))
                    dv(lambda e: e.tensor_tensor(out=tmp, in0=tmp2, in1=tmp, op=ALU.subtract))
                    ac(lambda e: e.activation(out=cs, in_=tmp, func=AF.Sin, scale=TWO_PI))

                lr, dtt, a_, tht, mag1, s1, c1, w1r, w1i, den, cfr, cfi, x1, x2 = [t[:] for t in sm]
                dv(lambda e: e.tensor_scalar(out=lr, in0=lam_r[:], scalar1=-1e-4, scalar2=None, op0=ALU.min))
                ac(lambda e: e.activation(out=dtt, in_=ls1[:], func=AF.Exp))
                dv(lambda e: e.tensor_tensor(out=a_, in0=lr, in1=dtt, op=ALU.mult))
                dv(lambda e: e.tensor_tensor(out=tht, in0=lam_i[:], in1=dtt, op=ALU.mult))
                dv(lambda e: e.tensor_scalar(out=tht, in0=tht, scalar1=1.0 / TWO_PI, scalar2=None, op0=ALU.mult))
                ac(lambda e: e.activation(out=mag1, in_=a_, func=AF.Exp))
                sincos(tht, s1, c1, x1, x2)
                dv(lambda e: e.tensor_tensor(out=w1r, in0=mag1, in1=c1, op=ALU.mult))
                dv(lambda e: e.tensor_tensor(out=w1i, in0=mag1, in1=s1, op=ALU.mult))
                dv(lambda e: e.tensor_scalar(out=w1r, in0=w1r, scalar1=-1.0, scalar2=None, op0=ALU.add))
                dv(lambda e: e.tensor_tensor(out=den, in0=lr, in1=lr, op=ALU.mult))
                dv(lambda e: e.tensor_tensor(out=x1, in0=lam_i[:], in1=lam_i[:], op=ALU.mult))
                dv(lambda e: e.tensor_tensor(out=den, in0=den, in1=x1, op=ALU.add))
                dv(lambda e: e.reciprocal(out=den, in_=den))
                dv(lambda e: e.tensor_tensor(out=x1, in0=w1r, in1=lr, op=ALU.mult))
                dv(lambda e: e.tensor_tensor(out=x2, in0=w1i, in1=lam_i[:], op=ALU.mult))
                dv(lambda e: e.tensor_tensor(out=x1, in0=x1, in1=x2, op=ALU.add))
                dv(lambda e: e.tensor_tensor(out=cfr, in0=x1, in1=den, op=ALU.mult))
                dv(lambda e: e.tensor_tensor(out=x1, in0=w1i, in1=lr, op=ALU.mult))
                dv(lambda e: e.tensor_tensor(out=x2, in0=w1r, in1=lam_i[:], op=ALU.mult))
                dv(lambda e: e.tensor_tensor(out=x1, in0=x1, in1=x2, op=ALU.subtract))
                dv(lambda e: e.tensor_tensor(out=cfi, in0=x1, in1=den, op=ALU.mult))
                bc = lambda t: t.unsqueeze(2).to_broadcast([128, 32, 16])
                dv(lambda e: e.tensor_tensor(out=tb1[:], in0=br1[:], in1=bc(cfr), op=ALU.mult))
                dv(lambda e: e.tensor_tensor(out=tb2[:], in0=bi1[:], in1=bc(cfi), op=ALU.mult))
                dv(lambda e: e.tensor_tensor(out=Bbr[:], in0=tb1[:], in1=tb2[:], op=ALU.subtract))
                dv(lambda e: e.tensor_tensor(out=tb1[:], in0=bi1[:], in1=bc(cfr), op=ALU.mult))
                dv(lambda e: e.tensor_tensor(out=tb2[:], in0=br1[:], in1=bc(cfi), op=ALU.mult))
                dv(lambda e: e.tensor_tensor(out=Bbi[:], in0=tb1[:], in1=tb2[:], op=ALU.add))
                exb = exps[:].unsqueeze(3).to_broadcast([128, 4, 8, 32])
                ab = lambda t: t.unsqueeze(1).unsqueeze(1).to_broadcast([128, 4, 8, 32])
                dv(lambda e: e.tensor_tensor(out=EA[:], in0=exb, in1=ab(a_), op=ALU.mult))
                dv(lambda e: e.tensor_tensor(out=ET[:], in0=exb, in1=ab(tht), op=ALU.mult))
                ac(lambda e: e.activation(out=EA[:], in_=EA[:], func=AF.Exp))
                sincos(ET[:], Wi[:], Wr[:], tw[:], ET[:])
                dv(lambda e: e.tensor_tensor(out=Wr[:], in0=Wr[:], in1=EA[:], op=ALU.mult))
                dv(lambda e: e.tensor_tensor(out=Wi[:], in0=Wi[:], in1=EA[:], op=ALU.mult))
                dv(lambda e: e.tensor_scalar(out=x1, in0=a_, scalar1=8.0, scalar2=None, op0=ALU.mult))
                ac(lambda e: e.activation(out=rho8[:], in_=x1, func=AF.Exp))
                dv(lambda e: e.tensor_scalar(out=tt8[:], in0=tht, scalar1=8.0, scalar2=None, op0=ALU.mult))

                st_tmp.close()
                df.barrier(bart[:, 0:1])
                PK = ["pre"]
                KT = sb("KT", [128, 4, 128], BF16)
                ELTr = sb("ELTr", [128, 4, 128], BF16)
                ELTi = sb("ELTi", [128, 4, 128], BF16)
                CLr = sb("CLr", [128, 4, 128], BF16)
                nCLi = sb("nCLi", [128, 4, 128], BF16)
                Rr = sb("Rr", [128, 4, 128], BF16)
                Ri = sb("Ri", [128, 4, 128], BF16)
                Qr = sb("Qr", [128, 4, 128], BF16)
                nQi = sb("nQi", [128, 4, 128], BF16)
                ELr = sb("ELr", [128, 4, 128], BF16)
                ELi = sb("ELi", [128, 4, 128], BF16)
                p1 = sb("p1", [128, 4, 128], F32)
                p2 = sb("p2", [128, 4, 128], F32)
                scr = mT[:, 4:8, :].rearrange("p c t -> p (c t)").bitcast(F32).rearrange("p (n b) -> p n b", b=544)
                SETS = []
                for si_ in range(2):
                    d_ = {}
                    for ti_, nm in enumerate(("Er", "Ei", "cos", "sin", "gr", "gi", "sr", "si")):
                        d_[nm] = scr[:, si_ * 8 + ti_, :]
                    d_["tA"] = sb("tA%d" % si_, [128, 544], F32)[:]
                    d_["Zr"] = sb("Zr%d" % si_, [128, 544], BF16)
                    d_["Zi"] = sb("Zi%d" % si_, [128, 544], BF16)
                    d_["zg"] = sb("zg%d" % si_, [128, 544], BF16)
                    d_["id"] = si_
                    SETS.append(d_)
                    A("pool", lambda e, d_=d_: e.memset(d_["Zr"][:], 0.0), (), ["Zr%d" % si_])
                    A("pool", lambda e, d_=d_: e.memset(d_["Zi"][:], 0.0), (), ["Zi%d" % si_])

                def cprod(outr, outi, l, Xr_, Xi_, g0, neg_i):
                    wv = lambda W_: W_[:, l, :, g0:g0 + 4].rearrange("p s g -> p g s").unsqueeze(3).to_broadcast([128, 4, 8, 16])
                    xv = lambda X_: X_[:, g0:g0 + 4, :].unsqueeze(2).to_broadcast([128, 4, 8, 16])
                    o4 = lambda t: t[:].rearrange("p g (s m) -> p g s m", m=16)
                    dv(lambda e: e.tensor_tensor(out=o4(p1), in0=wv(Wr), in1=xv(Xr_), op=ALU.mult))
                    dv(lambda e: e.tensor_tensor(out=o4(p2), in0=wv(Wi), in1=xv(Xi_), op=ALU.mult))
                    dv(lambda e: e.tensor_tensor(out=outr[:], in0=p1[:], in1=p2[:], op=ALU.subtract))
                    dv(lambda e: e.tensor_tensor(out=o4(p1), in0=wv(Wr), in1=xv(Xi_), op=ALU.mult))
                    dv(lambda e: e.tensor_tensor(out=o4(p2), in0=wv(Wi), in1=xv(Xr_), op=ALU.mult))
                    if neg_i:
                        dv(lambda e: e.scalar_tensor_tensor(out=outi[:], in0=p1[:], scalar=-1.0, in1=p2[:], op0=ALU.mult, op1=ALU.subtract))
                    else:
                        dv(lambda e: e.tensor_tensor(out=outi[:], in0=p1[:], in1=p2[:], op=ALU.add))

                for qq in range(8):
                    g0 = qq * 4
                    cprod(Rr, Ri, 0, Bbr, Bbi, g0, False)
                    cprod(ELr, ELi, 1, Bbr, Bbi, g0, False)
                    cprod(Qr, nQi, 2, cr1, ci1, g0, True)
                    cprod(CLr, nCLi, 3, cr1, ci1, g0, True)
                    for q in range(4):
                        for h_ in range(2):
                            hb = h_ * 64
                            bank = mm[h_]
                            bk = "mm%d" % h_
                            A("pe", lambda e, bank=bank, hb=hb, q=q: e.matmul(bank[:, 0:128], lhsT=Rr[hb:hb + 64, q, :], rhs=Qr[hb:hb + 64, q, :], start=True, stop=False), PK, [bk])
                            A("pe", lambda e, bank=bank, hb=hb, q=q: e.matmul(bank[:, 0:128], lhsT=Ri[hb:hb + 64, q, :], rhs=nQi[hb:hb + 64, q, :], start=False, stop=True), PK, [bk])
                        A("dve", lambda e: e.tensor_tensor(out=p1[:, 0, :], in0=mm[0][:, 0:128], in1=mask[:, 0, :], op=ALU.mult), ["mm0"] + PK, PK)
                        A("dve", lambda e: e.tensor_tensor(out=p2[:, 0, :], in0=mm[1][:, 0:128], in1=mask[:, 1, :], op=ALU.mult), ["mm1"] + PK, PK)
                        A("dve", lambda e: e.tensor_tensor(out=p1[:, 0, :], in0=p1[:, 0, :], in1=p2[:, 0, :], op=ALU.add), PK, PK)
                        A("dve", lambda e, q=q, g0=g0: e.scalar_tensor_tensor(out=KT[:, q, :], in0=idn32[:], scalar=drep[:, g0 + q:g0 + q + 1], in1=p1[:, 0, :], op0=ALU.mult, op1=ALU.add), PK + ["drep", "idn32"], PK)
                        A("pe", lambda e, q=q: e.transpose(trp[:, 0, :], ELr[:, q, :], idn[:]), PK + ["idn"], ["trp"])
                        A("pe", lambda e, q=q: e.transpose(trp[:, 1, :], ELi[:, q, :], idn[:]), PK + ["idn"], ["trp"])
                        A("act", lambda e, q=q: e.copy(out=ELTr[:, q, :], in_=trp[:, 0, :]), ["trp"] + PK, PK)
                        A("act", lambda e, q=q: e.copy(out=ELTi[:, q, :], in_=trp[:, 1, :]), ["trp"] + PK, PK)
                    def grp(q, g, S):
                        sid = S["id"]
                        K_ = lambda nm: "%s%d" % (nm, sid)
                        Eb = stp[sid]
                        ebk = "stp%d" % sid
                        pvo = sid * 128
                        Er, Ei, cosT, sinT, gr, gi, sr, si, tA = S["Er"], S["Ei"], S["cos"], S["sin"], S["gr"], S["gi"], S["sr"], S["si"], S["tA"]
                        Zr_, Zi_, z_ = S["Zr"], S["Zi"], S["zg"]

                        def st0():
                            for ri, ELT_ in enumerate((ELTr, ELTi)):
                                for (b0, nb) in PIECES:
                                    if b0 < 512:
                                        yo = Eb[:, ri * 512 + b0:ri * 512 + b0 + nb]
                                        wk_ = [ebk]
                                    else:
                                        yo = pv[:, pvo + ri * 64:pvo + ri * 64 + nb]
                                        wk_ = ["pv"]
                                    A("pe", lambda e, yo=yo, ELT_=ELT_, b0=b0, nb=nb: e.matmul(yo, lhsT=ELT_[:, q, :], rhs=Uv[:, g, b0:b0 + nb], start=True, stop=True), PK + ["U"], wk_)

                        def st1():
                            for ri, E_ in enumerate((Er, Ei)):
                                ek = K_("Er" if ri == 0 else "Ei")
                                A("act", lambda e, E_=E_, ri=ri: e.copy(out=E_[0:64, 32:544], in_=Eb[0:64, ri * 512:(ri + 1) * 512]), [ebk], [ek])
                                A("act", lambda e, E_=E_, ri=ri: e.copy(out=E_[0:64, 0:32], in_=pv[0:64, pvo + ri * 64:pvo + ri * 64 + 32]), ["pv"], [ek])
                                A("act", lambda e, E_=E_, ri=ri: e.copy(out=E_[64:128, 543:31:-1], in_=Eb[64:128, ri * 512:(ri + 1) * 512]), [ebk], [ek])
                                A("act", lambda e, E_=E_, ri=ri: e.copy(out=E_[64:128, 31::-1], in_=pv[64:128, pvo + ri * 64:pvo + ri * 64 + 32]), ["pv"], [ek])
                            A("act", lambda e: e.activation(out=gr, in_=iota[:], func=AF.Copy, scale=tt8[:, g:g + 1]), PK + ["iota", K_("gr")], [K_("gr")])
                            A("act", lambda e: e.activation(out=gi, in_=gr, func=AF.Copy, bias=MAG), [K_("gr"), K_("gi")], [K_("gi")])

                        def st2():
                            A("act", lambda e: e.activation(out=gi, in_=gi, func=AF.Copy, bias=-MAG), [K_("gi")], [K_("gi")])
                            A("dve", lambda e: e.tensor_tensor(out=gi, in0=gr, in1=gi, op=ALU.subtract), [K_("gr"), K_("gi")], [K_("gi")])

                        def st3():
                            A("act", lambda e: e.activation(out=sinT, in_=gi, func=AF.Sin, scale=TWO_PI), [K_("gi")], [K_("sin")])
                            A("act", lambda e: e.activation(out=gr, in_=gr, func=AF.Copy, bias=0.25), [K_("gr")], [K_("gr")])
                            A("act", lambda e: e.activation(out=gi, in_=gr, func=AF.Copy, bias=MAG), [K_("gr"), K_("gi"), K_("sin")], [K_("gi")])

                        def st4():
                            A("act", lambda e: e.activation(out=gi, in_=gi, func=AF.Copy, bias=-MAG), [K_("gi")], [K_("gi")])
                            A("dve", lambda e: e.tensor_tensor(out=gi, in0=gr, in1=gi, op=ALU.subtract), [K_("gr"), K_("gi")], [K_("gi")])

                        def st5():
                            A("act", lambda e: e.activation(out=cosT, in_=gi, func=AF.Sin, scale=TWO_PI), [K_("gi")], [K_("cos")])

                        def st6():
                            A("dve", lambda e: e.tensor_tensor(out=gr, in0=Er, in1=cosT, op=ALU.mult), [K_("Er"), K_("cos"), K_("gr")], [K_("gr")])
                            A("dve", lambda e: e.tensor_tensor(out=tA, in0=Ei, in1=sinT, op=ALU.mult), [K_("Ei"), K_("sin")], [K_("tA")])
                            A("dve", lambda e: e.tensor_tensor(out=gr, in0=gr, in1=tA, op=ALU.add), [K_("gr"), K_("tA")], [K_("gr")])
                            A("dve", lambda e: e.tensor_tensor(out=gi, in0=Ei, in1=cosT, op=ALU.mult), [K_("Ei"), K_("cos"), K_("gi")], [K_("gi")])
                            A("dve", lambda e: e.tensor_tensor(out=tA, in0=Er, in1=sinT, op=ALU.mult), [K_("Er"), K_("sin"), K_("tA")], [K_("tA")])
                            A("dve", lambda e: e.tensor_tensor(out=gi, in0=gi, in1=tA, op=ALU.subtract), [K_("gi"), K_("tA")], [K_("gi")])

                        def st7():
                            rb = rho8[:, g:g + 1].to_broadcast([128, 544])
                            A("dve", lambda e: e.tensor_tensor_scan(out=sr, data0=rb, data1=gr, initial=0.0, op0=ALU.mult, op1=ALU.add), [K_("gr")] + PK, [K_("sr")])
                            A("dve", lambda e: e.tensor_tensor_scan(out=si, data0=rb, data1=gi, initial=0.0, op0=ALU.mult, op1=ALU.add), [K_("gi")] + PK, [K_("si")])

                        def rot_out(Z_, zk, c1, k1, c2, k2, op):
                            A("dve", lambda e: e.tensor_tensor(out=Er, in0=sr, in1=c1, op=ALU.mult), [K_("sr"), k1, K_("Er")], [K_("Er")])
                            A("dve", lambda e: e.tensor_tensor(out=Ei, in0=si, in1=c2, op=ALU.mult), [K_("si"), k2, K_("Ei")], [K_("Ei")])
                            A("dve", lambda e: e.tensor_tensor(out=Z_[0:64, 0:512], in0=Er[0:64, 31:543], in1=Ei[0:64, 31:543], op=op), [K_("Er"), K_("Ei")], [zk])
                            A("dve", lambda e: e.tensor_tensor(out=Z_[0:64, 513:544], in0=Er[0:64, 0:31], in1=Ei[0:64, 0:31], op=op), [K_("Er"), K_("Ei")], [zk])
                            A("dve", lambda e: e.tensor_tensor(out=Z_[64:128, 542::-1], in0=Er[64:128, 0:543], in1=Ei[64:128, 0:543], op=op), [K_("Er"), K_("Ei")], [zk])

                        def st8r():
                            rot_out(Zr_, K_("Zr"), cosT, K_("cos"), sinT, K_("sin"), ALU.subtract)

                        def st9r():
                            rot_out(Zi_, K_("Zi"), sinT, K_("sin"), cosT, K_("cos"), ALU.add)

                        def st8():
                            for (b0, nb) in PIECES:
                                if b0 < 512:
                                    yo = mm[0][:, b0:b0 + nb]
                                    wk_ = ["mm0"]
                                else:
                                    yo = mm[1][:, 0:nb]
                                    wk_ = ["mm1"]
                                A("pe", lambda e, yo=yo, b0=b0, nb=nb: e.matmul(yo, lhsT=KT[:, q, :], rhs=Uv[:, g, b0:b0 + nb], start=True, stop=False), PK + ["U"], wk_)
                                A("pe", lambda e, yo=yo, b0=b0, nb=nb: e.matmul(yo, lhsT=CLr[:, q, :], rhs=Zr_[:, b0:b0 + nb], start=False, stop=False), PK + [K_("Zr")], wk_)
                                A("pe", lambda e, yo=yo, b0=b0, nb=nb: e.matmul(yo, lhsT=nCLi[:, q, :], rhs=Zi_[:, b0:b0 + nb], start=False, stop=True), PK + [K_("Zi")], wk_)
                            A("act", lambda e: e.activation(out=z_[:, 0:512], in_=mm[0][:, 0:512], func=AF.Gelu), ["mm0"], [K_("zg")])
                            A("act", lambda e: e.activation(out=z_[:, 512:544], in_=mm[1][:, 0:32], func=AF.Gelu), ["mm1"], [K_("zg")])
                            for t8 in range(8):
                                dma("sp", zs[t8, g], z_[t8 * 16:(t8 + 1) * 16, :], [K_("zg")], ["zs"])

                        return [st0, st1, st2, st3, st4, st5, st6, st7, st8r, st9r], st8

                    for pr in range(2):
                        gA, gB = g0 + 2 * pr, g0 + 2 * pr + 1
                        stA, yA = grp(2 * pr, gA, SETS[0])
                        stB, yB = grp(2 * pr + 1, gB, SETS[1])
                        for k_ in range(len(stA)):
                            stA[k_]()
                            stB[k_]()
                        yA()
                        yB()
            df.barrier(bart[:, 0:1])

            with contextlib.ExitStack() as st:
              if "s5c" not in _SKIP:
                sb = lambda name, shape, dt: st.enter_context(nc.sbuf_tensor(uniq(name), shape, dt))
                Wg = sb("Wg", [128, 4, 1024], BF16)
                bg = sb("bg", [128, 8], F32)
                sgd = sb("sgd", [128, T], BF16)
                zsb = [sb("zsb%d" % i, [128, 4, 544], BF16) for i in range(2)]
                sig = [sb("sig%d" % i, [128, 544], F32) for i in range(2)]
                v1 = [sb("v1%d" % i, [128, 544], F32) for i in range(2)]
                wgd = sb("wgd", [128, 8, 128], BF16)
                for h_ in range(2):
                    dma("pool", Wg[:, :, h_ * 512:(h_ + 1) * 512], glu_w[j].rearrange("(c p) n -> p c n", p=128)[:, :, h_ * 512:(h_ + 1) * 512], (), ["Wg"])
                dma("sp", bg[:], glu_bT[j], (), ["bg"])
                for k in range(4):
                    dma("pool", wgd[:], Wd[:, :, 2048 + k * 128:2048 + (k + 1) * 128], (), ["wgd"])
                    inproj_fm(wgd, "wgd", NTILES, lambda bank, bk, n0, nn: A(
                        "act", lambda e: e.activation(out=sgd[:, n0:n0 + nn], in_=bank[:, 0:nn], func=AF.Silu), [bk], ["sgd"]))

                    def banks(n):
                        if n % 2 == 0:
                            return (mm[0], "mm0"), (mm[1], "mm1"), (pv[:, 0:32], "pv"), (pv[:, 256:288], "pv")
                        return (stp[0][:, 0:512], "stp0a"), (stp[0][:, 512:1024], "stp0b"), (stp[1][:, 0:32], "stp1a"), (stp[1][:, 512:544], "stp1b")

                    def c_load(n):
                        zb, zbk = zsb[n % 2], "zsb%d" % (n % 2)
                        dma("sp", zb[:], zs[n].rearrange("g m b -> (g m) b").rearrange("(c p) b -> p c b", p=128), ["zs"], [zbk])

                    def c_mm(n):
                        zb, zbk = zsb[n % 2], "zsb%d" % (n % 2)
                        (vb, vk), (gb_, gk), (vp, vpk), (gp_, gpk) = banks(n)
                        for (b0, nb) in PIECES:
                            for vg in range(2):
                                if b0 < 512:
                                    yo = (vb if vg == 0 else gb_)[:, b0:b0 + nb]
                                    wk_ = [vk if vg == 0 else gk]
                                else:
                                    yo = vp if vg == 0 else gp_
                                    wk_ = [vpk if vg == 0 else gpk]
                                col = (vg * 4 + k) * 128
                                for c in range(4):
                                    A("pe", lambda e, yo=yo, c=c, col=col, zb=zb, b0=b0, nb=nb: e.matmul(yo, lhsT=Wg[:, c, col:col + 128], rhs=zb[:, c, b0:b0 + nb], start=(c == 0), stop=(c == 3)),
                                      ["Wg", zbk], wk_)

                    def c_sig(n):
                        (vb, vk), (gb_, gk), (vp, vpk), (gp_, gpk) = banks(n)
                        sg_, sgk = sig[n % 2], "sig%d" % (n % 2)
                        A("act", lambda e, gb_=gb_, sg_=sg_, k=k: e.activation(out=sg_[:, 0:512], in_=gb_[:, 0:512], func=AF.Sigmoid, bias=bg[:, 4 + k:5 + k]), [gk, "bg"], [sgk])
                        A("act", lambda e, gp_=gp_, sg_=sg_, k=k: e.activation(out=sg_[:, 512:544], in_=gp_, func=AF.Sigmoid, bias=bg[:, 4 + k:5 + k]), [gpk, "bg"], [sgk])

                    def c_stt(n):
                        (vb, vk), (gb_, gk), (vp, vpk), (gp_, gpk) = banks(n)
                        sg_, sgk = sig[n % 2], "sig%d" % (n % 2)
                        v_, v1k = v1[n % 2], "v1%d" % (n % 2)
                        A("dve", lambda e, vb=vb, sg_=sg_, v_=v_, k=k: e.scalar_tensor_tensor(out=v_[:, 0:512], in0=vb[:, 0:512], scalar=bg[:, k:k + 1], in1=sg_[:, 0:512], op0=ALU.add, op1=ALU.mult), [vk, sgk, "bg"], [v1k])
                        A("dve", lambda e, vp=vp, sg_=sg_, v_=v_, k=k: e.scalar_tensor_tensor(out=v_[:, 512:544], in0=vp, scalar=bg[:, k:k + 1], in1=sg_[:, 512:544], op0=ALU.add, op1=ALU.mult), [vpk, sgk, "bg"], [v1k])

                    def c_out(n):
                        v_, v1k = v1[n % 2], "v1%d" % (n % 2)
                        A("pool", lambda e, v_=v_, n=n, k=k: e.tensor_tensor(out=mT[:, 4 + k, n::8], in0=v_[:], in1=sgd[:, n::8], op=ALU.mult), [v1k, "sgd"], [("mT", 4 + k, t) for t in range(NT)])

                    pipeline([c_load, c_mm, c_sig, c_stt, c_out], 8, "s5c")
            df.barrier(bart[:, 0:1])

            with contextlib.ExitStack() as st:
              if "gmlp" not in _SKIP:
                sb = lambda name, shape, dt: st.enter_context(nc.sbuf_tensor(uniq(name), shape, dt))
                Wv = sb("Wv", [128, 8, 512], BF16)
                Wu = sb("Wu", [128, 8, 512], BF16)
                Wc = sb("Wc", [128, 8, 512], BF16)
                wsT = sb("wsT", [128, 4, 128], BF16)
                sgT = sb("sgT", [128, 4], F32)
                bsb = sb("bsb", [128, 512], F32)
                st6 = sb("st6", [128, 12], F32)
                mv = sb("mv", [128, 4 * NT], F32)
                vn = [sb("vn%d" % i, [128, 512], BF16) for i in range(2)]
                sgc = [sb("sgc%d" % i, [128, 512], BF16) for i in range(2)]
                mx = [sb("mx%d" % i, [128, 512], F32) for i in range(2)]
                dma("pool", Wu[:], Wd[:, :, 0:512], (), ["Wu"])
                dma("pool", Wv[:], Wd[:, :, 512:1024], (), ["Wv"])
                dma("pool", Wc[:], Wd[:, :, 1024:1536], (), ["Wc"])
                dma("pool", wsT[:], sgu_wT[j], (), ["wsT"])
                dma("sp", sgT[:], sgu_gT[j], (), ["sgT"])
                dma("sp", bsb[:], sgu_b[j:j + 1, :].partition_broadcast(128), (), ["bsb"])
                vsb = [sb("vsb%d" % i, [128, 512], F32) for i in range(4)]
                tmx = [sb("tmx%d" % i, [128, 512], F32) for i in range(2)]
                MEAN = lambda n: mv[:, 4 * n:4 * n + 1]
                VAR = lambda n: mv[:, 4 * n + 1:4 * n + 2]
                RSg = lambda n: mv[:, 4 * n + 2:4 * n + 3]
                VS = lambda n: (vsb[n % 4], "vsb%d" % (n % 4))

                def g_v(n):
                    bank, bk = mm[n % 2], "mm%d" % (n % 2)
                    for k in range(8):
                        A("pe", lambda e, bank=bank, k=k, n=n: e.matmul(bank[:, :], lhsT=xb[:, k, n * 128:(n + 1) * 128], rhs=Wv[:, k, :], start=(k == 0), stop=(k == 7)), ["Wv", ("xb", n)], [bk])

                def g_cp(n):
                    bank, bk = mm[n % 2], "mm%d" % (n % 2)
                    vs_, vsk = VS(n)
                    A("act", lambda e, bank=bank, vs_=vs_: e.copy(out=vs_[:], in_=bank[:, :]), [bk], [vsk])

                def g_bn(n):
                    vs_, vsk = VS(n)
                    s6 = st6[:, (n % 2) * 6:(n % 2) * 6 + 6]
                    A("dve", lambda e, vs_=vs_, s6=s6: e.bn_stats(out=s6, in_=vs_[:]), [vsk], [("st6", n % 2)])
                    A("dve", lambda e, n=n, s6=s6: e.bn_aggr(out=mv[:, 4 * n:4 * n + 2], in_=s6), [("st6", n % 2)], [("mv", n)])

                def g_r1(n):
                    A("dve", lambda e, n=n: e.tensor_scalar(out=RSg(n), in0=VAR(n), scalar1=1.0, scalar2=EPS, op0=ALU.mult, op1=ALU.add), [("mv", n)], [("grs", n)])

                def g_r2(n):
                    A("act", lambda e, n=n: e.activation(out=RSg(n), in_=RSg(n), func=AF.Sqrt), [("grs", n)], [("grs", n)])

                def g_vn(n):
                    vs_, vsk = VS(n)
                    v_, vk = vn[n % 2], "vn%d" % (n % 2)
                    A("dve", lambda e, n=n: e.reciprocal(out=RSg(n), in_=RSg(n)), [("grs", n)], [("grs", n)])
                    A("dve", lambda e, vs_=vs_, v_=v_, n=n: e.tensor_scalar(out=v_[:], in0=vs_[:], scalar1=MEAN(n), scalar2=RSg(n), op0=ALU.subtract, op1=ALU.mult), [vsk, ("mv", n), ("grs", n)], [vk])

                def g_mm(n):
                    v_, vk = vn[n % 2], "vn%d" % (n % 2)
                    for g in range(4):
                        A("pe", lambda e, v_=v_, g=g: e.matmul(pv[:, g * 128:(g + 1) * 128], lhsT=v_[:, g * 128:(g + 1) * 128], rhs=wsT[:, g, :], start=True, stop=True), [vk, "wsT"], ["pv"])
                    sp_ = stp[n % 2]
                    for half, W_, wkk in ((0, Wu, "Wu"), (1, Wc, "Wc")):
                        sk = "stp%d%s" % (n % 2, "ab"[half])
                        for g in range(4):
                            for k in range(8):
                                A("pe", lambda e, sp_=sp_, half=half, W_=W_, g=g, k=k, n=n: e.matmul(sp_[:, half * 512 + g * 128:half * 512 + (g + 1) * 128], lhsT=W_[:, k, g * 128:(g + 1) * 128], rhs=xb[:, k, n * 128:(n + 1) * 128], start=(k == 0), stop=(k == 7)),
                                  [wkk, ("xb", n)], [sk])

                def g_ep(n):
                    sp_ = stp[n % 2]
                    c_, ck = sgc[n % 2], "sgc%d" % (n % 2)
                    m_, mk = mx[n % 2], "mx%d" % (n % 2)
                    t_, tk_ = tmx[n % 2], "tmx%d" % (n % 2)
                    A("act", lambda e, sp_=sp_, c_=c_: e.activation(out=c_[:], in_=sp_[:, 512:1024], func=AF.Silu), ["stp%db" % (n % 2)], [ck])
                    for g in range(4):
                        A("dve", lambda e, m_=m_, g=g: e.scalar_tensor_tensor(out=m_[:, g * 128:(g + 1) * 128], in0=pv[:, g * 128:(g + 1) * 128], scalar=sgT[:, g:g + 1], in1=bsb[:, g * 128:(g + 1) * 128], op0=ALU.mult, op1=ALU.add),
                          ["pv", "sgT", "bsb"], [mk])
                    A("dve", lambda e, m_=m_, sp_=sp_, t_=t_: e.tensor_tensor(out=t_[:], in0=m_[:], in1=sp_[:, 0:512], op=ALU.mult), [mk, "stp%da" % (n % 2)], [tk_])

                def g_out(n):
                    c_, ck = sgc[n % 2], "sgc%d" % (n % 2)
                    t_, tk_ = tmx[n % 2], "tmx%d" % (n % 2)
                    A("pool", lambda e, t_=t_, c_=c_, n=n: e.tensor_tensor(out=mT[:, 0:4, n * 128:(n + 1) * 128], in0=t_[:].rearrange("p (g c) -> p g c", g=4), in1=c_[:].rearrange("p (g c) -> p g c", g=4), op=ALU.mult),
                      [tk_, ck], [("mT", g, n) for g in range(4)])

                pipeline([g_v, g_cp, g_bn, g_r1, g_r2, g_vn, g_mm, g_ep, g_out], NT, "gmlp")
            df.barrier(bart[:, 0:1])

        for L in range(n_layers):
            last = (L == n_layers - 1)
            (prenorm_old if "oldpre" in _SKIP else prenorm)(L)
            if L % 2 == 0:
                if "even" not in _SKIP:
                    even_mixer(L)
            else:
                odd_mixer(L)
            (post_old if "oldpost" in _SKIP else post)(L, last)
        df.emit()
    return nc


_CACHE = {}


def kernel(x, c, ctx, c_ctx, w_ada, b_ada, pre_g, post_g, w_in_even, w_out_even, na_rpb,
           w_in_odd, w_out_odd, sgu_w, sgu_b, sgu_g, s5_lam_re, s5_lam_im, s5_log_step,
           s5_b_re, s5_b_im, s5_c_re, s5_c_im, s5_d, glu_w, glu_b, _n_layers=DEPTH):
    f = lambda a: np.ascontiguousarray(np.asarray(a), dtype=np.float32)
    B = x.shape[0]
    if _n_layers not in _CACHE:
        _CACHE[_n_layers] = build_program(_n_layers)
    nc = _CACHE[_n_layers]
    F1, H, CH, F256 = fnet_consts()
    eb = np.stack([build_ebias(f(na_rpb[j])) for j in range(2)]).reshape(2, 12, 128, 21 * 128)
    shared = dict(w_ada=f(w_ada), b_ada=f(b_ada), pre_g=f(pre_g), post_g=f(post_g), w_in_even=f(w_in_even),
                  w_out_even=f(w_out_even), ebias=eb, cF1=F1, cH=H, cCH=CH, cF256=F256)
    ex, mk, io = s5_consts()
    lam1 = np.stack([f(s5_lam_re), f(s5_lam_im)], axis=1)
    lam1 = np.ascontiguousarray(lam1.transpose(0, 1, 2, 4, 3)).reshape(2, 2, 128, 32)
    b1 = np.stack([f(s5_b_re), f(s5_b_im)], axis=1)
    b1 = np.ascontiguousarray(b1.transpose(0, 1, 2, 4, 3, 5)).reshape(2, 2, 128, 32, 16)
    c1 = np.stack([f(s5_c_re), f(s5_c_im)], axis=1)
    c1 = np.ascontiguousarray(c1.transpose(0, 1, 2, 5, 3, 4)).reshape(2, 2, 128, 32, 16)
    drep = np.ascontiguousarray(np.tile(f(s5_d).reshape(2, 32, 16).transpose(0, 2, 1), (1, 8, 1)))
    shared.update(dict(
        w_in_odd=f(w_in_odd), w_out_odd=f(w_out_odd),
        sgu_wT=np.ascontiguousarray(f(sgu_w).transpose(0, 3, 1, 2)),
        sgu_gT=np.ascontiguousarray(f(sgu_g).reshape(2, 4, 128).transpose(0, 2, 1)),
        sgu_b=f(sgu_b).reshape(2, 512), s5_lam1=lam1, s5_ls=f(s5_log_step), s5_b1=b1, s5_c1=c1, s5_drep=drep,
        glu_w=f(glu_w), glu_bT=np.ascontiguousarray(f(glu_b).reshape(2, 8, 128).transpose(0, 2, 1)),
        cEXPS=ex, cMASK=mk, cIOTA=io))
    cc = f(c_ctx).reshape(8, 128).T
    in_maps = []
    for b in range(B):
        m = dict(shared)
        m["xr"] = np.concatenate([f(x[b]), f(ctx[b])], axis=0)
        m["cT"] = np.ascontiguousarray(np.concatenate([f(c[b]).reshape(8, 128).T, cc], axis=1))
        in_maps.append(m)
    res = run_bass_kernel_spmd(nc, in_maps, core_ids=list(range(B)))
    return np.stack([np.asarray(r["out"], dtype=np.float32) for r in res.results], axis=0)
```

```python
import contextlib
import numpy as np
import concourse.bass as bass
import concourse.mybir as mybir
from concourse.bass_utils import run_bass_kernel_spmd

F32 = mybir.dt.float32
BF16 = mybir.dt.bfloat16
AF = mybir.ActivationFunctionType
ALU = mybir.AluOpType

ENGS = ("pe", "act", "dve", "pool", "sp")
RING = 8
SELF_SYNC = ("act", "dve", "pool")

D = 1024
T = 4352
TL = 4096
NT = 34
EPS = 1e-6
DEPTH = 4


class _Op:
    __slots__ = ("eng", "fn", "deps", "dma", "flag", "cnt", "ring", "target")


class DF:
    def __init__(self, nc):
        self.nc = nc
        self.ops = []
        self.lw = {}
        self.rd = {}
        self.ndma = {e: 0 for e in ENGS}
        self.last_on = {}
        self.bar = None
        self.dma_since_bar = []

    def add(self, eng, fn, reads=(), writes=(), dma=False):
        idx = len(self.ops)
        deps = set()
        if self.bar is not None:
            deps.add(self.bar)
        for r in reads:
            w = self.lw.get(r)
            if w is not None:
                deps.add(w)
        for r in writes:
            w = self.lw.get(r)
            if w is not None:
                deps.add(w)
            deps.update(self.rd.get(r, ()))
        for r in reads:
            self.rd.setdefault(r, []).append(idx)
        for r in writes:
            self.lw[r] = idx
            self.rd[r] = []
        o = _Op()
        o.eng, o.fn, o.deps, o.dma, o.flag, o.cnt = eng, fn, deps, dma, False, 0
        o.ring = o.target = None
        if dma:
            n = self.ndma[eng]
            self.ndma[eng] = n + 1
            o.ring = n % RING
            o.target = 16 * (n // RING + 1)
            self.dma_since_bar.append(idx)
        else:
            self.last_on[eng] = idx
        self.ops.append(o)
        return idx

    def barrier(self, tile):
        idx = len(self.ops)
        deps = set(self.last_on.values()) | set(self.dma_since_bar)
        if self.bar is not None:
            deps.add(self.bar)
        o = _Op()
        o.eng, o.fn, o.deps, o.dma, o.flag, o.cnt = "pool", (lambda e: e.memset(tile, 0.0)), deps, False, False, 0
        o.ring = o.target = None
        self.ops.append(o)
        self.last_on["pool"] = idx
        self.bar = idx
        self.dma_since_bar = []
        self.lw = {}
        self.rd = {}

    def emit(self):
        nc = self.nc
        ops = self.ops
        for o in ops:
            for d in o.deps:
                p = ops[d]
                if p.dma:
                    continue
                if p.eng != o.eng or (o.eng in SELF_SYNC) or o.dma:
                    p.flag = True
        cnt = {e: 0 for e in ENGS}
        for o in ops:
            if o.flag and not o.dma:
                cnt[o.eng] += 1
                o.cnt = cnt[o.eng]
        with contextlib.ExitStack() as st:
            csem = {e: st.enter_context(nc.semaphore("c_" + e)) for e in ENGS}
            dsem = {e: [st.enter_context(nc.semaphore("d_%s%d" % (e, i))) for i in range(RING)]
                    for e in ("sp", "pool", "act")}
            block = st.enter_context(nc.Block())
            ndma = self.ndma

            def run(engname, eng):
                waited_c = {e: 0 for e in ENGS}
                waited_d = {}
                for o in ops:
                    if o.eng != engname:
                        continue
                    for d in sorted(o.deps):
                        p = ops[d]
                        if p.dma:
                            key = (p.eng, p.ring)
                            if waited_d.get(key, 0) < p.target:
                                eng.wait_ge(dsem[p.eng][p.ring], p.target)
                                waited_d[key] = p.target
                        else:
                            if p.eng == engname and not (engname in SELF_SYNC or o.dma):
                                continue
                            if waited_c[p.eng] < p.cnt:
                                eng.wait_ge(csem[p.eng], p.cnt)
                                waited_c[p.eng] = p.cnt
                    if o.dma and o.target > 16:
                        key = (engname, o.ring)
                        if waited_d.get(key, 0) < o.target - 16:
                            eng.wait_ge(dsem[engname][o.ring], o.target - 16)
                            waited_d[key] = o.target - 16
                    ins = o.fn(eng)
                    if o.dma:
                        ins.then_inc(dsem[engname][o.ring], 16)
                    elif o.flag:
                        ins.then_inc(csem[engname], 1)
                if engname in dsem:
                    n = ndma[engname]
                    for r in range(min(n, RING)):
                        last = ((n - 1 - r) // RING) * RING + r
                        eng.wait_ge(dsem[engname][r], 16 * (last // RING + 1))

            @block.tensor
            def _(eng):
                run("pe", eng)

            @block.scalar
            def _(eng):
                run("act", eng)

            @block.vector
            def _(eng):
                run("dve", eng)

            @block.gpsimd
            def _(eng):
                run("pool", eng)

            @block.sync
            def _(eng):
                run("sp", eng)


NA_COMBOS = ([(2, kt) for kt in range(0, 5)] + [(0, kt) for kt in range(4)] + [(1, kt) for kt in range(4)]
             + [(30, kt) for kt in range(28, 32)] + [(31, kt) for kt in range(28, 32)])


def na_pattern_base(i):
    if 2 <= i <= 29:
        return 0, list(range(i - 2, i + 3))
    if i == 0:
        return 5, [0, 1, 2, 3]
    if i == 1:
        return 9, [0, 1, 2, 3]
    if i == 30:
        return 13, [28, 29, 30, 31]
    return 17, [28, 29, 30, 31]


def build_ebias(rpb):
    out = np.empty((12, 128, 21, 128), np.float32)
    a = np.arange(2)
    c = np.arange(64)
    for pi, (i, kt) in enumerate(NA_COMBOS):
        kr = (2 * kt + a)[:, None, None, None]
        r = (2 * i + a)[None, None, :, None]
        ck = c[None, :, None, None]
        cq = c[None, None, None, :]
        r0 = np.clip(r - 4, 0, 56)
        c0 = np.clip(cq - 8, 0, 48)
        ok = (kr >= r0) & (kr < r0 + 8) & (ck >= c0) & (ck < c0 + 16)
        dr = np.clip(kr - r + 7, 0, 14)
        dc = np.clip(ck - cq + 15, 0, 30)
        ok, dr, dc = np.broadcast_arrays(ok, dr, dc)
        vals = rpb[:, dr, dc]
        vals = np.where(ok[None], vals, np.float32(-30000.0))
        out[:, :, pi, :] = vals.reshape(12, 128, 128)
    return out


def fnet_consts():
    i64 = np.arange(64)
    ang = 2 * np.pi * np.outer(i64, i64) / 64.0
    F1 = np.concatenate([np.cos(ang), -np.sin(ang)], axis=1)
    t2 = i64[:, None, None]
    k1 = i64[None, :, None]
    k2 = i64[None, None, :]
    ph = -2 * np.pi * (t2 * k1 / 4096.0 + t2 * k2 / 64.0)
    Gr, Gi = np.cos(ph) / 64.0, np.sin(ph) / 64.0
    H = np.empty((64, 64, 2, 128))
    H[:, :, 0, 0:64] = Gr
    H[:, :, 0, 64:128] = Gi
    H[:, :, 1, 0:64] = -Gi
    H[:, :, 1, 64:128] = Gr
    A = np.cos(ang) / 8.0
    B = np.sin(ang) / 8.0
    CH = np.zeros((128, 6, 128))
    CH[0:64, 0, 0:64] = A
    CH[0:64, 1, 0:64] = B
    CH[0:64, 2, 64:128] = A
    CH[0:64, 3, 64:128] = B
    CH[0:64, 4, 0:64] = A
    CH[64:128, 4, 64:128] = A
    CH[0:64, 5, 0:64] = B
    CH[64:128, 5, 64:128] = B
    i256 = np.arange(256)
    a256 = 2 * np.pi * np.outer(i256, i256) / 256.0
    F256 = np.concatenate([np.cos(a256), -np.sin(a256)], axis=1) / 16.0
    F256 = F256.reshape(2, 128, 512).transpose(1, 0, 2)
    f = lambda x: np.ascontiguousarray(x, dtype=np.float32)
    return f(F1), f(H), f(CH), f(F256)


def s5_consts():
    s8 = (np.arange(128) // 16)
    ex = np.zeros((128, 4, 8), np.float32)
    sv = np.arange(8, dtype=np.float32)
    ex[0:64, 0] = -sv
    ex[64:128, 0] = sv
    ex[0:64, 1] = 7 - sv
    ex[64:128, 1] = sv
    ex[0:64, 2] = sv
    ex[64:128, 2] = -sv
    ex[0:64, 3] = sv + 1
    ex[64:128, 3] = 8 - sv
    mk = np.zeros((128, 2, 128), np.float32)
    mk[:, 0, :] = (s8[:, None] <= s8[None, :])
    mk[:, 1, :] = (s8[:, None] >= s8[None, :])
    io = np.ascontiguousarray(np.broadcast_to(np.arange(544, dtype=np.float32), (128, 544)))
    return ex, mk, io


import os
_SKIP = set(os.environ.get("MK_SKIP", "").split(","))


def build_program(n_layers=DEPTH):
    nc = bass.Bass("TRN2", target_bir_lowering=False)
    dt_in = lambda name, shape: nc.dram_tensor(name, list(shape), F32, kind="ExternalInput").ap()
    xr = dt_in("xr", [T, D])
    cT = dt_in("cT", [128, 16])
    w_ada = dt_in("w_ada", [DEPTH, D, 3 * D])
    b_ada = dt_in("b_ada", [DEPTH, 3 * D])
    pre_g = dt_in("pre_g", [DEPTH, D])
    post_g = dt_in("post_g", [DEPTH, D])
    w_in_even = dt_in("w_in_even", [2, D, 3584])
    w_out_even = dt_in("w_out_even", [2, D, D])
    ebias = dt_in("ebias", [2, 12, 128, 21 * 128])
    cF1 = dt_in("cF1", [64, 128])
    cH = dt_in("cH", [64, 64, 2, 128])
    cCH = dt_in("cCH", [128, 6, 128])
    cF256 = dt_in("cF256", [128, 2, 512])
    w_in_odd = dt_in("w_in_odd", [2, D, 2560])
    w_out_odd = dt_in("w_out_odd", [2, D, D])
    sgu_wT = dt_in("sgu_wT", [2, 128, 4, 128])
    sgu_gT = dt_in("sgu_gT", [2, 128, 4])
    sgu_b = dt_in("sgu_b", [2, 512])
    s5_lam1 = dt_in("s5_lam1", [2, 2, 128, 32])
    s5_ls = dt_in("s5_ls", [2, 2, 32])
    s5_b1 = dt_in("s5_b1", [2, 2, 128, 32, 16])
    s5_c1 = dt_in("s5_c1", [2, 2, 128, 32, 16])
    s5_drep = dt_in("s5_drep", [2, 128, 32])
    glu_w = dt_in("glu_w", [2, 512, 1024])
    glu_bT = dt_in("glu_bT", [2, 128, 8])
    cEXPS = dt_in("cEXPS", [128, 4, 8])
    cMASK = dt_in("cMASK", [128, 2, 128])
    cIOTA = dt_in("cIOTA", [128, 544])
    zs = nc.dram_tensor("zs", [8, 32, 16, 544], BF16).ap()
    cHb = nc.dram_tensor("cHb", [64, 64, 2, 128], BF16).ap()
    out = nc.dram_tensor("out", [TL, D], F32, kind="ExternalOutput").ap()
    xs = nc.dram_tensor("xs", [T, D], F32).ap()
    modscr = nc.dram_tensor("modscr", [DEPTH, 2, 3 * D], F32).ap()

    df = DF(nc)
    A = df.add
    _uid = [0]

    def uniq(name):
        _uid[0] += 1
        return "%s_%d" % (name, _uid[0])

    def dma(eng, o, i, r=(), w=()):
        if eng == "pool":
            A(eng, lambda e, o=o, i=i: e.dma_start(out=o, in_=i, max_dma_last_dim=2048), r, w, dma=True)
        else:
            A(eng, lambda e, o=o, i=i: e.dma_start(out=o, in_=i), r, w, dma=True)

    def xbkeys(n0, nn):
        return [("xb", t) for t in range(n0 // 128, (n0 + nn + 127) // 128)]

    NTILES = [(n * 512, 512) for n in range(8)] + [(4096, 256)]

    with contextlib.ExitStack() as gst:
        sbg = lambda name, shape, dt: gst.enter_context(nc.sbuf_tensor(uniq(name), shape, dt))
        psg = lambda name, shape, dt: gst.enter_context(nc.psum_tensor(name, shape, dt))
        xb = sbg("xb", [128, 8, T], BF16)
        mT = sbg("mT", [128, 8, T], BF16)
        idn = sbg("idn", [128, 128], BF16)
        idn32 = sbg("idn32", [128, 128], F32)
        bart = sbg("bart", [128, 2], F32)
        mm = [psg("mm%d" % i, [128, 512], F32) for i in range(2)]
        stp = [psg("stp%d" % i, [128, 1024], F32) for i in range(2)]
        pv = psg("pv", [128, 512], F32)
        trp = psg("trp", [128, 8, 128], BF16)
        mmi = [0]

        def next_mm():
            mmi[0] ^= 1
            return mm[mmi[0]], "mm%d" % mmi[0]

        A("pool", lambda e: e.memset(idn32[:], 1.0), (), ["idn32"])
        A("pool", lambda e: e.affine_select(out=idn32[:], in_=idn32[:], pattern=[[-1, 128]], compare_op=ALU.is_equal,
                                            fill=0.0, base=0, channel_multiplier=1), ["idn32"], ["idn32"])
        A("dve", lambda e: e.tensor_copy(out=idn[:], in_=idn32[:]), ["idn32"], ["idn"])

        with contextlib.ExitStack() as st:
            sb = lambda name, shape, dt: st.enter_context(nc.sbuf_tensor(uniq(name), shape, dt))
            c32 = sb("c32", [128, 16], F32)
            sc = sb("sc", [128, 16], F32)
            LC = sb("LC", [128, 8, 64], BF16)
            wa = [sb("wa%d" % i, [128, 8, 512], BF16) for i in range(2)]
            bada = sb("bada", [64, 3 * D], F32)
            modrow = sb("modrow", [64, 3 * D], F32)
            for g8_ in range(8):
                dma("pool", cHb[:, g8_ * 8:(g8_ + 1) * 8, :, :], cH[:, g8_ * 8:(g8_ + 1) * 8, :, :], (), [("cHb", g8_)])
            dma("sp", c32[:], cT, (), ["c32"])
            A("act", lambda e: e.activation(out=sc[:], in_=c32[:], func=AF.Silu), ["c32"], ["sc"])
            A("pool", lambda e: e.memset(LC[:], 0.0), (), ["LC"])
            A("dve", lambda e: e.tensor_copy(out=LC[:, :, 0:1], in_=sc[:, 0:8].rearrange("p (k o) -> p k o", o=1)), ["sc", "LC"], ["LC"])
            A("dve", lambda e: e.tensor_copy(out=LC[:, :, 32:33], in_=sc[:, 8:16].rearrange("p (k o) -> p k o", o=1)), ["sc", "LC"], ["LC"])
            wi = 0
            for L in range(n_layers):
                dma("sp", bada[:], b_ada[L:L + 1, :].partition_broadcast(64), (), ["bada"])
                for n in range(6):
                    wt = wa[wi % 2]
                    wk = "wa%d" % (wi % 2)
                    wi += 1
                    dma("pool", wt[:], w_ada[L].rearrange("(k p) n -> p k n", p=128)[:, :, n * 512:(n + 1) * 512], (), [wk])
                    bank, bk = next_mm()
                    for k in range(8):
                        A("pe", lambda e, bank=bank, wt=wt, k=k: e.matmul(bank[0:64, :], lhsT=LC[:, k, :], rhs=wt[:, k, :], start=(k == 0), stop=(k == 7)),
                          ["LC", wk], [bk])
                    A("dve", lambda e, bank=bank, n=n: e.tensor_tensor(out=modrow[:, n * 512:(n + 1) * 512], in0=bank[0:64, :], in1=bada[:, n * 512:(n + 1) * 512], op=ALU.add),
                      [bk, "bada"], ["modrow"])
                dma("sp", modscr[L, 0:1, :], modrow[0:1, :], ["modrow"], [("modscr", L)])
                dma("sp", modscr[L, 1:2, :], modrow[32:33, :], ["modrow"], [("modscr", L)])
        df.barrier(bart[:, 0:1])

        def rstd_from_ss(ssum, rs, keys_in, key_out, scale):
            A("dve", lambda e: e.tensor_scalar(out=rs, in0=ssum, scalar1=scale, scalar2=EPS, op0=ALU.mult, op1=ALU.add), keys_in, [key_out])
            A("act", lambda e: e.activation(out=rs, in_=rs, func=AF.Sqrt), [key_out], [key_out])
            A("dve", lambda e: e.reciprocal(out=rs, in_=rs), [key_out], [key_out])

        def prenorm_old(L):
            src = xr if L == 0 else xs
            with contextlib.ExitStack() as st:
                sb = lambda name, shape, dt: st.enter_context(nc.sbuf_tensor(uniq(name), shape, dt))
                xt = [sb("xt%d" % i, [128, D], F32) for i in range(3)]
                tmp = [sb("ptmp%d" % i, [128, D], F32) for i in range(2)]
                hl = [sb("hl%d" % i, [128, D], BF16) for i in range(2)]
                junk = sb("junk", [128, D], BF16)
                gsc = [sb("gsc%d" % i, [128, D], F32) for i in range(2)]
                shb = [sb("shb%d" % i, [128, D], F32) for i in range(2)]
                pgb = sb("pgb", [128, D], F32)
                stat = sb("pstat", [128, 4 * NT], F32)
                dma("sp", pgb[:], pre_g[L:L + 1, :].partition_broadcast(128), (), ["pgb"])
                for w in range(2):
                    dma("sp", gsc[w][:], modscr[L, w:w + 1, D:2 * D].partition_broadcast(128), [("modscr", L)], ["gsc%d" % w])
                    dma("sp", shb[w][:], modscr[L, w:w + 1, 0:D].partition_broadcast(128), [("modscr", L)], ["shb%d" % w])
                    A("dve", lambda e, w=w: e.scalar_tensor_tensor(out=gsc[w][:], in0=gsc[w][:], scalar=1.0, in1=pgb[:], op0=ALU.add, op1=ALU.mult),
                      ["gsc%d" % w, "pgb"], ["gsc%d" % w])
                for t in range(NT):
                    w = 0 if t < 32 else 1
                    x_ = xt[t % 3]
                    xk = "xt%d" % (t % 3)
                    tm = tmp[t % 2]
                    tk = "ptmp%d" % (t % 2)
                    h_ = hl[t % 2]
                    hk = "hl%d" % (t % 2)
                    ss = stat[:, 4 * t:4 * t + 1]
                    rs = stat[:, 4 * t + 1:4 * t + 2]
                    dma("sp", x_[:], src[t * 128:(t + 1) * 128, :], [("xres", t)], [xk])
                    A("act", lambda e, x_=x_, ss=ss: e.activation(out=junk[:], in_=x_[:], func=AF.Square, accum_out=ss), [xk], ["junk", ("pss", t)])
                    rstd_from_ss(ss, rs, [("pss", t)], ("prs", t), 1.0 / D)
                    A("dve", lambda e, x_=x_, rs=rs, tm=tm, w=w: e.scalar_tensor_tensor(out=tm[:], in0=x_[:], scalar=rs, in1=gsc[w][:], op0=ALU.mult, op1=ALU.mult),
                      [xk, ("prs", t), "gsc%d" % w], [tk])
                    A("pool", lambda e, tm=tm, h_=h_, w=w: e.tensor_tensor(out=h_[:], in0=tm[:], in1=shb[w][:], op=ALU.add), [tk, "shb%d" % w], [hk])
                    for k in range(8):
                        A("pe", lambda e, h_=h_, k=k: e.transpose(trp[:, k, :], h_[:, k * 128:(k + 1) * 128], idn[:]), [hk, "idn"], ["trp"])
                    if t % 2 == 0:
                        A("act", lambda e, t=t: e.copy(out=xb[:, :, t * 128:(t + 1) * 128], in_=trp[:]), ["trp"], [("xb", t)])
                    else:
                        A("dve", lambda e, t=t: e.tensor_copy(out=xb[:, :, t * 128:(t + 1) * 128], in_=trp[:]), ["trp"], [("xb", t)])
            df.barrier(bart[:, 0:1])

        def post_old(L, last):
            src = xr if L == 0 else xs
            j = L // 2
            wo_d = w_out_even[j] if L % 2 == 0 else w_out_odd[j]
            ntile = 32 if last else NT
            with contextlib.ExitStack() as st:
                sb = lambda name, shape, dt: st.enter_context(nc.sbuf_tensor(uniq(name), shape, dt))
                wo = sb("wo", [128, 8, D], BF16)
                xt = [sb("qxt%d" % i, [128, D], F32) for i in range(2)]
                t1 = [sb("qt1%d" % i, [128, D], F32) for i in range(2)]
                t2 = [sb("qt2%d" % i, [128, D], F32) for i in range(2)]
                junk = sb("qjunk", [128, 512], BF16)
                gp = [sb("gp%d" % i, [128, D], F32) for i in range(2)]
                pgb = sb("qpgb", [128, D], F32)
                stat = sb("qstat", [128, 4 * NT], F32)
                for h in range(2):
                    dma("pool", wo[:, :, h * 512:(h + 1) * 512], wo_d.rearrange("(k p) n -> p k n", p=128)[:, :, h * 512:(h + 1) * 512], (), ["wo"])
                dma("sp", pgb[:], post_g[L:L + 1, :].partition_broadcast(128), (), ["qpgb"])
                for w in range(2):
                    dma("sp", gp[w][:], modscr[L, w:w + 1, 2 * D:3 * D].partition_broadcast(128), [("modscr", L)], ["gp%d" % w])
                    A("dve", lambda e, w=w: e.tensor_tensor(out=gp[w][:], in0=gp[w][:], in1=pgb[:], op=ALU.mult), ["gp%d" % w, "qpgb"], ["gp%d" % w])
                for t in range(ntile):
                    w = 0 if t < 32 else 1
                    yps = stp[t % 2]
                    yk = "stp%d" % (t % 2)
                    x_ = xt[t % 2]
                    xk = "qxt%d" % (t % 2)
                    a_ = t1[t % 2]
                    ak = "qt1%d" % (t % 2)
                    b_ = t2[t % 2]
                    bk = "qt2%d" % (t % 2)
                    for h in range(2):
                        for k in range(8):
                            A("pe", lambda e, yps=yps, h=h, k=k, t=t: e.matmul(yps[:, h * 512:(h + 1) * 512], lhsT=mT[:, k, t * 128:(t + 1) * 128], rhs=wo[:, k, h * 512:(h + 1) * 512], start=(k == 0), stop=(k == 7)),
                              [("mT", k, t), "wo"], [yk])
                    dma("sp", x_[:], src[t * 128:(t + 1) * 128, :], [("xres", t)], [xk])
                    for h in range(2):
                        A("act", lambda e, yps=yps, h=h, t=t: e.activation(out=junk[:], in_=yps[:, h * 512:(h + 1) * 512], func=AF.Square, accum_out=stat[:, 4 * t + h:4 * t + h + 1]),
                          [yk], ["qjunk", ("qss", t, h)])
                    A("dve", lambda e, t=t: e.tensor_tensor(out=stat[:, 4 * t + 2:4 * t + 3], in0=stat[:, 4 * t:4 * t + 1], in1=stat[:, 4 * t + 1:4 * t + 2], op=ALU.add),
                      [("qss", t, 0), ("qss", t, 1)], [("qs2", t)])
                    rs = stat[:, 4 * t + 3:4 * t + 4]
                    rstd_from_ss(stat[:, 4 * t + 2:4 * t + 3], rs, [("qs2", t)], ("qrs", t), 1.0 / D)
                    A("dve", lambda e, yps=yps, a_=a_, w=w: e.tensor_tensor(out=a_[:], in0=yps[:], in1=gp[w][:], op=ALU.mult), [yk, "gp%d" % w], [ak])
                    A("act", lambda e, a_=a_, b_=b_, rs=rs: e.activation(out=b_[:], in_=a_[:], func=AF.Copy, scale=rs), [ak, ("qrs", t)], [bk])
                    A("pool", lambda e, b_=b_, x_=x_: e.tensor_tensor(out=b_[:], in0=b_[:], in1=x_[:], op=ALU.add), [bk, xk], [bk])
                    dst = out[t * 128:(t + 1) * 128, :] if (last and t < 32) else xs[t * 128:(t + 1) * 128, :]
                    dma("sp", dst, b_[:], [bk], [("xres", t)])
            df.barrier(bart[:, 0:1])

        def prenorm(L):
            src = xr if L == 0 else xs
            with contextlib.ExitStack() as st:
                sb = lambda name, shape, dt: st.enter_context(nc.sbuf_tensor(uniq(name), shape, dt))
                NX = 6
                xt = [sb("xt%d" % i, [128, D], F32) for i in range(NX)]
                tmp = [sb("ptmp%d" % i, [128, D], F32) for i in range(2)]
                hl = [sb("hl%d" % i, [128, D], BF16) for i in range(2)]
                junk = sb("junk", [128, D], BF16)
                gsc = [sb("gsc%d" % i, [128, D], F32) for i in range(2)]
                shb = [sb("shb%d" % i, [128, D], F32) for i in range(2)]
                pgb = sb("pgb", [128, D], F32)
                stat = sb("pstat", [128, 4 * NT], F32)
                dma("sp", pgb[:], pre_g[L:L + 1, :].partition_broadcast(128), (), ["pgb"])
                for w in range(2):
                    dma("sp", gsc[w][:], modscr[L, w:w + 1, D:2 * D].partition_broadcast(128), [("modscr", L)], ["gsc%d" % w])
                    dma("sp", shb[w][:], modscr[L, w:w + 1, 0:D].partition_broadcast(128), [("modscr", L)], ["shb%d" % w])
                    A("dve", lambda e, w=w: e.scalar_tensor_tensor(out=gsc[w][:], in0=gsc[w][:], scalar=1.0, in1=pgb[:], op0=ALU.add, op1=ALU.mult),
                      ["gsc%d" % w, "pgb"], ["gsc%d" % w])
                X = lambda t: (xt[t % NX], "xt%d" % (t % NX))
                SS = lambda t: stat[:, 4 * t:4 * t + 1]
                RS = lambda t: stat[:, 4 * t + 1:4 * t + 2]

                def p_load(t):
                    x_, xk = X(t)
                    dma("sp", x_[:], src[t * 128:(t + 1) * 128, :], [("xres", t)], [xk])

                def p_sq(t):
                    x_, xk = X(t)
                    A("act", lambda e, x_=x_, ss=SS(t): e.activation(out=junk[:], in_=x_[:], func=AF.Square, accum_out=ss), [xk], ["junk", ("pss", t)])

                def p_r1(t):
                    A("dve", lambda e, t=t: e.tensor_scalar(out=RS(t), in0=SS(t), scalar1=1.0 / D, scalar2=EPS, op0=ALU.mult, op1=ALU.add), [("pss", t)], [("prs", t)])

                def p_r2(t):
                    A("act", lambda e, t=t: e.activation(out=RS(t), in_=RS(t), func=AF.Sqrt), [("prs", t)], [("prs", t)])

                def p_r3(t):
                    A("dve", lambda e, t=t: e.reciprocal(out=RS(t), in_=RS(t)), [("prs", t)], [("prs", t)])

                def p_stt(t):
                    w = 0 if t < 32 else 1
                    x_, xk = X(t)
                    tm, tk = tmp[t % 2], "ptmp%d" % (t % 2)
                    A("dve", lambda e, x_=x_, t=t, tm=tm, w=w: e.scalar_tensor_tensor(out=tm[:], in0=x_[:], scalar=RS(t), in1=gsc[w][:], op0=ALU.mult, op1=ALU.mult),
                      [xk, ("prs", t), "gsc%d" % w], [tk])

                def p_add(t):
                    w = 0 if t < 32 else 1
                    tm, tk = tmp[t % 2], "ptmp%d" % (t % 2)
                    h_, hk = hl[t % 2], "hl%d" % (t % 2)
                    A("pool", lambda e, tm=tm, h_=h_, w=w: e.tensor_tensor(out=h_[:], in0=tm[:], in1=shb[w][:], op=ALU.add), [tk, "shb%d" % w], [hk])

                def p_tr(t):
                    h_, hk = hl[t % 2], "hl%d" % (t % 2)
                    for k in range(8):
                        A("pe", lambda e, h_=h_, k=k: e.transpose(trp[:, k, :], h_[:, k * 128:(k + 1) * 128], idn[:]), [hk, "idn"], ["trp"])

                def p_ev(t):
                    if t % 2 == 0:
                        A("act", lambda e, t=t: e.copy(out=xb[:, :, t * 128:(t + 1) * 128], in_=trp[:]), ["trp"], [("xb", t)])
                    else:
                        A("dve", lambda e, t=t: e.tensor_copy(out=xb[:, :, t * 128:(t + 1) * 128], in_=trp[:]), ["trp"], [("xb", t)])

                pipeline([p_load, p_sq, p_r1, p_r2, p_r3, p_stt, p_add, p_tr, p_ev], NT, "pre")
            df.barrier(bart[:, 0:1])

        def post(L, last):
            src = xr if L == 0 else xs
            j = L // 2
            wo_d = w_out_even[j] if L % 2 == 0 else w_out_odd[j]
            ntile = 32 if last else NT
            with contextlib.ExitStack() as st:
                sb = lambda name, shape, dt: st.enter_context(nc.sbuf_tensor(uniq(name), shape, dt))
                wo = sb("wo", [128, 8, D], BF16)
                NA_, NB_, NXq = 4, 3, 3
                xt = [sb("qxt%d" % i, [128, D], F32) for i in range(NXq)]
                t1 = [sb("qt1%d" % i, [128, D], F32) for i in range(NA_)]
                t2 = [sb("qt2%d" % i, [128, D], F32) for i in range(NB_)]
                junk = sb("qjunk", [128, D], BF16)
                gp = [sb("gp%d" % i, [128, D], F32) for i in range(2)]
                pgb = sb("qpgb", [128, D], F32)
                stat = sb("qstat", [128, 4 * NT], F32)
                for h in range(2):
                    dma("pool", wo[:, :, h * 512:(h + 1) * 512], wo_d.rearrange("(k p) n -> p k n", p=128)[:, :, h * 512:(h + 1) * 512], (), ["wo"])
                dma("sp", pgb[:], post_g[L:L + 1, :].partition_broadcast(128), (), ["qpgb"])
                for w in range(2):
                    dma("sp", gp[w][:], modscr[L, w:w + 1, 2 * D:3 * D].partition_broadcast(128), [("modscr", L)], ["gp%d" % w])
                    A("dve", lambda e, w=w: e.tensor_tensor(out=gp[w][:], in0=gp[w][:], in1=pgb[:], op=ALU.mult), ["gp%d" % w, "qpgb"], ["gp%d" % w])
                YP = lambda t: (stp[t % 2], "stp%d" % (t % 2))
                XQ = lambda t: (xt[t % NXq], "qxt%d" % (t % NXq))
                TA = lambda t: (t1[t % NA_], "qt1%d" % (t % NA_))
                TB = lambda t: (t2[t % NB_], "qt2%d" % (t % NB_))
                SS = lambda t: stat[:, 4 * t:4 * t + 1]
                RS = lambda t: stat[:, 4 * t + 1:4 * t + 2]

                def q_mm(t):
                    yps, yk = YP(t)
                    for h in range(2):
                        for k in range(8):
                            A("pe", lambda e, yps=yps, h=h, k=k, t=t: e.matmul(yps[:, h * 512:(h + 1) * 512], lhsT=mT[:, k, t * 128:(t + 1) * 128], rhs=wo[:, k, h * 512:(h + 1) * 512], start=(k == 0), stop=(k == 7)),
                              [("mT", k, t), "wo"], [yk])

                def q_sq(t):
                    yps, yk = YP(t)
                    a_, ak = TA(t)
                    w = 0 if t < 32 else 1
                    for h in range(2):
                        A("act", lambda e, yps=yps, t=t, h=h: e.activation(out=junk[:, h * 512:(h + 1) * 512], in_=yps[:, h * 512:(h + 1) * 512], func=AF.Square, accum_out=stat[:, 4 * t + 2 + h:4 * t + 3 + h]), [yk], ["qjunk", ("qssh", t, h)])
                    A("dve", lambda e, yps=yps, a_=a_, w=w: e.tensor_tensor(out=a_[:], in0=yps[:], in1=gp[w][:], op=ALU.mult), [yk, "gp%d" % w, ("qssh", t, 0), ("qssh", t, 1)], [ak])

                def q_r1(t):
                    A("dve", lambda e, t=t: e.tensor_tensor(out=SS(t), in0=stat[:, 4 * t + 2:4 * t + 3], in1=stat[:, 4 * t + 3:4 * t + 4], op=ALU.add), [("qssh", t, 0), ("qssh", t, 1)], [("qss", t)])
                    A("dve", lambda e, t=t: e.tensor_scalar(out=RS(t), in0=SS(t), scalar1=1.0 / D, scalar2=EPS, op0=ALU.mult, op1=ALU.add), [("qss", t)], [("qrs", t)])

                def q_r2(t):
                    A("act", lambda e, t=t: e.activation(out=RS(t), in_=RS(t), func=AF.Sqrt), [("qrs", t)], [("qrs", t)])
                    x_, xk = XQ(t)
                    dma("sp", x_[:], src[t * 128:(t + 1) * 128, :], [("xres", t)], [xk])

                def q_r3(t):
                    A("dve", lambda e, t=t: e.reciprocal(out=RS(t), in_=RS(t)), [("qrs", t)], [("qrs", t)])

                def q_sc(t):
                    a_, ak = TA(t)
                    b_, bk = TB(t)
                    A("act", lambda e, a_=a_, b_=b_, t=t: e.activation(out=b_[:], in_=a_[:], func=AF.Copy, scale=RS(t)), [ak, ("qrs", t)], [bk])

                def q_add(t):
                    b_, bk = TB(t)
                    x_, xk = XQ(t)
                    A("pool", lambda e, b_=b_, x_=x_: e.tensor_tensor(out=b_[:], in0=b_[:], in1=x_[:], op=ALU.add), [bk, xk], [bk])

                def q_st(t):
                    b_, bk = TB(t)
                    dst = out[t * 128:(t + 1) * 128, :] if (last and t < 32) else xs[t * 128:(t + 1) * 128, :]
                    dma("sp", dst, b_[:], [bk], [("xres", t)])

                pipeline([q_mm, q_sq, q_r1, q_r2, q_r3, q_sc, q_add, q_st], ntile, "post")
            df.barrier(bart[:, 0:1])

        def pipeline(stages, N, key=""):
            if "noskew" in _SKIP or ("noskew_" + key) in _SKIP:
                for n_ in range(N):
                    for st_ in stages:
                        st_(n_)
                return
            K_ = len(stages)
            for step in range(N + K_ - 1):
                for k_ in reversed(range(K_)):
                    n_ = step - k_
                    if 0 <= n_ < N:
                        stages[k_](n_)

        def inproj_fm(wt, wk, tiles, evac):
            for (n0, nn) in tiles:
                bank, bk = next_mm()
                for k in range(8):
                    A("pe", lambda e, bank=bank, k=k, n0=n0, nn=nn: e.matmul(bank[:, 0:nn], lhsT=wt[:, k, :], rhs=xb[:, k, n0:n0 + nn], start=(k == 0), stop=(k == 7)),
                      [wk] + xbkeys(n0, nn), [bk])
                evac(bank, bk, n0, nn)

        def even_mixer(L):
            j = L // 2
            Wd = w_in_even[j].rearrange("(k p) n -> p k n", p=128)
            with contextlib.ExitStack() as st:
                sb = lambda name, shape, dt: st.enter_context(nc.sbuf_tensor(uniq(name), shape, dt))
                wch = [sb("wch%d" % i, [128, 8, 128], BF16) for i in range(3)]
                wci = [0]

                def load_w(c0, ncols=128):
                    i = wci[0] % 3
                    wci[0] += 1
                    dma("pool", wch[i][:, :, 0:ncols], Wd[:, :, c0:c0 + ncols], (), ["wch%d" % i])
                    return wch[i], "wch%d" % i

                with contextlib.ExitStack() as st2:
                    sb2 = lambda name, shape, dt: st2.enter_context(nc.sbuf_tensor(uniq(name), shape, dt))
                    sga = sb2("sga", [128, T], BF16)
                    X = sb2("fX", [64, 64, 128], BF16)
                    Z = sb2("fZ", [64, 64, 128], BF16)
                    Pg = [sb2("fP%d" % i, [128, 8, 128], BF16) for i in range(2)]
                    Hs = [sb2("fH%d" % i, [64, 8, 2, 128], BF16) for i in range(2)]
                    F1 = sb2("fF1", [64, 128], BF16)
                    CH = sb2("fCH", [128, 6, 128], BF16)
                    F256 = sb2("fF256", [128, 2, 512], BF16)
                    Xc = sb2("fXc", [128, 2, 128], BF16)
                    Pc = sb2("fPc", [128, 512], BF16)
                    dma("pool", F1[:], cF1, (), ["fF1"])
                    dma("pool", CH[:], cCH, (), ["fCH"])
                    dma("pool", F256[:], cF256, (), ["fF256"])
                    hcount = 0
                    pcount = 0
                    for half in range(2):
                        wt, wk = load_w(256 + half * 128)
                        inproj_fm(wt, wk, NTILES, lambda bank, bk, n0, nn: A(
                            "act", lambda e: e.activation(out=sga[:, n0:n0 + nn], in_=bank[:, 0:nn], func=AF.Silu), [bk], [("sga", n0)]))
                        sgakeys = [("sga", n0) for (n0, nn) in NTILES]
                        wt, wk = load_w(half * 128)
                        for g4 in range(16):
                            bank, bk = next_mm()
                            for q in range(4):
                                t2 = g4 * 4 + q
                                for k in range(8):
                                    A("pe", lambda e, bank=bank, q=q, k=k, t2=t2, wt=wt: e.matmul(bank[0:64, q * 128:(q + 1) * 128], lhsT=xb[:, k, t2:TL:64], rhs=wt[:, k, :], start=(k == 0), stop=(k == 7)),
                                      [wk] + [("xb", t) for t in range(32)], [bk])
                            A("act", lambda e, bank=bank, g4=g4: e.copy(out=X[:, g4 * 4:(g4 + 1) * 4, :], in_=bank[0:64, :].rearrange("p (q c) -> p q c", q=4)), [bk], ["fX"])
                        for tl in range(2):
                            bank, bk = next_mm()
                            for k in range(8):
                                A("pe", lambda e, bank=bank, k=k, tl=tl, wt=wt: e.matmul(bank[:, 0:128], lhsT=xb[:, k, TL + tl * 128:TL + (tl + 1) * 128], rhs=wt[:, k, :], start=(k == 0), stop=(k == 7)),
                                  [wk, ("xb", 32 + tl)], [bk])
                            A("dve", lambda e, bank=bank, tl=tl: e.tensor_copy(out=Xc[:, tl, :], in_=bank[:, 0:128]), [bk], ["fXc"])
                        bank, bk = next_mm()
                        for tl in range(2):
                            A("pe", lambda e, bank=bank, tl=tl: e.matmul(bank[:, :], lhsT=Xc[:, tl, :], rhs=F256[:, tl, :], start=(tl == 0), stop=(tl == 1)), ["fXc", "fF256"], [bk])
                        A("dve", lambda e, bank=bank: e.tensor_copy(out=Pc[:], in_=bank[:, :]), [bk], ["fPc"])
                        bank, bk = next_mm()
                        A("pe", lambda e, bank=bank: e.matmul(bank[:, 0:256], lhsT=CH[:, 4, :], rhs=Pc[:, 0:256], start=True, stop=False), ["fPc", "fCH"], [bk])
                        A("pe", lambda e, bank=bank: e.matmul(bank[:, 0:256], lhsT=CH[:, 5, :], rhs=Pc[:, 256:512], start=False, stop=True), ["fPc", "fCH"], [bk])
                        A("dve", lambda e, bank=bank, half=half: e.tensor_tensor(out=mT[:, half, TL:T], in0=bank[:, 0:256], in1=sga[:, TL:T], op=ALU.mult),
                          [bk] + sgakeys, [("mT", half, 32), ("mT", half, 33)])
                        for qd in range(2):
                            pb = qd * 64
                            for c4 in range(16):
                                bank, bk = next_mm()
                                for q in range(4):
                                    c = qd * 64 + c4 * 4 + q
                                    A("pe", lambda e, bank=bank, q=q, c=c: e.matmul(bank[0:64, q * 128:(q + 1) * 128], lhsT=X[:, :, c], rhs=F1[:, :], start=True, stop=True),
                                      ["fX", "fF1"], [bk])
                                if c4 % 2 == 0:
                                    A("act", lambda e, bank=bank, c4=c4: e.copy(out=Z[:, c4 * 4:(c4 + 1) * 4, :], in_=bank[0:64, :].rearrange("p (q c) -> p q c", q=4)), [bk], ["fZ"])
                                else:
                                    A("dve", lambda e, bank=bank, c4=c4: e.tensor_copy(out=Z[:, c4 * 4:(c4 + 1) * 4, :], in_=bank[0:64, :].rearrange("p (q c) -> p q c", q=4)), [bk], ["fZ"])
                            for g8 in range(8):
                                Hb = Hs[hcount % 2]
                                hk = "fH%d" % (hcount % 2)
                                hcount += 1
                                dma("sp", Hb[:], cHb[:, g8 * 8:(g8 + 1) * 8, :, :], (), [hk])
                                Pb = Pg[pcount % 2]
                                pk = "fP%d" % (pcount % 2)
                                pcount += 1
                                for b2 in range(2):
                                    bank, bk = next_mm()
                                    for q in range(4):
                                        kk = b2 * 4 + q
                                        k1 = g8 * 8 + kk
                                        for ri in range(2):
                                            A("pe", lambda e, bank=bank, q=q, kk=kk, k1=k1, ri=ri, Hb=Hb: e.matmul(bank[0:64, q * 128:(q + 1) * 128], lhsT=Z[:, :, ri * 64 + k1], rhs=Hb[:, kk, ri, :], start=(ri == 0), stop=(ri == 1)),
                                              ["fZ", hk], [bk])
                                    A("act" if b2 == 0 else "dve",
                                      (lambda e, bank=bank, b2=b2, Pb=Pb: e.copy(out=Pb[0:64, b2 * 4:(b2 + 1) * 4, :], in_=bank[0:64, :].rearrange("p (q c) -> p q c", q=4))) if b2 == 0 else
                                      (lambda e, bank=bank, b2=b2, Pb=Pb: e.tensor_copy(out=Pb[0:64, b2 * 4:(b2 + 1) * 4, :], in_=bank[0:64, :].rearrange("p (q c) -> p q c", q=4))),
                                      [bk], [pk])
                                bank, bk = next_mm()
                                ia, ib = (0, 1) if qd == 0 else (2, 3)
                                mcols = 64 if qd == 0 else 128
                                A("pe", lambda e, bank=bank, Pb=Pb, ia=ia, mcols=mcols: e.matmul(bank[0:mcols, :], lhsT=CH[0:64, ia, 0:mcols], rhs=Pb[0:64, :, 0:64], start=True, stop=False), [pk, "fCH"], [bk])
                                A("pe", lambda e, bank=bank, Pb=Pb, ib=ib, mcols=mcols: e.matmul(bank[0:mcols, :], lhsT=CH[0:64, ib, 0:mcols], rhs=Pb[0:64, :, 64:128], start=False, stop=True), [pk, "fCH"], [bk])
                                A("dve", lambda e, bank=bank, pb=pb, half=half, g8=g8: e.tensor_tensor(
                                    out=mT[pb:pb + 64, half, 0:TL].rearrange("p (k2 k1) -> p k1 k2", k1=64)[:, g8 * 8:(g8 + 1) * 8, :],
                                    in0=bank[pb:pb + 64, :].rearrange("p (a b) -> p a b", a=8),
                                    in1=sga[pb:pb + 64, 0:TL].rearrange("p (k2 k1) -> p k1 k2", k1=64)[:, g8 * 8:(g8 + 1) * 8, :], op=ALU.mult),
                                  [bk] + sgakeys, [("mT", half, t) for t in range(32)])
                df.barrier(bart[:, 0:1])

                with contextlib.ExitStack() as st2:
                    sb2 = lambda name, shape, dt: st2.enter_context(nc.sbuf_tensor(uniq(name), shape, dt))
                    qT = sb2("qT", [128, T], BF16)
                    kT = sb2("kT", [128, T], BF16)
                    sgb = sb2("sgb", [128, T], BF16)
                    vaug = sb2("vaug", [128, NT, 130], BF16)
                    eb32 = sb2("eb32", [128, 7 * 128], F32)
                    Eh = [sb2("Eh%d" % i, [128, 21 * 128], BF16) for i in range(2)]
                    PT = [sb2("PT%d" % i, [128, 7 * 128], BF16) for i in range(3)]
                    onb = sb2("onb", [128, NT, 128], BF16)
                    rden = sb2("rden", [128, 64], F32)
                    A("pool", lambda e: e.memset(vaug[:], 1.0), (), [("vaug", t4) for t4 in range(9)])
                    pti = 0
                    rdi = 0
                    for hp in range(6):
                        wt, wk = load_w(512 + hp * 128)
                        inproj_fm(wt, wk, NTILES, lambda bank, bk, n0, nn: A(
                            "act", lambda e: e.copy(out=qT[:, n0:n0 + nn], in_=bank[:, 0:nn]), [bk], [("qT", n0)]))
                        wt, wk = load_w(1280 + hp * 128)
                        inproj_fm(wt, wk, NTILES, lambda bank, bk, n0, nn: A(
                            "dve", lambda e: e.tensor_copy(out=kT[:, n0:n0 + nn], in_=bank[:, 0:nn]), [bk], [("kT", n0)]))
                        wt, wk = load_w(2816 + hp * 128)
                        inproj_fm(wt, wk, NTILES, lambda bank, bk, n0, nn: A(
                            "act", lambda e: e.activation(out=sgb[:, n0:n0 + nn], in_=bank[:, 0:nn], func=AF.Silu), [bk], [("sgb", n0)]))
                        wt, wk = load_w(2048 + hp * 128)
                        for t4 in range(9):
                            bank, bk = next_mm()
                            nq = 4 if t4 < 8 else 2
                            for q in range(nq):
                                t = t4 * 4 + q
                                for k in range(8):
                                    A("pe", lambda e, bank=bank, q=q, k=k, t=t, wt=wt: e.matmul(bank[:, q * 128:(q + 1) * 128], lhsT=xb[:, k, t * 128:(t + 1) * 128], rhs=wt[:, k, :], start=(k == 0), stop=(k == 7)),
                                      [wk, ("xb", t)], [bk])
                            for hh in range(2):
                                A("dve" if hh == 0 else "act",
                                  (lambda e, bank=bank, t4=t4, nq=nq, hh=hh: e.tensor_copy(out=vaug[:, t4 * 4:t4 * 4 + nq, hh * 65:hh * 65 + 64], in_=bank[:, 0:nq * 128].rearrange("p (q c) -> p q c", q=nq)[:, :, hh * 64:(hh + 1) * 64])) if hh == 0 else
                                  (lambda e, bank=bank, t4=t4, nq=nq, hh=hh: e.copy(out=vaug[:, t4 * 4:t4 * 4 + nq, hh * 65:hh * 65 + 64], in_=bank[:, 0:nq * 128].rearrange("p (q c) -> p q c", q=nq)[:, :, hh * 64:(hh + 1) * 64])),
                                  [bk], [("vaug", t4)])
                        for hh in range(2):
                            h = hp * 2 + hh
                            E = Eh[hh]
                            ek = "Eh%d" % hh
                            for part in range(3):
                                dma("sp", eb32[:], ebias[j, h, :, part * 896:(part + 1) * 896], (), ["eb32"])
                                A("act", lambda e, E=E, part=part: e.activation(out=E[:, part * 896:(part + 1) * 896], in_=eb32[:], func=AF.Exp), ["eb32"], [ek])
                        its = [(hh, i) for hh in range(2) for i in range(NT)]

                        def geo(n):
                            hh, i = its[n]
                            if i < 32:
                                pbase, lt = na_pattern_base(i)
                                kts = lt + [32, 33]
                            else:
                                pbase, lt = None, []
                                kts = [32, 33]
                            return hh, i, pbase, lt, kts

                        def s_qk(n):
                            hh, i, pbase, lt, kts = geo(n)
                            hb = hh * 64
                            sp_ = stp[n % 2]
                            sk = "stp%d" % (n % 2)
                            for a_, kt in enumerate(kts):
                                A("pe", lambda e, sp_=sp_, a_=a_, kt=kt, i=i, hb=hb: e.matmul(sp_[:, a_ * 128:(a_ + 1) * 128], lhsT=kT[hb:hb + 64, kt * 128:(kt + 1) * 128], rhs=qT[hb:hb + 64, i * 128:(i + 1) * 128], start=True, stop=True),
                                  [("kT", (kt // 4) * 512), ("qT", (i // 4) * 512)], [sk])

                        def s_exp(n):
                            hh, i, pbase, lt, kts = geo(n)
                            nk = len(kts)
                            sp_ = stp[n % 2]
                            sk = "stp%d" % (n % 2)
                            P_ = PT[n % 3]
                            pk = "PT%d" % (n % 3)
                            A("act", lambda e, sp_=sp_, P_=P_, nk=nk: e.activation(out=P_[:, 0:nk * 128], in_=sp_[:, 0:nk * 128], func=AF.Exp, scale=0.125), [sk], [pk])

                        def s_mul(n):
                            hh, i, pbase, lt, kts = geo(n)
                            P_ = PT[n % 3]
                            pk = "PT%d" % (n % 3)
                            if lt:
                                nl = len(lt)
                                E = Eh[hh]
                                A("dve", lambda e, P_=P_, nl=nl, E=E, pbase=pbase: e.tensor_tensor(out=P_[:, 0:nl * 128], in0=P_[:, 0:nl * 128], in1=E[:, pbase * 128:(pbase + nl) * 128], op=ALU.mult),
                                  [pk, "Eh%d" % hh], [pk])

                        def s_pv(n):
                            hh, i, pbase, lt, kts = geo(n)
                            nk = len(kts)
                            P_ = PT[n % 3]
                            pk = "PT%d" % (n % 3)
                            pvb = mm[n % 2]
                            for a_, kt in enumerate(kts):
                                A("pe", lambda e, P_=P_, a_=a_, kt=kt, hh=hh, nk=nk, pvb=pvb: e.matmul(pvb[:, 0:65], lhsT=P_[:, a_ * 128:(a_ + 1) * 128], rhs=vaug[:, kt, hh * 65:hh * 65 + 65], start=(a_ == 0), stop=(a_ == nk - 1)),
                                  [pk, ("vaug", kt // 4)], ["mm%d" % (n % 2)])

                        def s_rec(n):
                            pvb = mm[n % 2]
                            rd = rden[:, n % 64:n % 64 + 1]
                            A("dve", lambda e, rd=rd, pvb=pvb: e.reciprocal(out=rd, in_=pvb[:, 64:65]), ["mm%d" % (n % 2)], [("rden", n % 64)])

                        def s_norm(n):
                            hh, i, pbase, lt, kts = geo(n)
                            pvb = mm[n % 2]
                            rd = rden[:, n % 64:n % 64 + 1]
                            A("dve", lambda e, i=i, hh=hh, rd=rd, pvb=pvb: e.tensor_scalar(out=onb[:, i, hh * 64:(hh + 1) * 64], in0=pvb[:, 0:64], scalar1=rd, scalar2=None, op0=ALU.mult), ["mm%d" % (n % 2), ("rden", n % 64)], [("on", i, hh)])

                        def burst(i):
                            if i % 8 == 7 or i == NT - 1:
                                i0 = (i // 8) * 8
                                return i0, i - i0 + 1
                            return None

                        def s_tr(n):
                            hh, i, pbase, lt, kts = geo(n)
                            if hh == 1 and burst(i):
                                i0, nb_ = burst(i)
                                for r_ in range(nb_):
                                    ii = i0 + r_
                                    A("pe", lambda e, ii=ii, r_=r_: e.transpose(trp[:, r_, :], onb[:, ii, :], idn[:]), [("on", ii, 0), ("on", ii, 1), "idn"], ["trp"])

                        def s_gate(n):
                            hh, i, pbase, lt, kts = geo(n)
                            if hh == 1 and burst(i):
                                i0, nb_ = burst(i)
                                A("dve", lambda e, i0=i0, nb_=nb_, hp=hp: e.tensor_tensor(out=mT[:, 2 + hp, i0 * 128:(i0 + nb_) * 128], in0=trp[:, 0:nb_, :].rearrange("p a b -> p (a b)"), in1=sgb[:, i0 * 128:(i0 + nb_) * 128], op=ALU.mult),
                                  ["trp"] + [("sgb", ((i0 + r_) // 4) * 512) for r_ in range(nb_)], [("mT", 2 + hp, i0 + r_) for r_ in range(nb_)])

                        pipeline([s_qk, s_exp, s_mul, s_pv, s_rec, s_norm, s_tr, s_gate], len(its), "att")
            df.barrier(bart[:, 0:1])

        MAG = 12582912.0
        TWO_PI = 2.0 * np.pi

        def odd_mixer(L):
            j = L // 2
            Wd = w_in_odd[j].rearrange("(k p) n -> p k n", p=128)
            Uv = mT[:, 0:4, :].rearrange("p c t -> p (c t)").rearrange("p (g b) -> p g b", b=544)
            PIECES = [(0, 256), (256, 256), (512, 32)]

            with contextlib.ExitStack() as st:
              if "s5a" not in _SKIP:
                sb = lambda name, shape, dt: st.enter_context(nc.sbuf_tensor(uniq(name), shape, dt))
                Ws = sb("Ws", [128, 8, 512], BF16)
                Stm = [sb("Stm%d" % i, [128, 32, 8, 16], BF16) for i in range(5)]
                dma("pool", Ws[:], Wd[:, :, 1536:2048], (), ["Ws"])
                its_a = [(bt, t8) for bt in range(5) for t8 in range(8)]

                def a_mm(n):
                    bt, t8 = its_a[n]
                    nb = 128 if bt < 4 else 32
                    tok0 = 1024 * bt
                    bank, bk = mm[n % 2], "mm%d" % (n % 2)
                    for k in range(8):
                        A("pe", lambda e, bank=bank, k=k, nb=nb, tok0=tok0, t8=t8: e.matmul(bank[0:nb, :], lhsT=xb[:, k, tok0 + t8:tok0 + 8 * nb:8], rhs=Ws[:, k, :], start=(k == 0), stop=(k == 7)),
                          ["Ws"] + xbkeys(tok0, 8 * nb), [bk])

                def a_ev(n):
                    bt, t8 = its_a[n]
                    nb = 128 if bt < 4 else 32
                    bank, bk = mm[n % 2], "mm%d" % (n % 2)
                    S_ = Stm[bt]
                    sk = ("Stm", bt, t8)
                    if n % 2 == 0:
                        A("act", lambda e, bank=bank, nb=nb, S_=S_, t8=t8: e.copy(out=S_[0:nb, :, t8, :], in_=bank[0:nb, :].rearrange("p (g m) -> p g m", m=16)), [bk], [sk])
                    else:
                        A("dve", lambda e, bank=bank, nb=nb, S_=S_, t8=t8: e.tensor_copy(out=S_[0:nb, :, t8, :], in_=bank[0:nb, :].rearrange("p (g m) -> p g m", m=16)), [bk], [sk])

                pipeline([a_mm, a_ev], len(its_a), "s5a")
                its_b = [(bt, g8) for bt in range(5) for g8 in range(4)]

                def a_tr(n):
                    bt, g8 = its_b[n]
                    nb = 128 if bt < 4 else 32
                    S_ = Stm[bt]
                    for q in range(8):
                        g = g8 * 8 + q
                        A("pe", lambda e, S_=S_, nb=nb, g=g, q=q: e.transpose(trp[:, q, 0:nb], S_[0:nb, g, :, :].rearrange("p a b -> p (a b)"), idn[0:nb, 0:nb]), [("Stm", bt, t8) for t8 in range(8)] + ["idn"], ["trp"])

                def a_ut(n):
                    bt, g8 = its_b[n]
                    nb = 128 if bt < 4 else 32
                    if n % 2 == 0:
                        A("act", lambda e, g8=g8, bt=bt, nb=nb: e.copy(out=Uv[:, g8 * 8:(g8 + 1) * 8, bt * 128:bt * 128 + nb], in_=trp[:, :, 0:nb]), ["trp"], ["U"])
                    else:
                        A("dve", lambda e, g8=g8, bt=bt, nb=nb: e.tensor_copy(out=Uv[:, g8 * 8:(g8 + 1) * 8, bt * 128:bt * 128 + nb], in_=trp[:, :, 0:nb]), ["trp"], ["U"])

                pipeline([a_tr, a_ut], len(its_b), "s5a2")
            df.barrier(bart[:, 0:1])

            with contextlib.ExitStack() as st:
              if "s5b" not in _SKIP:
                sb = lambda name, shape, dt: st.enter_context(nc.sbuf_tensor(uniq(name), shape, dt))
                V = lambda e: e
                lam_r = sb("lam_r", [128, 32], F32)
                lam_i = sb("lam_i", [128, 32], F32)
                ls1 = sb("ls1", [128, 32], F32)
                st_tmp = contextlib.ExitStack()
                sbt = lambda name, shape, dt: st_tmp.enter_context(nc.sbuf_tensor(uniq(name), shape, dt))
                cr1 = sb("cr1", [128, 32, 16], F32)
                ci1 = sb("ci1", [128, 32, 16], F32)
                exps = sb("exps", [128, 4, 8], F32)
                mask = sb("mask", [128, 2, 128], F32)
                iota = sb("iota", [128, 544], F32)
                drep = sb("drep", [128, 32], F32)
                sm = [sb("sm%d" % i, [128, 32], F32) for i in range(14)]
                Bbr = sb("Bbr", [128, 32, 16], F32)
                Bbi = sb("Bbi", [128, 32, 16], F32)
                Wr = sb("Wr", [128, 4, 8, 32], F32)
                Wi = sb("Wi", [128, 4, 8, 32], F32)
                rho8 = sb("rho8", [128, 32], F32)
                tt8 = sb("tt8", [128, 32], F32)
                br1 = sbt("br1", [128, 32, 16], F32)
                bi1 = sbt("bi1", [128, 32, 16], F32)
                tb1 = sbt("tb1", [128, 32, 16], F32)
                tb2 = sbt("tb2", [128, 32, 16], F32)
                EA = sbt("EA", [128, 4, 8, 32], F32)
                ET = sbt("ET", [128, 4, 8, 32], F32)
                tw = sbt("tw", [128, 4, 8, 32], F32)
                dma("sp", lam_r[:], s5_lam1[j, 0], (), ["lam_r"])
                dma("sp", lam_i[:], s5_lam1[j, 1], (), ["lam_i"])
                for d_ in range(2):
                    dma("sp", ls1[d_ * 64:(d_ + 1) * 64, :], s5_ls[j, d_:d_ + 1, :].partition_broadcast(64), (), ["ls1"])
                dma("sp", br1[:], s5_b1[j, 0], (), ["br1"])
                dma("sp", bi1[:], s5_b1[j, 1], (), ["bi1"])
                dma("sp", cr1[:], s5_c1[j, 0], (), ["cr1"])
                dma("sp", ci1[:], s5_c1[j, 1], (), ["ci1"])
                dma("sp", exps[:], cEXPS, (), ["exps"])
                dma("sp", mask[:], cMASK, (), ["mask"])
                dma("sp", iota[:], cIOTA, (), ["iota"])
                dma("sp", drep[:], s5_drep[j], (), ["drep"])
                PK = ["pre"]

                def dv(fn):
                    A("dve", fn, PK + ["lam_r", "lam_i", "ls1", "br1", "bi1", "cr1", "ci1", "exps", "mask", "iota", "drep"], PK)

                def ac(fn):
                    A("act", fn, PK, PK)

                def sincos(tt, sn, cs, tmp, tmp2):
                    dv(lambda e: e.tensor_scalar(out=tmp, in0=tt, scalar1=MAG, scalar2=MAG, op0=ALU.add, op1=ALU.subtract))
                    dv(lambda e: e.tensor_tensor(out=tmp, in0=tt, in1=tmp, op=ALU.subtract))
                    ac(lambda e: e.activation(out=sn, in_=tmp, func=AF.Sin, scale=TWO_PI))
                    dv(lambda e: e.tensor_scalar(out=tmp2, in0=tt, scalar1=0.25, scalar2=None, op0=ALU.add))
                    dv(lambda e: e.tensor_scalar(out=tmp, in0=tmp2, scalar1=MAG, scalar2=MAG, op0=ALU.add, op1=ALU.subtract))
                    dv(lambda e: e.tensor_tensor(out=tmp, in0=tmp2, in1=tmp, op=ALU.subtract))
                    ac(lambda e: e.activation(out=cs, in_=tmp, func=AF.Sin, scale=TWO_PI))

                lr, dtt, a_, tht, mag1, s1, c1, w1r, w1i, den, cfr, cfi, x1, x2 = [t[:] for t in sm]
                dv(lambda e: e.tensor_scalar(out=lr, in0=lam_r[:], scalar1=-1e-4, scalar2=None, op0=ALU.min))
                ac(lambda e: e.activation(out=dtt, in_=ls1[:], func=AF.Exp))
                dv(lambda e: e.tensor_tensor(out=a_, in0=lr, in1=dtt, op=ALU.mult))
                dv(lambda e: e.tensor_tensor(out=tht, in0=lam_i[:], in1=dtt, op=ALU.mult))
                dv(lambda e: e.tensor_scalar(out=tht, in0=tht, scalar1=1.0 / TWO_PI, scalar2=None, op0=ALU.mult))
                ac(lambda e: e.activation(out=mag1, in_=a_, func=AF.Exp))
                sincos(tht, s1, c1, x1, x2)
                dv(lambda e: e.tensor_tensor(out=w1r, in0=mag1, in1=c1, op=ALU.mult))
                dv(lambda e: e.tensor_tensor(out=w1i, in0=mag1, in1=s1, op=ALU.mult))
                dv(lambda e: e.tensor_scalar(out=w1r, in0=w1r, scalar1=-1.0, scalar2=None, op0=ALU.add))
                dv(lambda e: e.tensor_tensor(out=den, in0=lr, in1=lr, op=ALU.mult))
                dv(lambda e: e.tensor_tensor(out=x1, in0=lam_i[:], in1=lam_i[:], op=ALU.mult))
                dv(lambda e: e.tensor_tensor(out=den, in0=den, in1=x1, op=ALU.add))
                dv(lambda e: e.reciprocal(out=den, in_=den))
                dv(lambda e: e.tensor_tensor(out=x1, in0=w1r, in1=lr, op=ALU.mult))
                dv(lambda e: e.tensor_tensor(out=x2, in0=w1i, in1=lam_i[:], op=ALU.mult))
                dv(lambda e: e.tensor_tensor(out=x1, in0=x1, in1=x2, op=ALU.add))
                dv(lambda e: e.tensor_tensor(out=cfr, in0=x1, in1=den, op=ALU.mult))
                dv(lambda e: e.tensor_tensor(out=x1, in0=w1i, in1=lr, op=ALU.mult))
                dv(lambda e: e.tensor_tensor(out=x2, in0=w1r, in1=lam_i[:], op=ALU.mult))
                dv(lambda e: e.tensor_tensor(out=x1, in0=x1, in1=x2, op=ALU.subtract))
                dv(lambda e: e.tensor_tensor(out=cfi, in0=x1, in1=den, op=ALU.mult))
                bc = lambda t: t.unsqueeze(2).to_broadcast([128, 32, 16])
                dv(lambda e: e.tensor_tensor(out=tb1[:], in0=br1[:], in1=bc(cfr), op=ALU.mult))
                dv(lambda e: e.tensor_tensor(out=tb2[:], in0=bi1[:], in1=bc(cfi), op=ALU.mult))
                dv(lambda e: e.tensor_tensor(out=Bbr[:], in0=tb1[:], in1=tb2[:], op=ALU.subtract))
                dv(lambda e: e.tensor_tensor(out=tb1[:], in0=bi1[:], in1=bc(cfr), op=ALU.mult))
                dv(lambda e: e.tensor_tensor(out=tb2[:], in0=br1[:], in1=bc(cfi), op=ALU.mult))
                dv(lambda e: e.tensor_tensor(out=Bbi[:], in0=tb1[:], in1=tb2[:], op=ALU.add))
                exb = exps[:].unsqueeze(3).to_broadcast([128, 4, 8, 32])
                ab = lambda t: t.unsqueeze(1).unsqueeze(1).to_broadcast([128, 4, 8, 32])
                dv(lambda e: e.tensor_tensor(out=EA[:], in0=exb, in1=ab(a_), op=ALU.mult))
                dv(lambda e: e.tensor_tensor(out=ET[:], in0=exb, in1=ab(tht), op=ALU.mult))
                ac(lambda e: e.activation(out=EA[:], in_=EA[:], func=AF.Exp))
                sincos(ET[:], Wi[:], Wr[:], tw[:], ET[:])
                dv(lambda e: e.tensor_tensor(out=Wr[:], in0=Wr[:], in1=EA[:], op=ALU.mult))
                dv(lambda e: e.tensor_tensor(out=Wi[:], in0=Wi[:], in1=EA[:], op=ALU.mult))
                dv(lambda e: e.tensor_scalar(out=x1, in0=a_, scalar1=8.0, scalar2=None, op0=ALU.mult))
                ac(lambda e: e.activation(out=rho8[:], in_=x1, func=AF.Exp))
                dv(lambda e: e.tensor_scalar(out=tt8[:], in0=tht, scalar1=8.0, scalar2=None, op0=ALU.mult))

                st_tmp.close()
                df.barrier(bart[:, 0:1])
                PK = ["pre"]
                KT = sb("KT", [128, 4, 128], BF16)
                ELTr = sb("ELTr", [128, 4, 128], BF16)
                ELTi = sb("ELTi", [128, 4, 128], BF16)
                CLr = sb("CLr", [128, 4, 128], BF16)
                nCLi = sb("nCLi", [128, 4, 128], BF16)
                Rr = sb("Rr", [128, 4, 128], BF16)
                Ri = sb("Ri", [128, 4, 128], BF16)
                Qr = sb("Qr", [128, 4, 128], BF16)
                nQi = sb("nQi", [128, 4, 128], BF16)
                ELr = sb("ELr", [128, 4, 128], BF16)
                ELi = sb("ELi", [128, 4, 128], BF16)
                p1 = sb("p1", [128, 4, 128], F32)
                p2 = sb("p2", [128, 4, 128], F32)
                scr = mT[:, 4:8, :].rearrange("p c t -> p (c t)").bitcast(F32).rearrange("p (n b) -> p n b", b=544)
                SETS = []
                for si_ in range(2):
                    d_ = {}
                    for ti_, nm in enumerate(("Er", "Ei", "cos", "sin", "gr", "gi", "sr", "si")):
                        d_[nm] = scr[:, si_ * 8 + ti_, :]
                    d_["tA"] = sb("tA%d" % si_, [128, 544], F32)[:]
                    d_["tB"] = sb("tB%d" % si_, [128, 544], F32)[:]
                    d_["Zr"] = sb("Zr%d" % si_, [128, 544], BF16)
                    d_["Zi"] = sb("Zi%d" % si_, [128, 544], BF16)
                    d_["zg"] = sb("zg%d" % si_, [128, 544], BF16)
                    d_["id"] = si_
                    SETS.append(d_)
                    A("pool", lambda e, d_=d_: e.memset(d_["Zr"][:], 0.0), (), ["Zr%d" % si_])
                    A("pool", lambda e, d_=d_: e.memset(d_["Zi"][:], 0.0), (), ["Zi%d" % si_])

                def cprod(outr, outi, l, Xr_, Xi_, g0, neg_i):
                    wv = lambda W_: W_[:, l, :, g0:g0 + 4].rearrange("p s g -> p g s").unsqueeze(3).to_broadcast([128, 4, 8, 16])
                    xv = lambda X_: X_[:, g0:g0 + 4, :].unsqueeze(2).to_broadcast([128, 4, 8, 16])
                    o4 = lambda t: t[:].rearrange("p g (s m) -> p g s m", m=16)
                    dv(lambda e: e.tensor_tensor(out=o4(p1), in0=wv(Wr), in1=xv(Xr_), op=ALU.mult))
                    dv(lambda e: e.tensor_tensor(out=o4(p2), in0=wv(Wi), in1=xv(Xi_), op=ALU.mult))
                    dv(lambda e: e.tensor_tensor(out=outr[:], in0=p1[:], in1=p2[:], op=ALU.subtract))
                    dv(lambda e: e.tensor_tensor(out=o4(p1), in0=wv(Wr), in1=xv(Xi_), op=ALU.mult))
                    dv(lambda e: e.tensor_tensor(out=o4(p2), in0=wv(Wi), in1=xv(Xr_), op=ALU.mult))
                    if neg_i:
                        dv(lambda e: e.scalar_tensor_tensor(out=outi[:], in0=p1[:], scalar=-1.0, in1=p2[:], op0=ALU.mult, op1=ALU.subtract))
                    else:
                        dv(lambda e: e.tensor_tensor(out=outi[:], in0=p1[:], in1=p2[:], op=ALU.add))

                for qq in range(8):
                    g0 = qq * 4
                    cprod(Rr, Ri, 0, Bbr, Bbi, g0, False)
                    cprod(ELr, ELi, 1, Bbr, Bbi, g0, False)
                    cprod(Qr, nQi, 2, cr1, ci1, g0, True)
                    cprod(CLr, nCLi, 3, cr1, ci1, g0, True)
                    for q in range(4):
                        for h_ in range(2):
                            hb = h_ * 64
                            bank = mm[h_]
                            bk = "mm%d" % h_
                            A("pe", lambda e, bank=bank, hb=hb, q=q: e.matmul(bank[:, 0:128], lhsT=Rr[hb:hb + 64, q, :], rhs=Qr[hb:hb + 64, q, :], start=True, stop=False), PK, [bk])
                            A("pe", lambda e, bank=bank, hb=hb, q=q: e.matmul(bank[:, 0:128], lhsT=Ri[hb:hb + 64, q, :], rhs=nQi[hb:hb + 64, q, :], start=False, stop=True), PK, [bk])
                        A("dve", lambda e: e.tensor_tensor(out=p1[:, 0, :], in0=mm[0][:, 0:128], in1=mask[:, 0, :], op=ALU.mult), ["mm0"] + PK, PK)
                        A("dve", lambda e: e.tensor_tensor(out=p2[:, 0, :], in0=mm[1][:, 0:128], in1=mask[:, 1, :], op=ALU.mult), ["mm1"] + PK, PK)
                        A("dve", lambda e: e.tensor_tensor(out=p1[:, 0, :], in0=p1[:, 0, :], in1=p2[:, 0, :], op=ALU.add), PK, PK)
                        A("dve", lambda e, q=q, g0=g0: e.scalar_tensor_tensor(out=KT[:, q, :], in0=idn32[:], scalar=drep[:, g0 + q:g0 + q + 1], in1=p1[:, 0, :], op0=ALU.mult, op1=ALU.add), PK + ["drep", "idn32"], PK)
                        A("pe", lambda e, q=q: e.transpose(trp[:, 0, :], ELr[:, q, :], idn[:]), PK + ["idn"], ["trp"])
                        A("pe", lambda e, q=q: e.transpose(trp[:, 1, :], ELi[:, q, :], idn[:]), PK + ["idn"], ["trp"])
                        A("act", lambda e, q=q: e.copy(out=ELTr[:, q, :], in_=trp[:, 0, :]), ["trp"] + PK, PK)
                        A("act", lambda e, q=q: e.copy(out=ELTi[:, q, :], in_=trp[:, 1, :]), ["trp"] + PK, PK)
                    def grp(q, g, S):
                        sid = S["id"]
                        K_ = lambda nm: "%s%d" % (nm, sid)
                        Eb = stp[sid]
                        ebk = "stp%d" % sid
                        pvo = sid * 128
                        Er, Ei, cosT, sinT, gr, gi, sr, si, tA = S["Er"], S["Ei"], S["cos"], S["sin"], S["gr"], S["gi"], S["sr"], S["si"], S["tA"]
                        Zr_, Zi_, z_ = S["Zr"], S["Zi"], S["zg"]

                        def st0():
                            for ri, ELT_ in enumerate((ELTr, ELTi)):
                                for (b0, nb) in PIECES:
                                    if b0 < 512:
                                        yo = Eb[:, ri * 512 + b0:ri * 512 + b0 + nb]
                                        wk_ = [ebk]
                                    else:
                                        yo = pv[:, pvo + ri * 64:pvo + ri * 64 + nb]
                                        wk_ = ["pv"]
                                    A("pe", lambda e, yo=yo, ELT_=ELT_, b0=b0, nb=nb: e.matmul(yo, lhsT=ELT_[:, q, :], rhs=Uv[:, g, b0:b0 + nb], start=True, stop=True), PK + ["U"], wk_)

                        def st1():
                            for ri, E_ in enumerate((Er, Ei)):
                                ek = K_("Er" if ri == 0 else "Ei")
                                A("act", lambda e, E_=E_, ri=ri: e.copy(out=E_[0:64, 32:544], in_=Eb[0:64, ri * 512:(ri + 1) * 512]), [ebk], [ek])
                                A("act", lambda e, E_=E_, ri=ri: e.copy(out=E_[0:64, 0:32], in_=pv[0:64, pvo + ri * 64:pvo + ri * 64 + 32]), ["pv"], [ek])
                                A("act", lambda e, E_=E_, ri=ri: e.copy(out=E_[64:128, 543:31:-1], in_=Eb[64:128, ri * 512:(ri + 1) * 512]), [ebk], [ek])
                                A("act", lambda e, E_=E_, ri=ri: e.copy(out=E_[64:128, 31::-1], in_=pv[64:128, pvo + ri * 64:pvo + ri * 64 + 32]), ["pv"], [ek])
                            A("act", lambda e: e.activation(out=gr, in_=iota[:], func=AF.Copy, scale=tt8[:, g:g + 1]), PK + ["iota", K_("gr")], [K_("gr")])
                            A("act", lambda e: e.activation(out=gi, in_=gr, func=AF.Copy, bias=MAG), [K_("gr"), K_("gi")], [K_("gi")])

                        def st2():
                            A("act", lambda e: e.activation(out=gi, in_=gi, func=AF.Copy, bias=-MAG), [K_("gi")], [K_("gi")])
                            A("dve", lambda e: e.tensor_tensor(out=gi, in0=gr, in1=gi, op=ALU.subtract), [K_("gr"), K_("gi")], [K_("gi")])

                        def st3():
                            A("act", lambda e: e.activation(out=sinT, in_=gi, func=AF.Sin, scale=TWO_PI), [K_("gi")], [K_("sin")])
                            A("act", lambda e: e.activation(out=gr, in_=gr, func=AF.Copy, bias=0.25), [K_("gr")], [K_("gr")])
                            A("act", lambda e: e.activation(out=gi, in_=gr, func=AF.Copy, bias=MAG), [K_("gr"), K_("gi"), K_("sin")], [K_("gi")])

                        def st4():
                            A("act", lambda e: e.activation(out=gi, in_=gi, func=AF.Copy, bias=-MAG), [K_("gi")], [K_("gi")])
                            A("dve", lambda e: e.tensor_tensor(out=gi, in0=gr, in1=gi, op=ALU.subtract), [K_("gr"), K_("gi")], [K_("gi")])

                        def st5():
                            A("act", lambda e: e.activation(out=cosT, in_=gi, func=AF.Sin, scale=TWO_PI), [K_("gi")], [K_("cos")])

                        tB = S["tB"]

                        def st6a():
                            A("dve", lambda e: e.tensor_tensor(out=gr, in0=Er, in1=cosT, op=ALU.mult), [K_("Er"), K_("cos"), K_("gr")], [K_("gr")])
                            A("dve", lambda e: e.tensor_tensor(out=tA, in0=Ei, in1=sinT, op=ALU.mult), [K_("Ei"), K_("sin")], [K_("tA")])
                            A("dve", lambda e: e.tensor_tensor(out=gi, in0=Ei, in1=cosT, op=ALU.mult), [K_("Ei"), K_("cos"), K_("gi")], [K_("gi")])
                            A("dve", lambda e: e.tensor_tensor(out=tB, in0=Er, in1=sinT, op=ALU.mult), [K_("Er"), K_("sin")], [K_("tB")])

                        def st6b():
                            A("dve", lambda e: e.tensor_tensor(out=gr, in0=gr, in1=tA, op=ALU.add), [K_("gr"), K_("tA")], [K_("gr")])
                            A("dve", lambda e: e.tensor_tensor(out=gi, in0=gi, in1=tB, op=ALU.subtract), [K_("gi"), K_("tB")], [K_("gi")])

                        def st7():
                            rb = rho8[:, g:g + 1].to_broadcast([128, 544])
                            A("dve", lambda e: e.tensor_tensor_scan(out=sr, data0=rb, data1=gr, initial=0.0, op0=ALU.mult, op1=ALU.add), [K_("gr")] + PK, [K_("sr")])
                            A("dve", lambda e: e.tensor_tensor_scan(out=si, data0=rb, data1=gi, initial=0.0, op0=ALU.mult, op1=ALU.add), [K_("gi")] + PK, [K_("si")])

                        def st8a():
                            A("dve", lambda e: e.tensor_tensor(out=Er, in0=sr, in1=cosT, op=ALU.mult), [K_("sr"), K_("cos"), K_("Er")], [K_("Er")])
                            A("dve", lambda e: e.tensor_tensor(out=Ei, in0=si, in1=sinT, op=ALU.mult), [K_("si"), K_("sin"), K_("Ei")], [K_("Ei")])
                            A("dve", lambda e: e.tensor_tensor(out=gr, in0=sr, in1=sinT, op=ALU.mult), [K_("sr"), K_("sin"), K_("gr")], [K_("gr")])
                            A("dve", lambda e: e.tensor_tensor(out=gi, in0=si, in1=cosT, op=ALU.mult), [K_("si"), K_("cos"), K_("gi")], [K_("gi")])

                        def wr(Z_, zk, P1, k1, P2, k2, op):
                            A("dve", lambda e: e.tensor_tensor(out=Z_[0:64, 0:512], in0=P1[0:64, 31:543], in1=P2[0:64, 31:543], op=op), [k1, k2], [zk])
                            A("dve", lambda e: e.tensor_tensor(out=Z_[0:64, 513:544], in0=P1[0:64, 0:31], in1=P2[0:64, 0:31], op=op), [k1, k2], [zk])
                            A("dve", lambda e: e.tensor_tensor(out=Z_[64:128, 542::-1], in0=P1[64:128, 0:543], in1=P2[64:128, 0:543], op=op), [k1, k2], [zk])

                        def st8b():
                            wr(Zr_, K_("Zr"), Er, K_("Er"), Ei, K_("Ei"), ALU.subtract)
                            wr(Zi_, K_("Zi"), gr, K_("gr"), gi, K_("gi"), ALU.add)

                        def st8():
                            for (b0, nb) in PIECES:
                                if b0 < 512:
                                    yo = mm[0][:, b0:b0 + nb]
                                    wk_ = ["mm0"]
                                else:
                                    yo = mm[1][:, 0:nb]
                                    wk_ = ["mm1"]
                                A("pe", lambda e, yo=yo, b0=b0, nb=nb: e.matmul(yo, lhsT=KT[:, q, :], rhs=Uv[:, g, b0:b0 + nb], start=True, stop=False), PK + ["U"], wk_)
                                A("pe", lambda e, yo=yo, b0=b0, nb=nb: e.matmul(yo, lhsT=CLr[:, q, :], rhs=Zr_[:, b0:b0 + nb], start=False, stop=False), PK + [K_("Zr")], wk_)
                                A("pe", lambda e, yo=yo, b0=b0, nb=nb: e.matmul(yo, lhsT=nCLi[:, q, :], rhs=Zi_[:, b0:b0 + nb], start=False, stop=True), PK + [K_("Zi")], wk_)
                            A("act", lambda e: e.activation(out=z_[:, 0:512], in_=mm[0][:, 0:512], func=AF.Gelu), ["mm0"], [K_("zg")])
                            A("act", lambda e: e.activation(out=z_[:, 512:544], in_=mm[1][:, 0:32], func=AF.Gelu), ["mm1"], [K_("zg")])
                            for t8 in range(8):
                                dma("sp", zs[t8, g], z_[t8 * 16:(t8 + 1) * 16, :], [K_("zg")], ["zs"])

                        return [st0, st1, st2, st3, st4, st5, st6a, st6b, st7, st8a, st8b], st8

                    for pr in range(2):
                        gA, gB = g0 + 2 * pr, g0 + 2 * pr + 1
                        stA, yA = grp(2 * pr, gA, SETS[0])
                        stB, yB = grp(2 * pr + 1, gB, SETS[1])
                        for k_ in range(len(stA)):
                            stA[k_]()
                            stB[k_]()
                        yA()
                        yB()
            df.barrier(bart[:, 0:1])

            with contextlib.ExitStack() as st:
              if "s5c" not in _SKIP:
                sb = lambda name, shape, dt: st.enter_context(nc.sbuf_tensor(uniq(name), shape, dt))
                Wg = sb("Wg", [128, 4, 1024], BF16)
                bg = sb("bg", [128, 8], F32)
                sgd = sb("sgd", [128, T], BF16)
                zsb = [sb("zsb%d" % i, [128, 4, 544], BF16) for i in range(2)]
                sig = [sb("sig%d" % i, [128, 544], F32) for i in range(2)]
                v1 = [sb("v1%d" % i, [128, 544], F32) for i in range(2)]
                wgd = sb("wgd", [128, 8, 128], BF16)
                for h_ in range(2):
                    dma("pool", Wg[:, :, h_ * 512:(h_ + 1) * 512], glu_w[j].rearrange("(c p) n -> p c n", p=128)[:, :, h_ * 512:(h_ + 1) * 512], (), ["Wg"])
                dma("sp", bg[:], glu_bT[j], (), ["bg"])
                for k in range(4):
                    dma("pool", wgd[:], Wd[:, :, 2048 + k * 128:2048 + (k + 1) * 128], (), ["wgd"])
                    inproj_fm(wgd, "wgd", NTILES, lambda bank, bk, n0, nn: A(
                        "act", lambda e: e.activation(out=sgd[:, n0:n0 + nn], in_=bank[:, 0:nn], func=AF.Silu), [bk], ["sgd"]))

                    def banks(n):
                        if n % 2 == 0:
                            return (mm[0], "mm0"), (mm[1], "mm1"), (pv[:, 0:32], "pv"), (pv[:, 256:288], "pv")
                        return (stp[0][:, 0:512], "stp0a"), (stp[0][:, 512:1024], "stp0b"), (stp[1][:, 0:32], "stp1a"), (stp[1][:, 512:544], "stp1b")

                    def c_load(n):
                        zb, zbk = zsb[n % 2], "zsb%d" % (n % 2)
                        dma("sp", zb[:], zs[n].rearrange("g m b -> (g m) b").rearrange("(c p) b -> p c b", p=128), ["zs"], [zbk])

                    def c_mm(n):
                        zb, zbk = zsb[n % 2], "zsb%d" % (n % 2)
                        (vb, vk), (gb_, gk), (vp, vpk), (gp_, gpk) = banks(n)
                        for (b0, nb) in PIECES:
                            for vg in range(2):
                                if b0 < 512:
                                    yo = (vb if vg == 0 else gb_)[:, b0:b0 + nb]
                                    wk_ = [vk if vg == 0 else gk]
                                else:
                                    yo = vp if vg == 0 else gp_
                                    wk_ = [vpk if vg == 0 else gpk]
                                col = (vg * 4 + k) * 128
                                for c in range(4):
                                    A("pe", lambda e, yo=yo, c=c, col=col, zb=zb, b0=b0, nb=nb: e.matmul(yo, lhsT=Wg[:, c, col:col + 128], rhs=zb[:, c, b0:b0 + nb], start=(c == 0), stop=(c == 3)),
                                      ["Wg", zbk], wk_)

                    def c_sig(n):
                        (vb, vk), (gb_, gk), (vp, vpk), (gp_, gpk) = banks(n)
                        sg_, sgk = sig[n % 2], "sig%d" % (n % 2)
                        A("act", lambda e, gb_=gb_, sg_=sg_, k=k: e.activation(out=sg_[:, 0:512], in_=gb_[:, 0:512], func=AF.Sigmoid, bias=bg[:, 4 + k:5 + k]), [gk, "bg"], [sgk])
                        A("act", lambda e, gp_=gp_, sg_=sg_, k=k: e.activation(out=sg_[:, 512:544], in_=gp_, func=AF.Sigmoid, bias=bg[:, 4 + k:5 + k]), [gpk, "bg"], [sgk])

                    def c_stt(n):
                        (vb, vk), (gb_, gk), (vp, vpk), (gp_, gpk) = banks(n)
                        sg_, sgk = sig[n % 2], "sig%d" % (n % 2)
                        v_, v1k = v1[n % 2], "v1%d" % (n % 2)
                        A("dve", lambda e, vb=vb, sg_=sg_, v_=v_, k=k: e.scalar_tensor_tensor(out=v_[:, 0:512], in0=vb[:, 0:512], scalar=bg[:, k:k + 1], in1=sg_[:, 0:512], op0=ALU.add, op1=ALU.mult), [vk, sgk, "bg"], [v1k])
                        A("dve", lambda e, vp=vp, sg_=sg_, v_=v_, k=k: e.scalar_tensor_tensor(out=v_[:, 512:544], in0=vp, scalar=bg[:, k:k + 1], in1=sg_[:, 512:544], op0=ALU.add, op1=ALU.mult), [vpk, sgk, "bg"], [v1k])

                    def c_out(n):
                        v_, v1k = v1[n % 2], "v1%d" % (n % 2)
                        A("pool", lambda e, v_=v_, n=n, k=k: e.tensor_tensor(out=mT[:, 4 + k, n::8], in0=v_[:], in1=sgd[:, n::8], op=ALU.mult), [v1k, "sgd"], [("mT", 4 + k, t) for t in range(NT)])

                    pipeline([c_load, c_mm, c_sig, c_stt, c_out], 8, "s5c")
            df.barrier(bart[:, 0:1])

            with contextlib.ExitStack() as st:
              if "gmlp" not in _SKIP:
                sb = lambda name, shape, dt: st.enter_context(nc.sbuf_tensor(uniq(name), shape, dt))
                Wv = sb("Wv", [128, 8, 512], BF16)
                Wu = sb("Wu", [128, 8, 512], BF16)
                Wc = sb("Wc", [128, 8, 512], BF16)
                wsT = sb("wsT", [128, 4, 128], BF16)
                sgT = sb("sgT", [128, 4], F32)
                bsb = sb("bsb", [128, 512], F32)
                st6 = sb("st6", [128, 12], F32)
                mv = sb("mv", [128, 4 * NT], F32)
                vn = [sb("vn%d" % i, [128, 512], BF16) for i in range(2)]
                sgc = [sb("sgc%d" % i, [128, 512], BF16) for i in range(2)]
                mx = [sb("mx%d" % i, [128, 512], F32) for i in range(2)]
                dma("pool", Wu[:], Wd[:, :, 0:512], (), ["Wu"])
                dma("pool", Wv[:], Wd[:, :, 512:1024], (), ["Wv"])
                dma("pool", Wc[:], Wd[:, :, 1024:1536], (), ["Wc"])
                dma("pool", wsT[:], sgu_wT[j], (), ["wsT"])
                dma("sp", sgT[:], sgu_gT[j], (), ["sgT"])
                dma("sp", bsb[:], sgu_b[j:j + 1, :].partition_broadcast(128), (), ["bsb"])
                vsb = [sb("vsb%d" % i, [128, 512], F32) for i in range(4)]
                tmx = [sb("tmx%d" % i, [128, 512], F32) for i in range(2)]
                MEAN = lambda n: mv[:, 4 * n:4 * n + 1]
                VAR = lambda n: mv[:, 4 * n + 1:4 * n + 2]
                RSg = lambda n: mv[:, 4 * n + 2:4 * n + 3]
                VS = lambda n: (vsb[n % 4], "vsb%d" % (n % 4))

                def g_v(n):
                    bank, bk = mm[n % 2], "mm%d" % (n % 2)
                    for k in range(8):
                        A("pe", lambda e, bank=bank, k=k, n=n: e.matmul(bank[:, :], lhsT=xb[:, k, n * 128:(n + 1) * 128], rhs=Wv[:, k, :], start=(k == 0), stop=(k == 7)), ["Wv", ("xb", n)], [bk])

                def g_cp(n):
                    bank, bk = mm[n % 2], "mm%d" % (n % 2)
                    vs_, vsk = VS(n)
                    A("act", lambda e, bank=bank, vs_=vs_: e.copy(out=vs_[:], in_=bank[:, :]), [bk], [vsk])

                def g_bn(n):
                    vs_, vsk = VS(n)
                    s6 = st6[:, (n % 2) * 6:(n % 2) * 6 + 6]
                    A("dve", lambda e, vs_=vs_, s6=s6: e.bn_stats(out=s6, in_=vs_[:]), [vsk], [("st6", n % 2)])
                    A("dve", lambda e, n=n, s6=s6: e.bn_aggr(out=mv[:, 4 * n:4 * n + 2], in_=s6), [("st6", n % 2)], [("mv", n)])

                def g_r1(n):
                    A("dve", lambda e, n=n: e.tensor_scalar(out=RSg(n), in0=VAR(n), scalar1=1.0, scalar2=EPS, op0=ALU.mult, op1=ALU.add), [("mv", n)], [("grs", n)])

                def g_r2(n):
                    A("act", lambda e, n=n: e.activation(out=RSg(n), in_=RSg(n), func=AF.Sqrt), [("grs", n)], [("grs", n)])

                def g_vn(n):
                    vs_, vsk = VS(n)
                    v_, vk = vn[n % 2], "vn%d" % (n % 2)
                    A("dve", lambda e, n=n: e.reciprocal(out=RSg(n), in_=RSg(n)), [("grs", n)], [("grs", n)])
                    A("dve", lambda e, vs_=vs_, v_=v_, n=n: e.tensor_scalar(out=v_[:], in0=vs_[:], scalar1=MEAN(n), scalar2=RSg(n), op0=ALU.subtract, op1=ALU.mult), [vsk, ("mv", n), ("grs", n)], [vk])

                def g_mm(n):
                    v_, vk = vn[n % 2], "vn%d" % (n % 2)
                    for g in range(4):
                        A("pe", lambda e, v_=v_, g=g: e.matmul(pv[:, g * 128:(g + 1) * 128], lhsT=v_[:, g * 128:(g + 1) * 128], rhs=wsT[:, g, :], start=True, stop=True), [vk, "wsT"], ["pv"])
                    sp_ = stp[n % 2]
                    for half, W_, wkk in ((0, Wu, "Wu"), (1, Wc, "Wc")):
                        sk = "stp%d%s" % (n % 2, "ab"[half])
                        for g in range(4):
                            for k in range(8):
                                A("pe", lambda e, sp_=sp_, half=half, W_=W_, g=g, k=k, n=n: e.matmul(sp_[:, half * 512 + g * 128:half * 512 + (g + 1) * 128], lhsT=W_[:, k, g * 128:(g + 1) * 128], rhs=xb[:, k, n * 128:(n + 1) * 128], start=(k == 0), stop=(k == 7)),
                                  [wkk, ("xb", n)], [sk])

                def g_ep(n):
                    sp_ = stp[n % 2]
                    c_, ck = sgc[n % 2], "sgc%d" % (n % 2)
                    m_, mk = mx[n % 2], "mx%d" % (n % 2)
                    t_, tk_ = tmx[n % 2], "tmx%d" % (n % 2)
                    A("act", lambda e, sp_=sp_, c_=c_: e.activation(out=c_[:], in_=sp_[:, 512:1024], func=AF.Silu), ["stp%db" % (n % 2)], [ck])
                    for g in range(4):
                        A("dve", lambda e, m_=m_, g=g: e.scalar_tensor_tensor(out=m_[:, g * 128:(g + 1) * 128], in0=pv[:, g * 128:(g + 1) * 128], scalar=sgT[:, g:g + 1], in1=bsb[:, g * 128:(g + 1) * 128], op0=ALU.mult, op1=ALU.add),
                          ["pv", "sgT", "bsb"], [mk])
                    A("dve", lambda e, m_=m_, sp_=sp_, t_=t_: e.tensor_tensor(out=t_[:], in0=m_[:], in1=sp_[:, 0:512], op=ALU.mult), [mk, "stp%da" % (n % 2)], [tk_])

                def g_out(n):
                    c_, ck = sgc[n % 2], "sgc%d" % (n % 2)
                    t_, tk_ = tmx[n % 2], "tmx%d" % (n % 2)
                    A("pool", lambda e, t_=t_, c_=c_, n=n: e.tensor_tensor(out=mT[:, 0:4, n * 128:(n + 1) * 128], in0=t_[:].rearrange("p (g c) -> p g c", g=4), in1=c_[:].rearrange("p (g c) -> p g c", g=4), op=ALU.mult),
                      [tk_, ck], [("mT", g, n) for g in range(4)])

                pipeline([g_v, g_cp, g_bn, g_r1, g_r2, g_vn, g_mm, g_ep, g_out], NT, "gmlp")
            df.barrier(bart[:, 0:1])

        for L in range(n_layers):
            last = (L == n_layers - 1)
            (prenorm_old if "oldpre" in _SKIP else prenorm)(L)
            if L % 2 == 0:
                if "even" not in _SKIP:
                    even_mixer(L)
            else:
                odd_mixer(L)
            (post_old if "oldpost" in _SKIP else post)(L, last)
        df.emit()
    return nc


_CACHE = {}


def kernel(x, c, ctx, c_ctx, w_ada, b_ada, pre_g, post_g, w_in_even, w_out_even, na_rpb,
           w_in_odd, w_out_odd, sgu_w, sgu_b, sgu_g, s5_lam_re, s5_lam_im, s5_log_step,
           s5_b_re, s5_b_im, s5_c_re, s5_c_im, s5_d, glu_w, glu_b, _n_layers=DEPTH):
    f = lambda a: np.ascontiguousarray(np.asarray(a), dtype=np.float32)
    B = x.shape[0]
    if _n_layers not in _CACHE:
        _CACHE[_n_layers] = build_program(_n_layers)
    nc = _CACHE[_n_layers]
    F1, H, CH, F256 = fnet_consts()
    eb = np.stack([build_ebias(f(na_rpb[j])) for j in range(2)]).reshape(2, 12, 128, 21 * 128)
    shared = dict(w_ada=f(w_ada), b_ada=f(b_ada), pre_g=f(pre_g), post_g=f(post_g), w_in_even=f(w_in_even),
                  w_out_even=f(w_out_even), ebias=eb, cF1=F1, cH=H, cCH=CH, cF256=F256)
    ex, mk, io = s5_consts()
    lam1 = np.stack([f(s5_lam_re), f(s5_lam_im)], axis=1)
    lam1 = np.ascontiguousarray(lam1.transpose(0, 1, 2, 4, 3)).reshape(2, 2, 128, 32)
    b1 = np.stack([f(s5_b_re), f(s5_b_im)], axis=1)
    b1 = np.ascontiguousarray(b1.transpose(0, 1, 2, 4, 3, 5)).reshape(2, 2, 128, 32, 16)
    c1 = np.stack([f(s5_c_re), f(s5_c_im)], axis=1)
    c1 = np.ascontiguousarray(c1.transpose(0, 1, 2, 5, 3, 4)).reshape(2, 2, 128, 32, 16)
    drep = np.ascontiguousarray(np.tile(f(s5_d).reshape(2, 32, 16).transpose(0, 2, 1), (1, 8, 1)))
    shared.update(dict(
        w_in_odd=f(w_in_odd), w_out_odd=f(w_out_odd),
        sgu_wT=np.ascontiguousarray(f(sgu_w).transpose(0, 3, 1, 2)),
        sgu_gT=np.ascontiguousarray(f(sgu_g).reshape(2, 4, 128).transpose(0, 2, 1)),
        sgu_b=f(sgu_b).reshape(2, 512), s5_lam1=lam1, s5_ls=f(s5_log_step), s5_b1=b1, s5_c1=c1, s5_drep=drep,
        glu_w=f(glu_w), glu_bT=np.ascontiguousarray(f(glu_b).reshape(2, 8, 128).transpose(0, 2, 1)),
        cEXPS=ex, cMASK=mk, cIOTA=io))
    cc = f(c_ctx).reshape(8, 128).T
    in_maps = []
    for b in range(B):
        m = dict(shared)
        m["xr"] = np.concatenate([f(x[b]), f(ctx[b])], axis=0)
        m["cT"] = np.ascontiguousarray(np.concatenate([f(c[b]).reshape(8, 128).T, cc], axis=1))
        in_maps.append(m)
    res = run_bass_kernel_spmd(nc, in_maps, core_ids=list(range(B)))
    return np.stack([np.asarray(r["out"], dtype=np.float32) for r in res.results], axis=0)
```

```python
import contextlib
import numpy as np
import concourse.bass as bass
import concourse.mybir as mybir
from concourse.bass_utils import run_bass_kernel_spmd

F32 = mybir.dt.float32
BF16 = mybir.dt.bfloat16
AF = mybir.ActivationFunctionType
ALU = mybir.AluOpType

ENGS = ("pe", "act", "dve", "pool", "sp")
RING = 8
SELF_SYNC = ("act", "dve", "pool")

D = 1024
T = 4352
TL = 4096
NT = 34
EPS = 1e-6
DEPTH = 4


class _Op:
    __slots__ = ("eng", "fn", "deps", "dma", "flag", "cnt", "ring", "target")


class DF:
    def __init__(self, nc):
        self.nc = nc
        self.ops = []
        self.lw = {}
        self.rd = {}
        self.ndma = {e: 0 for e in ENGS}
        self.last_on = {}
        self.bar = None
        self.dma_since_bar = []

    def add(self, eng, fn, reads=(), writes=(), dma=False):
        idx = len(self.ops)
        deps = set()
        if self.bar is not None:
            deps.add(self.bar)
        for r in reads:
            w = self.lw.get(r)
            if w is not None:
                deps.add(w)
        for r in writes:
            w = self.lw.get(r)
            if w is not None:
                deps.add(w)
            deps.update(self.rd.get(r, ()))
        for r in reads:
            self.rd.setdefault(r, []).append(idx)
        for r in writes:
            self.lw[r] = idx
            self.rd[r] = []
        o = _Op()
        o.eng, o.fn, o.deps, o.dma, o.flag, o.cnt = eng, fn, deps, dma, False, 0
        o.ring = o.target = None
        if dma:
            n = self.ndma[eng]
            self.ndma[eng] = n + 1
            o.ring = n % RING
            o.target = 16 * (n // RING + 1)
            self.dma_since_bar.append(idx)
        else:
            self.last_on[eng] = idx
        self.ops.append(o)
        return idx

    def barrier(self, tile):
        idx = len(self.ops)
        deps = set(self.last_on.values()) | set(self.dma_since_bar)
        if self.bar is not None:
            deps.add(self.bar)
        o = _Op()
        o.eng, o.fn, o.deps, o.dma, o.flag, o.cnt = "pool", (lambda e: e.memset(tile, 0.0)), deps, False, False, 0
        o.ring = o.target = None
        self.ops.append(o)
        self.last_on["pool"] = idx
        self.bar = idx
        self.dma_since_bar = []
        self.lw = {}
        self.rd = {}

    def emit(self):
        nc = self.nc
        ops = self.ops
        for o in ops:
            for d in o.deps:
                p = ops[d]
                if p.dma:
                    continue
                if p.eng != o.eng or (o.eng in SELF_SYNC) or o.dma:
                    p.flag = True
        cnt = {e: 0 for e in ENGS}
        for o in ops:
            if o.flag and not o.dma:
                cnt[o.eng] += 1
                o.cnt = cnt[o.eng]
        with contextlib.ExitStack() as st:
            csem = {e: st.enter_context(nc.semaphore("c_" + e)) for e in ENGS}
            dsem = {e: [st.enter_context(nc.semaphore("d_%s%d" % (e, i))) for i in range(RING)]
                    for e in ("sp", "pool", "act")}
            block = st.enter_context(nc.Block())
            ndma = self.ndma

            def run(engname, eng):
                waited_c = {e: 0 for e in ENGS}
                waited_d = {}
                for o in ops:
                    if o.eng != engname:
                        continue
                    for d in sorted(o.deps):
                        p = ops[d]
                        if p.dma:
                            key = (p.eng, p.ring)
                            if waited_d.get(key, 0) < p.target:
                                eng.wait_ge(dsem[p.eng][p.ring], p.target)
                                waited_d[key] = p.target
                        else:
                            if p.eng == engname and not (engname in SELF_SYNC or o.dma):
                                continue
                            if waited_c[p.eng] < p.cnt:
                                eng.wait_ge(csem[p.eng], p.cnt)
                                waited_c[p.eng] = p.cnt
                    if o.dma and o.target > 16:
                        key = (engname, o.ring)
                        if waited_d.get(key, 0) < o.target - 16:
                            eng.wait_ge(dsem[engname][o.ring], o.target - 16)
                            waited_d[key] = o.target - 16
                    ins = o.fn(eng)
                    if o.dma:
                        ins.then_inc(dsem[engname][o.ring], 16)
                    elif o.flag:
                        ins.then_inc(csem[engname], 1)
                if engname in dsem:
                    n = ndma[engname]
                    for r in range(min(n, RING)):
                        last = ((n - 1 - r) // RING) * RING + r
                        eng.wait_ge(dsem[engname][r], 16 * (last // RING + 1))

            @block.tensor
            def _(eng):
                run("pe", eng)

            @block.scalar
            def _(eng):
                run("act", eng)

            @block.vector
            def _(eng):
                run("dve", eng)

            @block.gpsimd
            def _(eng):
                run("pool", eng)

            @block.sync
            def _(eng):
                run("sp", eng)


NA_COMBOS = ([(2, kt) for kt in range(0, 5)] + [(0, kt) for kt in range(4)] + [(1, kt) for kt in range(4)]
             + [(30, kt) for kt in range(28, 32)] + [(31, kt) for kt in range(28, 32)])


def na_pattern_base(i):
    if 2 <= i <= 29:
        return 0, list(range(i - 2, i + 3))
    if i == 0:
        return 5, [0, 1, 2, 3]
    if i == 1:
        return 9, [0, 1, 2, 3]
    if i == 30:
        return 13, [28, 29, 30, 31]
    return 17, [28, 29, 30, 31]


def build_ebias(rpb):
    out = np.empty((12, 128, 21, 128), np.float32)
    a = np.arange(2)
    c = np.arange(64)
    for pi, (i, kt) in enumerate(NA_COMBOS):
        kr = (2 * kt + a)[:, None, None, None]
        r = (2 * i + a)[None, None, :, None]
        ck = c[None, :, None, None]
        cq = c[None, None, None, :]
        r0 = np.clip(r - 4, 0, 56)
        c0 = np.clip(cq - 8, 0, 48)
        ok = (kr >= r0) & (kr < r0 + 8) & (ck >= c0) & (ck < c0 + 16)
        dr = np.clip(kr - r + 7, 0, 14)
        dc = np.clip(ck - cq + 15, 0, 30)
        ok, dr, dc = np.broadcast_arrays(ok, dr, dc)
        vals = rpb[:, dr, dc]
        vals = np.where(ok[None], vals, np.float32(-30000.0))
        out[:, :, pi, :] = vals.reshape(12, 128, 128)
    return out


def fnet_consts():
    i64 = np.arange(64)
    ang = 2 * np.pi * np.outer(i64, i64) / 64.0
    F1 = np.concatenate([np.cos(ang), -np.sin(ang)], axis=1)
    t2 = i64[:, None, None]
    k1 = i64[None, :, None]
    k2 = i64[None, None, :]
    ph = -2 * np.pi * (t2 * k1 / 4096.0 + t2 * k2 / 64.0)
    Gr, Gi = np.cos(ph) / 64.0, np.sin(ph) / 64.0
    H = np.empty((64, 64, 2, 128))
    H[:, :, 0, 0:64] = Gr
    H[:, :, 0, 64:128] = Gi
    H[:, :, 1, 0:64] = -Gi
    H[:, :, 1, 64:128] = Gr
    A = np.cos(ang) / 8.0
    B = np.sin(ang) / 8.0
    CH = np.zeros((128, 6, 128))
    CH[0:64, 0, 0:64] = A
    CH[0:64, 1, 0:64] = B
    CH[0:64, 2, 64:128] = A
    CH[0:64, 3, 64:128] = B
    CH[0:64, 4, 0:64] = A
    CH[64:128, 4, 64:128] = A
    CH[0:64, 5, 0:64] = B
    CH[64:128, 5, 64:128] = B
    i256 = np.arange(256)
    a256 = 2 * np.pi * np.outer(i256, i256) / 256.0
    F256 = np.concatenate([np.cos(a256), -np.sin(a256)], axis=1) / 16.0
    F256 = F256.reshape(2, 128, 512).transpose(1, 0, 2)
    f = lambda x: np.ascontiguousarray(x, dtype=np.float32)
    return f(F1), f(H), f(CH), f(F256)


def s5_consts():
    s8 = (np.arange(128) // 16)
    ex = np.zeros((128, 4, 8), np.float32)
    sv = np.arange(8, dtype=np.float32)
    ex[0:64, 0] = -sv
    ex[64:128, 0] = sv
    ex[0:64, 1] = 7 - sv
    ex[64:128, 1] = sv
    ex[0:64, 2] = sv
    ex[64:128, 2] = -sv
    ex[0:64, 3] = sv + 1
    ex[64:128, 3] = 8 - sv
    mk = np.zeros((128, 2, 128), np.float32)
    mk[:, 0, :] = (s8[:, None] <= s8[None, :])
    mk[:, 1, :] = (s8[:, None] >= s8[None, :])
    io = np.ascontiguousarray(np.broadcast_to(np.arange(544, dtype=np.float32), (128, 544)))
    return ex, mk, io


import os
_SKIP = set(os.environ.get("MK_SKIP", "").split(","))


def build_program(n_layers=DEPTH):
    nc = bass.Bass("TRN2", target_bir_lowering=False)
    dt_in = lambda name, shape: nc.dram_tensor(name, list(shape), F32, kind="ExternalInput").ap()
    xr = dt_in("xr", [T, D])
    cT = dt_in("cT", [128, 16])
    w_ada = dt_in("w_ada", [DEPTH, D, 3 * D])
    b_ada = dt_in("b_ada", [DEPTH, 3 * D])
    pre_g = dt_in("pre_g", [DEPTH, D])
    post_g = dt_in("post_g", [DEPTH, D])
    w_in_even = dt_in("w_in_even", [2, D, 3584])
    w_out_even = dt_in("w_out_even", [2, D, D])
    ebias = dt_in("ebias", [2, 12, 128, 21 * 128])
    cF1 = dt_in("cF1", [64, 128])
    cH = dt_in("cH", [64, 64, 2, 128])
    cCH = dt_in("cCH", [128, 6, 128])
    cF256 = dt_in("cF256", [128, 2, 512])
    w_in_odd = dt_in("w_in_odd", [2, D, 2560])
    w_out_odd = dt_in("w_out_odd", [2, D, D])
    sgu_wT = dt_in("sgu_wT", [2, 128, 4, 128])
    sgu_gT = dt_in("sgu_gT", [2, 128, 4])
    sgu_b = dt_in("sgu_b", [2, 512])
    s5_lam1 = dt_in("s5_lam1", [2, 2, 128, 32])
    s5_ls = dt_in("s5_ls", [2, 2, 32])
    s5_b1 = dt_in("s5_b1", [2, 2, 128, 32, 16])
    s5_c1 = dt_in("s5_c1", [2, 2, 128, 32, 16])
    s5_drep = dt_in("s5_drep", [2, 128, 32])
    glu_w = dt_in("glu_w", [2, 512, 1024])
    glu_bT = dt_in("glu_bT", [2, 128, 8])
    cEXPS = dt_in("cEXPS", [128, 4, 8])
    cMASK = dt_in("cMASK", [128, 2, 128])
    cIOTA = dt_in("cIOTA", [128, 544])
    zs = nc.dram_tensor("zs", [8, 32, 16, 544], BF16).ap()
    cHb = nc.dram_tensor("cHb", [64, 64, 2, 128], BF16).ap()
    out = nc.dram_tensor("out", [TL, D], F32, kind="ExternalOutput").ap()
    xs = nc.dram_tensor("xs", [T, D], F32).ap()
    modscr = nc.dram_tensor("modscr", [DEPTH, 2, 3 * D], F32).ap()

    df = DF(nc)
    A = df.add
    _uid = [0]

    def uniq(name):
        _uid[0] += 1
        return "%s_%d" % (name, _uid[0])

    def dma(eng, o, i, r=(), w=()):
        if eng == "pool":
            A(eng, lambda e, o=o, i=i: e.dma_start(out=o, in_=i, max_dma_last_dim=2048), r, w, dma=True)
        else:
            A(eng, lambda e, o=o, i=i: e.dma_start(out=o, in_=i), r, w, dma=True)

    def xbkeys(n0, nn):
        return [("xb", t) for t in range(n0 // 128, (n0 + nn + 127) // 128)]

    NTILES = [(n * 512, 512) for n in range(8)] + [(4096, 256)]

    with contextlib.ExitStack() as gst:
        sbg = lambda name, shape, dt: gst.enter_context(nc.sbuf_tensor(uniq(name), shape, dt))
        psg = lambda name, shape, dt: gst.enter_context(nc.psum_tensor(name, shape, dt))
        xb = sbg("xb", [128, 8, T], BF16)
        mT = sbg("mT", [128, 8, T], BF16)
        idn = sbg("idn", [128, 128], BF16)
        idn32 = sbg("idn32", [128, 128], F32)
        bart = sbg("bart", [128, 2], F32)
        mm = [psg("mm%d" % i, [128, 512], F32) for i in range(2)]
        stp = [psg("stp%d" % i, [128, 1024], F32) for i in range(2)]
        pv = psg("pv", [128, 512], F32)
        trp = psg("trp", [128, 8, 128], BF16)
        mmi = [0]

        def next_mm():
            mmi[0] ^= 1
            return mm[mmi[0]], "mm%d" % mmi[0]

        A("pool", lambda e: e.memset(idn32[:], 1.0), (), ["idn32"])
        A("pool", lambda e: e.affine_select(out=idn32[:], in_=idn32[:], pattern=[[-1, 128]], compare_op=ALU.is_equal,
                                            fill=0.0, base=0, channel_multiplier=1), ["idn32"], ["idn32"])
        A("dve", lambda e: e.tensor_copy(out=idn[:], in_=idn32[:]), ["idn32"], ["idn"])

        with contextlib.ExitStack() as st:
            sb = lambda name, shape, dt: st.enter_context(nc.sbuf_tensor(uniq(name), shape, dt))
            c32 = sb("c32", [128, 16], F32)
            sc = sb("sc", [128, 16], F32)
            LC = sb("LC", [128, 8, 64], BF16)
            wa = [sb("wa%d" % i, [128, 8, 512], BF16) for i in range(2)]
            bada = sb("bada", [64, 3 * D], F32)
            modrow = sb("modrow", [64, 3 * D], F32)
            for g8_ in range(8):
                dma("pool", cHb[:, g8_ * 8:(g8_ + 1) * 8, :, :], cH[:, g8_ * 8:(g8_ + 1) * 8, :, :], (), [("cHb", g8_)])
            dma("sp", c32[:], cT, (), ["c32"])
            A("act", lambda e: e.activation(out=sc[:], in_=c32[:], func=AF.Silu), ["c32"], ["sc"])
            A("pool", lambda e: e.memset(LC[:], 0.0), (), ["LC"])
            A("dve", lambda e: e.tensor_copy(out=LC[:, :, 0:1], in_=sc[:, 0:8].rearrange("p (k o) -> p k o", o=1)), ["sc", "LC"], ["LC"])
            A("dve", lambda e: e.tensor_copy(out=LC[:, :, 32:33], in_=sc[:, 8:16].rearrange("p (k o) -> p k o", o=1)), ["sc", "LC"], ["LC"])
            wi = 0
            for L in range(n_layers):
                dma("sp", bada[:], b_ada[L:L + 1, :].partition_broadcast(64), (), ["bada"])
                for n in range(6):
                    wt = wa[wi % 2]
                    wk = "wa%d" % (wi % 2)
                    wi += 1
                    dma("pool", wt[:], w_ada[L].rearrange("(k p) n -> p k n", p=128)[:, :, n * 512:(n + 1) * 512], (), [wk])
                    bank, bk = next_mm()
                    for k in range(8):
                        A("pe", lambda e, bank=bank, wt=wt, k=k: e.matmul(bank[0:64, :], lhsT=LC[:, k, :], rhs=wt[:, k, :], start=(k == 0), stop=(k == 7)),
                          ["LC", wk], [bk])
                    A("dve", lambda e, bank=bank, n=n: e.tensor_tensor(out=modrow[:, n * 512:(n + 1) * 512], in0=bank[0:64, :], in1=bada[:, n * 512:(n + 1) * 512], op=ALU.add),
                      [bk, "bada"], ["modrow"])
                dma("sp", modscr[L, 0:1, :], modrow[0:1, :], ["modrow"], [("modscr", L)])
                dma("sp", modscr[L, 1:2, :], modrow[32:33, :], ["modrow"], [("modscr", L)])
        df.barrier(bart[:, 0:1])

        def rstd_from_ss(ssum, rs, keys_in, key_out, scale):
            A("dve", lambda e: e.tensor_scalar(out=rs, in0=ssum, scalar1=scale, scalar2=EPS, op0=ALU.mult, op1=ALU.add), keys_in, [key_out])
            A("act", lambda e: e.activation(out=rs, in_=rs, func=AF.Sqrt), [key_out], [key_out])
            A("dve", lambda e: e.reciprocal(out=rs, in_=rs), [key_out], [key_out])

        def prenorm_old(L):
            src = xr if L == 0 else xs
            with contextlib.ExitStack() as st:
                sb = lambda name, shape, dt: st.enter_context(nc.sbuf_tensor(uniq(name), shape, dt))
                xt = [sb("xt%d" % i, [128, D], F32) for i in range(3)]
                tmp = [sb("ptmp%d" % i, [128, D], F32) for i in range(2)]
                hl = [sb("hl%d" % i, [128, D], BF16) for i in range(2)]
                junk = sb("junk", [128, D], BF16)
                gsc = [sb("gsc%d" % i, [128, D], F32) for i in range(2)]
                shb = [sb("shb%d" % i, [128, D], F32) for i in range(2)]
                pgb = sb("pgb", [128, D], F32)
                stat = sb("pstat", [128, 4 * NT], F32)
                dma("sp", pgb[:], pre_g[L:L + 1, :].partition_broadcast(128), (), ["pgb"])
                for w in range(2):
                    dma("sp", gsc[w][:], modscr[L, w:w + 1, D:2 * D].partition_broadcast(128), [("modscr", L)], ["gsc%d" % w])
                    dma("sp", shb[w][:], modscr[L, w:w + 1, 0:D].partition_broadcast(128), [("modscr", L)], ["shb%d" % w])
                    A("dve", lambda e, w=w: e.scalar_tensor_tensor(out=gsc[w][:], in0=gsc[w][:], scalar=1.0, in1=pgb[:], op0=ALU.add, op1=ALU.mult),
                      ["gsc%d" % w, "pgb"], ["gsc%d" % w])
                for t in range(NT):
                    w = 0 if t < 32 else 1
                    x_ = xt[t % 3]
                    xk = "xt%d" % (t % 3)
                    tm = tmp[t % 2]
                    tk = "ptmp%d" % (t % 2)
                    h_ = hl[t % 2]
                    hk = "hl%d" % (t % 2)
                    ss = stat[:, 4 * t:4 * t + 1]
                    rs = stat[:, 4 * t + 1:4 * t + 2]
                    dma("sp", x_[:], src[t * 128:(t + 1) * 128, :], [("xres", t)], [xk])
                    A("act", lambda e, x_=x_, ss=ss: e.activation(out=junk[:], in_=x_[:], func=AF.Square, accum_out=ss), [xk], ["junk", ("pss", t)])
                    rstd_from_ss(ss, rs, [("pss", t)], ("prs", t), 1.0 / D)
                    A("dve", lambda e, x_=x_, rs=rs, tm=tm, w=w: e.scalar_tensor_tensor(out=tm[:], in0=x_[:], scalar=rs, in1=gsc[w][:], op0=ALU.mult, op1=ALU.mult),
                      [xk, ("prs", t), "gsc%d" % w], [tk])
                    A("pool", lambda e, tm=tm, h_=h_, w=w: e.tensor_tensor(out=h_[:], in0=tm[:], in1=shb[w][:], op=ALU.add), [tk, "shb%d" % w], [hk])
                    for k in range(8):
                        A("pe", lambda e, h_=h_, k=k: e.transpose(trp[:, k, :], h_[:, k * 128:(k + 1) * 128], idn[:]), [hk, "idn"], ["trp"])
                    if t % 2 == 0:
                        A("act", lambda e, t=t: e.copy(out=xb[:, :, t * 128:(t + 1) * 128], in_=trp[:]), ["trp"], [("xb", t)])
                    else:
                        A("dve", lambda e, t=t: e.tensor_copy(out=xb[:, :, t * 128:(t + 1) * 128], in_=trp[:]), ["trp"], [("xb", t)])
            df.barrier(bart[:, 0:1])

        def post_old(L, last):
            src = xr if L == 0 else xs
            j = L // 2
            wo_d = w_out_even[j] if L % 2 == 0 else w_out_odd[j]
            ntile = 32 if last else NT
            with contextlib.ExitStack() as st:
                sb = lambda name, shape, dt: st.enter_context(nc.sbuf_tensor(uniq(name), shape, dt))
                wo = sb("wo", [128, 8, D], BF16)
                xt = [sb("qxt%d" % i, [128, D], F32) for i in range(2)]
                t1 = [sb("qt1%d" % i, [128, D], F32) for i in range(2)]
                t2 = [sb("qt2%d" % i, [128, D], F32) for i in range(2)]
                junk = sb("qjunk", [128, 512], BF16)
                gp = [sb("gp%d" % i, [128, D], F32) for i in range(2)]
                pgb = sb("qpgb", [128, D], F32)
                stat = sb("qstat", [128, 4 * NT], F32)
                for h in range(2):
                    dma("pool", wo[:, :, h * 512:(h + 1) * 512], wo_d.rearrange("(k p) n -> p k n", p=128)[:, :, h * 512:(h + 1) * 512], (), ["wo"])
                dma("sp", pgb[:], post_g[L:L + 1, :].partition_broadcast(128), (), ["qpgb"])
                for w in range(2):
                    dma("sp", gp[w][:], modscr[L, w:w + 1, 2 * D:3 * D].partition_broadcast(128), [("modscr", L)], ["gp%d" % w])
                    A("dve", lambda e, w=w: e.tensor_tensor(out=gp[w][:], in0=gp[w][:], in1=pgb[:], op=ALU.mult), ["gp%d" % w, "qpgb"], ["gp%d" % w])
                for t in range(ntile):
                    w = 0 if t < 32 else 1
                    yps = stp[t % 2]
                    yk = "stp%d" % (t % 2)
                    x_ = xt[t % 2]
                    xk = "qxt%d" % (t % 2)
                    a_ = t1[t % 2]
                    ak = "qt1%d" % (t % 2)
                    b_ = t2[t % 2]
                    bk = "qt2%d" % (t % 2)
                    for h in range(2):
                        for k in range(8):
                            A("pe", lambda e, yps=yps, h=h, k=k, t=t: e.matmul(yps[:, h * 512:(h + 1) * 512], lhsT=mT[:, k, t * 128:(t + 1) * 128], rhs=wo[:, k, h * 512:(h + 1) * 512], start=(k == 0), stop=(k == 7)),
                              [("mT", k, t), "wo"], [yk])
                    dma("sp", x_[:], src[t * 128:(t + 1) * 128, :], [("xres", t)], [xk])
                    for h in range(2):
                        A("act", lambda e, yps=yps, h=h, t=t: e.activation(out=junk[:], in_=yps[:, h * 512:(h + 1) * 512], func=AF.Square, accum_out=stat[:, 4 * t + h:4 * t + h + 1]),
                          [yk], ["qjunk", ("qss", t, h)])
                    A("dve", lambda e, t=t: e.tensor_tensor(out=stat[:, 4 * t + 2:4 * t + 3], in0=stat[:, 4 * t:4 * t + 1], in1=stat[:, 4 * t + 1:4 * t + 2], op=ALU.add),
                      [("qss", t, 0), ("qss", t, 1)], [("qs2", t)])
                    rs = stat[:, 4 * t + 3:4 * t + 4]
                    rstd_from_ss(stat[:, 4 * t + 2:4 * t + 3], rs, [("qs2", t)], ("qrs", t), 1.0 / D)
                    A("dve", lambda e, yps=yps, a_=a_, w=w: e.tensor_tensor(out=a_[:], in0=yps[:], in1=gp[w][:], op=ALU.mult), [yk, "gp%d" % w], [ak])
                    A("act", lambda e, a_=a_, b_=b_, rs=rs: e.activation(out=b_[:], in_=a_[:], func=AF.Copy, scale=rs), [ak, ("qrs", t)], [bk])
                    A("pool", lambda e, b_=b_, x_=x_: e.tensor_tensor(out=b_[:], in0=b_[:], in1=x_[:], op=ALU.add), [bk, xk], [bk])
                    dst = out[t * 128:(t + 1) * 128, :] if (last and t < 32) else xs[t * 128:(t + 1) * 128, :]
                    dma("sp", dst, b_[:], [bk], [("xres", t)])
            df.barrier(bart[:, 0:1])

        def prenorm(L):
            src = xr if L == 0 else xs
            with contextlib.ExitStack() as st:
                sb = lambda name, shape, dt: st.enter_context(nc.sbuf_tensor(uniq(name), shape, dt))
                NX = 6
                xt = [sb("xt%d" % i, [128, D], F32) for i in range(NX)]
                tmp = [sb("ptmp%d" % i, [128, D], F32) for i in range(2)]
                hl = [sb("hl%d" % i, [128, D], BF16) for i in range(2)]
                junk = sb("junk", [128, D], BF16)
                gsc = [sb("gsc%d" % i, [128, D], F32) for i in range(2)]
                shb = [sb("shb%d" % i, [128, D], F32) for i in range(2)]
                pgb = sb("pgb", [128, D], F32)
                stat = sb("pstat", [128, 4 * NT], F32)
                dma("sp", pgb[:], pre_g[L:L + 1, :].partition_broadcast(128), (), ["pgb"])
                for w in range(2):
                    dma("sp", gsc[w][:], modscr[L, w:w + 1, D:2 * D].partition_broadcast(128), [("modscr", L)], ["gsc%d" % w])
                    dma("sp", shb[w][:], modscr[L, w:w + 1, 0:D].partition_broadcast(128), [("modscr", L)], ["shb%d" % w])
                    A("dve", lambda e, w=w: e.scalar_tensor_tensor(out=gsc[w][:], in0=gsc[w][:], scalar=1.0, in1=pgb[:], op0=ALU.add, op1=ALU.mult),
                      ["gsc%d" % w, "pgb"], ["gsc%d" % w])
                X = lambda t: (xt[t % NX], "xt%d" % (t % NX))
                SS = lambda t: stat[:, 4 * t:4 * t + 1]
                RS = lambda t: stat[:, 4 * t + 1:4 * t + 2]

                def p_load(t):
                    x_, xk = X(t)
                    dma("sp", x_[:], src[t * 128:(t + 1) * 128, :], [("xres", t)], [xk])

                def p_sq(t):
                    x_, xk = X(t)
                    A("act", lambda e, x_=x_, ss=SS(t): e.activation(out=junk[:], in_=x_[:], func=AF.Square, accum_out=ss), [xk], ["junk", ("pss", t)])

                def p_r1(t):
                    A("dve", lambda e, t=t: e.tensor_scalar(out=RS(t), in0=SS(t), scalar1=1.0 / D, scalar2=EPS, op0=ALU.mult, op1=ALU.add), [("pss", t)], [("prs", t)])

                def p_r2(t):
                    A("act", lambda e, t=t: e.activation(out=RS(t), in_=RS(t), func=AF.Sqrt), [("prs", t)], [("prs", t)])

                def p_r3(t):
                    A("dve", lambda e, t=t: e.reciprocal(out=RS(t), in_=RS(t)), [("prs", t)], [("prs", t)])

                def p_stt(t):
                    w = 0 if t < 32 else 1
                    x_, xk = X(t)
                    tm, tk = tmp[t % 2], "ptmp%d" % (t % 2)
                    A("dve", lambda e, x_=x_, t=t, tm=tm, w=w: e.scalar_tensor_tensor(out=tm[:], in0=x_[:], scalar=RS(t), in1=gsc[w][:], op0=ALU.mult, op1=ALU.mult),
                      [xk, ("prs", t), "gsc%d" % w], [tk])

                def p_add(t):
                    w = 0 if t < 32 else 1
                    tm, tk = tmp[t % 2], "ptmp%d" % (t % 2)
                    h_, hk = hl[t % 2], "hl%d" % (t % 2)
                    A("pool", lambda e, tm=tm, h_=h_, w=w: e.tensor_tensor(out=h_[:], in0=tm[:], in1=shb[w][:], op=ALU.add), [tk, "shb%d" % w], [hk])

                def p_tr(t):
                    h_, hk = hl[t % 2], "hl%d" % (t % 2)
                    for k in range(8):
                        A("pe", lambda e, h_=h_, k=k: e.transpose(trp[:, k, :], h_[:, k * 128:(k + 1) * 128], idn[:]), [hk, "idn"], ["trp"])

                def p_ev(t):
                    if t % 2 == 0:
                        A("act", lambda e, t=t: e.copy(out=xb[:, :, t * 128:(t + 1) * 128], in_=trp[:]), ["trp"], [("xb", t)])
                    else:
                        A("dve", lambda e, t=t: e.tensor_copy(out=xb[:, :, t * 128:(t + 1) * 128], in_=trp[:]), ["trp"], [("xb", t)])

                pipeline([p_load, p_sq, p_r1, p_r2, p_r3, p_stt, p_add, p_tr, p_ev], NT, "pre")
            df.barrier(bart[:, 0:1])

        def post(L, last):
            src = xr if L == 0 else xs
            j = L // 2
            wo_d = w_out_even[j] if L % 2 == 0 else w_out_odd[j]
            ntile = 32 if last else NT
            with contextlib.ExitStack() as st:
                sb = lambda name, shape, dt: st.enter_context(nc.sbuf_tensor(uniq(name), shape, dt))
                wo = sb("wo", [128, 8, D], BF16)
                NA_, NB_, NXq = 4, 3, 3
                xt = [sb("qxt%d" % i, [128, D], F32) for i in range(NXq)]
                t1 = [sb("qt1%d" % i, [128, D], F32) for i in range(NA_)]
                t2 = [sb("qt2%d" % i, [128, D], F32) for i in range(NB_)]
                junk = sb("qjunk", [128, D], BF16)
                gp = [sb("gp%d" % i, [128, D], F32) for i in range(2)]
                pgb = sb("qpgb", [128, D], F32)
                stat = sb("qstat", [128, 4 * NT], F32)
                for h in range(2):
                    dma("pool", wo[:, :, h * 512:(h + 1) * 512], wo_d.rearrange("(k p) n -> p k n", p=128)[:, :, h * 512:(h + 1) * 512], (), ["wo"])
                dma("sp", pgb[:], post_g[L:L + 1, :].partition_broadcast(128), (), ["qpgb"])
                for w in range(2):
                    dma("sp", gp[w][:], modscr[L, w:w + 1, 2 * D:3 * D].partition_broadcast(128), [("modscr", L)], ["gp%d" % w])
                    A("dve", lambda e, w=w: e.tensor_tensor(out=gp[w][:], in0=gp[w][:], in1=pgb[:], op=ALU.mult), ["gp%d" % w, "qpgb"], ["gp%d" % w])
                YP = lambda t: (stp[t % 2], "stp%d" % (t % 2))
                XQ = lambda t: (xt[t % NXq], "qxt%d" % (t % NXq))
                TA = lambda t: (t1[t % NA_], "qt1%d" % (t % NA_))
                TB = lambda t: (t2[t % NB_], "qt2%d" % (t % NB_))
                SS = lambda t: stat[:, 4 * t:4 * t + 1]
                RS = lambda t: stat[:, 4 * t + 1:4 * t + 2]

                def q_mm(t):
                    yps, yk = YP(t)
                    for h in range(2):
                        for k in range(8):
                            A("pe", lambda e, yps=yps, h=h, k=k, t=t: e.matmul(yps[:, h * 512:(h + 1) * 512], lhsT=mT[:, k, t * 128:(t + 1) * 128], rhs=wo[:, k, h * 512:(h + 1) * 512], start=(k == 0), stop=(k == 7)),
                              [("mT", k, t), "wo"], [yk])

                def q_sq(t):
                    yps, yk = YP(t)
                    a_, ak = TA(t)
                    w = 0 if t < 32 else 1
                    for h in range(2):
                        A("act", lambda e, yps=yps, t=t, h=h: e.activation(out=junk[:, h * 512:(h + 1) * 512], in_=yps[:, h * 512:(h + 1) * 512], func=AF.Square, accum_out=stat[:, 4 * t + 2 + h:4 * t + 3 + h]), [yk], ["qjunk", ("qssh", t, h)])
                    A("dve", lambda e, yps=yps, a_=a_, w=w: e.tensor_tensor(out=a_[:], in0=yps[:], in1=gp[w][:], op=ALU.mult), [yk, "gp%d" % w, ("qssh", t, 0), ("qssh", t, 1)], [ak])

                def q_r1(t):
                    A("dve", lambda e, t=t: e.tensor_tensor(out=SS(t), in0=stat[:, 4 * t + 2:4 * t + 3], in1=stat[:, 4 * t + 3:4 * t + 4], op=ALU.add), [("qssh", t, 0), ("qssh", t, 1)], [("qss", t)])
                    A("dve", lambda e, t=t: e.tensor_scalar(out=RS(t), in0=SS(t), scalar1=1.0 / D, scalar2=EPS, op0=ALU.mult, op1=ALU.add), [("qss", t)], [("qrs", t)])

                def q_r2(t):
                    A("act", lambda e, t=t: e.activation(out=RS(t), in_=RS(t), func=AF.Sqrt), [("qrs", t)], [("qrs", t)])
                    x_, xk = XQ(t)
                    dma("sp", x_[:], src[t * 128:(t + 1) * 128, :], [("xres", t)], [xk])

                def q_r3(t):
                    A("dve", lambda e, t=t: e.reciprocal(out=RS(t), in_=RS(t)), [("qrs", t)], [("qrs", t)])

                def q_sc(t):
                    a_, ak = TA(t)
                    b_, bk = TB(t)
                    A("act", lambda e, a_=a_, b_=b_, t=t: e.activation(out=b_[:], in_=a_[:], func=AF.Copy, scale=RS(t)), [ak, ("qrs", t)], [bk])

                def q_add(t):
                    b_, bk = TB(t)
                    x_, xk = XQ(t)
                    A("pool", lambda e, b_=b_, x_=x_: e.tensor_tensor(out=b_[:], in0=b_[:], in1=x_[:], op=ALU.add), [bk, xk], [bk])

                def q_st(t):
                    b_, bk = TB(t)
                    dst = out[t * 128:(t + 1) * 128, :] if (last and t < 32) else xs[t * 128:(t + 1) * 128, :]
                    dma("sp", dst, b_[:], [bk], [("xres", t)])

                pipeline([q_mm, q_sq, q_r1, q_r2, q_r3, q_sc, q_add, q_st], ntile, "post")
            df.barrier(bart[:, 0:1])

        def pipeline(stages, N, key=""):
            if "noskew" in _SKIP or ("noskew_" + key) in _SKIP:
                for n_ in range(N):
                    for st_ in stages:
                        st_(n_)
                return
            K_ = len(stages)
            for step in range(N + K_ - 1):
                for k_ in reversed(range(K_)):
                    n_ = step - k_
                    if 0 <= n_ < N:
                        stages[k_](n_)

        def inproj_fm(wt, wk, tiles, evac):
            for (n0, nn) in tiles:
                bank, bk = next_mm()
                for k in range(8):
                    A("pe", lambda e, bank=bank, k=k, n0=n0, nn=nn: e.matmul(bank[:, 0:nn], lhsT=wt[:, k, :], rhs=xb[:, k, n0:n0 + nn], start=(k == 0), stop=(k == 7)),
                      [wk] + xbkeys(n0, nn), [bk])
                evac(bank, bk, n0, nn)

        def even_mixer(L):
            j = L // 2
            Wd = w_in_even[j].rearrange("(k p) n -> p k n", p=128)
            with contextlib.ExitStack() as st:
                sb = lambda name, shape, dt: st.enter_context(nc.sbuf_tensor(uniq(name), shape, dt))
                wch = [sb("wch%d" % i, [128, 8, 128], BF16) for i in range(3)]
                wci = [0]

                def load_w(c0, ncols=128):
                    i = wci[0] % 3
                    wci[0] += 1
                    dma("pool", wch[i][:, :, 0:ncols], Wd[:, :, c0:c0 + ncols], (), ["wch%d" % i])
                    return wch[i], "wch%d" % i

                with contextlib.ExitStack() as st2:
                    sb2 = lambda name, shape, dt: st2.enter_context(nc.sbuf_tensor(uniq(name), shape, dt))
                    sga = sb2("sga", [128, T], BF16)
                    X = sb2("fX", [64, 64, 128], BF16)
                    Z = sb2("fZ", [64, 64, 128], BF16)
                    Pg = [sb2("fP%d" % i, [128, 8, 128], BF16) for i in range(2)]
                    Hs = [sb2("fH%d" % i, [64, 8, 2, 128], BF16) for i in range(2)]
                    F1 = sb2("fF1", [64, 128], BF16)
                    CH = sb2("fCH", [128, 6, 128], BF16)
                    F256 = sb2("fF256", [128, 2, 512], BF16)
                    Xc = sb2("fXc", [128, 2, 128], BF16)
                    Pc = sb2("fPc", [128, 512], BF16)
                    dma("pool", F1[:], cF1, (), ["fF1"])
                    dma("pool", CH[:], cCH, (), ["fCH"])
                    dma("pool", F256[:], cF256, (), ["fF256"])
                    hcount = 0
                    pcount = 0
                    for half in range(2):
                        wt, wk = load_w(256 + half * 128)
                        inproj_fm(wt, wk, NTILES, lambda bank, bk, n0, nn: A(
                            "act", lambda e: e.activation(out=sga[:, n0:n0 + nn], in_=bank[:, 0:nn], func=AF.Silu), [bk], [("sga", n0)]))
                        sgakeys = [("sga", n0) for (n0, nn) in NTILES]
                        wt, wk = load_w(half * 128)
                        for g4 in range(16):
                            bank, bk = next_mm()
                            for q in range(4):
                                t2 = g4 * 4 + q
                                for k in range(8):
                                    A("pe", lambda e, bank=bank, q=q, k=k, t2=t2, wt=wt: e.matmul(bank[0:64, q * 128:(q + 1) * 128], lhsT=xb[:, k, t2:TL:64], rhs=wt[:, k, :], start=(k == 0), stop=(k == 7)),
                                      [wk] + [("xb", t) for t in range(32)], [bk])
                            A("act", lambda e, bank=bank, g4=g4: e.copy(out=X[:, g4 * 4:(g4 + 1) * 4, :], in_=bank[0:64, :].rearrange("p (q c) -> p q c", q=4)), [bk], ["fX"])
                        for tl in range(2):
                            bank, bk = next_mm()
                            for k in range(8):
                                A("pe", lambda e, bank=bank, k=k, tl=tl, wt=wt: e.matmul(bank[:, 0:128], lhsT=xb[:, k, TL + tl * 128:TL + (tl + 1) * 128], rhs=wt[:, k, :], start=(k == 0), stop=(k == 7)),
                                  [wk, ("xb", 32 + tl)], [bk])
                            A("dve", lambda e, bank=bank, tl=tl: e.tensor_copy(out=Xc[:, tl, :], in_=bank[:, 0:128]), [bk], ["fXc"])
                        bank, bk = next_mm()
                        for tl in range(2):
                            A("pe", lambda e, bank=bank, tl=tl: e.matmul(bank[:, :], lhsT=Xc[:, tl, :], rhs=F256[:, tl, :], start=(tl == 0), stop=(tl == 1)), ["fXc", "fF256"], [bk])
                        A("dve", lambda e, bank=bank: e.tensor_copy(out=Pc[:], in_=bank[:, :]), [bk], ["fPc"])
                        bank, bk = next_mm()
                        A("pe", lambda e, bank=bank: e.matmul(bank[:, 0:256], lhsT=CH[:, 4, :], rhs=Pc[:, 0:256], start=True, stop=False), ["fPc", "fCH"], [bk])
                        A("pe", lambda e, bank=bank: e.matmul(bank[:, 0:256], lhsT=CH[:, 5, :], rhs=Pc[:, 256:512], start=False, stop=True), ["fPc", "fCH"], [bk])
                        A("dve", lambda e, bank=bank, half=half: e.tensor_tensor(out=mT[:, half, TL:T], in0=bank[:, 0:256], in1=sga[:, TL:T], op=ALU.mult),
                          [bk] + sgakeys, [("mT", half, 32), ("mT", half, 33)])
                        for qd in range(2):
                            pb = qd * 64
                            for c4 in range(16):
                                bank, bk = next_mm()
                                for q in range(4):
                                    c = qd * 64 + c4 * 4 + q
                                    A("pe", lambda e, bank=bank, q=q, c=c: e.matmul(bank[0:64, q * 128:(q + 1) * 128], lhsT=X[:, :, c], rhs=F1[:, :], start=True, stop=True),
                                      ["fX", "fF1"], [bk])
                                if c4 % 2 == 0:
                                    A("act", lambda e, bank=bank, c4=c4: e.copy(out=Z[:, c4 * 4:(c4 + 1) * 4, :], in_=bank[0:64, :].rearrange("p (q c) -> p q c", q=4)), [bk], ["fZ"])
                                else:
                                    A("dve", lambda e, bank=bank, c4=c4: e.tensor_copy(out=Z[:, c4 * 4:(c4 + 1) * 4, :], in_=bank[0:64, :].rearrange("p (q c) -> p q c", q=4)), [bk], ["fZ"])
                            for g8 in range(8):
                                Hb = Hs[hcount % 2]
                                hk = "fH%d" % (hcount % 2)
                                hcount += 1
                                dma("sp", Hb[:], cHb[:, g8 * 8:(g8 + 1) * 8, :, :], (), [hk])
                                Pb = Pg[pcount % 2]
                                pk = "fP%d" % (pcount % 2)
                                pcount += 1
                                for b2 in range(2):
                                    bank, bk = next_mm()
                                    for q in range(4):
                                        kk = b2 * 4 + q
                                        k1 = g8 * 8 + kk
                                        for ri in range(2):
                                            A("pe", lambda e, bank=bank, q=q, kk=kk, k1=k1, ri=ri, Hb=Hb: e.matmul(bank[0:64, q * 128:(q + 1) * 128], lhsT=Z[:, :, ri * 64 + k1], rhs=Hb[:, kk, ri, :], start=(ri == 0), stop=(ri == 1)),
                                              ["fZ", hk], [bk])
                                    A("act" if b2 == 0 else "dve",
                                      (lambda e, bank=bank, b2=b2, Pb=Pb: e.copy(out=Pb[0:64, b2 * 4:(b2 + 1) * 4, :], in_=bank[0:64, :].rearrange("p (q c) -> p q c", q=4))) if b2 == 0 else
                                      (lambda e, bank=bank, b2=b2, Pb=Pb: e.tensor_copy(out=Pb[0:64, b2 * 4:(b2 + 1) * 4, :], in_=bank[0:64, :].rearrange("p (q c) -> p q c", q=4))),
                                      [bk], [pk])
                                bank, bk = next_mm()
                                ia, ib = (0, 1) if qd == 0 else (2, 3)
                                mcols = 64 if qd == 0 else 128
                                A("pe", lambda e, bank=bank, Pb=Pb, ia=ia, mcols=mcols: e.matmul(bank[0:mcols, :], lhsT=CH[0:64, ia, 0:mcols], rhs=Pb[0:64, :, 0:64], start=True, stop=False), [pk, "fCH"], [bk])
                                A("pe", lambda e, bank=bank, Pb=Pb, ib=ib, mcols=mcols: e.matmul(bank[0:mcols, :], lhsT=CH[0:64, ib, 0:mcols], rhs=Pb[0:64, :, 64:128], start=False, stop=True), [pk, "fCH"], [bk])
                                A("dve", lambda e, bank=bank, pb=pb, half=half, g8=g8: e.tensor_tensor(
                                    out=mT[pb:pb + 64, half, 0:TL].rearrange("p (k2 k1) -> p k1 k2", k1=64)[:, g8 * 8:(g8 + 1) * 8, :],
                                    in0=bank[pb:pb + 64, :].rearrange("p (a b) -> p a b", a=8),
                                    in1=sga[pb:pb + 64, 0:TL].rearrange("p (k2 k1) -> p k1 k2", k1=64)[:, g8 * 8:(g8 + 1) * 8, :], op=ALU.mult),
                                  [bk] + sgakeys, [("mT", half, t) for t in range(32)])
                df.barrier(bart[:, 0:1])

                with contextlib.ExitStack() as st2:
                    sb2 = lambda name, shape, dt: st2.enter_context(nc.sbuf_tensor(uniq(name), shape, dt))
                    qT = sb2("qT", [128, T], BF16)
                    kT = sb2("kT", [128, T], BF16)
                    sgb = sb2("sgb", [128, T], BF16)
                    vaug = sb2("vaug", [128, NT, 130], BF16)
                    eb32 = sb2("eb32", [128, 7 * 128], F32)
                    Eh = [sb2("Eh%d" % i, [128, 21 * 128], BF16) for i in range(2)]
                    PT = [sb2("PT%d" % i, [128, 7 * 128], BF16) for i in range(3)]
                    onb = sb2("onb", [128, NT, 128], BF16)
                    rden = sb2("rden", [128, 64], F32)
                    A("pool", lambda e: e.memset(vaug[:], 1.0), (), [("vaug", t4) for t4 in range(9)])
                    pti = 0
                    rdi = 0
                    for hp in range(6):
                        wt, wk = load_w(512 + hp * 128)
                        inproj_fm(wt, wk, NTILES, lambda bank, bk, n0, nn: A(
                            "act", lambda e: e.copy(out=qT[:, n0:n0 + nn], in_=bank[:, 0:nn]), [bk], [("qT", n0)]))
                        wt, wk = load_w(1280 + hp * 128)
                        inproj_fm(wt, wk, NTILES, lambda bank, bk, n0, nn: A(
                            "dve", lambda e: e.tensor_copy(out=kT[:, n0:n0 + nn], in_=bank[:, 0:nn]), [bk], [("kT", n0)]))
                        wt, wk = load_w(2816 + hp * 128)
                        inproj_fm(wt, wk, NTILES, lambda bank, bk, n0, nn: A(
                            "act", lambda e: e.activation(out=sgb[:, n0:n0 + nn], in_=bank[:, 0:nn], func=AF.Silu), [bk], [("sgb", n0)]))
                        wt, wk = load_w(2048 + hp * 128)
                        for t4 in range(9):
                            bank, bk = next_mm()
                            nq = 4 if t4 < 8 else 2
                            for q in range(nq):
                                t = t4 * 4 + q
                                for k in range(8):
                                    A("pe", lambda e, bank=bank, q=q, k=k, t=t, wt=wt: e.matmul(bank[:, q * 128:(q + 1) * 128], lhsT=xb[:, k, t * 128:(t + 1) * 128], rhs=wt[:, k, :], start=(k == 0), stop=(k == 7)),
                                      [wk, ("xb", t)], [bk])
                            for hh in range(2):
                                A("dve" if hh == 0 else "act",
                                  (lambda e, bank=bank, t4=t4, nq=nq, hh=hh: e.tensor_copy(out=vaug[:, t4 * 4:t4 * 4 + nq, hh * 65:hh * 65 + 64], in_=bank[:, 0:nq * 128].rearrange("p (q c) -> p q c", q=nq)[:, :, hh * 64:(hh + 1) * 64])) if hh == 0 else
                                  (lambda e, bank=bank, t4=t4, nq=nq, hh=hh: e.copy(out=vaug[:, t4 * 4:t4 * 4 + nq, hh * 65:hh * 65 + 64], in_=bank[:, 0:nq * 128].rearrange("p (q c) -> p q c", q=nq)[:, :, hh * 64:(hh + 1) * 64])),
                                  [bk], [("vaug", t4)])
                        for hh in range(2):
                            h = hp * 2 + hh
                            E = Eh[hh]
                            ek = "Eh%d" % hh
                            for part in range(3):
                                dma("sp", eb32[:], ebias[j, h, :, part * 896:(part + 1) * 896], (), ["eb32"])
                                A("act", lambda e, E=E, part=part: e.activation(out=E[:, part * 896:(part + 1) * 896], in_=eb32[:], func=AF.Exp), ["eb32"], [ek])
                        its = [(hh, i) for hh in range(2) for i in range(NT)]

                        def geo(n):
                            hh, i = its[n]
                            if i < 32:
                                pbase, lt = na_pattern_base(i)
                                kts = lt + [32, 33]
                            else:
                                pbase, lt = None, []
                                kts = [32, 33]
                            return hh, i, pbase, lt, kts

                        def s_qk(n):
                            hh, i, pbase, lt, kts = geo(n)
                            hb = hh * 64
                            sp_ = stp[n % 2]
                            sk = "stp%d" % (n % 2)
                            for a_, kt in enumerate(kts):
                                A("pe", lambda e, sp_=sp_, a_=a_, kt=kt, i=i, hb=hb: e.matmul(sp_[:, a_ * 128:(a_ + 1) * 128], lhsT=kT[hb:hb + 64, kt * 128:(kt + 1) * 128], rhs=qT[hb:hb + 64, i * 128:(i + 1) * 128], start=True, stop=True),
                                  [("kT", (kt // 4) * 512), ("qT", (i // 4) * 512)], [sk])

                        def s_exp(n):
                            hh, i, pbase, lt, kts = geo(n)
                            nk = len(kts)
                            sp_ = stp[n % 2]
                            sk = "stp%d" % (n % 2)
                            P_ = PT[n % 3]
                            pk = "PT%d" % (n % 3)
                            A("act", lambda e, sp_=sp_, P_=P_, nk=nk: e.activation(out=P_[:, 0:nk * 128], in_=sp_[:, 0:nk * 128], func=AF.Exp, scale=0.125), [sk], [pk])

                        def s_mul(n):
                            hh, i, pbase, lt, kts = geo(n)
                            P_ = PT[n % 3]
                            pk = "PT%d" % (n % 3)
                            if lt:
                                nl = len(lt)
                                E = Eh[hh]
                                A("dve", lambda e, P_=P_, nl=nl, E=E, pbase=pbase: e.tensor_tensor(out=P_[:, 0:nl * 128], in0=P_[:, 0:nl * 128], in1=E[:, pbase * 128:(pbase + nl) * 128], op=ALU.mult),
                                  [pk, "Eh%d" % hh], [pk])

                        def s_pv(n):
                            hh, i, pbase, lt, kts = geo(n)
                            nk = len(kts)
                            P_ = PT[n % 3]
                            pk = "PT%d" % (n % 3)
                            pvb = mm[n % 2]
                            for a_, kt in enumerate(kts):
                                A("pe", lambda e, P_=P_, a_=a_, kt=kt, hh=hh, nk=nk, pvb=pvb: e.matmul(pvb[:, 0:65], lhsT=P_[:, a_ * 128:(a_ + 1) * 128], rhs=vaug[:, kt, hh * 65:hh * 65 + 65], start=(a_ == 0), stop=(a_ == nk - 1)),
                                  [pk, ("vaug", kt // 4)], ["mm%d" % (n % 2)])

                        def s_rec(n):
                            pvb = mm[n % 2]
                            rd = rden[:, n % 64:n % 64 + 1]
                            A("dve", lambda e, rd=rd, pvb=pvb: e.reciprocal(out=rd, in_=pvb[:, 64:65]), ["mm%d" % (n % 2)], [("rden", n % 64)])

                        def s_norm(n):
                            hh, i, pbase, lt, kts = geo(n)
                            pvb = mm[n % 2]
                            rd = rden[:, n % 64:n % 64 + 1]
                            A("dve", lambda e, i=i, hh=hh, rd=rd, pvb=pvb: e.tensor_scalar(out=onb[:, i, hh * 64:(hh + 1) * 64], in0=pvb[:, 0:64], scalar1=rd, scalar2=None, op0=ALU.mult), ["mm%d" % (n % 2), ("rden", n % 64)], [("on", i, hh)])

                        def burst(i):
                            if i % 8 == 7 or i == NT - 1:
                                i0 = (i // 8) * 8
                                return i0, i - i0 + 1
                            return None

                        def s_tr(n):
                            hh, i, pbase, lt, kts = geo(n)
                            if hh == 1 and burst(i):
                                i0, nb_ = burst(i)
                                for r_ in range(nb_):
                                    ii = i0 + r_
                                    A("pe", lambda e, ii=ii, r_=r_: e.transpose(trp[:, r_, :], onb[:, ii, :], idn[:]), [("on", ii, 0), ("on", ii, 1), "idn"], ["trp"])

                        def s_gate(n):
                            hh, i, pbase, lt, kts = geo(n)
                            if hh == 1 and burst(i):
                                i0, nb_ = burst(i)
                                A("dve", lambda e, i0=i0, nb_=nb_, hp=hp: e.tensor_tensor(out=mT[:, 2 + hp, i0 * 128:(i0 + nb_) * 128], in0=trp[:, 0:nb_, :].rearrange("p a b -> p (a b)"), in1=sgb[:, i0 * 128:(i0 + nb_) * 128], op=ALU.mult),
                                  ["trp"] + [("sgb", ((i0 + r_) // 4) * 512) for r_ in range(nb_)], [("mT", 2 + hp, i0 + r_) for r_ in range(nb_)])

                        pipeline([s_qk, s_exp, s_mul, s_pv, s_rec, s_norm, s_tr, s_gate], len(its), "att")
            df.barrier(bart[:, 0:1])

        MAG = 12582912.0
        TWO_PI = 2.0 * np.pi

        def odd_mixer(L):
            j = L // 2
            Wd = w_in_odd[j].rearrange("(k p) n -> p k n", p=128)
            Uv = mT[:, 0:4, :].rearrange("p c t -> p (c t)").rearrange("p (g b) -> p g b", b=544)
            PIECES = [(0, 256), (256, 256), (512, 32)]

            with contextlib.ExitStack() as st:
              if "s5a" not in _SKIP:
                sb = lambda name, shape, dt: st.enter_context(nc.sbuf_tensor(uniq(name), shape, dt))
                Ws = sb("Ws", [128, 8, 512], BF16)
                Stm = [sb("Stm%d" % i, [128, 32, 8, 16], BF16) for i in range(5)]
                dma("pool", Ws[:], Wd[:, :, 1536:2048], (), ["Ws"])
                its_a = [(bt, t8) for bt in range(5) for t8 in range(8)]

                def a_mm(n):
                    bt, t8 = its_a[n]
                    nb = 128 if bt < 4 else 32
                    tok0 = 1024 * bt
                    bank, bk = mm[n % 2], "mm%d" % (n % 2)
                    for k in range(8):
                        A("pe", lambda e, bank=bank, k=k, nb=nb, tok0=tok0, t8=t8: e.matmul(bank[0:nb, :], lhsT=xb[:, k, tok0 + t8:tok0 + 8 * nb:8], rhs=Ws[:, k, :], start=(k == 0), stop=(k == 7)),
                          ["Ws"] + xbkeys(tok0, 8 * nb), [bk])

                def a_ev(n):
                    bt, t8 = its_a[n]
                    nb = 128 if bt < 4 else 32
                    bank, bk = mm[n % 2], "mm%d" % (n % 2)
                    S_ = Stm[bt]
                    sk = ("Stm", bt, t8)
                    if n % 2 == 0:
                        A("act", lambda e, bank=bank, nb=nb, S_=S_, t8=t8: e.copy(out=S_[0:nb, :, t8, :], in_=bank[0:nb, :].rearrange("p (g m) -> p g m", m=16)), [bk], [sk])
                    else:
                        A("dve", lambda e, bank=bank, nb=nb, S_=S_, t8=t8: e.tensor_copy(out=S_[0:nb, :, t8, :], in_=bank[0:nb, :].rearrange("p (g m) -> p g m", m=16)), [bk], [sk])

                pipeline([a_mm, a_ev], len(its_a), "s5a")
                its_b = [(bt, g8) for bt in range(5) for g8 in range(4)]

                def a_tr(n):
                    bt, g8 = its_b[n]
                    nb = 128 if bt < 4 else 32
                    S_ = Stm[bt]
                    for q in range(8):
                        g = g8 * 8 + q
                        A("pe", lambda e, S_=S_, nb=nb, g=g, q=q: e.transpose(trp[:, q, 0:nb], S_[0:nb, g, :, :].rearrange("p a b -> p (a b)"), idn[0:nb, 0:nb]), [("Stm", bt, t8) for t8 in range(8)] + ["idn"], ["trp"])

                def a_ut(n):
                    bt, g8 = its_b[n]
                    nb = 128 if bt < 4 else 32
                    if n % 2 == 0:
                        A("act", lambda e, g8=g8, bt=bt, nb=nb: e.copy(out=Uv[:, g8 * 8:(g8 + 1) * 8, bt * 128:bt * 128 + nb], in_=trp[:, :, 0:nb]), ["trp"], ["U"])
                    else:
                        A("dve", lambda e, g8=g8, bt=bt, nb=nb: e.tensor_copy(out=Uv[:, g8 * 8:(g8 + 1) * 8, bt * 128:bt * 128 + nb], in_=trp[:, :, 0:nb]), ["trp"], ["U"])

                pipeline([a_tr, a_ut], len(its_b), "s5a2")
            df.barrier(bart[:, 0:1])

            with contextlib.ExitStack() as st:
              if "s5b" not in _SKIP:
                sb = lambda name, shape, dt: st.enter_context(nc.sbuf_tensor(uniq(name), shape, dt))
                V = lambda e: e
                lam_r = sb("lam_r", [128, 32], F32)
                lam_i = sb("lam_i", [128, 32], F32)
                ls1 = sb("ls1", [128, 32], F32)
                st_tmp = contextlib.ExitStack()
                sbt = lambda name, shape, dt: st_tmp.enter_context(nc.sbuf_tensor(uniq(name), shape, dt))
                cr1 = sb("cr1", [128, 32, 16], F32)
                ci1 = sb("ci1", [128, 32, 16], F32)
                exps = sb("exps", [128, 4, 8], F32)
                mask = sb("mask", [128, 2, 128], F32)
                iota = sb("iota", [128, 544], F32)
                drep = sb("drep", [128, 32], F32)
                sm = [sb("sm%d" % i, [128, 32], F32) for i in range(14)]
                Bbr = sb("Bbr", [128, 32, 16], F32)
                Bbi = sb("Bbi", [128, 32, 16], F32)
                Wr = sb("Wr", [128, 4, 8, 32], F32)
                Wi = sb("Wi", [128, 4, 8, 32], F32)
                rho8 = sb("rho8", [128, 32], F32)
                tt8 = sb("tt8", [128, 32], F32)
                br1 = sbt("br1", [128, 32, 16], F32)
                bi1 = sbt("bi1", [128, 32, 16], F32)
                tb1 = sbt("tb1", [128, 32, 16], F32)
                tb2 = sbt("tb2", [128, 32, 16], F32)
                EA = sbt("EA", [128, 4, 8, 32], F32)
                ET = sbt("ET", [128, 4, 8, 32], F32)
                tw = sbt("tw", [128, 4, 8, 32], F32)
                dma("sp", lam_r[:], s5_lam1[j, 0], (), ["lam_r"])
                dma("sp", lam_i[:], s5_lam1[j, 1], (), ["lam_i"])
                for d_ in range(2):
                    dma("sp", ls1[d_ * 64:(d_ + 1) * 64, :], s5_ls[j, d_:d_ + 1, :].partition_broadcast(64), (), ["ls1"])
                dma("sp", br1[:], s5_b1[j, 0], (), ["br1"])
                dma("sp", bi1[:], s5_b1[j, 1], (), ["bi1"])
                dma("sp", cr1[:], s5_c1[j, 0], (), ["cr1"])
                dma("sp", ci1[:], s5_c1[j, 1], (), ["ci1"])
                dma("sp", exps[:], cEXPS, (), ["exps"])
                dma("sp", mask[:], cMASK, (), ["mask"])
                dma("sp", iota[:], cIOTA, (), ["iota"])
                dma("sp", drep[:], s5_drep[j], (), ["drep"])
                PK = ["pre"]

                def dv(fn):
                    A("dve", fn, PK + ["lam_r", "lam_i", "ls1", "br1", "bi1", "cr1", "ci1", "exps", "mask", "iota", "drep"], PK)

                def ac(fn):
                    A("act", fn, PK, PK)

                def sincos(tt, sn, cs, tmp, tmp2):
                    dv(lambda e: e.tensor_scalar(out=tmp, in0=tt, scalar1=MAG, scalar2=MAG, op0=ALU.add, op1=ALU.subtract))
                    dv(lambda e: e.tensor_tensor(out=tmp, in0=tt, in1=tmp, op=ALU.subtract))
                    ac(lambda e: e.activation(out=sn, in_=tmp, func=AF.Sin, scale=TWO_PI))
                    dv(lambda e: e.tensor_scalar(out=tmp2, in0=tt, scalar1=0.25, scalar2=None, op0=ALU.add))
                    dv(lambda e: e.tensor_scalar(out=tmp, in0=tmp2, scalar1=MAG, scalar2=MAG, op0=ALU.add, op1=ALU.subtract))
                    dv(lambda e: e.tensor_tensor(out=tmp, in0=tmp2, in1=tmp, op=ALU.subtract))
                    ac(lambda e: e.activation(out=cs, in_=tmp, func=AF.Sin, scale=TWO_PI))

                lr, dtt, a_, tht, mag1, s1, c1, w1r, w1i, den, cfr, cfi, x1, x2 = [t[:] for t in sm]
                dv(lambda e: e.tensor_scalar(out=lr, in0=lam_r[:], scalar1=-1e-4, scalar2=None, op0=ALU.min))
                ac(lambda e: e.activation(out=dtt, in_=ls1[:], func=AF.Exp))
                dv(lambda e: e.tensor_tensor(out=a_, in0=lr, in1=dtt, op=ALU.mult))
                dv(lambda e: e.tensor_tensor(out=tht, in0=lam_i[:], in1=dtt, op=ALU.mult))
                dv(lambda e: e.tensor_scalar(out=tht, in0=tht, scalar1=1.0 / TWO_PI, scalar2=None, op0=ALU.mult))
                ac(lambda e: e.activation(out=mag1, in_=a_, func=AF.Exp))
                sincos(tht, s1, c1, x1, x2)
                dv(lambda e: e.tensor_tensor(out=w1r, in0=mag1, in1=c1, op=ALU.mult))
                dv(lambda e: e.tensor_tensor(out=w1i, in0=mag1, in1=s1, op=ALU.mult))
                dv(lambda e: e.tensor_scalar(out=w1r, in0=w1r, scalar1=-1.0, scalar2=None, op0=ALU.add))
                dv(lambda e: e.tensor_tensor(out=den, in0=lr, in1=lr, op=ALU.mult))
                dv(lambda e: e.tensor_tensor(out=x1, in0=lam_i[:], in1=lam_i[:], op=ALU.mult))
                dv(lambda e: e.tensor_tensor(out=den, in0=den, in1=x1, op=ALU.add))
                dv(lambda e: e.reciprocal(out=den, in_=den))
                dv(lambda e: e.tensor_tensor(out=x1, in0=w1r, in1=lr, op=ALU.mult))
                dv(lambda e: e.tensor_tensor(out=x2, in0=w1i, in1=lam_i[:], op=ALU.mult))
                dv(lambda e: e.tensor_tensor(out=x1, in0=x1, in1=x2, op=ALU.add))
                dv(lambda e: e.tensor_tensor(out=cfr, in0=x1, in1=den, op=ALU.mult))
                dv(lambda e: e.tensor_tensor(out=x1, in0=w1i, in1=lr, op=ALU.mult))
                dv(lambda e: e.tensor_tensor(out=x2, in0=w1r, in1=lam_i[:], op=ALU.mult))
                dv(lambda e: e.tensor_tensor(out=x1, in0=x1, in1=x2, op=ALU.subtract))
                dv(lambda e: e.tensor_tensor(out=cfi, in0=x1, in1=den, op=ALU.mult))
                bc = lambda t: t.unsqueeze(2).to_broadcast([128, 32, 16])
                dv(lambda e: e.tensor_tensor(out=tb1[:], in0=br1[:], in1=bc(cfr), op=ALU.mult))
                dv(lambda e: e.tensor_tensor(out=tb2[:], in0=bi1[:], in1=bc(cfi), op=ALU.mult))
                dv(lambda e: e.tensor_tensor(out=Bbr[:], in0=tb1[:], in1=tb2[:], op=ALU.subtract))
                dv(lambda e: e.tensor_tensor(out=tb1[:], in0=bi1[:], in1=bc(cfr), op=ALU.mult))
                dv(lambda e: e.tensor_tensor(out=tb2[:], in0=br1[:], in1=bc(cfi), op=ALU.mult))
                dv(lambda e: e.tensor_tensor(out=Bbi[:], in0=tb1[:], in1=tb2[:], op=ALU.add))
                exb = exps[:].unsqueeze(3).to_broadcast([128, 4, 8, 32])
                ab = lambda t: t.unsqueeze(1).unsqueeze(1).to_broadcast([128, 4, 8, 32])
                dv(lambda e: e.tensor_tensor(out=EA[:], in0=exb, in1=ab(a_), op=ALU.mult))
                dv(lambda e: e.tensor_tensor(out=ET[:], in0=exb, in1=ab(tht), op=ALU.mult))
                ac(lambda e: e.activation(out=EA[:], in_=EA[:], func=AF.Exp))
                sincos(ET[:], Wi[:], Wr[:], tw[:], ET[:])
                dv(lambda e: e.tensor_tensor(out=Wr[:], in0=Wr[:], in1=EA[:], op=ALU.mult))
                dv(lambda e: e.tensor_tensor(out=Wi[:], in0=Wi[:], in1=EA[:], op=ALU.mult))
                dv(lambda e: e.tensor_scalar(out=x1, in0=a_, scalar1=8.0, scalar2=None, op0=ALU.mult))
                ac(lambda e: e.activation(out=rho8[:], in_=x1, func=AF.Exp))
                dv(lambda e: e.tensor_scalar(out=tt8[:], in0=tht, scalar1=8.0, scalar2=None, op0=ALU.mult))

                st_tmp.close()
                df.barrier(bart[:, 0:1])
                PK = ["pre"]
                KT = sb("KT", [128, 4, 128], BF16)
                ELTr = sb("ELTr", [128, 4, 128], BF16)
                ELTi = sb("ELTi", [128, 4, 128], BF16)
                CLr = sb("CLr", [128, 4, 128], BF16)
                nCLi = sb("nCLi", [128, 4, 128], BF16)
                Rr = sb("Rr", [128, 4, 128], BF16)
                Ri = sb("Ri", [128, 4, 128], BF16)
                Qr = sb("Qr", [128, 4, 128], BF16)
                nQi = sb("nQi", [128, 4, 128], BF16)
                ELr = sb("ELr", [128, 4, 128], BF16)
                ELi = sb("ELi", [128, 4, 128], BF16)
                scr = mT[:, 4:8, :].rearrange("p c t -> p (c t)").bitcast(F32).rearrange("p (n b) -> p n b", b=544)
                tA = sb("tAs", [128, 544], F32)[:]
                tB = sb("tBs", [128, 544], F32)[:]
                xtra = [sb("xs5_%d" % i, [128, 544], F32)[:] for i in range(8)]
                pool_tiles = [scr[:, i, :] for i in range(16)] + xtra
                SETS = []
                for si_ in range(4):
                    d_ = {}
                    for ti_, nm in enumerate(("Er", "Ei", "cos", "sin", "gr", "gi")):
                        d_[nm] = pool_tiles[si_ * 6 + ti_]
                    d_["Zr"] = sb("Zr%d" % si_, [128, 544], BF16)
                    d_["Zi"] = sb("Zi%d" % si_, [128, 544], BF16)
                    d_["zg"] = sb("zg%d" % si_, [128, 544], BF16)
                    d_["id"] = si_
                    SETS.append(d_)
                    A("pool", lambda e, d_=d_: e.memset(d_["Zr"][:], 0.0), (), ["Zr%d" % si_])
                    A("pool", lambda e, d_=d_: e.memset(d_["Zi"][:], 0.0), (), ["Zi%d" % si_])

                class _V:
                    def __init__(self, ap):
                        self.ap = ap

                    def __getitem__(self, k):
                        return self.ap[k]
                p1 = _V(tA[:, 0:512].rearrange("p (g c) -> p g c", g=4))
                p2 = _V(tB[:, 0:512].rearrange("p (g c) -> p g c", g=4))
                PK = ["pre", "tA", "tB"]

                def cprod(outr, outi, l, Xr_, Xi_, g0, neg_i):
                    wv = lambda W_: W_[:, l, :, g0:g0 + 4].rearrange("p s g -> p g s").unsqueeze(3).to_broadcast([128, 4, 8, 16])
                    xv = lambda X_: X_[:, g0:g0 + 4, :].unsqueeze(2).to_broadcast([128, 4, 8, 16])
                    o4 = lambda t: t[:].rearrange("p g (s m) -> p g s m", m=16)
                    dv(lambda e: e.tensor_tensor(out=o4(p1), in0=wv(Wr), in1=xv(Xr_), op=ALU.mult))
                    dv(lambda e: e.tensor_tensor(out=o4(p2), in0=wv(Wi), in1=xv(Xi_), op=ALU.mult))
                    dv(lambda e: e.tensor_tensor(out=outr[:], in0=p1[:], in1=p2[:], op=ALU.subtract))
                    dv(lambda e: e.tensor_tensor(out=o4(p1), in0=wv(Wr), in1=xv(Xi_), op=ALU.mult))
                    dv(lambda e: e.tensor_tensor(out=o4(p2), in0=wv(Wi), in1=xv(Xr_), op=ALU.mult))
                    if neg_i:
                        dv(lambda e: e.scalar_tensor_tensor(out=outi[:], in0=p1[:], scalar=-1.0, in1=p2[:], op0=ALU.mult, op1=ALU.subtract))
                    else:
                        dv(lambda e: e.tensor_tensor(out=outi[:], in0=p1[:], in1=p2[:], op=ALU.add))

                for qq in range(8):
                    g0 = qq * 4
                    cprod(Rr, Ri, 0, Bbr, Bbi, g0, False)
                    cprod(ELr, ELi, 1, Bbr, Bbi, g0, False)
                    cprod(Qr, nQi, 2, cr1, ci1, g0, True)
                    cprod(CLr, nCLi, 3, cr1, ci1, g0, True)
                    for q in range(4):
                        for h_ in range(2):
                            hb = h_ * 64
                            bank = mm[h_]
                            bk = "mm%d" % h_
                            A("pe", lambda e, bank=bank, hb=hb, q=q: e.matmul(bank[:, 0:128], lhsT=Rr[hb:hb + 64, q, :], rhs=Qr[hb:hb + 64, q, :], start=True, stop=False), PK, [bk])
                            A("pe", lambda e, bank=bank, hb=hb, q=q: e.matmul(bank[:, 0:128], lhsT=Ri[hb:hb + 64, q, :], rhs=nQi[hb:hb + 64, q, :], start=False, stop=True), PK, [bk])
                        A("dve", lambda e: e.tensor_tensor(out=p1[:, 0, :], in0=mm[0][:, 0:128], in1=mask[:, 0, :], op=ALU.mult), ["mm0"] + PK, PK)
                        A("dve", lambda e: e.tensor_tensor(out=p2[:, 0, :], in0=mm[1][:, 0:128], in1=mask[:, 1, :], op=ALU.mult), ["mm1"] + PK, PK)
                        A("dve", lambda e: e.tensor_tensor(out=p1[:, 0, :], in0=p1[:, 0, :], in1=p2[:, 0, :], op=ALU.add), PK, PK)
                        A("dve", lambda e, q=q, g0=g0: e.scalar_tensor_tensor(out=KT[:, q, :], in0=idn32[:], scalar=drep[:, g0 + q:g0 + q + 1], in1=p1[:, 0, :], op0=ALU.mult, op1=ALU.add), PK + ["drep", "idn32"], PK)
                        A("pe", lambda e, q=q: e.transpose(trp[:, 0, :], ELr[:, q, :], idn[:]), PK + ["idn"], ["trp"])
                        A("pe", lambda e, q=q: e.transpose(trp[:, 1, :], ELi[:, q, :], idn[:]), PK + ["idn"], ["trp"])
                        A("act", lambda e, q=q: e.copy(out=ELTr[:, q, :], in_=trp[:, 0, :]), ["trp"] + PK, PK)
                        A("act", lambda e, q=q: e.copy(out=ELTi[:, q, :], in_=trp[:, 1, :]), ["trp"] + PK, PK)
                    def grp(q, g, S):
                        sid = S["id"]
                        K_ = lambda nm: "%s%d" % (nm, sid)
                        Eb = stp[sid % 2]
                        ebk = "stp%d" % (sid % 2)
                        pvo = sid * 128
                        Er, Ei, cosT, sinT, gr, gi = S["Er"], S["Ei"], S["cos"], S["sin"], S["gr"], S["gi"]
                        sr, si = Er, Ei
                        Zr_, Zi_, z_ = S["Zr"], S["Zi"], S["zg"]

                        def st0():
                            for ri, ELT_ in enumerate((ELTr, ELTi)):
                                for (b0, nb) in PIECES:
                                    if b0 < 512:
                                        yo = Eb[:, ri * 512 + b0:ri * 512 + b0 + nb]
                                        wk_ = [ebk]
                                    else:
                                        yo = pv[:, pvo + ri * 64:pvo + ri * 64 + nb]
                                        wk_ = ["pv"]
                                    A("pe", lambda e, yo=yo, ELT_=ELT_, b0=b0, nb=nb: e.matmul(yo, lhsT=ELT_[:, q, :], rhs=Uv[:, g, b0:b0 + nb], start=True, stop=True), PK + ["U"], wk_)

                        def st1():
                            for ri, E_ in enumerate((Er, Ei)):
                                ek = K_("Er" if ri == 0 else "Ei")
                                A("act", lambda e, E_=E_, ri=ri: e.copy(out=E_[0:64, 32:544], in_=Eb[0:64, ri * 512:(ri + 1) * 512]), [ebk], [ek])
                                A("act", lambda e, E_=E_, ri=ri: e.copy(out=E_[0:64, 0:32], in_=pv[0:64, pvo + ri * 64:pvo + ri * 64 + 32]), ["pv"], [ek])
                                A("act", lambda e, E_=E_, ri=ri: e.copy(out=E_[64:128, 543:31:-1], in_=Eb[64:128, ri * 512:(ri + 1) * 512]), [ebk], [ek])
                                A("act", lambda e, E_=E_, ri=ri: e.copy(out=E_[64:128, 31::-1], in_=pv[64:128, pvo + ri * 64:pvo + ri * 64 + 32]), ["pv"], [ek])
                            A("act", lambda e: e.activation(out=gr, in_=iota[:], func=AF.Copy, scale=tt8[:, g:g + 1]), PK + ["iota", K_("gr")], [K_("gr")])
                            A("act", lambda e: e.activation(out=gi, in_=gr, func=AF.Copy, bias=MAG), [K_("gr"), K_("gi")], [K_("gi")])

                        def st2():
                            A("act", lambda e: e.activation(out=gi, in_=gi, func=AF.Copy, bias=-MAG), [K_("gi")], [K_("gi")])
                            A("dve", lambda e: e.tensor_tensor(out=gi, in0=gr, in1=gi, op=ALU.subtract), [K_("gr"), K_("gi")], [K_("gi")])

                        def st3():
                            A("act", lambda e: e.activation(out=sinT, in_=gi, func=AF.Sin, scale=TWO_PI), [K_("gi")], [K_("sin")])
                            A("act", lambda e: e.activation(out=gr, in_=gr, func=AF.Copy, bias=0.25), [K_("gr")], [K_("gr")])
                            A("act", lambda e: e.activation(out=gi, in_=gr, func=AF.Copy, bias=MAG), [K_("gr"), K_("gi"), K_("sin")], [K_("gi")])

                        def st4():
                            A("act", lambda e: e.activation(out=gi, in_=gi, func=AF.Copy, bias=-MAG), [K_("gi")], [K_("gi")])
                            A("dve", lambda e: e.tensor_tensor(out=gi, in0=gr, in1=gi, op=ALU.subtract), [K_("gr"), K_("gi")], [K_("gi")])

                        def st5():
                            A("act", lambda e: e.activation(out=cosT, in_=gi, func=AF.Sin, scale=TWO_PI), [K_("gi")], [K_("cos")])


                        def st6a():
                            A("dve", lambda e: e.tensor_tensor(out=gr, in0=Er, in1=cosT, op=ALU.mult), [K_("Er"), K_("cos"), K_("gr")], [K_("gr")])
                            A("dve", lambda e: e.tensor_tensor(out=tA, in0=Ei, in1=sinT, op=ALU.mult), [K_("Ei"), K_("sin")], ["tA"])
                            A("dve", lambda e: e.tensor_tensor(out=gi, in0=Ei, in1=cosT, op=ALU.mult), [K_("Ei"), K_("cos"), K_("gi")], [K_("gi")])
                            A("dve", lambda e: e.tensor_tensor(out=tB, in0=Er, in1=sinT, op=ALU.mult), [K_("Er"), K_("sin")], ["tB"])

                        def st6b():
                            A("dve", lambda e: e.tensor_tensor(out=gr, in0=gr, in1=tA, op=ALU.add), [K_("gr"), "tA"], [K_("gr")])
                            A("dve", lambda e: e.tensor_tensor(out=gi, in0=gi, in1=tB, op=ALU.subtract), [K_("gi"), "tB"], [K_("gi")])

                        def st7():
                            rb = rho8[:, g:g + 1].to_broadcast([128, 544])
                            A("dve", lambda e: e.tensor_tensor_scan(out=sr, data0=rb, data1=gr, initial=0.0, op0=ALU.mult, op1=ALU.add), [K_("gr")] + PK, [K_("Er")])
                            A("dve", lambda e: e.tensor_tensor_scan(out=si, data0=rb, data1=gi, initial=0.0, op0=ALU.mult, op1=ALU.add), [K_("gi")] + PK, [K_("Ei")])

                        def st8a():
                            A("dve", lambda e: e.tensor_tensor(out=gr, in0=sr, in1=cosT, op=ALU.mult), [K_("Er"), K_("cos"), K_("gr")], [K_("gr")])
                            A("dve", lambda e: e.tensor_tensor(out=gi, in0=si, in1=sinT, op=ALU.mult), [K_("Ei"), K_("sin"), K_("gi")], [K_("gi")])

                        def wr(Z_, zk, op):
                            A("dve", lambda e: e.tensor_tensor(out=Z_[0:64, 0:512], in0=gr[0:64, 31:543], in1=gi[0:64, 31:543], op=op), [K_("gr"), K_("gi")], [zk])
                            A("dve", lambda e: e.tensor_tensor(out=Z_[0:64, 513:544], in0=gr[0:64, 0:31], in1=gi[0:64, 0:31], op=op), [K_("gr"), K_("gi")], [zk])
                            A("dve", lambda e: e.tensor_tensor(out=Z_[64:128, 542::-1], in0=gr[64:128, 0:543], in1=gi[64:128, 0:543], op=op), [K_("gr"), K_("gi")], [zk])

                        def st8b():
                            wr(Zr_, K_("Zr"), ALU.subtract)

                        def st8c():
                            A("dve", lambda e: e.tensor_tensor(out=gr, in0=sr, in1=sinT, op=ALU.mult), [K_("Er"), K_("sin"), K_("gr")], [K_("gr")])
                            A("dve", lambda e: e.tensor_tensor(out=gi, in0=si, in1=cosT, op=ALU.mult), [K_("Ei"), K_("cos"), K_("gi")], [K_("gi")])

                        def st8d():
                            wr(Zi_, K_("Zi"), ALU.add)

                        def st8():
                            for (b0, nb) in PIECES:
                                if b0 < 512:
                                    yo = mm[0][:, b0:b0 + nb]
                                    wk_ = ["mm0"]
                                else:
                                    yo = mm[1][:, 0:nb]
                                    wk_ = ["mm1"]
                                A("pe", lambda e, yo=yo, b0=b0, nb=nb: e.matmul(yo, lhsT=KT[:, q, :], rhs=Uv[:, g, b0:b0 + nb], start=True, stop=False), PK + ["U"], wk_)
                                A("pe", lambda e, yo=yo, b0=b0, nb=nb: e.matmul(yo, lhsT=CLr[:, q, :], rhs=Zr_[:, b0:b0 + nb], start=False, stop=False), PK + [K_("Zr")], wk_)
                                A("pe", lambda e, yo=yo, b0=b0, nb=nb: e.matmul(yo, lhsT=nCLi[:, q, :], rhs=Zi_[:, b0:b0 + nb], start=False, stop=True), PK + [K_("Zi")], wk_)
                            A("act", lambda e: e.activation(out=z_[:, 0:512], in_=mm[0][:, 0:512], func=AF.Gelu), ["mm0"], [K_("zg")])
                            A("act", lambda e: e.activation(out=z_[:, 512:544], in_=mm[1][:, 0:32], func=AF.Gelu), ["mm1"], [K_("zg")])
                            for t8 in range(8):
                                dma("sp", zs[t8, g], z_[t8 * 16:(t8 + 1) * 16, :], [K_("zg")], ["zs"])

                        return [st0, st1], [st2, st3, st4, st5], [lambda: (st6a(), st6b()), st7, st8a, st8b, st8c, st8d], st8

                    G4 = [grp(q_, g0 + q_, SETS[q_]) for q_ in range(4)]
                    for pr in range(2):
                        for k_ in range(2):
                            for q_ in (2 * pr, 2 * pr + 1):
                                G4[q_][0][k_]()
                    for k_ in range(4):
                        for q_ in range(4):
                            G4[q_][1][k_]()
                    for k_ in range(6):
                        for q_ in range(4):
                            G4[q_][2][k_]()
                    for q_ in range(4):
                        G4[q_][3]()
            df.barrier(bart[:, 0:1])

            with contextlib.ExitStack() as st:
              if "s5c" not in _SKIP:
                sb = lambda name, shape, dt: st.enter_context(nc.sbuf_tensor(uniq(name), shape, dt))
                Wg = sb("Wg", [128, 4, 1024], BF16)
                bg = sb("bg", [128, 8], F32)
                sgd = sb("sgd", [128, T], BF16)
                zsb = [sb("zsb%d" % i, [128, 4, 544], BF16) for i in range(2)]
                sig = [sb("sig%d" % i, [128, 544], F32) for i in range(2)]
                v1 = [sb("v1%d" % i, [128, 544], F32) for i in range(2)]
                wgd = sb("wgd", [128, 8, 128], BF16)
                for h_ in range(2):
                    dma("pool", Wg[:, :, h_ * 512:(h_ + 1) * 512], glu_w[j].rearrange("(c p) n -> p c n", p=128)[:, :, h_ * 512:(h_ + 1) * 512], (), ["Wg"])
                dma("sp", bg[:], glu_bT[j], (), ["bg"])
                for k in range(4):
                    dma("pool", wgd[:], Wd[:, :, 2048 + k * 128:2048 + (k + 1) * 128], (), ["wgd"])
                    inproj_fm(wgd, "wgd", NTILES, lambda bank, bk, n0, nn: A(
                        "act", lambda e: e.activation(out=sgd[:, n0:n0 + nn], in_=bank[:, 0:nn], func=AF.Silu), [bk], ["sgd"]))

                    def banks(n):
                        if n % 2 == 0:
                            return (mm[0], "mm0"), (mm[1], "mm1"), (pv[:, 0:32], "pv"), (pv[:, 256:288], "pv")
                        return (stp[0][:, 0:512], "stp0a"), (stp[0][:, 512:1024], "stp0b"), (stp[1][:, 0:32], "stp1a"), (stp[1][:, 512:544], "stp1b")

                    def c_load(n):
                        zb, zbk = zsb[n % 2], "zsb%d" % (n % 2)
                        dma("sp", zb[:], zs[n].rearrange("g m b -> (g m) b").rearrange("(c p) b -> p c b", p=128), ["zs"], [zbk])

                    def c_mm(n):
                        zb, zbk = zsb[n % 2], "zsb%d" % (n % 2)
                        (vb, vk), (gb_, gk), (vp, vpk), (gp_, gpk) = banks(n)
                        for (b0, nb) in PIECES:
                            for vg in range(2):
                                if b0 < 512:
                                    yo = (vb if vg == 0 else gb_)[:, b0:b0 + nb]
                                    wk_ = [vk if vg == 0 else gk]
                                else:
                                    yo = vp if vg == 0 else gp_
                                    wk_ = [vpk if vg == 0 else gpk]
                                col = (vg * 4 + k) * 128
                                for c in range(4):
                                    A("pe", lambda e, yo=yo, c=c, col=col, zb=zb, b0=b0, nb=nb: e.matmul(yo, lhsT=Wg[:, c, col:col + 128], rhs=zb[:, c, b0:b0 + nb], start=(c == 0), stop=(c == 3)),
                                      ["Wg", zbk], wk_)

                    def c_sig(n):
                        (vb, vk), (gb_, gk), (vp, vpk), (gp_, gpk) = banks(n)
                        sg_, sgk = sig[n % 2], "sig%d" % (n % 2)
                        A("act", lambda e, gb_=gb_, sg_=sg_, k=k: e.activation(out=sg_[:, 0:512], in_=gb_[:, 0:512], func=AF.Sigmoid, bias=bg[:, 4 + k:5 + k]), [gk, "bg"], [sgk])
                        A("act", lambda e, gp_=gp_, sg_=sg_, k=k: e.activation(out=sg_[:, 512:544], in_=gp_, func=AF.Sigmoid, bias=bg[:, 4 + k:5 + k]), [gpk, "bg"], [sgk])

                    def c_stt(n):
                        (vb, vk), (gb_, gk), (vp, vpk), (gp_, gpk) = banks(n)
                        sg_, sgk = sig[n % 2], "sig%d" % (n % 2)
                        v_, v1k = v1[n % 2], "v1%d" % (n % 2)
                        A("dve", lambda e, vb=vb, sg_=sg_, v_=v_, k=k: e.scalar_tensor_tensor(out=v_[:, 0:512], in0=vb[:, 0:512], scalar=bg[:, k:k + 1], in1=sg_[:, 0:512], op0=ALU.add, op1=ALU.mult), [vk, sgk, "bg"], [v1k])
                        A("dve", lambda e, vp=vp, sg_=sg_, v_=v_, k=k: e.scalar_tensor_tensor(out=v_[:, 512:544], in0=vp, scalar=bg[:, k:k + 1], in1=sg_[:, 512:544], op0=ALU.add, op1=ALU.mult), [vpk, sgk, "bg"], [v1k])

                    def c_out(n):
                        v_, v1k = v1[n % 2], "v1%d" % (n % 2)
                        A("pool", lambda e, v_=v_, n=n, k=k: e.tensor_tensor(out=mT[:, 4 + k, n::8], in0=v_[:], in1=sgd[:, n::8], op=ALU.mult), [v1k, "sgd"], [("mT", 4 + k, t) for t in range(NT)])

                    pipeline([c_load, c_mm, c_sig, c_stt, c_out], 8, "s5c")
            df.barrier(bart[:, 0:1])

            with contextlib.ExitStack() as st:
              if "gmlp" not in _SKIP:
                sb = lambda name, shape, dt: st.enter_context(nc.sbuf_tensor(uniq(name), shape, dt))
                Wv = sb("Wv", [128, 8, 512], BF16)
                Wu = sb("Wu", [128, 8, 512], BF16)
                Wc = sb("Wc", [128, 8, 512], BF16)
                wsT = sb("wsT", [128, 4, 128], BF16)
                sgT = sb("sgT", [128, 4], F32)
                bsb = sb("bsb", [128, 512], F32)
                st6 = sb("st6", [128, 12], F32)
                mv = sb("mv", [128, 4 * NT], F32)
                vn = [sb("vn%d" % i, [128, 512], BF16) for i in range(2)]
                sgc = [sb("sgc%d" % i, [128, 512], BF16) for i in range(2)]
                mx = [sb("mx%d" % i, [128, 512], F32) for i in range(2)]
                dma("pool", Wu[:], Wd[:, :, 0:512], (), ["Wu"])
                dma("pool", Wv[:], Wd[:, :, 512:1024], (), ["Wv"])
                dma("pool", Wc[:], Wd[:, :, 1024:1536], (), ["Wc"])
                dma("pool", wsT[:], sgu_wT[j], (), ["wsT"])
                dma("sp", sgT[:], sgu_gT[j], (), ["sgT"])
                dma("sp", bsb[:], sgu_b[j:j + 1, :].partition_broadcast(128), (), ["bsb"])
                vsb = [sb("vsb%d" % i, [128, 512], F32) for i in range(4)]
                tmx = [sb("tmx%d" % i, [128, 512], F32) for i in range(2)]
                MEAN = lambda n: mv[:, 4 * n:4 * n + 1]
                VAR = lambda n: mv[:, 4 * n + 1:4 * n + 2]
                RSg = lambda n: mv[:, 4 * n + 2:4 * n + 3]
                VS = lambda n: (vsb[n % 4], "vsb%d" % (n % 4))

                def g_v(n):
                    bank, bk = mm[n % 2], "mm%d" % (n % 2)
                    for k in range(8):
                        A("pe", lambda e, bank=bank, k=k, n=n: e.matmul(bank[:, :], lhsT=xb[:, k, n * 128:(n + 1) * 128], rhs=Wv[:, k, :], start=(k == 0), stop=(k == 7)), ["Wv", ("xb", n)], [bk])

                def g_cp(n):
                    bank, bk = mm[n % 2], "mm%d" % (n % 2)
                    vs_, vsk = VS(n)
                    A("act", lambda e, bank=bank, vs_=vs_: e.copy(out=vs_[:], in_=bank[:, :]), [bk], [vsk])

                def g_bn(n):
                    vs_, vsk = VS(n)
                    s6 = st6[:, (n % 2) * 6:(n % 2) * 6 + 6]
                    A("dve", lambda e, vs_=vs_, s6=s6: e.bn_stats(out=s6, in_=vs_[:]), [vsk], [("st6", n % 2)])
                    A("dve", lambda e, n=n, s6=s6: e.bn_aggr(out=mv[:, 4 * n:4 * n + 2], in_=s6), [("st6", n % 2)], [("mv", n)])

                def g_r1(n):
                    A("dve", lambda e, n=n: e.tensor_scalar(out=RSg(n), in0=VAR(n), scalar1=1.0, scalar2=EPS, op0=ALU.mult, op1=ALU.add), [("mv", n)], [("grs", n)])

                def g_r2(n):
                    A("act", lambda e, n=n: e.activation(out=RSg(n), in_=RSg(n), func=AF.Sqrt), [("grs", n)], [("grs", n)])

                def g_vn(n):
                    vs_, vsk = VS(n)
                    v_, vk = vn[n % 2], "vn%d" % (n % 2)
                    A("dve", lambda e, n=n: e.reciprocal(out=RSg(n), in_=RSg(n)), [("grs", n)], [("grs", n)])
                    A("dve", lambda e, vs_=vs_, v_=v_, n=n: e.tensor_scalar(out=v_[:], in0=vs_[:], scalar1=MEAN(n), scalar2=RSg(n), op0=ALU.subtract, op1=ALU.mult), [vsk, ("mv", n), ("grs", n)], [vk])

                def g_mm(n):
                    v_, vk = vn[n % 2], "vn%d" % (n % 2)
                    for g in range(4):
                        A("pe", lambda e, v_=v_, g=g: e.matmul(pv[:, g * 128:(g + 1) * 128], lhsT=v_[:, g * 128:(g + 1) * 128], rhs=wsT[:, g, :], start=True, stop=True), [vk, "wsT"], ["pv"])
                    sp_ = stp[n % 2]
                    for half, W_, wkk in ((0, Wu, "Wu"), (1, Wc, "Wc")):
                        sk = "stp%d%s" % (n % 2, "ab"[half])
                        for g in range(4):
                            for k in range(8):
                                A("pe", lambda e, sp_=sp_, half=half, W_=W_, g=g, k=k, n=n: e.matmul(sp_[:, half * 512 + g * 128:half * 512 + (g + 1) * 128], lhsT=W_[:, k, g * 128:(g + 1) * 128], rhs=xb[:, k, n * 128:(n + 1) * 128], start=(k == 0), stop=(k == 7)),
                                  [wkk, ("xb", n)], [sk])

                def g_ep(n):
                    sp_ = stp[n % 2]
                    c_, ck = sgc[n % 2], "sgc%d" % (n % 2)
                    m_, mk = mx[n % 2], "mx%d" % (n % 2)
                    t_, tk_ = tmx[n % 2], "tmx%d" % (n % 2)
                    A("act", lambda e, sp_=sp_, c_=c_: e.activation(out=c_[:], in_=sp_[:, 512:1024], func=AF.Silu), ["stp%db" % (n % 2)], [ck])
                    for g in range(4):
                        A("dve", lambda e, m_=m_, g=g: e.scalar_tensor_tensor(out=m_[:, g * 128:(g + 1) * 128], in0=pv[:, g * 128:(g + 1) * 128], scalar=sgT[:, g:g + 1], in1=bsb[:, g * 128:(g + 1) * 128], op0=ALU.mult, op1=ALU.add),
                          ["pv", "sgT", "bsb"], [mk])
                    A("dve", lambda e, m_=m_, sp_=sp_, t_=t_: e.tensor_tensor(out=t_[:], in0=m_[:], in1=sp_[:, 0:512], op=ALU.mult), [mk, "stp%da" % (n % 2)], [tk_])

                def g_out(n):
                    c_, ck = sgc[n % 2], "sgc%d" % (n % 2)
                    t_, tk_ = tmx[n % 2], "tmx%d" % (n % 2)
                    A("pool", lambda e, t_=t_, c_=c_, n=n: e.tensor_tensor(out=mT[:, 0:4, n * 128:(n + 1) * 128], in0=t_[:].rearrange("p (g c) -> p g c", g=4), in1=c_[:].rearrange("p (g c) -> p g c", g=4), op=ALU.mult),
                      [tk_, ck], [("mT", g, n) for g in range(4)])

                pipeline([g_v, g_cp, g_bn, g_r1, g_r2, g_vn, g_mm, g_ep, g_out], NT, "gmlp")
            df.barrier(bart[:, 0:1])

        for L in range(n_layers):
            last = (L == n_layers - 1)
            (prenorm_old if "oldpre" in _SKIP else prenorm)(L)
            if L % 2 == 0:
                if "even" not in _SKIP:
                    even_mixer(L)
            else:
                odd_mixer(L)
            (post_old if "oldpost" in _SKIP else post)(L, last)
        df.emit()
    return nc


_CACHE = {}


def kernel(x, c, ctx, c_ctx, w_ada, b_ada, pre_g, post_g, w_in_even, w_out_even, na_rpb,
           w_in_odd, w_out_odd, sgu_w, sgu_b, sgu_g, s5_lam_re, s5_lam_im, s5_log_step,
           s5_b_re, s5_b_im, s5_c_re, s5_c_im, s5_d, glu_w, glu_b, _n_layers=DEPTH):
    f = lambda a: np.ascontiguousarray(np.asarray(a), dtype=np.float32)
    B = x.shape[0]
    if _n_layers not in _CACHE:
        _CACHE[_n_layers] = build_program(_n_layers)
    nc = _CACHE[_n_layers]
    F1, H, CH, F256 = fnet_consts()
    eb = np.stack([build_ebias(f(na_rpb[j])) for j in range(2)]).reshape(2, 12, 128, 21 * 128)
    shared = dict(w_ada=f(w_ada), b_ada=f(b_ada), pre_g=f(pre_g), post_g=f(post_g), w_in_even=f(w_in_even),
                  w_out_even=f(w_out_even), ebias=eb, cF1=F1, cH=H, cCH=CH, cF256=F256)
    ex, mk, io = s5_consts()
    lam1 = np.stack([f(s5_lam_re), f(s5_lam_im)], axis=1)
    lam1 = np.ascontiguousarray(lam1.transpose(0, 1, 2, 4, 3)).reshape(2, 2, 128, 32)
    b1 = np.stack([f(s5_b_re), f(s5_b_im)], axis=1)
    b1 = np.ascontiguousarray(b1.transpose(0, 1, 2, 4, 3, 5)).reshape(2, 2, 128, 32, 16)
    c1 = np.stack([f(s5_c_re), f(s5_c_im)], axis=1)
    c1 = np.ascontiguousarray(c1.transpose(0, 1, 2, 5, 3, 4)).reshape(2, 2, 128, 32, 16)
    drep = np.ascontiguousarray(np.tile(f(s5_d).reshape(2, 32, 16).transpose(0, 2, 1), (1, 8, 1)))
    shared.update(dict(
        w_in_odd=f(w_in_odd), w_out_odd=f(w_out_odd),
        sgu_wT=np.ascontiguousarray(f(sgu_w).transpose(0, 3, 1, 2)),
        sgu_gT=np.ascontiguousarray(f(sgu_g).reshape(2, 4, 128).transpose(0, 2, 1)),
        sgu_b=f(sgu_b).reshape(2, 512), s5_lam1=lam1, s5_ls=f(s5_log_step), s5_b1=b1, s5_c1=c1, s5_drep=drep,
        glu_w=f(glu_w), glu_bT=np.ascontiguousarray(f(glu_b).reshape(2, 8, 128).transpose(0, 2, 1)),
        cEXPS=ex, cMASK=mk, cIOTA=io))
    cc = f(c_ctx).reshape(8, 128).T
    in_maps = []
    for b in range(B):
        m = dict(shared)
        m["xr"] = np.concatenate([f(x[b]), f(ctx[b])], axis=0)
        m["cT"] = np.ascontiguousarray(np.concatenate([f(c[b]).reshape(8, 128).T, cc], axis=1))
        in_maps.append(m)
    res = run_bass_kernel_spmd(nc, in_maps, core_ids=list(range(B)))
    return np.stack([np.asarray(r["out"], dtype=np.float32) for r in res.results], axis=0)
```
